# Optimizing a Trainium2 kernel written in Bass

```python
import jax, jax.numpy as jnp
from jax import lax
import numpy as np

D_MODEL = 1024
BATCH = 2
SEQ = 16384
DEPTH = 1

CHUNK = 64
EPS = 1e-6
GDN_HEADS = 4
GDN_DK = 128
GDN_DV = 128
GDN_CONV = 4
ATT_HEADS = 8
ATT_DH = 64
ATT_BAND = 9
REL_CLIP = 128
D_FF = 2816
FFN_CONV = 3

KEY_A = GDN_HEADS * GDN_DK
VAL_A = GDN_HEADS * GDN_DV
WIDTH_B = ATT_HEADS * ATT_DH
IN_SIZES = (KEY_A, KEY_A, VAL_A, VAL_A, GDN_HEADS, GDN_HEADS,
            WIDTH_B, WIDTH_B, WIDTH_B, D_MODEL, D_MODEL)
IN_SPLITS = tuple(sum(IN_SIZES[:i + 1]) for i in range(len(IN_SIZES) - 1))
D_IN = sum(IN_SIZES)
CONV_A = 2 * KEY_A + VAL_A

kernel_name = "hybrid_gdn_bandattn_convffn"


def rmsnorm(x, w):
    xf = x.astype(jnp.float32)
    y = xf * lax.rsqrt(jnp.mean(xf * xf, axis=-1, keepdims=True) + EPS)
    return (y * w.astype(jnp.float32)).astype(x.dtype)


def l2norm(x):
    xf = x.astype(jnp.float32)
    return xf * lax.rsqrt(jnp.sum(xf * xf, axis=-1, keepdims=True) + EPS)


def causal_dwconv(x, w):
    width = w.shape[0]
    return lax.conv_general_dilated(
        x, w[:, None, :].astype(x.dtype), window_strides=(1,), padding=[(width - 1, 0)],
        dimension_numbers=('NWC', 'WIO', 'NWC'), feature_group_count=x.shape[-1])


def gated_delta_rule(q, k, v, g, beta):
    b_, t_, h_, dk = q.shape
    dv = v.shape[-1]
    n = t_ // CHUNK

    def to_chunks(a):
        return a.astype(jnp.float32).reshape(b_, n, CHUNK, h_, -1).transpose(1, 0, 3, 2, 4)

    q = to_chunks(q) * (dk ** -0.5)
    k = to_chunks(k)
    v = to_chunks(v)
    g = g.astype(jnp.float32).reshape(b_, n, CHUNK, h_).transpose(1, 0, 3, 2)
    beta = beta.astype(jnp.float32).reshape(b_, n, CHUNK, h_).transpose(1, 0, 3, 2)

    G = jnp.cumsum(g, axis=-1)
    idx = jnp.arange(CHUNK)
    strict = idx[:, None] > idx[None, :]
    incl = idx[:, None] >= idx[None, :]
    diff = G[..., :, None] - G[..., None, :]
    dec_strict = jnp.exp(jnp.where(strict, diff, -jnp.inf))
    dec_incl = jnp.exp(jnp.where(incl, diff, -jnp.inf))
    gam = jnp.exp(G)

    a_mat = beta[..., :, None] * jnp.einsum('nbhid,nbhjd->nbhij', k, k) * dec_strict
    eye = jnp.eye(CHUNK, dtype=jnp.float32)
    rhs = jnp.concatenate([(beta * gam)[..., None] * k, beta[..., None] * v], axis=-1)
    sol = lax.linalg.triangular_solve(a_mat + eye, rhs, left_side=True, lower=True)
    w_c = sol[..., :dk]
    uv_c = sol[..., dk:]
    p_c = jnp.einsum('nbhid,nbhjd->nbhij', q, k) * dec_incl
    qg_c = q * gam[..., None]
    kd_c = k * jnp.exp(G[..., -1:] - G)[..., None]
    gl_c = gam[..., -1]

    def step(s, xs):
        w_i, uv_i, p_i, qg_i, kd_i, gl_i = xs
        u = uv_i - jnp.einsum('bhid,bhde->bhie', w_i, s)
        o = jnp.einsum('bhid,bhde->bhie', qg_i, s) + jnp.einsum('bhij,bhje->bhie', p_i, u)
        s = gl_i[..., None, None] * s + jnp.einsum('bhid,bhie->bhde', kd_i, u)
        return s, o

    s0 = jnp.zeros((b_, h_, dk, dv), jnp.float32)
    _, o = lax.scan(step, s0, (w_c, uv_c, p_c, qg_c, kd_c, gl_c))
    return o.transpose(1, 0, 3, 2, 4).reshape(b_, t_, h_, dv)


def chunk_band_attention(q, k, v, rel_table):
    b_, t_, h_, dh = q.shape
    n = t_ // CHUNK
    band = ATT_BAND * CHUNK
    lead = (ATT_BAND - 1) * CHUNK
    kp = jnp.pad(k, ((0, 0), (lead, 0), (0, 0), (0, 0)))
    vp = jnp.pad(v, ((0, 0), (lead, 0), (0, 0), (0, 0)))
    r = jnp.arange(CHUNK)
    j = jnp.arange(band)
    dist = lead + r[:, None] - j[None, :]
    bias = rel_table.astype(jnp.float32)[:, jnp.clip(dist, -REL_CLIP, REL_CLIP) + REL_CLIP]
    scale = dh ** -0.5

    def one_chunk(c):
        start = c * CHUNK
        qc = lax.dynamic_slice_in_dim(q, start, CHUNK, axis=1)
        kc = lax.dynamic_slice_in_dim(kp, start, band, axis=1)
        vc = lax.dynamic_slice_in_dim(vp, start, band, axis=1)
        s = jnp.einsum('bqhd,bkhd->bhqk', qc, kc).astype(jnp.float32) * scale + bias
        valid = j >= lead - start
        s = jnp.where(valid[None, None, None, :], s, -jnp.inf)
        p = jax.nn.softmax(s, axis=-1).astype(v.dtype)
        return jnp.einsum('bhqk,bkhd->bqhd', p, vc)

    out = lax.map(one_chunk, jnp.arange(n))
    return out.transpose(1, 0, 2, 3, 4).reshape(b_, t_, h_ * dh)


def setup_inputs(seed: int = 0) -> dict:
    key = jax.random.key(seed)
    ks = jax.random.split(key, 20)
    f32 = jnp.float32
    nrm = lambda k, shape, s: jax.random.normal(k, shape, f32) * s
    gain = lambda k, shape: 1.0 + 0.02 * jax.random.normal(k, shape, f32)
    dt = jnp.exp(jax.random.uniform(ks[4], (DEPTH, GDN_HEADS), f32, np.log(1e-3), np.log(1e-1)))
    return {
        "x": nrm(ks[0], (BATCH, SEQ, D_MODEL), 1.0),
        "norm_mix_w": gain(ks[1], (DEPTH, D_MODEL)),
        "w_in": nrm(ks[2], (DEPTH, D_MODEL, D_IN), D_MODEL ** -0.5),
        "conv_qkv_w": nrm(ks[3], (DEPTH, GDN_CONV, CONV_A), GDN_CONV ** -0.5),
        "a_log": jnp.log(jax.random.uniform(ks[5], (DEPTH, GDN_HEADS), f32, 1.0, 16.0)),
        "dt_bias": dt + jnp.log(-jnp.expm1(-dt)),
        "gdn_norm_w": gain(ks[6], (DEPTH, GDN_DV)),
        "w_branch_a": nrm(ks[7], (DEPTH, VAL_A, D_MODEL), VAL_A ** -0.5),
        "w_branch_b": nrm(ks[8], (DEPTH, WIDTH_B, D_MODEL), WIDTH_B ** -0.5),
        "rel_bias": nrm(ks[9], (DEPTH, ATT_HEADS, 2 * REL_CLIP + 1), 0.5),
        "w_out": nrm(ks[10], (DEPTH, D_MODEL, D_MODEL), D_MODEL ** -0.5),
        "norm_ffn_w": gain(ks[11], (DEPTH, D_MODEL)),
        "w_up": nrm(ks[12], (DEPTH, D_MODEL, 2 * D_FF), D_MODEL ** -0.5),
        "conv_ffn_w": nrm(ks[13], (DEPTH, FFN_CONV, 2 * D_FF), FFN_CONV ** -0.5),
        "conv_ffn_b": nrm(ks[14], (DEPTH, 2 * D_FF), 0.02),
        "w_down": nrm(ks[15], (DEPTH, D_FF, D_MODEL), D_FF ** -0.5),
        "norm_final_w": gain(ks[16], (D_MODEL,)),
    }


def reference(x, norm_mix_w, w_in, conv_qkv_w, a_log, dt_bias, gdn_norm_w, w_branch_a, w_branch_b,
              rel_bias, w_out, norm_ffn_w, w_up, conv_ffn_w, conv_ffn_b, w_down, norm_final_w):
    b_, t_, _ = x.shape
    for l in range(DEPTH):
        h = rmsnorm(x, norm_mix_w[l])
        proj = h @ w_in[l]
        qa, ka, va, za, ba, aa, qb, kb, vb, ga, gb = jnp.split(proj, IN_SPLITS, axis=-1)

        qkv = jax.nn.silu(causal_dwconv(jnp.concatenate([qa, ka, va], axis=-1), conv_qkv_w[l]))
        qa, ka, va = jnp.split(qkv, (KEY_A, 2 * KEY_A), axis=-1)
        qa = l2norm(qa.reshape(b_, t_, GDN_HEADS, GDN_DK))
        ka = l2norm(ka.reshape(b_, t_, GDN_HEADS, GDN_DK))
        va = va.reshape(b_, t_, GDN_HEADS, GDN_DV)
        beta = jax.nn.sigmoid(ba.astype(jnp.float32))
        g = -jnp.exp(a_log[l].astype(jnp.float32)) * jax.nn.softplus(
            aa.astype(jnp.float32) + dt_bias[l].astype(jnp.float32))
        oa = gated_delta_rule(qa, ka, va, g, beta)
        za = za.reshape(b_, t_, GDN_HEADS, GDN_DV).astype(jnp.float32)
        oa = (rmsnorm(oa, gdn_norm_w[l]) * jax.nn.silu(za)).astype(x.dtype).reshape(b_, t_, VAL_A)

        ob = chunk_band_attention(qb.reshape(b_, t_, ATT_HEADS, ATT_DH),
                                  kb.reshape(b_, t_, ATT_HEADS, ATT_DH),
                                  vb.reshape(b_, t_, ATT_HEADS, ATT_DH), rel_bias[l])

        mix = jax.nn.sigmoid(ga) * (oa @ w_branch_a[l]) + jax.nn.sigmoid(gb) * (ob @ w_branch_b[l])
        x = x + mix @ w_out[l]

        h = rmsnorm(x, norm_ffn_w[l])
        u = causal_dwconv(h @ w_up[l], conv_ffn_w[l]) + conv_ffn_b[l]
        gate, up = jnp.split(u, 2, axis=-1)
        x = x + (jax.nn.silu(gate) * up) @ w_down[l]
    return rmsnorm(x, norm_final_w)
```

```python
from collections import defaultdict
from contextlib import ExitStack

import numpy as np
import concourse.bass as bass
import concourse.mybir as mybir
from concourse.bass_utils import run_bass_kernel_spmd

F32 = mybir.dt.float32
BF16 = mybir.dt.bfloat16
AF = mybir.ActivationFunctionType
ALU = mybir.AluOpType

D = 1024
NCH = 8
EPS = 1e-6
CH = 64
DK = 128
D_IN = 5640
D_FF = 2816
NEG = -30000.0


PSUM_PREFIXES = ("psl", "ptb", "ppj", "ps_tr", "aps_tr", "bps_tr", "pbig", "bpbig", "ps_o")


class _Rec:
    def __getattr__(self, name):
        def f(*a, **k):
            self.call = (name, a, k)
            return self
        return f


class Prog:
    ENGS = ("pe", "act", "dve", "pool", "sp")

    def __init__(self, nc):
        self.nc = nc
        self.streams = {e: [] for e in self.ENGS}
        self.count = defaultdict(int)
        self.lastw = {}
        self.readers = defaultdict(list)
        self.waited = defaultdict(int)
        self.nops = 0
        self.epoch = 0
        import os
        self.cut = int(os.environ["PCUT"]) if "PCUT" in os.environ else None

    def _dep(self, eng, rec):
        semkey, val = rec[0], rec[1]
        if eng == "pool" and semkey.startswith("dma_cc@"):
            return
        if self.waited[(eng, semkey)] < val:
            self.waited[(eng, semkey)] = val
            self.streams[eng].append(("wait", semkey, val))

    def op(self, eng, fn, reads=(), writes=(), chan=None, inc_override=None):
        if self.cut is not None and self.nops >= self.cut:
            return
        isdma = chan is not None
        for k in reads:
            w = self.lastw.get(k)
            if w is not None:
                self._dep(eng, w)
            if k.startswith(PSUM_PREFIXES):
                for r in self.readers[k]:
                    if r[2] != eng:
                        self._dep(eng, r)
        for k in writes:
            w = self.lastw.get(k)
            if w is not None:
                if not (w[2] == eng == "pe" and not w[3] and not isdma):
                    self._dep(eng, w)
            for r in self.readers[k]:
                if r[2] != eng or r[3] or isdma:
                    self._dep(eng, r)
        if isdma:
            semkey, inc = "dma_%s@%d" % (chan, self.epoch), (inc_override or 16)
        else:
            semkey, inc = "%s@%d" % (eng, self.epoch), 1
        self.count[semkey] += inc
        rec = (semkey, self.count[semkey], eng, isdma)
        rec_ = _Rec()
        fn(rec_)
        self.streams[eng].append(("op", rec_.call, semkey, inc))
        for k in writes:
            self.lastw[k] = rec
            self.readers[k] = []
        for k in reads:
            self.readers[k].append(rec)
        self.nops += 1

    def barrier(self):
        for e in self.ENGS:
            for semkey, val in list(self.count.items()):
                if val:
                    self._dep(e, (semkey, val))
        self.lastw.clear()
        self.readers.clear()
        self.epoch += 1

    def finish(self):
        for semkey, val in list(self.count.items()):
            if semkey.startswith("dma_"):
                self._dep("sp", (semkey, val))
        for semkey, val in list(self.count.items()):
            if not semkey.startswith("dma_") and val:
                self._dep("sp", (semkey, val))

    def emit(self, es):
        nc = self.nc
        sems = {}
        for i, k in enumerate(sorted(self.count)):
            sems[k] = es.enter_context(nc.semaphore("s%d" % i))
        block = es.enter_context(nc.Block())
        streams = self.streams

        def run(eng_handle, items):
            for it in items:
                if it[0] == "wait":
                    eng_handle.wait_ge(sems[it[1]], it[2])
                else:
                    name, a, k = it[1]
                    getattr(eng_handle, name)(*a, **k).then_inc(sems[it[2]], it[3])

        @block.tensor
        def _(e):
            run(e, streams["pe"])

        @block.scalar
        def _(e):
            run(e, streams["act"])

        @block.vector
        def _(e):
            run(e, streams["dve"])

        @block.gpsimd
        def _(e):
            run(e, streams["pool"])

        @block.sync
        def _(e):
            run(e, streams["sp"])


class Ctx:
    _uid = [0]

    def __init__(self, nc, es, P):
        self.nc, self.es, self.P = nc, es, P
        Ctx._uid[0] += 1
        self.n = Ctx._uid[0] * 1000

    def sb(self, shape, dt=F32, name=None):
        self.n += 1
        return self.es.enter_context(self.nc.sbuf_tensor("%s_%d" % (name or "t", self.n), list(shape), dt))

    def ps(self, shape, dt=F32, name=None):
        self.n += 1
        return self.es.enter_context(self.nc.psum_tensor("%s_%d" % (name or "p", self.n), list(shape), dt))


def chunk_consts():
    j = np.arange(128)
    same = (j[:, None] // CH) == (j[None, :] // CH)
    m1 = (same & (j[:, None] <= j[None, :])).astype(np.float32)
    m2 = (same & (j[:, None] > j[None, :])).astype(np.float32)
    ident = np.eye(128, dtype=np.float32)
    ones = np.ones((128, 128), np.float32)
    cind = np.zeros((128, 128), np.float32)
    cind[:64, 0] = 1.0
    cind[64:, 1] = 1.0
    return np.concatenate([m1, m2, ident, ones, cind], axis=1)


C_M1, C_M2, C_ID, C_ONES, C_CIND = 0, 128, 256, 384, 512
NCONST = 640


def make_epsc(P, A):
    epsc = A.sb([128, 2], F32, "epsc")
    P.op("pool", lambda e: e.memset(epsc[:, 0:1], D * EPS), writes=["epsc0"])
    P.op("pool", lambda e: e.memset(epsc[:, 1:2], EPS), reads=["epsc0"], writes=["epsc"])
    return epsc


def norm_block(P, epsc, x_blk, xkey, ss, rs, sskey, junk, junkkey, xn, xnkey, ps_tr, pskey, idb, hT_dst, hTkey,
               wrow=None, wkey=None):
    P.op("act", lambda e: e.activation(out=junk, in_=x_blk, func=AF.Square, accum_out=ss),
         reads=[xkey], writes=[junkkey, sskey])
    P.op("act", lambda e: e.activation(out=rs, in_=ss, func=AF.Ln, bias=epsc[:, 0:1]),
         reads=[sskey, "epsc"], writes=[sskey + "r0"])
    P.op("act", lambda e: e.activation(out=rs, in_=rs, func=AF.Exp, scale=-0.5),
         reads=[sskey + "r0"], writes=[sskey + "r"])
    if wrow is None:
        P.op("dve", lambda e: e.tensor_scalar(xn, x_blk, rs, None, ALU.mult),
             reads=[xkey, sskey + "r"], writes=[xnkey])
    else:
        P.op("dve", lambda e: e.scalar_tensor_tensor(out=xn, in0=x_blk, scalar=rs, in1=wrow,
                                                      op0=ALU.mult, op1=ALU.mult),
             reads=[xkey, sskey + "r", wkey], writes=[xnkey])
    for c in range(NCH):
        P.op("pe", lambda e, c=c: e.transpose(ps_tr[:, c, :], xn[:, c * 128:(c + 1) * 128], idb),
             reads=[xnkey, "consts_b"], writes=[pskey])
    P.op("act", lambda e: e.copy(hT_dst, ps_tr[:, :, :]), reads=[pskey], writes=[hTkey])


def build_phase1(nc, es, P, A, T, x1, w1, cw1, sc1, nw1, cst, o_out):
    NT = T // 512
    sb, ps = A.sb, A.ps
    cf = sb([128, NCONST], F32, "cf")
    cb = sb([128, NCONST], BF16, "cb")
    P.op("sp", lambda e: e.dma_start(out=cf[:], in_=cst[:, :]), writes=["consts_f"], chan="cf")
    P.op("dve", lambda e: e.tensor_copy(cb[:], cf[:]), reads=["consts_f"], writes=["consts_b"])
    m1f, m2f = cf[:, C_M1:C_M1 + 128], cf[:, C_M2:C_M2 + 128]
    idf, onesf, cindf = cf[:, C_ID:C_ID + 128], cf[:, C_ONES:C_ONES + 128], cf[:, C_CIND:C_CIND + 2]
    idb, onesb = cb[:, C_ID:C_ID + 128], cb[:, C_ONES:C_ONES + 128]

    epsc = make_epsc(P, A)
    wf = sb([128, NCH, 386], F32, "wf")
    wb = sb([128, NCH, 386], BF16, "wb")
    nw = sb([128, NCH], F32, "nw")
    cw = sb([128, 12], F32, "cw")
    sc = sb([128, 2], F32, "sc")
    negA = sb([128, 1], F32, "negA")
    P.op("sp", lambda e: e.dma_start(out=wf[:], in_=w1.rearrange("(c p) n -> p c n", p=128)), writes=["wf"], chan="wf")
    P.op("sp", lambda e: e.dma_start(out=nw[:], in_=nw1[:, :]), writes=["nw"], chan="nw")
    P.op("sp", lambda e: e.dma_start(out=cw[:], in_=cw1[:, :]), writes=["cw"], chan="cw")
    P.op("sp", lambda e: e.dma_start(out=sc[:], in_=sc1[:, :]), writes=["sc"], chan="sc")
    for c in range(NCH):
        P.op("dve", lambda e, c=c: e.tensor_scalar(wb[:, c, :], wf[:, c, :], nw[:, c:c + 1], 32.0, ALU.mult, ALU.mult),
             reads=["wf", "nw"], writes=["wb"])
    P.op("act", lambda e: e.activation(out=negA[:], in_=sc[:, 0:1], func=AF.Exp), reads=["sc"], writes=["negA0"])
    P.op("dve", lambda e: e.tensor_scalar(negA[:], negA[:], -1.0, None, ALU.mult), reads=["negA0"], writes=["negA"])

    xt = [sb([128, 4, D], F32, "xt") for _ in range(2)]
    junk = sb([128, D], BF16, "junk")
    ss = sb([128, 8], F32, "ss")
    rs = sb([128, 8], F32, "rs")
    xn = [sb([128, D], BF16, "xn") for _ in range(2)]
    hT = [sb([128, NCH, 512], BF16, "hT") for _ in range(2)]
    cbuf = [sb([128, 3 + 512], F32, "cbuf") for _ in range(3)]
    acc = [sb([128, 512], F32, "acc") for _ in range(3)]
    sil = [sb([128, 512], F32, "sil") for _ in range(2)]
    sq = [sb([128, 512], BF16, "sq") for _ in range(2)]
    rn = [sb([128, 512], F32, "rn") for _ in range(2)]
    QT = [sb([128, 512], BF16, "QT") for _ in range(2)]
    KT = [sb([128, 512], BF16, "KT") for _ in range(2)]
    VT = [sb([128, 512], BF16, "VT") for _ in range(2)]
    bdt = [sb([128, 4, 2], F32, "bdt") for _ in range(2)]
    gsc = [sb([128, 8, 4], F32, "gsc") for _ in range(2)]
    gM = [sb([128, 128], F32, "gM") for _ in range(2)]
    rgc = [sb([128, 2], F32, "rgc") for _ in range(2)]
    D1 = [sb([128, 128], F32, "D1") for _ in range(2)]
    D2 = [sb([128, 128], F32, "D2") for _ in range(2)]
    smx = [sb([128, 4], F32, "smx") for _ in range(2)]
    bg = [sb([128, 1], F32, "bg") for _ in range(2)]
    bgK = [sb([128, 128], BF16, "bgK") for _ in range(2)]
    KD = [sb([128, 128], BF16, "KD") for _ in range(2)]
    bV = [sb([128, 128], BF16, "bV") for _ in range(2)]
    Bm = [sb([128, 128], F32, "Bm") for _ in range(2)]
    Bq = [sb([128, 128], F32, "Bq") for _ in range(2)]
    Nq = [sb([128, 128], F32, "Nq") for _ in range(2)]
    Rq = [sb([128, 128], F32, "Rq") for _ in range(2)]
    Rt = [sb([128, 128], F32, "Rt") for _ in range(2)]
    TTb = [sb([128, 128], BF16, "TTb") for _ in range(2)]
    PT = [sb([128, 128], BF16, "PT") for _ in range(2)]
    PTm = [sb([128, 128], F32, "PTm") for _ in range(2)]
    nWT = [sb([128, 128], BF16, "nWT") for _ in range(2)]
    Ub = [sb([128, 128], BF16, "Ub") for _ in range(2)]
    pus = [sb([128, 128], F32, "pus") for _ in range(2)]
    Osb = [sb([128, 128], F32, "Osb") for _ in range(2)]
    Sf = [sb([128, 128], F32, "Sf") for _ in range(2)]
    Sb = [sb([128, 128], BF16, "Sb") for _ in range(2)]

    ps_tr = ps([128, NCH, 128], BF16, "ps_tr")
    ps_tb = ps([128, 8, 128], BF16, "ps_tb")
    ps_pj = [ps([128, 512], F32, "ps_pj") for _ in range(2)]
    ps_sl = [ps([128, 4, 128], F32, "ps_sl") for _ in range(4)]
    slot_i = [0]

    def bank():
        i = slot_i[0] % 4
        slot_i[0] += 1
        return ps_sl[i], "psl%d" % i

    tb_i = [0]

    def tslot():
        i = tb_i[0] % 8
        tb_i[0] += 1
        return ps_tb[:, i, :], "ptb"

    pj_i = [0]

    def pjslot():
        i = pj_i[0] % 2
        pj_i[0] += 1
        return ps_pj[i], "ppj%d" % i

    P.dbg = dict(QT=QT, KT=KT, VT=VT, gsc=gsc, hT=hT, bdt=bdt, D1=D1, D2=D2, TTb=TTb, smx=smx, bgK=bgK, KD=KD, bV=bV,
                 PT=PT, nWT=nWT, Sf=Sf, Bq=Bq, Nq=Nq, Ub=Ub, cbuf=cbuf, acc=acc, wb=wb, Osb=Osb, Bm=Bm)
    P.op("pool", lambda e: e.memset(Sf[0][:], 0.0), writes=["Sf0"])
    P.op("pool", lambda e: e.memset(Sb[0][:], 0.0), writes=["Sb0"])
    for g in range(3):
        P.op("pool", lambda e, g=g: e.memset(cbuf[g][:, 0:3], 0.0), writes=["cbufh%d" % g])
    sidx = [0]
    chain_q = []

    for ti in range(NT):
        tp = ti % 2
        xk = "xt%d" % tp
        P.op("sp", lambda e, ti=ti, tp=tp: e.dma_start(
            out=xt[tp][:], in_=x1[ti * 512:(ti + 1) * 512, :].rearrange("(j p) d -> p j d", p=128)),
            writes=[xk], chan=xk)
        hk = "hT%d" % tp
        for j in range(4):
            bp = j % 2
            norm_block(P, epsc, xt[tp][:, j, :], xk, ss[:, j + 4 * tp:j + 4 * tp + 1], rs[:, j + 4 * tp:j + 4 * tp + 1],
                       "ss%d_%d" % (tp, j), junk[:], "junk", xn[bp][:], "xn%d" % bp, ps_tr, "ps_tr", idb,
                       hT[tp][:, :, j * 128:(j + 1) * 128], hk)
        for g in range(3):
            pj, pjk = pjslot()
            for c in range(NCH):
                P.op("pe", lambda e, g=g, c=c, pj=pj: e.matmul(pj[:], lhsT=wb[:, c, g * 128:(g + 1) * 128],
                                                              rhs=hT[tp][:, c, :], start=(c == 0), stop=(c == NCH - 1)),
                     reads=["wb", hk], writes=[pjk])
            P.op("act", lambda e, g=g, pj=pj: e.copy(cbuf[g][:, 3:515], pj[:]), reads=[pjk], writes=["cbufm%d" % g])
        pj, pjk = pjslot()
        for j in range(4):
            for c in range(NCH):
                P.op("pe", lambda e, j=j, c=c, pj=pj: e.matmul(pj[:, 2 * j:2 * j + 2], lhsT=hT[tp][:, c, j * 128:(j + 1) * 128],
                                                              rhs=wb[:, c, 384:386], start=(c == 0), stop=(c == NCH - 1)),
                     reads=["wb", hk], writes=[pjk])
        bk = "bdt%d" % tp
        P.op("dve", lambda e, pj=pj: e.tensor_copy(bdt[tp][:].rearrange("p a b -> p (a b)"), pj[:, 0:8]), reads=[pjk], writes=[bk])
        for g in range(3):
            ck = ["cbufh%d" % g, "cbufm%d" % g]
            ak = "acc%d" % g
            P.op("dve", lambda e, g=g: e.tensor_scalar(acc[g][:], cbuf[g][:, 0:512], cw[:, 4 * g:4 * g + 1], None, ALU.mult),
                 reads=ck + ["cw"], writes=[ak])
            for k in range(1, 4):
                P.op("dve", lambda e, g=g, k=k: e.scalar_tensor_tensor(
                    out=acc[g][:], in0=cbuf[g][:, k:k + 512], scalar=cw[:, 4 * g + k:4 * g + k + 1], in1=acc[g][:],
                    op0=ALU.mult, op1=ALU.add), reads=ck + ["cw", ak], writes=[ak])
            P.op("pool", lambda e, g=g: e.tensor_copy(cbuf[g][:, 0:3], cbuf[g][:, 512:515]),
                 reads=["cbufm%d" % g, ak], writes=["cbufh%d" % g])
        qk, kk, vk = "QT%d" % tp, "KT%d" % tp, "VT%d" % tp
        P.op("act", lambda e: e.activation(out=VT[tp][:], in_=acc[2][:], func=AF.Silu), reads=["acc2"], writes=[vk])
        for g in range(2):
            P.op("act", lambda e, g=g: e.activation(out=sil[g][:], in_=acc[g][:], func=AF.Silu), reads=["acc%d" % g], writes=["sil%d" % g])
            P.op("act", lambda e, g=g: e.activation(out=sq[g][:], in_=sil[g][:], func=AF.Square), reads=["sil%d" % g], writes=["sq%d" % g])
            pj, pjk = pjslot()
            P.op("pe", lambda e, g=g, pj=pj: e.matmul(pj[:], lhsT=onesb, rhs=sq[g][:], start=True, stop=True),
                 reads=["consts_b", "sq%d" % g], writes=[pjk])
            P.op("act", lambda e, g=g, pj=pj: e.activation(out=rn[g][:], in_=pj[:], func=AF.Ln, bias=epsc[:, 1:2]),
                 reads=[pjk, "epsc"], writes=["rn%da" % g])
            P.op("act", lambda e, g=g: e.activation(out=rn[g][:], in_=rn[g][:], func=AF.Exp, scale=-0.5),
                 reads=["rn%da" % g], writes=["rn%d" % g])
        P.op("dve", lambda e: e.scalar_tensor_tensor(out=QT[tp][:], in0=sil[0][:], scalar=float(DK) ** -0.5, in1=rn[0][:],
                                                      op0=ALU.mult, op1=ALU.mult), reads=["sil0", "rn0"], writes=[qk])
        P.op("dve", lambda e: e.tensor_tensor(out=KT[tp][:], in0=sil[1][:], in1=rn[1][:], op=ALU.mult), reads=["sil1", "rn1"], writes=[kk])
        G = gsc[tp]
        gk = "gsc%d" % tp
        xg, ax, ee, ll, sp_, gg, be, nbe = (G[:, i, :] for i in range(8))
        P.op("dve", lambda e: e.tensor_scalar(xg, bdt[tp][:, :, 1], sc[:, 1:2], None, ALU.add), reads=[bk, "sc"], writes=[gk + "a"])
        P.op("dve", lambda e: e.scalar_tensor_tensor(out=ax, in0=xg, scalar=-1.0, in1=xg, op0=ALU.mult, op1=ALU.max), reads=[gk + "a"], writes=[gk + "b"])
        P.op("act", lambda e: e.activation(out=ee, in_=ax, func=AF.Exp, scale=-1.0), reads=[gk + "b"], writes=[gk + "c"])
        P.op("act", lambda e: e.activation(out=ll, in_=ee, func=AF.Ln, bias=1.0), reads=[gk + "c"], writes=[gk + "d"])
        P.op("dve", lambda e: e.scalar_tensor_tensor(out=sp_, in0=xg, scalar=0.0, in1=ll, op0=ALU.max, op1=ALU.add),
             reads=[gk + "a", gk + "d"], writes=[gk + "e"])
        P.op("dve", lambda e: e.tensor_scalar(gg, sp_, negA[:, 0:1], None, ALU.mult), reads=[gk + "e", "negA"], writes=[gk + "g"])
        P.op("act", lambda e: e.activation(out=be, in_=bdt[tp][:, :, 0], func=AF.Sigmoid), reads=[bk], writes=[gk + "be"])
        P.op("dve", lambda e: e.tensor_scalar(nbe, be, -1.0, None, ALU.mult), reads=[gk + "be"], writes=[gk + "nb"])

        for j in range(4):
            blk = ti * 4 + j
            bp = blk % 2
            s = "_%d" % bp
            cs = slice(j * 128, (j + 1) * 128)
            g_j, be_j, nbe_j = gg[:, j:j + 1], be[:, j:j + 1], nbe[:, j:j + 1]
            P.op("dve", lambda e, bp=bp, g_j=g_j: e.tensor_scalar(gM[bp][:], m1f, g_j, None, ALU.mult),
                 reads=["consts_f", gk + "g"], writes=["gM" + s])
            P.op("dve", lambda e, bp=bp, g_j=g_j: e.tensor_scalar(rgc[bp][:], cindf, g_j, None, ALU.mult),
                 reads=["consts_f", gk + "g"], writes=["rgc" + s])
            bkA, d1k = bank()
            d1, d2, smp = bkA[:, 0, :], bkA[:, 1, :], bkA[:, 2, :]
            d2k = smk = d1k
            P.op("pe", lambda e, bp=bp, d1=d1: e.matmul(d1, lhsT=gM[bp][:], rhs=m2f, start=True, stop=True),
                 reads=["gM" + s, "consts_f"], writes=[d1k])
            P.op("pe", lambda e, bp=bp, d2=d2: e.matmul(d2, lhsT=m2f, rhs=gM[bp][:], start=True, stop=True),
                 reads=["gM" + s, "consts_f"], writes=[d2k])
            P.op("pe", lambda e, smp=smp, g_j=g_j: e.matmul(smp[:, 0:1], lhsT=m1f, rhs=g_j, start=True, stop=True),
                 reads=[gk + "g", "consts_f"], writes=[smk])
            P.op("pe", lambda e, smp=smp, g_j=g_j: e.matmul(smp[:, 1:2], lhsT=m2f, rhs=g_j, start=True, stop=True),
                 reads=[gk + "g", "consts_f"], writes=[smk])
            P.op("pe", lambda e, smp=smp, bp=bp: e.matmul(smp[:, 2:4], lhsT=onesf, rhs=rgc[bp][:], start=True, stop=True),
                 reads=["rgc" + s, "consts_f"], writes=[smk])
            P.op("act", lambda e, bp=bp, d1=d1: e.activation(out=D1[bp][:], in_=d1, func=AF.Exp), reads=[d1k], writes=["D1" + s])
            P.op("act", lambda e, bp=bp, d2=d2: e.activation(out=D2[bp][:], in_=d2, func=AF.Exp), reads=[d2k], writes=["D2" + s])
            P.op("act", lambda e, bp=bp, smp=smp: e.activation(out=smx[bp][:], in_=smp[:, 0:4], func=AF.Exp), reads=[smk], writes=["smx" + s])
            gam, kdf = smx[bp][:, 0:1], smx[bp][:, 1:2]
            P.op("dve", lambda e, bp=bp, be_j=be_j, gam=gam: e.tensor_tensor(out=bg[bp][:], in0=be_j, in1=gam, op=ALU.mult),
                 reads=[gk + "be", "smx" + s], writes=["bg" + s])
            kt_, ktk = tslot()
            vt_, vtk = tslot()
            P.op("pe", lambda e, kt_=kt_, cs=cs: e.transpose(kt_, KT[tp][:, cs], idb), reads=[kk, "consts_b"], writes=[ktk])
            P.op("pe", lambda e, vt_=vt_, cs=cs: e.transpose(vt_, VT[tp][:, cs], idb), reads=[vk, "consts_b"], writes=[vtk])
            P.op("dve", lambda e, bp=bp, kt_=kt_: e.tensor_scalar(bgK[bp][:], kt_, bg[bp][:, 0:1], None, ALU.mult),
                 reads=[ktk, "bg" + s], writes=["bgK" + s])
            P.op("act", lambda e, bp=bp, kt_=kt_, kdf=kdf: e.activation(out=KD[bp][:], in_=kt_, func=AF.Copy, scale=kdf),
                 reads=[ktk, "smx" + s], writes=["KD" + s])
            P.op("act", lambda e, bp=bp, vt_=vt_, be_j=be_j: e.activation(out=bV[bp][:], in_=vt_, func=AF.Copy, scale=be_j),
                 reads=[vtk, gk + "be"], writes=["bV" + s])
            bkB, grk = bank()
            gr, kq, nt_ = bkB[:, 0, :], bkB[:, 1, :], bkB[:, 2, :]
            kqk = grk
            P.op("pe", lambda e, gr=gr, cs=cs: e.matmul(gr, lhsT=KT[tp][:, cs], rhs=KT[tp][:, cs], start=True, stop=True),
                 reads=[kk], writes=[grk])
            P.op("pe", lambda e, kq=kq, cs=cs: e.matmul(kq, lhsT=KT[tp][:, cs], rhs=QT[tp][:, cs], start=True, stop=True),
                 reads=[kk, qk], writes=[kqk])
            P.op("dve", lambda e, bp=bp, gr=gr: e.tensor_tensor(out=Bm[bp][:], in0=gr, in1=D1[bp][:], op=ALU.mult),
                 reads=[grk, "D1" + s], writes=["Bm" + s])
            P.op("pool", lambda e, bp=bp: e.tensor_tensor(out=PTm[bp][:], in0=D2[bp][:], in1=m1f, op=ALU.mult),
                 reads=["D2" + s, "consts_f"], writes=["PTm" + s])
            P.op("dve", lambda e, bp=bp, kq=kq: e.tensor_tensor(out=PT[bp][:], in0=kq, in1=PTm[bp][:], op=ALU.mult),
                 reads=[kqk, "PTm" + s], writes=["PT" + s])
            Bc, Nc, Rc, Rtc = "B" + s, "N" + s, "R" + s, "Rt" + s
            P.op("dve", lambda e, bp=bp, nbe_j=nbe_j: e.scalar_tensor_tensor(out=Bq[bp][:], in0=Bm[bp][:], scalar=nbe_j, in1=m2f,
                                                                              op0=ALU.mult, op1=ALU.mult),
                 reads=["Bm" + s, gk + "nb", "consts_f"], writes=[Bc])
            bkC, ntk = bank()
            nt_ = bkC[:, 0, :]
            P.op("pe", lambda e, bp=bp, nt_=nt_: e.transpose(nt_, Bq[bp][:], idf), reads=[Bc, "consts_f"], writes=[ntk])
            P.op("act", lambda e, bp=bp, nt_=nt_: e.copy(Nq[bp][:], nt_), reads=[ntk], writes=[Nc])
            P.op("pool", lambda e, bp=bp: e.tensor_tensor(out=Rt[bp][:], in0=Bq[bp][:], in1=idf, op=ALU.add),
                 reads=[Bc, "consts_f"], writes=[Rtc])
            P.op("dve", lambda e, bp=bp: e.tensor_tensor(out=Rq[bp][:], in0=Nq[bp][:], in1=idf, op=ALU.add),
                 reads=[Nc, "consts_f"], writes=[Rc])
            for lvl in range(5):
                last = lvl == 4
                bkD, n2k = bank()
                n2, b2 = bkD[:, 0, :], bkD[:, 1, :]
                b2k = n2k
                P.op("pe", lambda e, bp=bp, n2=n2: e.matmul(n2, lhsT=Bq[bp][:], rhs=Nq[bp][:], start=True, stop=True),
                     reads=[Bc, Nc], writes=[n2k])
                if not last:
                    P.op("pe", lambda e, bp=bp, b2=b2: e.matmul(b2, lhsT=Nq[bp][:], rhs=Bq[bp][:], start=True, stop=True),
                         reads=[Bc, Nc], writes=[b2k])
                P.op("act", lambda e, bp=bp, n2=n2: e.copy(Nq[bp][:], n2), reads=[n2k], writes=[Nc])
                if not last:
                    P.op("act", lambda e, bp=bp, b2=b2: e.copy(Bq[bp][:], b2), reads=[b2k], writes=[Bc])
                bkE, r2k = bank()
                r2, t2 = bkE[:, 0, :], bkE[:, 1, :]
                t2k = r2k
                P.op("pe", lambda e, bp=bp, r2=r2: e.matmul(r2, lhsT=Rt[bp][:], rhs=Nq[bp][:], start=True, stop=True),
                     reads=[Rtc, Nc], writes=[r2k])
                if not last:
                    P.op("pe", lambda e, bp=bp, t2=t2: e.matmul(t2, lhsT=Rq[bp][:], rhs=Bq[bp][:], start=True, stop=True),
                         reads=[Rc, Bc], writes=[t2k])
                    P.op("dve", lambda e, bp=bp, r2=r2: e.tensor_tensor(out=Rq[bp][:], in0=r2, in1=Rq[bp][:], op=ALU.add),
                         reads=[r2k, Rc], writes=[Rc])
                    P.op("dve", lambda e, bp=bp, t2=t2: e.tensor_tensor(out=Rt[bp][:], in0=t2, in1=Rt[bp][:], op=ALU.add),
                         reads=[t2k, Rtc], writes=[Rtc])
                else:
                    P.op("dve", lambda e, bp=bp, r2=r2: e.tensor_tensor(out=TTb[bp][:], in0=r2, in1=Rq[bp][:], op=ALU.add),
                         reads=[r2k, Rc], writes=["TTb" + s])
            bkF, wtk = bank()
            wt = bkF[:, 0, :]
            P.op("pe", lambda e, bp=bp, wt=wt: e.matmul(wt, lhsT=bgK[bp][:], rhs=TTb[bp][:], start=True, stop=True),
                 reads=["bgK" + s, "TTb" + s], writes=[wtk])
            P.op("act", lambda e, bp=bp, wt=wt: e.mul(nWT[bp][:], wt, -1.0), reads=[wtk], writes=["nWT" + s])

            def chain(bp=bp, s=s, cs=cs, tp=tp, qk=qk, blk=blk, gam=gam):
                for c in range(2):
                    r = slice(64 * c, 64 * c + 64)
                    si = sidx[0]
                    so, sn_ = si % 2, (si + 1) % 2
                    sidx[0] += 1
                    bkG, uk = bank()
                    u, qs = bkG[:, 0, :], bkG[:, 1, :]
                    qsk = uk
                    P.op("pe", lambda e, u=u: e.matmul(u, lhsT=TTb[bp][r, :], rhs=bV[bp][r, :], start=True, stop=False),
                         reads=["TTb" + s, "bV" + s], writes=[uk])
                    P.op("pe", lambda e, u=u: e.matmul(u, lhsT=nWT[bp][:], rhs=Sb[so][:], start=False, stop=True),
                         reads=["nWT" + s, "Sb%d" % so], writes=[uk])
                    P.op("pe", lambda e, qs=qs: e.matmul(qs, lhsT=QT[tp][:, cs], rhs=Sb[so][:], start=True, stop=True),
                         reads=[qk, "Sb%d" % so], writes=[qsk])
                    P.op("dve", lambda e, u=u: e.tensor_copy(Ub[bp][r, :], u[r, :]), reads=[uk], writes=["Ub%s_%d" % (s, c)])
                    bkH, snk = bank()
                    sn = bkH[:, 0, :]
                    P.op("pe", lambda e, sn=sn: e.matmul(sn, lhsT=KD[bp][r, :], rhs=Ub[bp][r, :], start=True, stop=True),
                         reads=["KD" + s, "Ub%s_%d" % (s, c)], writes=[snk])
                    P.op("dve", lambda e, sn=sn, c=c: e.scalar_tensor_tensor(
                        out=Sf[sn_][:], in0=Sf[so][:], scalar=smx[bp][:, 2 + c:3 + c], in1=sn, op0=ALU.mult, op1=ALU.add),
                        reads=["Sf%d" % so, "smx" + s, snk], writes=["Sf%d" % sn_])
                    P.op("act", lambda e: e.copy(Sb[sn_][:], Sf[sn_][:]), reads=["Sf%d" % sn_], writes=["Sb%d" % sn_])
                    bkI, puk = bank()
                    pu = bkI[:, 0, :]
                    P.op("pe", lambda e, pu=pu: e.matmul(pu, lhsT=PT[bp][r, :], rhs=Ub[bp][r, :], start=True, stop=True),
                         reads=["PT" + s, "Ub%s_%d" % (s, c)], writes=[puk])
                    P.op("act", lambda e, pu=pu: e.copy(pus[bp][r, :], pu[r, :]), reads=[puk], writes=["pus%s_%d" % (s, c)])
                    P.op("dve", lambda e, qs=qs: e.scalar_tensor_tensor(
                        out=Osb[bp][r, :], in0=qs[r, :], scalar=gam[r, :], in1=pus[bp][r, :], op0=ALU.mult, op1=ALU.add),
                        reads=[qsk, "smx" + s, "pus%s_%d" % (s, c)], writes=["Osb%s_%d" % (s, c)])
                P.op("sp", lambda e: e.dma_start(out=o_out[blk * 128:(blk + 1) * 128, :], in_=Osb[bp][:]),
                     reads=["Osb%s_0" % s, "Osb%s_1" % s], chan="ost%d" % bp)

            chain_q.append(chain)
            if len(chain_q) > 1:
                chain_q.pop(0)()
    while chain_q:
        chain_q.pop(0)()


def prep_weight(P, stg, stgkey, src2d, n_c, ncols, dst, dst_col0, scale_fn, dkey, skeys, cnt, dst_c0=0):
    pw = min(2048 // n_c, ncols)
    for col in range(0, ncols, pw):
        w = min(pw, ncols - col)
        b = cnt[0] % len(stg)
        cnt[0] += 1
        sv = stg[b][:, 0:n_c * w].rearrange("p (c n) -> p c n", c=n_c)
        k = stgkey + str(b)
        P.op("sp", lambda e: e.dma_start(out=sv, in_=src2d[:, col:col + w].rearrange("(c p) n -> p c n", p=128)),
             writes=[k], chan=k)
        dv = dst[:, dst_c0:dst_c0 + n_c, dst_col0 + col:dst_col0 + col + w]
        if scale_fn is None:
            eng = "act" if (cnt[0] % 2) else "dve"
            if eng == "act":
                P.op("act", lambda e: e.copy(dv, sv), reads=[k], writes=[dkey])
            else:
                P.op("dve", lambda e: e.tensor_copy(dv, sv), reads=[k], writes=[dkey])
        else:
            for c in range(n_c):
                sc_ = scale_fn(c)
                if c % 2:
                    P.op("act", lambda e: e.activation(out=dst[:, c, dst_col0 + col:dst_col0 + col + w], in_=sv[:, c, :],
                                                       func=AF.Copy, scale=sc_), reads=[k] + skeys, writes=[dkey])
                else:
                    P.op("dve", lambda e: e.tensor_scalar(dst[:, c, dst_col0 + col:dst_col0 + col + w], sv[:, c, :], sc_, None, ALU.mult),
                         reads=[k] + skeys, writes=[dkey])


TB = 2
TW = TB * 128


def load_consts(P, A, cst, pre):
    cf = A.sb([128, NCONST], F32, "cf")
    cb = A.sb([128, NCONST], BF16, "cb")
    P.op("sp", lambda e: e.dma_start(out=cf[:], in_=cst[:, :]), writes=[pre + "consts_f"], chan=pre + "cf")
    P.op("dve", lambda e: e.tensor_copy(cb[:], cf[:]), reads=[pre + "consts_f"], writes=["consts_b"])
    return cf, cb


def build_phase2a(nc, P, A, NTM, x2, oa2, validc, w_in, w_ba, w_bb, w_out, nwm_d, gnw_d, biasT_d, cst, xmid,
                  o_all=None, qsel_d=None, RT=None):
    NT2 = NTM + 3
    TC = NTM * TW
    sb, ps = A.sb, A.ps
    cf, cb = load_consts(P, A, cst, "a")
    idb = cb[:, C_ID:C_ID + 128]
    epsc = make_epsc(P, A)
    nwm = sb([128, NCH], F32, "nwm")
    gnw = sb([128, 1], F32, "gnw")
    P.op("sp", lambda e: e.dma_start(out=nwm[:], in_=nwm_d[:, :]), writes=["nwm0"], chan="nwm")
    P.op("sp", lambda e: e.dma_start(out=gnw[:], in_=gnw_d[:, :]), writes=["gnw"], chan="gnw")
    P.op("dve", lambda e: e.tensor_scalar(nwm[:], nwm[:], 32.0, None, ALU.mult), reads=["nwm0"], writes=["nwm"])
    Wi = sb([128, NCH, 4096], BF16, "Wi")
    WbA = sb([128, 4, 1024], BF16, "WbA")
    WbB = sb([128, 4, 1024], BF16, "WbB")
    Wo = sb([128, NCH, 1024], BF16, "Wo")
    es_stg = ExitStack()
    stg = [es_stg.enter_context(nc.sbuf_tensor("astg%d" % i, [128, 2048], F32)) for i in range(2)]
    cnt = [0]
    prep_weight(P, stg, "astg", w_in[:, 1536:2048], 8, 512, Wi, 0, lambda c: nwm[:, c:c + 1], "Wi", ["nwm"], cnt)
    prep_weight(P, stg, "astg", w_in[:, 2056:5640], 8, 3584, Wi, 512, lambda c: nwm[:, c:c + 1], "Wi", ["nwm"], cnt)
    prep_weight(P, stg, "astg", w_ba, 4, 1024, WbA, 0, lambda c: gnw[:, 0:1], "WbA", ["gnw"], cnt)
    prep_weight(P, stg, "astg", w_bb, 4, 1024, WbB, 0, None, "WbB", [], cnt)
    prep_weight(P, stg, "astg", w_out, 8, 1024, Wo, 0, None, "Wo", [], cnt)
    es_stg.close()
    P.barrier()
    biasT = sb([128, 8, 640], F32, "biasT")
    P.op("sp", lambda e: e.dma_start(out=biasT[:], in_=biasT_d[:, :, :]), writes=["biasT"], chan="biasT")
    valid = sb([128, NT2 * TB], F32, "valid")
    P.op("sp", lambda e: e.dma_start(out=valid[:], in_=validc[:, :]), writes=["valid"], chan="valid")
    ones8 = sb([128, 8, 1], F32, "ones8")
    P.op("pool", lambda e: e.memset(ones8[:], 1.0), writes=["ones8"])

    xt = [sb([128, TB, D], F32, "xt") for _ in range(2)]
    junk = sb([128, D], BF16, "junk")
    ss = sb([128, 8], F32, "ss")
    rs = sb([128, 8], F32, "rs")
    xn = [sb([128, D], BF16, "xn") for _ in range(2)]
    hT = sb([128, NCH, TW], BF16, "hT")
    KTb = sb([128, 4, 8 * 128], BF16, "KTb")
    Vaug = sb([128, 8, 8, 65], BF16, "Vaug")
    QTb = sb([128, 4, TW], BF16, "QTb")
    zs = sb([128, TB, 512], F32, "zs")
    oat = sb([128, TB, 512], F32, "oat")
    cands = None
    if o_all is not None:
        cand = sb([128, 4, 512], F32, "cand")
        if RT is None:
            cands = [(lambda r, q_=q_: o_all[q_ * TC + r:q_ * TC + r + 128, :].rearrange("p (h d) -> p h d", h=4)) for q_ in range(4)]
        else:
            o_view = o_all.rearrange("(r t) d -> t r d", r=8)
            cands = [(lambda r, b_=c_ // 4, q_=c_ % 4: o_view[q_ * TC + r:q_ * TC + r + 128, 4 * b_:4 * b_ + 4, :]) for c_ in range(8)]
        qsel = sb([128, len(cands)], F32, "qsel")
        P.op("sp", lambda e: e.dma_start(out=qsel[:], in_=qsel_d[:, :]), writes=["qsel"], chan="qsel")
    ssa = sb([128, 4], F32, "ssa")
    ra = sb([128, 4], F32, "ra")
    oan = sb([128, 512], BF16, "oan")
    oaT = sb([128, 4, TW], BF16, "oaT")
    ob = sb([128, 512], BF16, "ob")
    obT = sb([128, 4, TW], BF16, "obT")
    scs = [sb([128, 640], F32, "scs")] * 2
    PTb = [sb([128, 640], BF16, "PTb") for _ in range(2)]
    rden = sb([128, 8], F32, "rden")
    sg = [sb([128, 2 * TW], F32, "sg") for _ in range(2)]
    tt_ = [sb([128, 2 * TW], F32, "tt") for _ in range(2)]
    mixT = sb([128, NCH, TW], BF16, "mixT")

    ps_tr = ps([128, NCH, 128], BF16, "ps_tr")
    pbig = [ps([128, 512], F32, "pbig") for _ in range(5)]
    ps_o = [ps([128, 4, 65], F32, "ps_o") for _ in range(2)]
    bi = [0]

    def big():
        i = bi[0] % 5
        bi[0] += 1
        return pbig[i], "pbig%d" % i

    scale_q = 64.0 ** -0.5
    for tt in range(NT2):
        tp = tt % 2
        xk = "axt%d" % tp
        P.op("sp", lambda e: e.dma_start(out=xt[tp][:], in_=x2[tt * TW:(tt + 1) * TW, :].rearrange("(j p) d -> p j d", p=128)),
             writes=[xk], chan=xk)
        for j in range(TB):
            norm_block(P, epsc, xt[tp][:, j, :], xk, ss[:, j:j + 1], rs[:, j:j + 1], "ass%d" % j, junk[:], "ajunk",
                       xn[j % 2][:], "axn%d" % (j % 2), ps_tr, "aps_tr", idb, hT[:, :, j * 128:(j + 1) * 128], "ahT")
        ring0 = (tt * TB) % 8
        for m in range(4):
            pb_, pk = big()
            for c in range(NCH):
                P.op("pe", lambda e: e.matmul(pb_[:, 0:TW], lhsT=Wi[:, c, 1024 + m * 128:1024 + (m + 1) * 128], rhs=hT[:, c, :],
                                              start=(c == 0), stop=(c == NCH - 1)), reads=["Wi", "ahT"], writes=[pk])
            P.op("act", lambda e: e.copy(KTb[:, m, ring0 * 128:ring0 * 128 + TW], pb_[:, 0:TW]), reads=[pk], writes=["KTb"])
        for j in range(TB):
            slot = ring0 + j
            pb_, pk = big()
            for c in range(NCH):
                P.op("pe", lambda e: e.matmul(pb_[:, :], lhsT=hT[:, c, j * 128:(j + 1) * 128], rhs=Wi[:, c, 1536:2048],
                                              start=(c == 0), stop=(c == NCH - 1)), reads=["Wi", "ahT"], writes=[pk])
            P.op("dve", lambda e: e.tensor_copy(Vaug[:, slot, :, 0:64], pb_[:, :].rearrange("p (h d) -> p h d", h=8)),
                 reads=[pk], writes=["Vaug"])
            P.op("act", lambda e: e.activation(out=Vaug[:, slot, :, 64:65], in_=ones8[:], func=AF.Copy,
                                               scale=valid[:, tt * TB + j:tt * TB + j + 1]),
                 reads=["ones8", "valid"], writes=["Vaug"])
        if tt < 2:
            continue
        for m in range(4):
            pb_, pk = big()
            for c in range(NCH):
                P.op("pe", lambda e: e.matmul(pb_[:, 0:TW], lhsT=Wi[:, c, 512 + m * 128:512 + (m + 1) * 128], rhs=hT[:, c, :],
                                              start=(c == 0), stop=(c == NCH - 1)), reads=["Wi", "ahT"], writes=[pk])
            P.op("act", lambda e: e.mul(QTb[:, m, :], pb_[:, 0:TW], scale_q), reads=[pk], writes=["QTb"])
        for j in range(TB):
            pb_, pk = big()
            for c in range(NCH):
                P.op("pe", lambda e: e.matmul(pb_[:, :], lhsT=hT[:, c, j * 128:(j + 1) * 128], rhs=Wi[:, c, 0:512],
                                              start=(c == 0), stop=(c == NCH - 1)), reads=["Wi", "ahT"], writes=[pk])
            P.op("act", lambda e: e.activation(out=zs[:, j, :], in_=pb_[:, :], func=AF.Silu), reads=[pk], writes=["zs%d" % j])
        if o_all is None:
            P.op("sp", lambda e: e.dma_start(out=oat[:], in_=oa2[(tt - 2) * TW:(tt - 1) * TW, :].rearrange("(j p) d -> p j d", p=128)),
                 writes=["oat"], chan="oat")
        else:
            for j in range(TB):
                for cc_, cf_ in enumerate(cands):
                    k = cc_ % 4
                    ck_ = "cand_%d" % k
                    P.op("sp", lambda e: e.dma_start(out=cand[:, k, :].rearrange("p (h d) -> p h d", h=4),
                                                     in_=cf_((tt - 2) * TW + j * 128)),
                         reads=["oall"], writes=[ck_], chan=ck_)
                    if cc_ == 0:
                        P.op("dve", lambda e: e.tensor_scalar(oat[:, j, :], cand[:, k, :], qsel[:, cc_:cc_ + 1], None, ALU.mult),
                             reads=[ck_, "qsel"], writes=["oat"])
                    else:
                        P.op("dve", lambda e: e.scalar_tensor_tensor(out=oat[:, j, :], in0=cand[:, k, :], scalar=qsel[:, cc_:cc_ + 1],
                                                                     in1=oat[:, j, :], op0=ALU.mult, op1=ALU.add),
                             reads=[ck_, "qsel", "oat"], writes=["oat"])
        for j in range(TB):
            g = tt * TB + j
            for h in range(8):
                m, r = h // 2, slice(64 * (h % 2), 64 * (h % 2) + 64)
                p1, p1k = big()
                p2, p2k = big()
                for kb in range(5):
                    slot = (g - 4 + kb) % 8
                    dst = p1[:, kb * 128:(kb + 1) * 128] if kb < 4 else p2[:, 0:128]
                    P.op("pe", lambda e: e.matmul(dst, lhsT=KTb[r, m, slot * 128:(slot + 1) * 128], rhs=QTb[r, m, j * 128:(j + 1) * 128],
                                                  start=True, stop=True), reads=["KTb", "QTb"], writes=[p1k if kb < 4 else p2k])
                sp_ = h % 2
                P.op("dve", lambda e: e.tensor_tensor(out=scs[sp_][:, 0:512], in0=p1[:, :], in1=biasT[:, h, 0:512], op=ALU.add),
                     reads=[p1k, "biasT"], writes=["scsa"])
                P.op("dve", lambda e: e.tensor_tensor(out=scs[sp_][:, 512:640], in0=p2[:, 0:128], in1=biasT[:, h, 512:640], op=ALU.add),
                     reads=[p2k, "biasT"], writes=["scsb"])
                P.op("act", lambda e: e.activation(out=PTb[sp_][:], in_=scs[sp_][:], func=AF.Exp),
                     reads=["scsa", "scsb"], writes=["PTb%d" % sp_])
                for kb in range(5):
                    slot = (g - 4 + kb) % 8
                    P.op("pe", lambda e: e.matmul(ps_o[h // 4][:, h % 4, :], lhsT=PTb[sp_][:, kb * 128:(kb + 1) * 128],
                                                  rhs=Vaug[:, slot, h, :], start=(kb == 0), stop=(kb == 4)),
                         reads=["PTb%d" % sp_, "Vaug"], writes=["ps_o%d" % (h // 4)])
            for hg in range(2):
                P.op("dve", lambda e: e.tensor_scalar(rden[:, hg * 4:hg * 4 + 4], ps_o[hg][:, :, 64], 1e-30, None, ALU.add),
                     reads=["ps_o%d" % hg], writes=["rden%da" % hg])
                P.op("dve", lambda e: e.reciprocal(rden[:, hg * 4:hg * 4 + 4], rden[:, hg * 4:hg * 4 + 4]),
                     reads=["rden%da" % hg], writes=["rden%d" % hg])
            for h in range(8):
                P.op("act", lambda e: e.activation(out=ob[:, h * 64:(h + 1) * 64], in_=ps_o[h // 4][:, h % 4, 0:64], func=AF.Copy,
                                                   scale=rden[:, h:h + 1]), reads=["ps_o%d" % (h // 4), "rden%d" % (h // 4)], writes=["ob"])
            for c in range(4):
                P.op("pe", lambda e: e.transpose(ps_tr[:, c, :], ob[:, c * 128:(c + 1) * 128], idb), reads=["ob", "consts_b"], writes=["aps_tr"])
            P.op("act", lambda e: e.copy(obT[:, :, j * 128:(j + 1) * 128], ps_tr[:, 0:4, :]), reads=["aps_tr"], writes=["obT"])
            for hh in range(4):
                P.op("act", lambda e: e.activation(out=junk[:, 0:128], in_=oat[:, j, hh * 128:(hh + 1) * 128], func=AF.Square,
                                                   accum_out=ssa[:, hh:hh + 1]), reads=["oat"], writes=["ajunk", "ssa"])
            P.op("act", lambda e: e.activation(out=ra[:], in_=ssa[:], func=AF.Ln, scale=1.0 / 128.0, bias=epsc[:, 1:2]),
                 reads=["ssa", "epsc"], writes=["ra0"])
            P.op("act", lambda e: e.activation(out=ra[:], in_=ra[:], func=AF.Exp, scale=-0.5), reads=["ra0"], writes=["ra"])
            for hh in range(4):
                P.op("dve", lambda e: e.scalar_tensor_tensor(out=oan[:, hh * 128:(hh + 1) * 128], in0=oat[:, j, hh * 128:(hh + 1) * 128],
                                                             scalar=ra[:, hh:hh + 1], in1=zs[:, j, hh * 128:(hh + 1) * 128],
                                                             op0=ALU.mult, op1=ALU.mult), reads=["oat", "ra", "zs%d" % j], writes=["oan"])
            for c in range(4):
                P.op("pe", lambda e: e.transpose(ps_tr[:, 4 + c, :], oan[:, c * 128:(c + 1) * 128], idb), reads=["oan", "consts_b"], writes=["aps_tr"])
            P.op("act", lambda e: e.copy(oaT[:, :, j * 128:(j + 1) * 128], ps_tr[:, 4:8, :]), reads=["aps_tr"], writes=["oaT"])
        for mo in range(8):
            py, pyk = big()
            pg, pgk = big()
            for half, (Wb, src, skey) in enumerate(((WbA, oaT, "oaT"), (WbB, obT, "obT"))):
                for c in range(4):
                    P.op("pe", lambda e: e.matmul(py[:, half * TW:(half + 1) * TW], lhsT=Wb[:, c, mo * 128:(mo + 1) * 128], rhs=src[:, c, :],
                                                  start=(c == 0), stop=(c == 3)), reads=["WbA", "WbB", skey], writes=[pyk])
            for half in range(2):
                col0 = 2048 + half * 1024 + mo * 128
                for c in range(NCH):
                    P.op("pe", lambda e: e.matmul(pg[:, half * TW:(half + 1) * TW], lhsT=Wi[:, c, col0:col0 + 128], rhs=hT[:, c, :],
                                                  start=(c == 0), stop=(c == NCH - 1)), reads=["Wi", "ahT"], writes=[pgk])
            q2 = mo % 2
            P.op("act", lambda e: e.activation(out=sg[q2][:], in_=pg[:, :], func=AF.Sigmoid), reads=[pgk], writes=["sg%d" % q2])
            P.op("dve", lambda e: e.tensor_tensor(out=tt_[q2][:], in0=py[:, :], in1=sg[q2][:], op=ALU.mult),
                 reads=[pyk, "sg%d" % q2], writes=["tt%d" % q2])
            P.op("pool", lambda e: e.tensor_tensor(out=mixT[:, mo, :], in0=tt_[q2][:, 0:TW], in1=tt_[q2][:, TW:2 * TW], op=ALU.add),
                 reads=["tt%d" % q2], writes=["mixT"])
        for j in range(TB):
            for half in range(2):
                po, pok = big()
                for c in range(NCH):
                    P.op("pe", lambda e: e.matmul(po[:, :], lhsT=mixT[:, c, j * 128:(j + 1) * 128], rhs=Wo[:, c, half * 512:(half + 1) * 512],
                                                  start=(c == 0), stop=(c == NCH - 1)), reads=["mixT", "Wo"], writes=[pok])
                P.op("dve", lambda e: e.tensor_tensor(out=xt[tp][:, j, half * 512:(half + 1) * 512], in0=po[:, :],
                                                      in1=xt[tp][:, j, half * 512:(half + 1) * 512], op=ALU.add), reads=[pok, xk], writes=[xk])
        P.op("sp", lambda e: e.dma_start(out=xmid[(tt - 2) * TW:(tt - 1) * TW, :].rearrange("(j p) d -> p j d", p=128), in_=xt[tp][:]),
             reads=[xk], writes=["xmid_d"], chan="xmst%d" % tp)


def build_phase2b(nc, P, A, NTM, xmid, w_up, w_down, nwf_d, cfw_d, cfb_d, wfin_d, cst, out2):
    sb, ps = A.sb, A.ps
    cf, cb = load_consts(P, A, cst, "b")
    idb = cb[:, C_ID:C_ID + 128]
    epsc = make_epsc(P, A)
    nwf = sb([128, NCH], F32, "nwf")
    P.op("sp", lambda e: e.dma_start(out=nwf[:], in_=nwf_d[:, :]), writes=["nwf0"], chan="nwf")
    P.op("dve", lambda e: e.tensor_scalar(nwf[:], nwf[:], 32.0, None, ALU.mult), reads=["nwf0"], writes=["nwf"])
    cfw = sb([128, 44, 3], F32, "cfw")
    cfb = sb([128, 44], F32, "cfb")
    wfb = sb([128, D], F32, "wfb")
    P.op("sp", lambda e: e.dma_start(out=cfw[:], in_=cfw_d[:, :, :]), writes=["cfw"], chan="cfw")
    P.op("sp", lambda e: e.dma_start(out=cfb[:], in_=cfb_d[:, :]), writes=["cfb"], chan="cfb")
    P.op("sp", lambda e: e.dma_start(out=wfb[:], in_=wfin_d[:, :]), writes=["wfb0"], chan="wfb")
    P.op("pool", lambda e: e.tensor_scalar(wfb[:], wfb[:], 32.0, None, ALU.mult), reads=["wfb0"], writes=["wfb"])
    Wu = sb([128, NCH, 2 * D_FF], BF16, "Wu")
    Wd = sb([128, 22, D], BF16, "Wd")
    es_stg = ExitStack()
    stg = [es_stg.enter_context(nc.sbuf_tensor("bstg%d" % i, [128, 2048], F32)) for i in range(2)]
    cnt = [0]
    prep_weight(P, stg, "bstg", w_up, 8, 2 * D_FF, Wu, 0, lambda c: nwf[:, c:c + 1], "Wu", ["nwf"], cnt)
    prep_weight(P, stg, "bstg", w_down[0:1408, :], 11, D, Wd, 0, None, "Wd", [], cnt, dst_c0=0)
    prep_weight(P, stg, "bstg", w_down[1408:2816, :], 11, D, Wd, 0, None, "Wd", [], cnt, dst_c0=11)
    es_stg.close()
    P.barrier()

    xm = [sb([128, TB, D], F32, "xm") for _ in range(2)]
    junk = sb([128, D], BF16, "junk")
    ss = sb([128, 8], F32, "ss")
    rs = sb([128, 8], F32, "rs")
    xn = [sb([128, D], BF16, "xn") for _ in range(2)]
    h2T = sb([128, NCH, TW], BF16, "h2T")
    ubuf = [sb([128, 2, TW + 2], F32, "ubuf") for _ in range(2)]
    cv = [sb([128, 2, TW], F32, "cv") for _ in range(2)]
    sgt = [sb([128, TW], F32, "sgt") for _ in range(2)]
    uh = sb([128, 22, 2, 2], F32, "uh")
    actT = sb([128, 22, TW], BF16, "actT")
    outt = [sb([128, D], F32, "outt") for _ in range(2)]
    P.op("pool", lambda e: e.memset(uh[:], 0.0), writes=["uh"])

    ps_tr = ps([128, NCH, 128], BF16, "ps_tr")
    pbig = [ps([128, 512], F32, "pbig") for _ in range(6)]
    bi = [0]

    def big():
        i = bi[0] % 6
        bi[0] += 1
        return pbig[i], "bpbig%d" % i

    for u in range(NTM + 1):
        tp = u % 2
        xk = "bxm%d" % tp
        P.op("sp", lambda e: e.dma_start(out=xm[tp][:], in_=xmid[u * TW:(u + 1) * TW, :].rearrange("(j p) d -> p j d", p=128)),
             reads=["xmid_d"], writes=[xk], chan=xk)
        for j in range(TB):
            norm_block(P, epsc, xm[tp][:, j, :], xk, ss[:, j:j + 1], rs[:, j:j + 1], "bss%d" % j, junk[:], "bjunk",
                       xn[j % 2][:], "bxn%d" % (j % 2), ps_tr, "bps_tr", idb, h2T[:, :, j * 128:(j + 1) * 128], "h2T")
        for m in range(22):
            q2 = m % 2
            pg, pgk = big()
            for half in range(2):
                col0 = half * D_FF + m * 128
                for c in range(NCH):
                    P.op("pe", lambda e: e.matmul(pg[:, half * TW:(half + 1) * TW], lhsT=Wu[:, c, col0:col0 + 128], rhs=h2T[:, c, :],
                                                  start=(c == 0), stop=(c == NCH - 1)), reads=["Wu", "h2T"], writes=[pgk])
            uk = "ubuf%d" % q2
            P.op("pool", lambda e: e.tensor_copy(ubuf[q2][:, :, 0:2], uh[:, m, :, :]), reads=["uh"], writes=[uk + "h"])
            P.op("act", lambda e: e.copy(ubuf[q2][:, :, 2:TW + 2], pg[:, :].rearrange("p (s n) -> p s n", s=2)), reads=[pgk], writes=[uk])
            P.op("pool", lambda e: e.tensor_copy(uh[:, m, :, :], ubuf[q2][:, :, TW:TW + 2]), reads=[uk, uk + "h"], writes=["uh"])
            if u == 0:
                continue
            ck = "cv%d" % q2
            for s_ in range(2):
                ch = s_ * 22 + m
                eng = "dve"
                P.op("act", lambda e: e.activation(out=cv[q2][:, s_, :], in_=ubuf[q2][:, s_, 0:TW], func=AF.Identity,
                                                   scale=cfw[:, ch, 0:1], bias=cfb[:, ch:ch + 1]),
                     reads=[uk, uk + "h", "cfw", "cfb"], writes=[ck + str(s_)])
                for k in range(1, 3):
                    P.op(eng, lambda e: e.scalar_tensor_tensor(out=cv[q2][:, s_, :], in0=ubuf[q2][:, s_, k:k + TW], scalar=cfw[:, ch, k:k + 1],
                                                               in1=cv[q2][:, s_, :], op0=ALU.mult, op1=ALU.add),
                         reads=[uk, uk + "h", "cfw", ck + str(s_)], writes=[ck + str(s_)])
            P.op("act", lambda e: e.activation(out=sgt[q2][:], in_=cv[q2][:, 0, :], func=AF.Silu), reads=[ck + "0"], writes=["sgt%d" % q2])
            P.op("dve", lambda e: e.tensor_tensor(out=actT[:, m, :], in0=sgt[q2][:], in1=cv[q2][:, 1, :], op=ALU.mult),
                 reads=["sgt%d" % q2, ck + "1"], writes=["actT"])
        if u == 0:
            continue
        for j in range(TB):
            for half in range(2):
                po, pok = big()
                for m in range(22):
                    P.op("pe", lambda e: e.matmul(po[:, :], lhsT=actT[:, m, j * 128:(j + 1) * 128], rhs=Wd[:, m, half * 512:(half + 1) * 512],
                                                  start=(m == 0), stop=(m == 21)), reads=["actT", "Wd"], writes=[pok])
                P.op("dve", lambda e: e.tensor_tensor(out=xm[tp][:, j, half * 512:(half + 1) * 512], in0=po[:, :],
                                                      in1=xm[tp][:, j, half * 512:(half + 1) * 512], op=ALU.add), reads=[pok, xk], writes=[xk])
            o2 = j % 2
            P.op("act", lambda e: e.activation(out=junk[:], in_=xm[tp][:, j, :], func=AF.Square, accum_out=ss[:, 4 + j:5 + j]),
                 reads=[xk], writes=["bjunk", "fss%d" % j])
            P.op("act", lambda e: e.activation(out=rs[:, 4 + j:5 + j], in_=ss[:, 4 + j:5 + j], func=AF.Ln, bias=epsc[:, 0:1]),
                 reads=["fss%d" % j, "epsc"], writes=["frs%da" % j])
            P.op("act", lambda e: e.activation(out=rs[:, 4 + j:5 + j], in_=rs[:, 4 + j:5 + j], func=AF.Exp, scale=-0.5),
                 reads=["frs%da" % j], writes=["frs%d" % j])
            P.op("dve", lambda e: e.scalar_tensor_tensor(out=outt[o2][:], in0=xm[tp][:, j, :], scalar=rs[:, 4 + j:5 + j], in1=wfb[:],
                                                         op0=ALU.mult, op1=ALU.mult), reads=[xk, "frs%d" % j, "wfb"], writes=["outt%d" % o2])
            P.op("sp", lambda e: e.dma_start(out=out2[(u - 1) * TW + j * 128:(u - 1) * TW + (j + 1) * 128, :], in_=outt[o2][:]),
                 reads=["outt%d" % o2], chan="ost%d" % o2)


def _phase1_inputs(inp, T):
    x = np.asarray(inp["x"], np.float32)
    w_in = np.asarray(inp["w_in"], np.float32)[0]
    conv = np.asarray(inp["conv_qkv_w"], np.float32)[0]
    a_log = np.asarray(inp["a_log"], np.float32)[0]
    dtb = np.asarray(inp["dt_bias"], np.float32)[0]
    nw = np.asarray(inp["norm_mix_w"], np.float32)[0]
    cst = chunk_consts()
    maps = []
    for core in range(8):
        b, h = core // 4, core % 4
        cols = np.concatenate([np.arange(h * 128, (h + 1) * 128), 512 + np.arange(h * 128, (h + 1) * 128),
                               1024 + np.arange(h * 128, (h + 1) * 128), [2048 + h], [2052 + h]])
        w1 = np.ascontiguousarray(w_in[:, cols])
        cw = np.zeros((128, 12), np.float32)
        for g in range(3):
            cw[:, 4 * g:4 * g + 4] = conv[:, g * 512 + h * 128:g * 512 + (h + 1) * 128].T
        sc = np.zeros((128, 2), np.float32)
        sc[:, 0] = a_log[h]
        sc[:, 1] = dtb[h]
        maps.append({"x1": np.ascontiguousarray(x[b, :T]), "w1": w1, "cw1": cw, "sc1": sc,
                     "nw1": np.ascontiguousarray(nw.reshape(8, 128).T), "cst": cst})
    return maps


def build_p1_program(T):
    nc = bass.Bass("TRN2", target_bir_lowering=False)
    x1 = nc.dram_tensor("x1", [T, D], F32, kind="ExternalInput").ap()
    w1 = nc.dram_tensor("w1", [D, 386], F32, kind="ExternalInput").ap()
    cw1 = nc.dram_tensor("cw1", [128, 12], F32, kind="ExternalInput").ap()
    sc1 = nc.dram_tensor("sc1", [128, 2], F32, kind="ExternalInput").ap()
    nw1 = nc.dram_tensor("nw1", [128, 8], F32, kind="ExternalInput").ap()
    cst = nc.dram_tensor("cst", [128, NCONST], F32, kind="ExternalInput").ap()
    o_out = nc.dram_tensor("o1", [T, 128], F32, kind="ExternalOutput").ap()
    es = ExitStack()
    P = Prog(nc)
    A = Ctx(nc, es, P)
    build_phase1(nc, es, P, A, T, x1, w1, cw1, sc1, nw1, cst, o_out)
    P.finish()
    P.emit(es)
    es.close()
    return nc, P


def run_phase1(inp, T):
    nc, P = build_p1_program(T)
    maps = _phase1_inputs(inp, T)
    res = run_bass_kernel_spmd(nc, maps, core_ids=list(range(8)))
    o = np.zeros((2, T, 4, 128), np.float32)
    for core in range(8):
        o[core // 4, :, core % 4, :] = res.results[core]["o1"]
    return o


def _bias_tile(rel):
    ki = np.arange(128)[:, None]
    qi = np.arange(128)[None, :]
    out = np.zeros((128, 8, 640), np.float32)
    for kb in range(5):
        dist = qi - ki + (4 - kb) * 128
        idx = np.clip(dist, -128, 128) + 128
        cdiff = 2 * (4 - kb) + qi // 64 - ki // 64
        ok = (cdiff >= 0) & (cdiff <= 8)
        for h in range(8):
            out[:, h, kb * 128:(kb + 1) * 128] = np.where(ok, rel[h][idx], NEG)
    return out


def _phase2_inputs(inp, o1, T):
    TC = T // 4
    NTM = TC // TW
    x = np.asarray(inp["x"], np.float32)
    w_in = np.ascontiguousarray(np.asarray(inp["w_in"], np.float32)[0])
    cfw_ = np.asarray(inp["conv_ffn_w"], np.float32)[0]
    cfb_ = np.asarray(inp["conv_ffn_b"], np.float32)[0]
    shared = {
        "w_in": w_in,
        "w_ba": np.ascontiguousarray(np.asarray(inp["w_branch_a"], np.float32)[0]),
        "w_bb": np.ascontiguousarray(np.asarray(inp["w_branch_b"], np.float32)[0]),
        "w_out": np.ascontiguousarray(np.asarray(inp["w_out"], np.float32)[0]),
        "w_up": np.ascontiguousarray(np.asarray(inp["w_up"], np.float32)[0]),
        "w_down": np.ascontiguousarray(np.asarray(inp["w_down"], np.float32)[0]),
        "nwm": np.ascontiguousarray(np.asarray(inp["norm_mix_w"], np.float32)[0].reshape(8, 128).T),
        "nwf": np.ascontiguousarray(np.asarray(inp["norm_ffn_w"], np.float32)[0].reshape(8, 128).T),
        "gnw": np.ascontiguousarray(np.asarray(inp["gdn_norm_w"], np.float32)[0].reshape(128, 1)),
        "biasT": _bias_tile(np.asarray(inp["rel_bias"], np.float32)[0]),
        "cfw": np.ascontiguousarray(cfw_.reshape(3, 44, 128).transpose(2, 1, 0)),
        "cfb": np.ascontiguousarray(cfb_.reshape(44, 128).T),
        "wfin": np.ascontiguousarray(np.broadcast_to(np.asarray(inp["norm_final_w"], np.float32)[None, :], (128, D))),
        "cst2": chunk_consts(),
    }
    maps = []
    for core in range(8):
        b, q = core // 4, core % 4
        t0 = q * TC
        lo = t0 - 3 * TW
        x2 = np.zeros(((NTM + 3) * TW, D), np.float32)
        s0 = max(lo, 0)
        x2[s0 - lo:] = x[b, s0:t0 + TC]
        pos = lo + np.arange((NTM + 3) * TW)
        valid = (pos >= 0).astype(np.float32).reshape((NTM + 3) * TB, 128).T
        m = dict(shared)
        m["x2"] = x2
        m["validc"] = np.ascontiguousarray(valid)
        if o1 is not None:
            lo2 = t0 - TW
            oa2 = np.zeros(((NTM + 1) * TW, 512), np.float32)
            s1 = max(lo2, 0)
            oa2[s1 - lo2:] = o1[b, s1:t0 + TC].reshape(-1, 512)
            m["oa2"] = oa2
        maps.append(m)
    return maps


def _declare_p2(nc, NTM, with_oa):
    d = {}
    def inp(name, shape):
        d[name] = nc.dram_tensor(name, list(shape), F32, kind="ExternalInput").ap()
    inp("x2", [(NTM + 3) * TW, D])
    if with_oa:
        inp("oa2", [(NTM + 1) * TW, 512])
    inp("validc", [128, (NTM + 3) * TB])
    inp("w_in", [D, D_IN]); inp("w_ba", [512, D]); inp("w_bb", [512, D]); inp("w_out", [D, D])
    inp("w_up", [D, 2 * D_FF]); inp("w_down", [D_FF, D]); inp("nwm", [128, 8]); inp("nwf", [128, 8]); inp("gnw", [128, 1])
    inp("biasT", [128, 8, 640]); inp("cfw", [128, 44, 3]); inp("cfb", [128, 44]); inp("wfin", [128, D]); inp("cst2", [128, NCONST])
    d["xmid"] = nc.dram_tensor("xmid", [(NTM + 1) * TW, D], F32, kind="Internal").ap()
    d["out2"] = nc.dram_tensor("out2", [NTM * TW, D], F32, kind="ExternalOutput").ap()
    return d


def _emit_phase2(nc, P, d, NTM, oa_ap, o_all=None, qsel_d=None, RT=None):
    es_a = ExitStack()
    build_phase2a(nc, P, Ctx(nc, es_a, P), NTM, d["x2"], oa_ap, d["validc"], d["w_in"], d["w_ba"], d["w_bb"], d["w_out"],
                  d["nwm"], d["gnw"], d["biasT"], d["cst2"], d["xmid"], o_all=o_all, qsel_d=qsel_d, RT=RT)
    es_a.close()
    P.barrier()
    es_b = ExitStack()
    build_phase2b(nc, P, Ctx(nc, es_b, P), NTM, d["xmid"], d["w_up"], d["w_down"], d["nwf"], d["cfw"], d["cfb"], d["wfin"],
                  d["cst2"], d["out2"])
    es_b.close()


def build_p2_program(T):
    NTM = (T // 4) // TW
    nc = bass.Bass("TRN2", target_bir_lowering=False)
    d = _declare_p2(nc, NTM, True)
    P = Prog(nc)
    _emit_phase2(nc, P, d, NTM, d["oa2"])
    P.finish()
    es = ExitStack()
    P.emit(es)
    es.close()
    return nc, P


def run_phase2(inp, o1, T):
    nc, P = build_p2_program(T)
    maps = _phase2_inputs(inp, o1, T)
    res = run_bass_kernel_spmd(nc, maps, core_ids=list(range(8)))
    TC = T // 4
    out = np.zeros((2, T, D), np.float32)
    for core in range(8):
        out[core // 4, (core % 4) * TC:(core % 4 + 1) * TC] = res.results[core]["out2"]
    return out


T_FULL = 16384


def build_fused_program(T):
    NTM = (T // 4) // TW
    RT = T + TW
    nc = bass.Bass("TRN2", target_bir_lowering=False)
    x1 = nc.dram_tensor("x1", [T, D], F32, kind="ExternalInput").ap()
    w1 = nc.dram_tensor("w1", [D, 386], F32, kind="ExternalInput").ap()
    cw1 = nc.dram_tensor("cw1", [128, 12], F32, kind="ExternalInput").ap()
    sc1 = nc.dram_tensor("sc1", [128, 2], F32, kind="ExternalInput").ap()
    nw1 = nc.dram_tensor("nw1", [128, 8], F32, kind="ExternalInput").ap()
    cst = nc.dram_tensor("cst", [128, NCONST], F32, kind="ExternalInput").ap()
    qsel_d = nc.dram_tensor("qsel", [128, 8], F32, kind="ExternalInput").ap()
    o_loc = nc.dram_tensor("o_loc", [RT, 128], F32, kind="Internal").ap()
    o_all = nc.dram_tensor("o_all", [8 * RT, 128], F32, kind="Internal").ap()
    d = _declare_p2(nc, NTM, False)
    P = Prog(nc)
    es1 = ExitStack()
    A1 = Ctx(nc, es1, P)
    zt = A1.sb([128, 128], F32, "zt")
    P.op("pool", lambda e: e.memset(zt[:], 0.0), writes=["zt"])
    for i in range(TW // 128):
        P.op("sp", lambda e: e.dma_start(out=o_loc[i * 128:(i + 1) * 128, :], in_=zt[:]), reads=["zt"], chan="zt")
    build_phase1(nc, es1, P, A1, T, x1, w1, cw1, sc1, nw1, cst, o_loc[TW:RT, :])
    es1.close()
    P.barrier()
    P.op("pool", lambda e: e.collective_compute("AllGather", ALU.bypass, replica_groups=[list(range(8))],
                                                ins=[o_loc[:, :]], outs=[o_all[:, :]]),
         writes=["oall"], chan="cc", inc_override=1)
    _emit_phase2(nc, P, d, NTM, None, o_all=o_all, qsel_d=qsel_d, RT=RT)
    P.finish()
    es = ExitStack()
    P.emit(es)
    es.close()
    return nc, P


def build_fused_nocc(T):
    NTM = (T // 4) // TW
    RT = T + TW
    nc = bass.Bass("TRN2", target_bir_lowering=False)
    x1 = nc.dram_tensor("x1", [T, D], F32, kind="ExternalInput").ap()
    w1a = nc.dram_tensor("w1a", [4, D, 386], F32, kind="ExternalInput").ap()
    cw1a = nc.dram_tensor("cw1a", [4, 128, 12], F32, kind="ExternalInput").ap()
    sc1a = nc.dram_tensor("sc1a", [4, 128, 2], F32, kind="ExternalInput").ap()
    nw1 = nc.dram_tensor("nw1", [128, 8], F32, kind="ExternalInput").ap()
    cst = nc.dram_tensor("cst", [128, NCONST], F32, kind="ExternalInput").ap()
    qsel_d = nc.dram_tensor("qsel", [128, 4], F32, kind="ExternalInput").ap()
    o_loc = nc.dram_tensor("o_loc", [RT, 512], F32, kind="Internal").ap()
    d = _declare_p2(nc, NTM, False)
    P = Prog(nc)
    for h in range(4):
        es1 = ExitStack()
        A1 = Ctx(nc, es1, P)
        if h == 0:
            zt = A1.sb([128, 512], F32, "zt")
            P.op("pool", lambda e: e.memset(zt[:], 0.0), writes=["zt"])
            for i in range(TW // 128):
                P.op("sp", lambda e: e.dma_start(out=o_loc[i * 128:(i + 1) * 128, :], in_=zt[:]), reads=["zt"], chan="zt")
        build_phase1(nc, es1, P, A1, T, x1, w1a[h], cw1a[h], sc1a[h], nw1, cst, o_loc[TW:RT, h * 128:(h + 1) * 128])
        es1.close()
        P.barrier()
    _emit_phase2(nc, P, d, NTM, None, o_all=o_loc, qsel_d=qsel_d, RT=None)
    P.finish()
    es = ExitStack()
    P.emit(es)
    es.close()
    return nc, P


def run_fused_nocc(inp, T):
    nc, P = build_fused_nocc(T)
    m1 = _phase1_inputs(inp, T)
    m2 = _phase2_inputs(inp, None, T)
    maps = []
    for core in range(8):
        b, q = core // 4, core % 4
        m = dict(m2[core])
        m["x1"] = m1[core]["x1"]
        m["nw1"] = m1[core]["nw1"]
        m["cst"] = m1[core]["cst"]
        m["w1a"] = np.stack([m1[4 * b + h]["w1"] for h in range(4)])
        m["cw1a"] = np.stack([m1[4 * b + h]["cw1"] for h in range(4)])
        m["sc1a"] = np.stack([m1[4 * b + h]["sc1"] for h in range(4)])
        qs = np.zeros((128, 4), np.float32)
        qs[:, q] = 1.0
        m["qsel"] = qs
        maps.append(m)
    res = run_bass_kernel_spmd(nc, maps, core_ids=list(range(8)))
    TC = T // 4
    out = np.zeros((2, T, D), np.float32)
    for core in range(8):
        out[core // 4, (core % 4) * TC:(core % 4 + 1) * TC] = res.results[core]["out2"]
    return out


def run_fused(inp, T):
    nc, P = build_fused_program(T)
    m1 = _phase1_inputs(inp, T)
    m2 = _phase2_inputs(inp, None, T)
    maps = []
    for core in range(8):
        m = dict(m1[core])
        m.update(m2[core])
        qs = np.zeros((128, 8), np.float32)
        qs[:, core] = 1.0
        m["qsel"] = qs
        maps.append(m)
    res = run_bass_kernel_spmd(nc, maps, core_ids=list(range(8)))
    TC = T // 4
    out = np.zeros((2, T, D), np.float32)
    for core in range(8):
        out[core // 4, (core % 4) * TC:(core % 4 + 1) * TC] = res.results[core]["out2"]
    return out


def kernel(**inputs):
    return run_fused_nocc(inputs, T_FULL)
```

```python
from collections import defaultdict
from contextlib import ExitStack

import numpy as np
import concourse.bass as bass
import concourse.mybir as mybir
from concourse.bass_utils import run_bass_kernel_spmd

F32 = mybir.dt.float32
BF16 = mybir.dt.bfloat16
AF = mybir.ActivationFunctionType
ALU = mybir.AluOpType

D = 1024
NCH = 8
EPS = 1e-6
CH = 64
DK = 128
D_IN = 5640
D_FF = 2816
NEG = -30000.0


PSUM_PREFIXES = ("psl", "ptb", "ppj", "ps_tr", "aps_tr", "bps_tr", "pbig", "bpbig", "ps_o")


class _Rec:
    def __getattr__(self, name):
        def f(*a, **k):
            self.call = (name, a, k)
            return self
        return f


class Prog:
    ENGS = ("pe", "act", "dve", "pool", "sp")

    def __init__(self, nc):
        self.nc = nc
        self.streams = {e: [] for e in self.ENGS}
        self.count = defaultdict(int)
        self.lastw = {}
        self.readers = defaultdict(list)
        self.waited = defaultdict(int)
        self.nops = 0
        self.epoch = 0
        import os
        self.cut = int(os.environ["PCUT"]) if "PCUT" in os.environ else None

    def _dep(self, eng, rec):
        semkey, val = rec[0], rec[1]
        if eng == "pool" and semkey.startswith("dma_cc@"):
            return
        if self.waited[(eng, semkey)] < val:
            self.waited[(eng, semkey)] = val
            self.streams[eng].append(("wait", semkey, val))

    def op(self, eng, fn, reads=(), writes=(), chan=None, inc_override=None):
        if self.cut is not None and self.nops >= self.cut:
            return
        isdma = chan is not None
        for k in reads:
            w = self.lastw.get(k)
            if w is not None:
                self._dep(eng, w)
            if k.startswith(PSUM_PREFIXES):
                for r in self.readers[k]:
                    if r[2] != eng:
                        self._dep(eng, r)
        for k in writes:
            w = self.lastw.get(k)
            if w is not None:
                if not (w[2] == eng == "pe" and not w[3] and not isdma):
                    self._dep(eng, w)
            for r in self.readers[k]:
                if r[2] != eng or r[3] or isdma:
                    self._dep(eng, r)
        if isdma:
            semkey, inc = "dma_%s@%d" % (chan, self.epoch), (inc_override or 16)
        else:
            semkey, inc = "%s@%d" % (eng, self.epoch), 1
        self.count[semkey] += inc
        rec = (semkey, self.count[semkey], eng, isdma)
        rec_ = _Rec()
        fn(rec_)
        self.streams[eng].append(("op", rec_.call, semkey, inc))
        for k in writes:
            self.lastw[k] = rec
            self.readers[k] = []
        for k in reads:
            self.readers[k].append(rec)
        self.nops += 1

    def barrier(self):
        for e in self.ENGS:
            for semkey, val in list(self.count.items()):
                if val:
                    self._dep(e, (semkey, val))
        self.lastw.clear()
        self.readers.clear()
        self.epoch += 1

    def finish(self):
        for semkey, val in list(self.count.items()):
            if semkey.startswith("dma_"):
                self._dep("sp", (semkey, val))
        for semkey, val in list(self.count.items()):
            if not semkey.startswith("dma_") and val:
                self._dep("sp", (semkey, val))

    def emit(self, es):
        nc = self.nc
        sems = {}
        for i, k in enumerate(sorted(self.count)):
            sems[k] = es.enter_context(nc.semaphore("s%d" % i))
        block = es.enter_context(nc.Block())
        streams = self.streams

        def run(eng_handle, items):
            for it in items:
                if it[0] == "wait":
                    eng_handle.wait_ge(sems[it[1]], it[2])
                else:
                    name, a, k = it[1]
                    getattr(eng_handle, name)(*a, **k).then_inc(sems[it[2]], it[3])

        @block.tensor
        def _(e):
            run(e, streams["pe"])

        @block.scalar
        def _(e):
            run(e, streams["act"])

        @block.vector
        def _(e):
            run(e, streams["dve"])

        @block.gpsimd
        def _(e):
            run(e, streams["pool"])

        @block.sync
        def _(e):
            run(e, streams["sp"])


class Ctx:
    _uid = [0]

    def __init__(self, nc, es, P):
        self.nc, self.es, self.P = nc, es, P
        Ctx._uid[0] += 1
        self.n = Ctx._uid[0] * 1000

    def sb(self, shape, dt=F32, name=None):
        self.n += 1
        return self.es.enter_context(self.nc.sbuf_tensor("%s_%d" % (name or "t", self.n), list(shape), dt))

    def ps(self, shape, dt=F32, name=None):
        self.n += 1
        return self.es.enter_context(self.nc.psum_tensor("%s_%d" % (name or "p", self.n), list(shape), dt))


def chunk_consts():
    j = np.arange(128)
    same = (j[:, None] // CH) == (j[None, :] // CH)
    m1 = (same & (j[:, None] <= j[None, :])).astype(np.float32)
    m2 = (same & (j[:, None] > j[None, :])).astype(np.float32)
    ident = np.eye(128, dtype=np.float32)
    ones = np.ones((128, 128), np.float32)
    cind = np.zeros((128, 128), np.float32)
    cind[:64, 0] = 1.0
    cind[64:, 1] = 1.0
    return np.concatenate([m1, m2, ident, ones, cind], axis=1)


C_M1, C_M2, C_ID, C_ONES, C_CIND = 0, 128, 256, 384, 512
NCONST = 640


def make_epsc(P, A):
    epsc = A.sb([128, 2], F32, "epsc")
    P.op("pool", lambda e: e.memset(epsc[:, 0:1], D * EPS), writes=["epsc0"])
    P.op("pool", lambda e: e.memset(epsc[:, 1:2], EPS), reads=["epsc0"], writes=["epsc"])
    return epsc


def norm_block(P, epsc, x_blk, xkey, ss, rs, sskey, junk, junkkey, xn, xnkey, ps_tr, pskey, idb, hT_dst, hTkey,
               wrow=None, wkey=None):
    P.op("act", lambda e: e.activation(out=junk, in_=x_blk, func=AF.Square, accum_out=ss),
         reads=[xkey], writes=[junkkey, sskey])
    P.op("act", lambda e: e.activation(out=rs, in_=ss, func=AF.Ln, bias=epsc[:, 0:1]),
         reads=[sskey, "epsc"], writes=[sskey + "r0"])
    P.op("act", lambda e: e.activation(out=rs, in_=rs, func=AF.Exp, scale=-0.5),
         reads=[sskey + "r0"], writes=[sskey + "r"])
    if wrow is None:
        P.op("dve", lambda e: e.tensor_scalar(xn, x_blk, rs, None, ALU.mult),
             reads=[xkey, sskey + "r"], writes=[xnkey])
    else:
        P.op("dve", lambda e: e.scalar_tensor_tensor(out=xn, in0=x_blk, scalar=rs, in1=wrow,
                                                      op0=ALU.mult, op1=ALU.mult),
             reads=[xkey, sskey + "r", wkey], writes=[xnkey])
    for c in range(NCH):
        P.op("pe", lambda e, c=c: e.transpose(ps_tr[:, c, :], xn[:, c * 128:(c + 1) * 128], idb),
             reads=[xnkey, "consts_b"], writes=[pskey])
    P.op("act", lambda e: e.copy(hT_dst, ps_tr[:, :, :]), reads=[pskey], writes=[hTkey])


def build_phase1(nc, es, P, A, T, x1, w1, cw1, sc1, nw1, cst, o_out):
    NT = T // 512
    sb, ps = A.sb, A.ps
    cf = sb([128, NCONST], F32, "cf")
    cb = sb([128, NCONST], BF16, "cb")
    P.op("sp", lambda e: e.dma_start(out=cf[:], in_=cst[:, :]), writes=["consts_f"], chan="cf")
    P.op("dve", lambda e: e.tensor_copy(cb[:], cf[:]), reads=["consts_f"], writes=["consts_b"])
    m1f, m2f = cf[:, C_M1:C_M1 + 128], cf[:, C_M2:C_M2 + 128]
    idf, onesf, cindf = cf[:, C_ID:C_ID + 128], cf[:, C_ONES:C_ONES + 128], cf[:, C_CIND:C_CIND + 2]
    idb, onesb = cb[:, C_ID:C_ID + 128], cb[:, C_ONES:C_ONES + 128]

    epsc = make_epsc(P, A)
    wf = sb([128, NCH, 386], F32, "wf")
    wb = sb([128, NCH, 386], BF16, "wb")
    nw = sb([128, NCH], F32, "nw")
    cw = sb([128, 12], F32, "cw")
    sc = sb([128, 2], F32, "sc")
    negA = sb([128, 1], F32, "negA")
    P.op("sp", lambda e: e.dma_start(out=wf[:], in_=w1.rearrange("(c p) n -> p c n", p=128)), writes=["wf"], chan="wf")
    P.op("sp", lambda e: e.dma_start(out=nw[:], in_=nw1[:, :]), writes=["nw"], chan="nw")
    P.op("sp", lambda e: e.dma_start(out=cw[:], in_=cw1[:, :]), writes=["cw"], chan="cw")
    P.op("sp", lambda e: e.dma_start(out=sc[:], in_=sc1[:, :]), writes=["sc"], chan="sc")
    for c in range(NCH):
        P.op("dve", lambda e, c=c: e.tensor_scalar(wb[:, c, :], wf[:, c, :], nw[:, c:c + 1], 32.0, ALU.mult, ALU.mult),
             reads=["wf", "nw"], writes=["wb"])
    P.op("act", lambda e: e.activation(out=negA[:], in_=sc[:, 0:1], func=AF.Exp), reads=["sc"], writes=["negA0"])
    P.op("dve", lambda e: e.tensor_scalar(negA[:], negA[:], -1.0, None, ALU.mult), reads=["negA0"], writes=["negA"])

    xt = [sb([128, 4, D], F32, "xt") for _ in range(2)]
    junk = sb([128, D], BF16, "junk")
    ss = sb([128, 8], F32, "ss")
    rs = sb([128, 8], F32, "rs")
    xn = [sb([128, D], BF16, "xn") for _ in range(2)]
    hT = [sb([128, NCH, 512], BF16, "hT") for _ in range(2)]
    cbuf = [sb([128, 3 + 512], F32, "cbuf") for _ in range(3)]
    acc = [sb([128, 512], F32, "acc") for _ in range(3)]
    sil = [sb([128, 512], F32, "sil") for _ in range(2)]
    sq = [sb([128, 512], BF16, "sq") for _ in range(2)]
    rn = [sb([128, 512], F32, "rn") for _ in range(2)]
    QT = [sb([128, 512], BF16, "QT") for _ in range(2)]
    KT = [sb([128, 512], BF16, "KT") for _ in range(2)]
    VT = [sb([128, 512], BF16, "VT") for _ in range(2)]
    bdt = [sb([128, 4, 2], F32, "bdt") for _ in range(2)]
    gsc = [sb([128, 8, 4], F32, "gsc") for _ in range(2)]
    def four(shape, dt, name):
        return [sb(shape, dt, name) for _ in range(4)]

    def eight(shape, dt, name):
        return [[sb(shape, dt, name) for _ in range(4)] for _ in range(2)]

    gM = four([128, 128], F32, "gM")
    rgc = four([128, 2], F32, "rgc")
    D1 = four([128, 128], F32, "D1")
    D2 = four([128, 128], F32, "D2")
    bg = four([128, 1], F32, "bg")
    bgK = four([128, 128], BF16, "bgK")
    Bm = four([128, 128], F32, "Bm")
    Bq = four([128, 128], F32, "Bq")
    Nq = four([128, 128], F32, "Nq")
    Rq = four([128, 128], F32, "Rq")
    Rt = four([128, 128], F32, "Rt")
    PTm = four([128, 128], F32, "PTm")
    smx = eight([128, 4], F32, "smx")
    KD = eight([128, 128], BF16, "KD")
    bV = eight([128, 128], BF16, "bV")
    TTb = eight([128, 128], BF16, "TTb")
    PT = eight([128, 128], BF16, "PT")
    nWT = eight([128, 128], BF16, "nWT")
    Ub = [sb([128, 128], BF16, "Ub") for _ in range(2)]
    pus = [sb([128, 128], F32, "pus") for _ in range(2)]
    Osb = [sb([128, 128], F32, "Osb") for _ in range(2)]
    Sf = [sb([128, 128], F32, "Sf") for _ in range(2)]
    Sb = [sb([128, 128], BF16, "Sb") for _ in range(2)]

    ps_tr = ps([128, NCH, 128], BF16, "ps_tr")
    ps_tb = ps([128, 8, 128], BF16, "ps_tb")
    ps_pj = [ps([128, 512], F32, "ps_pj") for _ in range(1)]
    ps_ch = ps([128, 4, 128], F32, "ps_ch")
    ps_sl = [ps([128, 4, 128], F32, "ps_sl") for _ in range(4)]
    pj_i = [0]

    def pjslot():
        i = pj_i[0] % len(ps_pj)
        pj_i[0] += 1
        return ps_pj[i], "ppj%d" % i


    P.op("pool", lambda e: e.memset(Sf[0][:], 0.0), writes=["Sf0"])
    P.op("pool", lambda e: e.memset(Sb[0][:], 0.0), writes=["Sb0"])
    for g in range(3):
        P.op("pool", lambda e, g=g: e.memset(cbuf[g][:, 0:3], 0.0), writes=["cbufh%d" % g])
    sidx = [0]
    chain_q = []

    for ti in range(NT):
        tp = ti % 2
        xk = "xt%d" % tp
        P.op("sp", lambda e, ti=ti, tp=tp: e.dma_start(
            out=xt[tp][:], in_=x1[ti * 512:(ti + 1) * 512, :].rearrange("(j p) d -> p j d", p=128)),
            writes=[xk], chan=xk)
        hk = "hT%d" % tp
        for j in range(4):
            bp = j % 2
            norm_block(P, epsc, xt[tp][:, j, :], xk, ss[:, j + 4 * tp:j + 4 * tp + 1], rs[:, j + 4 * tp:j + 4 * tp + 1],
                       "ss%d_%d" % (tp, j), junk[:], "junk", xn[bp][:], "xn%d" % bp, ps_tr, "ps_tr", idb,
                       hT[tp][:, :, j * 128:(j + 1) * 128], hk)
        for g in range(3):
            pj, pjk = pjslot()
            for c in range(NCH):
                P.op("pe", lambda e, g=g, c=c, pj=pj: e.matmul(pj[:], lhsT=wb[:, c, g * 128:(g + 1) * 128],
                                                              rhs=hT[tp][:, c, :], start=(c == 0), stop=(c == NCH - 1)),
                     reads=["wb", hk], writes=[pjk])
            P.op("act", lambda e, g=g, pj=pj: e.copy(cbuf[g][:, 3:515], pj[:]), reads=[pjk], writes=["cbufm%d" % g])
        pj, pjk = pjslot()
        for j in range(4):
            for c in range(NCH):
                P.op("pe", lambda e, j=j, c=c, pj=pj: e.matmul(pj[:, 2 * j:2 * j + 2], lhsT=hT[tp][:, c, j * 128:(j + 1) * 128],
                                                              rhs=wb[:, c, 384:386], start=(c == 0), stop=(c == NCH - 1)),
                     reads=["wb", hk], writes=[pjk])
        bk = "bdt%d" % tp
        P.op("dve", lambda e, pj=pj: e.tensor_copy(bdt[tp][:].rearrange("p a b -> p (a b)"), pj[:, 0:8]), reads=[pjk], writes=[bk])
        for g in range(3):
            ck = ["cbufh%d" % g, "cbufm%d" % g]
            ak = "acc%d" % g
            P.op("dve", lambda e, g=g: e.tensor_scalar(acc[g][:], cbuf[g][:, 0:512], cw[:, 4 * g:4 * g + 1], None, ALU.mult),
                 reads=ck + ["cw"], writes=[ak])
            for k in range(1, 4):
                P.op("dve", lambda e, g=g, k=k: e.scalar_tensor_tensor(
                    out=acc[g][:], in0=cbuf[g][:, k:k + 512], scalar=cw[:, 4 * g + k:4 * g + k + 1], in1=acc[g][:],
                    op0=ALU.mult, op1=ALU.add), reads=ck + ["cw", ak], writes=[ak])
            P.op("pool", lambda e, g=g: e.tensor_copy(cbuf[g][:, 0:3], cbuf[g][:, 512:515]),
                 reads=["cbufm%d" % g, ak], writes=["cbufh%d" % g])
        qk, kk, vk = "QT%d" % tp, "KT%d" % tp, "VT%d" % tp
        P.op("act", lambda e: e.activation(out=VT[tp][:], in_=acc[2][:], func=AF.Silu), reads=["acc2"], writes=[vk])
        for g in range(2):
            P.op("act", lambda e, g=g: e.activation(out=sil[g][:], in_=acc[g][:], func=AF.Silu), reads=["acc%d" % g], writes=["sil%d" % g])
            P.op("act", lambda e, g=g: e.activation(out=sq[g][:], in_=sil[g][:], func=AF.Square), reads=["sil%d" % g], writes=["sq%d" % g])
            pj, pjk = pjslot()
            P.op("pe", lambda e, g=g, pj=pj: e.matmul(pj[:], lhsT=onesb, rhs=sq[g][:], start=True, stop=True),
                 reads=["consts_b", "sq%d" % g], writes=[pjk])
            P.op("act", lambda e, g=g, pj=pj: e.activation(out=rn[g][:], in_=pj[:], func=AF.Ln, bias=epsc[:, 1:2]),
                 reads=[pjk, "epsc"], writes=["rn%da" % g])
            P.op("act", lambda e, g=g: e.activation(out=rn[g][:], in_=rn[g][:], func=AF.Exp, scale=-0.5),
                 reads=["rn%da" % g], writes=["rn%d" % g])
        P.op("dve", lambda e: e.scalar_tensor_tensor(out=QT[tp][:], in0=sil[0][:], scalar=float(DK) ** -0.5, in1=rn[0][:],
                                                      op0=ALU.mult, op1=ALU.mult), reads=["sil0", "rn0"], writes=[qk])
        P.op("dve", lambda e: e.tensor_tensor(out=KT[tp][:], in0=sil[1][:], in1=rn[1][:], op=ALU.mult), reads=["sil1", "rn1"], writes=[kk])
        G = gsc[tp]
        gk = "gsc%d" % tp
        xg, ax, ee, ll, sp_, gg, be, nbe = (G[:, i, :] for i in range(8))
        P.op("dve", lambda e: e.tensor_scalar(xg, bdt[tp][:, :, 1], sc[:, 1:2], None, ALU.add), reads=[bk, "sc"], writes=[gk + "a"])
        P.op("dve", lambda e: e.scalar_tensor_tensor(out=ax, in0=xg, scalar=-1.0, in1=xg, op0=ALU.mult, op1=ALU.max), reads=[gk + "a"], writes=[gk + "b"])
        P.op("act", lambda e: e.activation(out=ee, in_=ax, func=AF.Exp, scale=-1.0), reads=[gk + "b"], writes=[gk + "c"])
        P.op("act", lambda e: e.activation(out=ll, in_=ee, func=AF.Ln, bias=1.0), reads=[gk + "c"], writes=[gk + "d"])
        P.op("dve", lambda e: e.scalar_tensor_tensor(out=sp_, in0=xg, scalar=0.0, in1=ll, op0=ALU.max, op1=ALU.add),
             reads=[gk + "a", gk + "d"], writes=[gk + "e"])
        P.op("dve", lambda e: e.tensor_scalar(gg, sp_, negA[:, 0:1], None, ALU.mult), reads=[gk + "e", "negA"], writes=[gk + "g"])
        P.op("act", lambda e: e.activation(out=be, in_=bdt[tp][:, :, 0], func=AF.Sigmoid), reads=[bk], writes=[gk + "be"])
        P.op("dve", lambda e: e.tensor_scalar(nbe, be, -1.0, None, ALU.mult), reads=[gk + "be"], writes=[gk + "nb"])

        def bk(j):
            return ps_sl[j], "psl%d" % j

        def stage_done():
            if chain_q:
                chain_q.pop(0)()

        J = range(4)
        sfx = ["_%d" % j for j in J]
        csl = [slice(j * 128, (j + 1) * 128) for j in J]
        g_ = [gg[:, j:j + 1] for j in J]
        be_ = [be[:, j:j + 1] for j in J]
        nbe_ = [nbe[:, j:j + 1] for j in J]
        ck = ["_%d_%d" % (tp, j) for j in J]
        for j in J:
            P.op("dve", lambda e: e.tensor_scalar(gM[j][:], m1f, g_[j], None, ALU.mult), reads=["consts_f", gk + "g"], writes=["gM" + sfx[j]])
            P.op("dve", lambda e: e.tensor_scalar(rgc[j][:], cindf, g_[j], None, ALU.mult), reads=["consts_f", gk + "g"], writes=["rgc" + sfx[j]])
        for j in J:
            b_, bkk = bk(j)
            P.op("pe", lambda e: e.matmul(b_[:, 0, :], lhsT=gM[j][:], rhs=m2f, start=True, stop=True), reads=["gM" + sfx[j], "consts_f"], writes=[bkk])
            P.op("pe", lambda e: e.matmul(b_[:, 1, :], lhsT=m2f, rhs=gM[j][:], start=True, stop=True), reads=["gM" + sfx[j], "consts_f"], writes=[bkk])
            P.op("pe", lambda e: e.matmul(b_[:, 2, 0:1], lhsT=m1f, rhs=g_[j], start=True, stop=True), reads=[gk + "g", "consts_f"], writes=[bkk])
            P.op("pe", lambda e: e.matmul(b_[:, 2, 1:2], lhsT=m2f, rhs=g_[j], start=True, stop=True), reads=[gk + "g", "consts_f"], writes=[bkk])
            P.op("pe", lambda e: e.matmul(b_[:, 2, 2:4], lhsT=onesf, rhs=rgc[j][:], start=True, stop=True), reads=["rgc" + sfx[j], "consts_f"], writes=[bkk])
        for j in J:
            b_, bkk = bk(j)
            P.op("act", lambda e: e.activation(out=D1[j][:], in_=b_[:, 0, :], func=AF.Exp), reads=[bkk], writes=["D1" + sfx[j]])
            P.op("act", lambda e: e.activation(out=D2[j][:], in_=b_[:, 1, :], func=AF.Exp), reads=[bkk], writes=["D2" + sfx[j]])
            P.op("act", lambda e: e.activation(out=smx[tp][j][:], in_=b_[:, 2, 0:4], func=AF.Exp), reads=[bkk], writes=["smx" + ck[j]])
        for j in J:
            P.op("dve", lambda e: e.tensor_tensor(out=bg[j][:], in0=be_[j], in1=smx[tp][j][:, 0:1], op=ALU.mult),
                 reads=[gk + "be", "smx" + ck[j]], writes=["bg" + sfx[j]])
            P.op("pool", lambda e: e.tensor_tensor(out=PTm[j][:], in0=D2[j][:], in1=m1f, op=ALU.mult), reads=["D2" + sfx[j], "consts_f"], writes=["PTm" + sfx[j]])
        stage_done()
        for j in J:
            P.op("pe", lambda e: e.transpose(ps_tb[:, 2 * j, :], KT[tp][:, csl[j]], idb), reads=[kk, "consts_b"], writes=["ptb"])
            P.op("pe", lambda e: e.transpose(ps_tb[:, 2 * j + 1, :], VT[tp][:, csl[j]], idb), reads=[vk, "consts_b"], writes=["ptb"])
        for j in J:
            P.op("dve", lambda e: e.tensor_scalar(bgK[j][:], ps_tb[:, 2 * j, :], bg[j][:, 0:1], None, ALU.mult), reads=["ptb", "bg" + sfx[j]], writes=["bgK" + sfx[j]])
        for j in J:
            P.op("act", lambda e: e.activation(out=KD[tp][j][:], in_=ps_tb[:, 2 * j, :], func=AF.Copy, scale=smx[tp][j][:, 1:2]),
                 reads=["ptb", "smx" + ck[j]], writes=["KD" + ck[j]])
            P.op("act", lambda e: e.activation(out=bV[tp][j][:], in_=ps_tb[:, 2 * j + 1, :], func=AF.Copy, scale=be_[j]),
                 reads=["ptb", gk + "be"], writes=["bV" + ck[j]])
        stage_done()
        for j in J:
            b_, bkk = bk(j)
            P.op("pe", lambda e: e.matmul(b_[:, 0, :], lhsT=KT[tp][:, csl[j]], rhs=KT[tp][:, csl[j]], start=True, stop=True), reads=[kk], writes=[bkk])
            P.op("pe", lambda e: e.matmul(b_[:, 1, :], lhsT=KT[tp][:, csl[j]], rhs=QT[tp][:, csl[j]], start=True, stop=True), reads=[kk, qk], writes=[bkk])
        for j in J:
            b_, bkk = bk(j)
            P.op("dve", lambda e: e.tensor_tensor(out=Bm[j][:], in0=b_[:, 0, :], in1=D1[j][:], op=ALU.mult), reads=[bkk, "D1" + sfx[j]], writes=["Bm" + sfx[j]])
            P.op("dve", lambda e: e.tensor_tensor(out=PT[tp][j][:], in0=b_[:, 1, :], in1=PTm[j][:], op=ALU.mult), reads=[bkk, "PTm" + sfx[j]], writes=["PT" + ck[j]])
            P.op("dve", lambda e: e.scalar_tensor_tensor(out=Bq[j][:], in0=Bm[j][:], scalar=nbe_[j], in1=m2f, op0=ALU.mult, op1=ALU.mult),
                 reads=["Bm" + sfx[j], gk + "nb", "consts_f"], writes=["B" + sfx[j]])
        stage_done()
        for j in J:
            b_, bkk = bk(j)
            P.op("pe", lambda e: e.transpose(b_[:, 2, :], Bq[j][:], idf), reads=["B" + sfx[j], "consts_f"], writes=[bkk])
        for j in J:
            b_, bkk = bk(j)
            P.op("act", lambda e: e.copy(Nq[j][:], b_[:, 2, :]), reads=[bkk], writes=["N" + sfx[j]])
            P.op("pool", lambda e: e.tensor_tensor(out=Rt[j][:], in0=Bq[j][:], in1=idf, op=ALU.add), reads=["B" + sfx[j], "consts_f"], writes=["Rt" + sfx[j]])
        for j in J:
            P.op("dve", lambda e: e.tensor_tensor(out=Rq[j][:], in0=Nq[j][:], in1=idf, op=ALU.add), reads=["N" + sfx[j], "consts_f"], writes=["R" + sfx[j]])
        stage_done()
        for lvl in range(5):
            last = lvl == 4
            for j in J:
                b_, bkk = bk(j)
                P.op("pe", lambda e: e.matmul(b_[:, 0, :], lhsT=Bq[j][:], rhs=Nq[j][:], start=True, stop=True), reads=["B" + sfx[j], "N" + sfx[j]], writes=[bkk])
                if not last:
                    P.op("pe", lambda e: e.matmul(b_[:, 1, :], lhsT=Nq[j][:], rhs=Bq[j][:], start=True, stop=True), reads=["B" + sfx[j], "N" + sfx[j]], writes=[bkk])
            for j in J:
                b_, bkk = bk(j)
                P.op("act", lambda e: e.copy(Nq[j][:], b_[:, 0, :]), reads=[bkk], writes=["N" + sfx[j]])
                if not last:
                    P.op("act", lambda e: e.copy(Bq[j][:], b_[:, 1, :]), reads=[bkk], writes=["B" + sfx[j]])
            stage_done()
            for j in J:
                b_, bkk = bk(j)
                P.op("pe", lambda e: e.matmul(b_[:, 2, :], lhsT=Rt[j][:], rhs=Nq[j][:], start=True, stop=True), reads=["Rt" + sfx[j], "N" + sfx[j]], writes=[bkk])
                if not last:
                    P.op("pe", lambda e: e.matmul(b_[:, 3, :], lhsT=Rq[j][:], rhs=Bq[j][:], start=True, stop=True), reads=["R" + sfx[j], "B" + sfx[j]], writes=[bkk])
            for j in J:
                b_, bkk = bk(j)
                if not last:
                    P.op("dve", lambda e: e.tensor_tensor(out=Rq[j][:], in0=b_[:, 2, :], in1=Rq[j][:], op=ALU.add), reads=[bkk, "R" + sfx[j]], writes=["R" + sfx[j]])
                    P.op("dve", lambda e: e.tensor_tensor(out=Rt[j][:], in0=b_[:, 3, :], in1=Rt[j][:], op=ALU.add), reads=[bkk, "Rt" + sfx[j]], writes=["Rt" + sfx[j]])
                else:
                    P.op("dve", lambda e: e.tensor_tensor(out=TTb[tp][j][:], in0=b_[:, 2, :], in1=Rq[j][:], op=ALU.add), reads=[bkk, "R" + sfx[j]], writes=["TTb" + ck[j]])
            stage_done()
        for j in J:
            b_, bkk = bk(j)
            P.op("pe", lambda e: e.matmul(b_[:, 0, :], lhsT=bgK[j][:], rhs=TTb[tp][j][:], start=True, stop=True), reads=["bgK" + sfx[j], "TTb" + ck[j]], writes=[bkk])
        for j in J:
            b_, bkk = bk(j)
            P.op("act", lambda e: e.mul(nWT[tp][j][:], b_[:, 0, :], -1.0), reads=[bkk], writes=["nWT" + ck[j]])
        stage_done()
        while chain_q:
            chain_q.pop(0)()

        def chunk_step(j, c, tp=tp, qk=qk, ck=ck, ti=ti, csl=csl):
            r = slice(64 * c, 64 * c + 64)
            si = sidx[0]
            so, sn_ = si % 2, (si + 1) % 2
            sidx[0] += 1
            o2 = j % 2
            u, qs, sn, pu = ps_ch[:, 0, :], ps_ch[:, 1, :], ps_ch[:, 2, :], ps_ch[:, 3, :]
            P.op("pe", lambda e: e.matmul(u, lhsT=TTb[tp][j][r, :], rhs=bV[tp][j][r, :], start=True, stop=False),
                 reads=["TTb" + ck[j], "bV" + ck[j]], writes=["ps_ch"])
            P.op("pe", lambda e: e.matmul(u, lhsT=nWT[tp][j][:], rhs=Sb[so][:], start=False, stop=True),
                 reads=["nWT" + ck[j], "Sb%d" % so], writes=["ps_ch"])
            P.op("pe", lambda e: e.matmul(qs, lhsT=QT[tp][:, csl[j]], rhs=Sb[so][:], start=True, stop=True),
                 reads=[qk, "Sb%d" % so], writes=["ps_ch"])
            P.op("dve", lambda e: e.tensor_copy(Ub[o2][r, :], u[r, :]), reads=["ps_ch"], writes=["Ub%d_%d" % (o2, c)])
            P.op("pe", lambda e: e.matmul(sn, lhsT=KD[tp][j][r, :], rhs=Ub[o2][r, :], start=True, stop=True),
                 reads=["KD" + ck[j], "Ub%d_%d" % (o2, c)], writes=["ps_ch"])
            P.op("pe", lambda e: e.matmul(pu, lhsT=PT[tp][j][r, :], rhs=Ub[o2][r, :], start=True, stop=True),
                 reads=["PT" + ck[j], "Ub%d_%d" % (o2, c)], writes=["ps_ch"])
            P.op("dve", lambda e: e.scalar_tensor_tensor(out=Sf[sn_][:], in0=Sf[so][:], scalar=smx[tp][j][:, 2 + c:3 + c], in1=sn,
                                                         op0=ALU.mult, op1=ALU.add), reads=["Sf%d" % so, "smx" + ck[j], "ps_ch"], writes=["Sf%d" % sn_])
            P.op("act", lambda e: e.copy(Sb[sn_][:], Sf[sn_][:]), reads=["Sf%d" % sn_], writes=["Sb%d" % sn_])
            P.op("dve", lambda e: e.tensor_copy(pus[o2][r, :], pu[r, :]), reads=["ps_ch"], writes=["pus%d_%d" % (o2, c)])
            P.op("dve", lambda e: e.scalar_tensor_tensor(out=Osb[o2][r, :], in0=qs[r, :], scalar=smx[tp][j][r, 0:1], in1=pus[o2][r, :],
                                                         op0=ALU.mult, op1=ALU.add), reads=["ps_ch", "smx" + ck[j], "pus%d_%d" % (o2, c)],
                 writes=["Osb%d_%d" % (o2, c)])
            if c == 1:
                blk = ti * 4 + j
                P.op("sp", lambda e: e.dma_start(out=o_out[blk * 128:(blk + 1) * 128, :], in_=Osb[o2][:]),
                     reads=["Osb%d_0" % o2, "Osb%d_1" % o2], chan="ost%d" % o2)

        for j in J:
            for c in range(2):
                chain_q.append(lambda j=j, c=c, f=chunk_step: f(j, c))

    while chain_q:
        chain_q.pop(0)()


def prep_weight(P, stg, stgkey, src2d, n_c, ncols, dst, dst_col0, scale_fn, dkey, skeys, cnt, dst_c0=0):
    pw = min(2048 // n_c, ncols)
    for col in range(0, ncols, pw):
        w = min(pw, ncols - col)
        b = cnt[0] % len(stg)
        cnt[0] += 1
        sv = stg[b][:, 0:n_c * w].rearrange("p (c n) -> p c n", c=n_c)
        k = stgkey + str(b)
        P.op("sp", lambda e: e.dma_start(out=sv, in_=src2d[:, col:col + w].rearrange("(c p) n -> p c n", p=128)),
             writes=[k], chan=k)
        dv = dst[:, dst_c0:dst_c0 + n_c, dst_col0 + col:dst_col0 + col + w]
        if scale_fn is None:
            eng = "act" if (cnt[0] % 2) else "dve"
            if eng == "act":
                P.op("act", lambda e: e.copy(dv, sv), reads=[k], writes=[dkey])
            else:
                P.op("dve", lambda e: e.tensor_copy(dv, sv), reads=[k], writes=[dkey])
        else:
            for c in range(n_c):
                sc_ = scale_fn(c)
                if c % 2:
                    P.op("act", lambda e: e.activation(out=dst[:, c, dst_col0 + col:dst_col0 + col + w], in_=sv[:, c, :],
                                                       func=AF.Copy, scale=sc_), reads=[k] + skeys, writes=[dkey])
                else:
                    P.op("dve", lambda e: e.tensor_scalar(dst[:, c, dst_col0 + col:dst_col0 + col + w], sv[:, c, :], sc_, None, ALU.mult),
                         reads=[k] + skeys, writes=[dkey])


TB = 2
TW = TB * 128


def load_consts(P, A, cst, pre):
    cf = A.sb([128, NCONST], F32, "cf")
    cb = A.sb([128, NCONST], BF16, "cb")
    P.op("sp", lambda e: e.dma_start(out=cf[:], in_=cst[:, :]), writes=[pre + "consts_f"], chan=pre + "cf")
    P.op("dve", lambda e: e.tensor_copy(cb[:], cf[:]), reads=[pre + "consts_f"], writes=["consts_b"])
    return cf, cb


def build_phase2a(nc, P, A, NTM, x2, oa2, validc, w_in, w_ba, w_bb, w_out, nwm_d, gnw_d, biasT_d, cst, xmid,
                  o_all=None, qsel_d=None, RT=None):
    NT2 = NTM + 3
    TC = NTM * TW
    sb, ps = A.sb, A.ps
    cf, cb = load_consts(P, A, cst, "a")
    idb = cb[:, C_ID:C_ID + 128]
    epsc = make_epsc(P, A)
    nwm = sb([128, NCH], F32, "nwm")
    gnw = sb([128, 1], F32, "gnw")
    P.op("sp", lambda e: e.dma_start(out=nwm[:], in_=nwm_d[:, :]), writes=["nwm0"], chan="nwm")
    P.op("sp", lambda e: e.dma_start(out=gnw[:], in_=gnw_d[:, :]), writes=["gnw"], chan="gnw")
    P.op("dve", lambda e: e.tensor_scalar(nwm[:], nwm[:], 32.0, None, ALU.mult), reads=["nwm0"], writes=["nwm"])
    Wi = sb([128, NCH, 4096], BF16, "Wi")
    WbA = sb([128, 4, 1024], BF16, "WbA")
    WbB = sb([128, 4, 1024], BF16, "WbB")
    Wo = sb([128, NCH, 1024], BF16, "Wo")
    es_stg = ExitStack()
    stg = [es_stg.enter_context(nc.sbuf_tensor("astg%d" % i, [128, 2048], F32)) for i in range(2)]
    cnt = [0]
    prep_weight(P, stg, "astg", w_in[:, 1536:2048], 8, 512, Wi, 0, lambda c: nwm[:, c:c + 1], "Wi", ["nwm"], cnt)
    prep_weight(P, stg, "astg", w_in[:, 2056:5640], 8, 3584, Wi, 512, lambda c: nwm[:, c:c + 1], "Wi", ["nwm"], cnt)
    prep_weight(P, stg, "astg", w_ba, 4, 1024, WbA, 0, lambda c: gnw[:, 0:1], "WbA", ["gnw"], cnt)
    prep_weight(P, stg, "astg", w_bb, 4, 1024, WbB, 0, None, "WbB", [], cnt)
    prep_weight(P, stg, "astg", w_out, 8, 1024, Wo, 0, None, "Wo", [], cnt)
    es_stg.close()
    P.barrier()
    biasT = sb([128, 8, 640], F32, "biasT")
    P.op("sp", lambda e: e.dma_start(out=biasT[:], in_=biasT_d[:, :, :]), writes=["biasT"], chan="biasT")
    valid = sb([128, NT2 * TB], F32, "valid")
    P.op("sp", lambda e: e.dma_start(out=valid[:], in_=validc[:, :]), writes=["valid"], chan="valid")
    ones8 = sb([128, 8, 1], F32, "ones8")
    P.op("pool", lambda e: e.memset(ones8[:], 1.0), writes=["ones8"])

    xt = [sb([128, TB, D], F32, "xt") for _ in range(2)]
    junk = sb([128, D], BF16, "junk")
    ss = sb([128, 8], F32, "ss")
    rs = sb([128, 8], F32, "rs")
    xn = [sb([128, D], BF16, "xn") for _ in range(2)]
    hT = sb([128, NCH, TW], BF16, "hT")
    KTb = sb([128, 4, 8 * 128], BF16, "KTb")
    Vaug = sb([128, 8, 8, 65], BF16, "Vaug")
    QTb = sb([128, 4, TW], BF16, "QTb")
    zs = sb([128, TB, 512], F32, "zs")
    oat = sb([128, TB, 512], F32, "oat")
    cands = None
    if o_all is not None:
        cand = sb([128, 4, 512], F32, "cand")
        if RT is None:
            cands = [(lambda r, q_=q_: o_all[q_ * TC + r:q_ * TC + r + 128, :].rearrange("p (h d) -> p h d", h=4)) for q_ in range(4)]
        else:
            o_view = o_all.rearrange("(r t) d -> t r d", r=8)
            cands = [(lambda r, b_=c_ // 4, q_=c_ % 4: o_view[q_ * TC + r:q_ * TC + r + 128, 4 * b_:4 * b_ + 4, :]) for c_ in range(8)]
        qsel = sb([128, len(cands)], F32, "qsel")
        P.op("sp", lambda e: e.dma_start(out=qsel[:], in_=qsel_d[:, :]), writes=["qsel"], chan="qsel")
    ssa = sb([128, 4], F32, "ssa")
    ra = sb([128, 4], F32, "ra")
    oan = sb([128, 512], BF16, "oan")
    oaT = sb([128, 4, TW], BF16, "oaT")
    ob = sb([128, 512], BF16, "ob")
    obT = sb([128, 4, TW], BF16, "obT")
    scs = [sb([128, 640], F32, "scs")] * 2
    PTb = [sb([128, 640], BF16, "PTb") for _ in range(2)]
    rden = sb([128, 8], F32, "rden")
    sg = [sb([128, 2 * TW], F32, "sg") for _ in range(2)]
    tt_ = [sb([128, 2 * TW], F32, "tt") for _ in range(2)]
    mixT = sb([128, NCH, TW], BF16, "mixT")

    ps_tr = ps([128, NCH, 128], BF16, "ps_tr")
    pbig = [ps([128, 512], F32, "pbig") for _ in range(5)]
    ps_o = [ps([128, 4, 65], F32, "ps_o") for _ in range(2)]
    bi = [0]

    def big():
        i = bi[0] % 5
        bi[0] += 1
        return pbig[i], "pbig%d" % i

    scale_q = 64.0 ** -0.5
    for tt in range(NT2):
        tp = tt % 2
        xk = "axt%d" % tp
        P.op("sp", lambda e: e.dma_start(out=xt[tp][:], in_=x2[tt * TW:(tt + 1) * TW, :].rearrange("(j p) d -> p j d", p=128)),
             writes=[xk], chan=xk)
        for j in range(TB):
            norm_block(P, epsc, xt[tp][:, j, :], xk, ss[:, j:j + 1], rs[:, j:j + 1], "ass%d" % j, junk[:], "ajunk",
                       xn[j % 2][:], "axn%d" % (j % 2), ps_tr, "aps_tr", idb, hT[:, :, j * 128:(j + 1) * 128], "ahT")
        ring0 = (tt * TB) % 8
        for m in range(4):
            pb_, pk = big()
            for c in range(NCH):
                P.op("pe", lambda e: e.matmul(pb_[:, 0:TW], lhsT=Wi[:, c, 1024 + m * 128:1024 + (m + 1) * 128], rhs=hT[:, c, :],
                                              start=(c == 0), stop=(c == NCH - 1)), reads=["Wi", "ahT"], writes=[pk])
            P.op("act", lambda e: e.copy(KTb[:, m, ring0 * 128:ring0 * 128 + TW], pb_[:, 0:TW]), reads=[pk], writes=["KTb"])
        for j in range(TB):
            slot = ring0 + j
            pb_, pk = big()
            for c in range(NCH):
                P.op("pe", lambda e: e.matmul(pb_[:, :], lhsT=hT[:, c, j * 128:(j + 1) * 128], rhs=Wi[:, c, 1536:2048],
                                              start=(c == 0), stop=(c == NCH - 1)), reads=["Wi", "ahT"], writes=[pk])
            P.op("dve", lambda e: e.tensor_copy(Vaug[:, slot, :, 0:64], pb_[:, :].rearrange("p (h d) -> p h d", h=8)),
                 reads=[pk], writes=["Vaug"])
            P.op("act", lambda e: e.activation(out=Vaug[:, slot, :, 64:65], in_=ones8[:], func=AF.Copy,
                                               scale=valid[:, tt * TB + j:tt * TB + j + 1]),
                 reads=["ones8", "valid"], writes=["Vaug"])
        if tt < 2:
            continue
        for m in range(4):
            pb_, pk = big()
            for c in range(NCH):
                P.op("pe", lambda e: e.matmul(pb_[:, 0:TW], lhsT=Wi[:, c, 512 + m * 128:512 + (m + 1) * 128], rhs=hT[:, c, :],
                                              start=(c == 0), stop=(c == NCH - 1)), reads=["Wi", "ahT"], writes=[pk])
            P.op("act", lambda e: e.mul(QTb[:, m, :], pb_[:, 0:TW], scale_q), reads=[pk], writes=["QTb"])
        for j in range(TB):
            pb_, pk = big()
            for c in range(NCH):
                P.op("pe", lambda e: e.matmul(pb_[:, :], lhsT=hT[:, c, j * 128:(j + 1) * 128], rhs=Wi[:, c, 0:512],
                                              start=(c == 0), stop=(c == NCH - 1)), reads=["Wi", "ahT"], writes=[pk])
            P.op("act", lambda e: e.activation(out=zs[:, j, :], in_=pb_[:, :], func=AF.Silu), reads=[pk], writes=["zs%d" % j])
        if o_all is None:
            P.op("sp", lambda e: e.dma_start(out=oat[:], in_=oa2[(tt - 2) * TW:(tt - 1) * TW, :].rearrange("(j p) d -> p j d", p=128)),
                 writes=["oat"], chan="oat")
        else:
            for j in range(TB):
                for cc_, cf_ in enumerate(cands):
                    k = cc_ % 4
                    ck_ = "cand_%d" % k
                    P.op("sp", lambda e: e.dma_start(out=cand[:, k, :].rearrange("p (h d) -> p h d", h=4),
                                                     in_=cf_((tt - 2) * TW + j * 128)),
                         reads=["oall"], writes=[ck_], chan=ck_)
                    if cc_ == 0:
                        P.op("dve", lambda e: e.tensor_scalar(oat[:, j, :], cand[:, k, :], qsel[:, cc_:cc_ + 1], None, ALU.mult),
                             reads=[ck_, "qsel"], writes=["oat"])
                    else:
                        P.op("dve", lambda e: e.scalar_tensor_tensor(out=oat[:, j, :], in0=cand[:, k, :], scalar=qsel[:, cc_:cc_ + 1],
                                                                     in1=oat[:, j, :], op0=ALU.mult, op1=ALU.add),
                             reads=[ck_, "qsel", "oat"], writes=["oat"])
        for j in range(TB):
            g = tt * TB + j
            for h in range(8):
                m, r = h // 2, slice(64 * (h % 2), 64 * (h % 2) + 64)
                p1, p1k = big()
                p2, p2k = big()
                for kb in range(5):
                    slot = (g - 4 + kb) % 8
                    dst = p1[:, kb * 128:(kb + 1) * 128] if kb < 4 else p2[:, 0:128]
                    P.op("pe", lambda e: e.matmul(dst, lhsT=KTb[r, m, slot * 128:(slot + 1) * 128], rhs=QTb[r, m, j * 128:(j + 1) * 128],
                                                  start=True, stop=True), reads=["KTb", "QTb"], writes=[p1k if kb < 4 else p2k])
                sp_ = h % 2
                P.op("dve", lambda e: e.tensor_tensor(out=scs[sp_][:, 0:512], in0=p1[:, :], in1=biasT[:, h, 0:512], op=ALU.add),
                     reads=[p1k, "biasT"], writes=["scsa"])
                P.op("dve", lambda e: e.tensor_tensor(out=scs[sp_][:, 512:640], in0=p2[:, 0:128], in1=biasT[:, h, 512:640], op=ALU.add),
                     reads=[p2k, "biasT"], writes=["scsb"])
                P.op("act", lambda e: e.activation(out=PTb[sp_][:], in_=scs[sp_][:], func=AF.Exp),
                     reads=["scsa", "scsb"], writes=["PTb%d" % sp_])
                for kb in range(5):
                    slot = (g - 4 + kb) % 8
                    P.op("pe", lambda e: e.matmul(ps_o[h // 4][:, h % 4, :], lhsT=PTb[sp_][:, kb * 128:(kb + 1) * 128],
                                                  rhs=Vaug[:, slot, h, :], start=(kb == 0), stop=(kb == 4)),
                         reads=["PTb%d" % sp_, "Vaug"], writes=["ps_o%d" % (h // 4)])
            for hg in range(2):
                P.op("dve", lambda e: e.tensor_scalar(rden[:, hg * 4:hg * 4 + 4], ps_o[hg][:, :, 64], 1e-30, None, ALU.add),
                     reads=["ps_o%d" % hg], writes=["rden%da" % hg])
                P.op("dve", lambda e: e.reciprocal(rden[:, hg * 4:hg * 4 + 4], rden[:, hg * 4:hg * 4 + 4]),
                     reads=["rden%da" % hg], writes=["rden%d" % hg])
            for h in range(8):
                P.op("act", lambda e: e.activation(out=ob[:, h * 64:(h + 1) * 64], in_=ps_o[h // 4][:, h % 4, 0:64], func=AF.Copy,
                                                   scale=rden[:, h:h + 1]), reads=["ps_o%d" % (h // 4), "rden%d" % (h // 4)], writes=["ob"])
            for c in range(4):
                P.op("pe", lambda e: e.transpose(ps_tr[:, c, :], ob[:, c * 128:(c + 1) * 128], idb), reads=["ob", "consts_b"], writes=["aps_tr"])
            P.op("act", lambda e: e.copy(obT[:, :, j * 128:(j + 1) * 128], ps_tr[:, 0:4, :]), reads=["aps_tr"], writes=["obT"])
            for hh in range(4):
                P.op("act", lambda e: e.activation(out=junk[:, 0:128], in_=oat[:, j, hh * 128:(hh + 1) * 128], func=AF.Square,
                                                   accum_out=ssa[:, hh:hh + 1]), reads=["oat"], writes=["ajunk", "ssa"])
            P.op("act", lambda e: e.activation(out=ra[:], in_=ssa[:], func=AF.Ln, scale=1.0 / 128.0, bias=epsc[:, 1:2]),
                 reads=["ssa", "epsc"], writes=["ra0"])
            P.op("act", lambda e: e.activation(out=ra[:], in_=ra[:], func=AF.Exp, scale=-0.5), reads=["ra0"], writes=["ra"])
            for hh in range(4):
                P.op("dve", lambda e: e.scalar_tensor_tensor(out=oan[:, hh * 128:(hh + 1) * 128], in0=oat[:, j, hh * 128:(hh + 1) * 128],
                                                             scalar=ra[:, hh:hh + 1], in1=zs[:, j, hh * 128:(hh + 1) * 128],
                                                             op0=ALU.mult, op1=ALU.mult), reads=["oat", "ra", "zs%d" % j], writes=["oan"])
            for c in range(4):
                P.op("pe", lambda e: e.transpose(ps_tr[:, 4 + c, :], oan[:, c * 128:(c + 1) * 128], idb), reads=["oan", "consts_b"], writes=["aps_tr"])
            P.op("act", lambda e: e.copy(oaT[:, :, j * 128:(j + 1) * 128], ps_tr[:, 4:8, :]), reads=["aps_tr"], writes=["oaT"])
        for mo in range(8):
            py, pyk = big()
            pg, pgk = big()
            for half, (Wb, src, skey) in enumerate(((WbA, oaT, "oaT"), (WbB, obT, "obT"))):
                for c in range(4):
                    P.op("pe", lambda e: e.matmul(py[:, half * TW:(half + 1) * TW], lhsT=Wb[:, c, mo * 128:(mo + 1) * 128], rhs=src[:, c, :],
                                                  start=(c == 0), stop=(c == 3)), reads=["WbA", "WbB", skey], writes=[pyk])
            for half in range(2):
                col0 = 2048 + half * 1024 + mo * 128
                for c in range(NCH):
                    P.op("pe", lambda e: e.matmul(pg[:, half * TW:(half + 1) * TW], lhsT=Wi[:, c, col0:col0 + 128], rhs=hT[:, c, :],
                                                  start=(c == 0), stop=(c == NCH - 1)), reads=["Wi", "ahT"], writes=[pgk])
            q2 = mo % 2
            P.op("act", lambda e: e.activation(out=sg[q2][:], in_=pg[:, :], func=AF.Sigmoid), reads=[pgk], writes=["sg%d" % q2])
            P.op("dve", lambda e: e.tensor_tensor(out=tt_[q2][:], in0=py[:, :], in1=sg[q2][:], op=ALU.mult),
                 reads=[pyk, "sg%d" % q2], writes=["tt%d" % q2])
            P.op("pool", lambda e: e.tensor_tensor(out=mixT[:, mo, :], in0=tt_[q2][:, 0:TW], in1=tt_[q2][:, TW:2 * TW], op=ALU.add),
                 reads=["tt%d" % q2], writes=["mixT"])
        for j in range(TB):
            for half in range(2):
                po, pok = big()
                for c in range(NCH):
                    P.op("pe", lambda e: e.matmul(po[:, :], lhsT=mixT[:, c, j * 128:(j + 1) * 128], rhs=Wo[:, c, half * 512:(half + 1) * 512],
                                                  start=(c == 0), stop=(c == NCH - 1)), reads=["mixT", "Wo"], writes=[pok])
                P.op("dve", lambda e: e.tensor_tensor(out=xt[tp][:, j, half * 512:(half + 1) * 512], in0=po[:, :],
                                                      in1=xt[tp][:, j, half * 512:(half + 1) * 512], op=ALU.add), reads=[pok, xk], writes=[xk])
        P.op("sp", lambda e: e.dma_start(out=xmid[(tt - 2) * TW:(tt - 1) * TW, :].rearrange("(j p) d -> p j d", p=128), in_=xt[tp][:]),
             reads=[xk], writes=["xmid_d"], chan="xmst%d" % tp)


def build_phase2b(nc, P, A, NTM, xmid, w_up, w_down, nwf_d, cfw_d, cfb_d, wfin_d, cst, out2):
    sb, ps = A.sb, A.ps
    cf, cb = load_consts(P, A, cst, "b")
    idb = cb[:, C_ID:C_ID + 128]
    epsc = make_epsc(P, A)
    nwf = sb([128, NCH], F32, "nwf")
    P.op("sp", lambda e: e.dma_start(out=nwf[:], in_=nwf_d[:, :]), writes=["nwf0"], chan="nwf")
    P.op("dve", lambda e: e.tensor_scalar(nwf[:], nwf[:], 32.0, None, ALU.mult), reads=["nwf0"], writes=["nwf"])
    cfw = sb([128, 44, 3], F32, "cfw")
    cfb = sb([128, 44], F32, "cfb")
    wfb = sb([128, D], F32, "wfb")
    P.op("sp", lambda e: e.dma_start(out=cfw[:], in_=cfw_d[:, :, :]), writes=["cfw"], chan="cfw")
    P.op("sp", lambda e: e.dma_start(out=cfb[:], in_=cfb_d[:, :]), writes=["cfb"], chan="cfb")
    P.op("sp", lambda e: e.dma_start(out=wfb[:], in_=wfin_d[:, :]), writes=["wfb0"], chan="wfb")
    P.op("pool", lambda e: e.tensor_scalar(wfb[:], wfb[:], 32.0, None, ALU.mult), reads=["wfb0"], writes=["wfb"])
    Wu = sb([128, NCH, 2 * D_FF], BF16, "Wu")
    Wd = sb([128, 22, D], BF16, "Wd")
    es_stg = ExitStack()
    stg = [es_stg.enter_context(nc.sbuf_tensor("bstg%d" % i, [128, 2048], F32)) for i in range(2)]
    cnt = [0]
    prep_weight(P, stg, "bstg", w_up, 8, 2 * D_FF, Wu, 0, lambda c: nwf[:, c:c + 1], "Wu", ["nwf"], cnt)
    prep_weight(P, stg, "bstg", w_down[0:1408, :], 11, D, Wd, 0, None, "Wd", [], cnt, dst_c0=0)
    prep_weight(P, stg, "bstg", w_down[1408:2816, :], 11, D, Wd, 0, None, "Wd", [], cnt, dst_c0=11)
    es_stg.close()
    P.barrier()

    xm = [sb([128, TB, D], F32, "xm") for _ in range(2)]
    junk = sb([128, D], BF16, "junk")
    ss = sb([128, 8], F32, "ss")
    rs = sb([128, 8], F32, "rs")
    xn = [sb([128, D], BF16, "xn") for _ in range(2)]
    h2T = sb([128, NCH, TW], BF16, "h2T")
    ubuf = [sb([128, 2, TW + 2], F32, "ubuf") for _ in range(2)]
    cv = [sb([128, 2, TW], F32, "cv") for _ in range(2)]
    sgt = [sb([128, TW], F32, "sgt") for _ in range(2)]
    uh = sb([128, 22, 2, 2], F32, "uh")
    actT = sb([128, 22, TW], BF16, "actT")
    outt = [sb([128, D], F32, "outt") for _ in range(2)]
    P.op("pool", lambda e: e.memset(uh[:], 0.0), writes=["uh"])

    ps_tr = ps([128, NCH, 128], BF16, "ps_tr")
    pbig = [ps([128, 512], F32, "pbig") for _ in range(6)]
    bi = [0]

    def big():
        i = bi[0] % 6
        bi[0] += 1
        return pbig[i], "bpbig%d" % i

    for u in range(NTM + 1):
        tp = u % 2
        xk = "bxm%d" % tp
        P.op("sp", lambda e: e.dma_start(out=xm[tp][:], in_=xmid[u * TW:(u + 1) * TW, :].rearrange("(j p) d -> p j d", p=128)),
             reads=["xmid_d"], writes=[xk], chan=xk)
        for j in range(TB):
            norm_block(P, epsc, xm[tp][:, j, :], xk, ss[:, j:j + 1], rs[:, j:j + 1], "bss%d" % j, junk[:], "bjunk",
                       xn[j % 2][:], "bxn%d" % (j % 2), ps_tr, "bps_tr", idb, h2T[:, :, j * 128:(j + 1) * 128], "h2T")
        for m in range(22):
            q2 = m % 2
            pg, pgk = big()
            for half in range(2):
                col0 = half * D_FF + m * 128
                for c in range(NCH):
                    P.op("pe", lambda e: e.matmul(pg[:, half * TW:(half + 1) * TW], lhsT=Wu[:, c, col0:col0 + 128], rhs=h2T[:, c, :],
                                                  start=(c == 0), stop=(c == NCH - 1)), reads=["Wu", "h2T"], writes=[pgk])
            uk = "ubuf%d" % q2
            P.op("pool", lambda e: e.tensor_copy(ubuf[q2][:, :, 0:2], uh[:, m, :, :]), reads=["uh"], writes=[uk + "h"])
            P.op("act", lambda e: e.copy(ubuf[q2][:, :, 2:TW + 2], pg[:, :].rearrange("p (s n) -> p s n", s=2)), reads=[pgk], writes=[uk])
            P.op("pool", lambda e: e.tensor_copy(uh[:, m, :, :], ubuf[q2][:, :, TW:TW + 2]), reads=[uk, uk + "h"], writes=["uh"])
            if u == 0:
                continue
            ck = "cv%d" % q2
            for s_ in range(2):
                ch = s_ * 22 + m
                eng = "dve"
                P.op("act", lambda e: e.activation(out=cv[q2][:, s_, :], in_=ubuf[q2][:, s_, 0:TW], func=AF.Identity,
                                                   scale=cfw[:, ch, 0:1], bias=cfb[:, ch:ch + 1]),
                     reads=[uk, uk + "h", "cfw", "cfb"], writes=[ck + str(s_)])
                for k in range(1, 3):
                    P.op(eng, lambda e: e.scalar_tensor_tensor(out=cv[q2][:, s_, :], in0=ubuf[q2][:, s_, k:k + TW], scalar=cfw[:, ch, k:k + 1],
                                                               in1=cv[q2][:, s_, :], op0=ALU.mult, op1=ALU.add),
                         reads=[uk, uk + "h", "cfw", ck + str(s_)], writes=[ck + str(s_)])
            P.op("act", lambda e: e.activation(out=sgt[q2][:], in_=cv[q2][:, 0, :], func=AF.Silu), reads=[ck + "0"], writes=["sgt%d" % q2])
            P.op("dve", lambda e: e.tensor_tensor(out=actT[:, m, :], in0=sgt[q2][:], in1=cv[q2][:, 1, :], op=ALU.mult),
                 reads=["sgt%d" % q2, ck + "1"], writes=["actT"])
        if u == 0:
            continue
        for j in range(TB):
            for half in range(2):
                po, pok = big()
                for m in range(22):
                    P.op("pe", lambda e: e.matmul(po[:, :], lhsT=actT[:, m, j * 128:(j + 1) * 128], rhs=Wd[:, m, half * 512:(half + 1) * 512],
                                                  start=(m == 0), stop=(m == 21)), reads=["actT", "Wd"], writes=[pok])
                P.op("dve", lambda e: e.tensor_tensor(out=xm[tp][:, j, half * 512:(half + 1) * 512], in0=po[:, :],
                                                      in1=xm[tp][:, j, half * 512:(half + 1) * 512], op=ALU.add), reads=[pok, xk], writes=[xk])
            o2 = j % 2
            P.op("act", lambda e: e.activation(out=junk[:], in_=xm[tp][:, j, :], func=AF.Square, accum_out=ss[:, 4 + j:5 + j]),
                 reads=[xk], writes=["bjunk", "fss%d" % j])
            P.op("act", lambda e: e.activation(out=rs[:, 4 + j:5 + j], in_=ss[:, 4 + j:5 + j], func=AF.Ln, bias=epsc[:, 0:1]),
                 reads=["fss%d" % j, "epsc"], writes=["frs%da" % j])
            P.op("act", lambda e: e.activation(out=rs[:, 4 + j:5 + j], in_=rs[:, 4 + j:5 + j], func=AF.Exp, scale=-0.5),
                 reads=["frs%da" % j], writes=["frs%d" % j])
            P.op("dve", lambda e: e.scalar_tensor_tensor(out=outt[o2][:], in0=xm[tp][:, j, :], scalar=rs[:, 4 + j:5 + j], in1=wfb[:],
                                                         op0=ALU.mult, op1=ALU.mult), reads=[xk, "frs%d" % j, "wfb"], writes=["outt%d" % o2])
            P.op("sp", lambda e: e.dma_start(out=out2[(u - 1) * TW + j * 128:(u - 1) * TW + (j + 1) * 128, :], in_=outt[o2][:]),
                 reads=["outt%d" % o2], chan="ost%d" % o2)


def _phase1_inputs(inp, T):
    x = np.asarray(inp["x"], np.float32)
    w_in = np.asarray(inp["w_in"], np.float32)[0]
    conv = np.asarray(inp["conv_qkv_w"], np.float32)[0]
    a_log = np.asarray(inp["a_log"], np.float32)[0]
    dtb = np.asarray(inp["dt_bias"], np.float32)[0]
    nw = np.asarray(inp["norm_mix_w"], np.float32)[0]
    cst = chunk_consts()
    maps = []
    for core in range(8):
        b, h = core // 4, core % 4
        cols = np.concatenate([np.arange(h * 128, (h + 1) * 128), 512 + np.arange(h * 128, (h + 1) * 128),
                               1024 + np.arange(h * 128, (h + 1) * 128), [2048 + h], [2052 + h]])
        w1 = np.ascontiguousarray(w_in[:, cols])
        cw = np.zeros((128, 12), np.float32)
        for g in range(3):
            cw[:, 4 * g:4 * g + 4] = conv[:, g * 512 + h * 128:g * 512 + (h + 1) * 128].T
        sc = np.zeros((128, 2), np.float32)
        sc[:, 0] = a_log[h]
        sc[:, 1] = dtb[h]
        maps.append({"x1": np.ascontiguousarray(x[b, :T]), "w1": w1, "cw1": cw, "sc1": sc,
                     "nw1": np.ascontiguousarray(nw.reshape(8, 128).T), "cst": cst})
    return maps


def build_p1_program(T):
    nc = bass.Bass("TRN2", target_bir_lowering=False)
    x1 = nc.dram_tensor("x1", [T, D], F32, kind="ExternalInput").ap()
    w1 = nc.dram_tensor("w1", [D, 386], F32, kind="ExternalInput").ap()
    cw1 = nc.dram_tensor("cw1", [128, 12], F32, kind="ExternalInput").ap()
    sc1 = nc.dram_tensor("sc1", [128, 2], F32, kind="ExternalInput").ap()
    nw1 = nc.dram_tensor("nw1", [128, 8], F32, kind="ExternalInput").ap()
    cst = nc.dram_tensor("cst", [128, NCONST], F32, kind="ExternalInput").ap()
    o_out = nc.dram_tensor("o1", [T, 128], F32, kind="ExternalOutput").ap()
    es = ExitStack()
    P = Prog(nc)
    A = Ctx(nc, es, P)
    build_phase1(nc, es, P, A, T, x1, w1, cw1, sc1, nw1, cst, o_out)
    P.finish()
    P.emit(es)
    es.close()
    return nc, P


def run_phase1(inp, T):
    nc, P = build_p1_program(T)
    maps = _phase1_inputs(inp, T)
    res = run_bass_kernel_spmd(nc, maps, core_ids=list(range(8)))
    o = np.zeros((2, T, 4, 128), np.float32)
    for core in range(8):
        o[core // 4, :, core % 4, :] = res.results[core]["o1"]
    return o


def _bias_tile(rel):
    ki = np.arange(128)[:, None]
    qi = np.arange(128)[None, :]
    out = np.zeros((128, 8, 640), np.float32)
    for kb in range(5):
        dist = qi - ki + (4 - kb) * 128
        idx = np.clip(dist, -128, 128) + 128
        cdiff = 2 * (4 - kb) + qi // 64 - ki // 64
        ok = (cdiff >= 0) & (cdiff <= 8)
        for h in range(8):
            out[:, h, kb * 128:(kb + 1) * 128] = np.where(ok, rel[h][idx], NEG)
    return out


def _phase2_inputs(inp, o1, T):
    TC = T // 4
    NTM = TC // TW
    x = np.asarray(inp["x"], np.float32)
    w_in = np.ascontiguousarray(np.asarray(inp["w_in"], np.float32)[0])
    cfw_ = np.asarray(inp["conv_ffn_w"], np.float32)[0]
    cfb_ = np.asarray(inp["conv_ffn_b"], np.float32)[0]
    shared = {
        "w_in": w_in,
        "w_ba": np.ascontiguousarray(np.asarray(inp["w_branch_a"], np.float32)[0]),
        "w_bb": np.ascontiguousarray(np.asarray(inp["w_branch_b"], np.float32)[0]),
        "w_out": np.ascontiguousarray(np.asarray(inp["w_out"], np.float32)[0]),
        "w_up": np.ascontiguousarray(np.asarray(inp["w_up"], np.float32)[0]),
        "w_down": np.ascontiguousarray(np.asarray(inp["w_down"], np.float32)[0]),
        "nwm": np.ascontiguousarray(np.asarray(inp["norm_mix_w"], np.float32)[0].reshape(8, 128).T),
        "nwf": np.ascontiguousarray(np.asarray(inp["norm_ffn_w"], np.float32)[0].reshape(8, 128).T),
        "gnw": np.ascontiguousarray(np.asarray(inp["gdn_norm_w"], np.float32)[0].reshape(128, 1)),
        "biasT": _bias_tile(np.asarray(inp["rel_bias"], np.float32)[0]),
        "cfw": np.ascontiguousarray(cfw_.reshape(3, 44, 128).transpose(2, 1, 0)),
        "cfb": np.ascontiguousarray(cfb_.reshape(44, 128).T),
        "wfin": np.ascontiguousarray(np.broadcast_to(np.asarray(inp["norm_final_w"], np.float32)[None, :], (128, D))),
        "cst2": chunk_consts(),
    }
    maps = []
    for core in range(8):
        b, q = core // 4, core % 4
        t0 = q * TC
        lo = t0 - 3 * TW
        x2 = np.zeros(((NTM + 3) * TW, D), np.float32)
        s0 = max(lo, 0)
        x2[s0 - lo:] = x[b, s0:t0 + TC]
        pos = lo + np.arange((NTM + 3) * TW)
        valid = (pos >= 0).astype(np.float32).reshape((NTM + 3) * TB, 128).T
        m = dict(shared)
        m["x2"] = x2
        m["validc"] = np.ascontiguousarray(valid)
        if o1 is not None:
            lo2 = t0 - TW
            oa2 = np.zeros(((NTM + 1) * TW, 512), np.float32)
            s1 = max(lo2, 0)
            oa2[s1 - lo2:] = o1[b, s1:t0 + TC].reshape(-1, 512)
            m["oa2"] = oa2
        maps.append(m)
    return maps


def _declare_p2(nc, NTM, with_oa):
    d = {}
    def inp(name, shape):
        d[name] = nc.dram_tensor(name, list(shape), F32, kind="ExternalInput").ap()
    inp("x2", [(NTM + 3) * TW, D])
    if with_oa:
        inp("oa2", [(NTM + 1) * TW, 512])
    inp("validc", [128, (NTM + 3) * TB])
    inp("w_in", [D, D_IN]); inp("w_ba", [512, D]); inp("w_bb", [512, D]); inp("w_out", [D, D])
    inp("w_up", [D, 2 * D_FF]); inp("w_down", [D_FF, D]); inp("nwm", [128, 8]); inp("nwf", [128, 8]); inp("gnw", [128, 1])
    inp("biasT", [128, 8, 640]); inp("cfw", [128, 44, 3]); inp("cfb", [128, 44]); inp("wfin", [128, D]); inp("cst2", [128, NCONST])
    d["xmid"] = nc.dram_tensor("xmid", [(NTM + 1) * TW, D], F32, kind="Internal").ap()
    d["out2"] = nc.dram_tensor("out2", [NTM * TW, D], F32, kind="ExternalOutput").ap()
    return d


def _emit_phase2(nc, P, d, NTM, oa_ap, o_all=None, qsel_d=None, RT=None):
    es_a = ExitStack()
    build_phase2a(nc, P, Ctx(nc, es_a, P), NTM, d["x2"], oa_ap, d["validc"], d["w_in"], d["w_ba"], d["w_bb"], d["w_out"],
                  d["nwm"], d["gnw"], d["biasT"], d["cst2"], d["xmid"], o_all=o_all, qsel_d=qsel_d, RT=RT)
    es_a.close()
    P.barrier()
    es_b = ExitStack()
    build_phase2b(nc, P, Ctx(nc, es_b, P), NTM, d["xmid"], d["w_up"], d["w_down"], d["nwf"], d["cfw"], d["cfb"], d["wfin"],
                  d["cst2"], d["out2"])
    es_b.close()


def build_p2_program(T):
    NTM = (T // 4) // TW
    nc = bass.Bass("TRN2", target_bir_lowering=False)
    d = _declare_p2(nc, NTM, True)
    P = Prog(nc)
    _emit_phase2(nc, P, d, NTM, d["oa2"])
    P.finish()
    es = ExitStack()
    P.emit(es)
    es.close()
    return nc, P


def run_phase2(inp, o1, T):
    nc, P = build_p2_program(T)
    maps = _phase2_inputs(inp, o1, T)
    res = run_bass_kernel_spmd(nc, maps, core_ids=list(range(8)))
    TC = T // 4
    out = np.zeros((2, T, D), np.float32)
    for core in range(8):
        out[core // 4, (core % 4) * TC:(core % 4 + 1) * TC] = res.results[core]["out2"]
    return out


T_FULL = 16384


def build_fused_program(T):
    NTM = (T // 4) // TW
    RT = T + TW
    nc = bass.Bass("TRN2", target_bir_lowering=False)
    x1 = nc.dram_tensor("x1", [T, D], F32, kind="ExternalInput").ap()
    w1 = nc.dram_tensor("w1", [D, 386], F32, kind="ExternalInput").ap()
    cw1 = nc.dram_tensor("cw1", [128, 12], F32, kind="ExternalInput").ap()
    sc1 = nc.dram_tensor("sc1", [128, 2], F32, kind="ExternalInput").ap()
    nw1 = nc.dram_tensor("nw1", [128, 8], F32, kind="ExternalInput").ap()
    cst = nc.dram_tensor("cst", [128, NCONST], F32, kind="ExternalInput").ap()
    qsel_d = nc.dram_tensor("qsel", [128, 8], F32, kind="ExternalInput").ap()
    o_loc = nc.dram_tensor("o_loc", [RT, 128], F32, kind="Internal").ap()
    o_all = nc.dram_tensor("o_all", [8 * RT, 128], F32, kind="Internal").ap()
    d = _declare_p2(nc, NTM, False)
    P = Prog(nc)
    es1 = ExitStack()
    A1 = Ctx(nc, es1, P)
    zt = A1.sb([128, 128], F32, "zt")
    P.op("pool", lambda e: e.memset(zt[:], 0.0), writes=["zt"])
    for i in range(TW // 128):
        P.op("sp", lambda e: e.dma_start(out=o_loc[i * 128:(i + 1) * 128, :], in_=zt[:]), reads=["zt"], chan="zt")
    build_phase1(nc, es1, P, A1, T, x1, w1, cw1, sc1, nw1, cst, o_loc[TW:RT, :])
    es1.close()
    P.barrier()
    P.op("pool", lambda e: e.collective_compute("AllGather", ALU.bypass, replica_groups=[list(range(8))],
                                                ins=[o_loc[:, :]], outs=[o_all[:, :]]),
         writes=["oall"], chan="cc", inc_override=1)
    _emit_phase2(nc, P, d, NTM, None, o_all=o_all, qsel_d=qsel_d, RT=RT)
    P.finish()
    es = ExitStack()
    P.emit(es)
    es.close()
    return nc, P


def build_fused_nocc(T):
    NTM = (T // 4) // TW
    RT = T + TW
    nc = bass.Bass("TRN2", target_bir_lowering=False)
    x1 = nc.dram_tensor("x1", [T, D], F32, kind="ExternalInput").ap()
    w1a = nc.dram_tensor("w1a", [4, D, 386], F32, kind="ExternalInput").ap()
    cw1a = nc.dram_tensor("cw1a", [4, 128, 12], F32, kind="ExternalInput").ap()
    sc1a = nc.dram_tensor("sc1a", [4, 128, 2], F32, kind="ExternalInput").ap()
    nw1 = nc.dram_tensor("nw1", [128, 8], F32, kind="ExternalInput").ap()
    cst = nc.dram_tensor("cst", [128, NCONST], F32, kind="ExternalInput").ap()
    qsel_d = nc.dram_tensor("qsel", [128, 4], F32, kind="ExternalInput").ap()
    o_loc = nc.dram_tensor("o_loc", [RT, 512], F32, kind="Internal").ap()
    d = _declare_p2(nc, NTM, False)
    P = Prog(nc)
    for h in range(4):
        es1 = ExitStack()
        A1 = Ctx(nc, es1, P)
        if h == 0:
            zt = A1.sb([128, 512], F32, "zt")
            P.op("pool", lambda e: e.memset(zt[:], 0.0), writes=["zt"])
            for i in range(TW // 128):
                P.op("sp", lambda e: e.dma_start(out=o_loc[i * 128:(i + 1) * 128, :], in_=zt[:]), reads=["zt"], chan="zt")
        build_phase1(nc, es1, P, A1, T, x1, w1a[h], cw1a[h], sc1a[h], nw1, cst, o_loc[TW:RT, h * 128:(h + 1) * 128])
        es1.close()
        P.barrier()
    _emit_phase2(nc, P, d, NTM, None, o_all=o_loc, qsel_d=qsel_d, RT=None)
    P.finish()
    es = ExitStack()
    P.emit(es)
    es.close()
    return nc, P


def run_fused_nocc(inp, T):
    nc, P = build_fused_nocc(T)
    m1 = _phase1_inputs(inp, T)
    m2 = _phase2_inputs(inp, None, T)
    maps = []
    for core in range(8):
        b, q = core // 4, core % 4
        m = dict(m2[core])
        m["x1"] = m1[core]["x1"]
        m["nw1"] = m1[core]["nw1"]
        m["cst"] = m1[core]["cst"]
        m["w1a"] = np.stack([m1[4 * b + h]["w1"] for h in range(4)])
        m["cw1a"] = np.stack([m1[4 * b + h]["cw1"] for h in range(4)])
        m["sc1a"] = np.stack([m1[4 * b + h]["sc1"] for h in range(4)])
        qs = np.zeros((128, 4), np.float32)
        qs[:, q] = 1.0
        m["qsel"] = qs
        maps.append(m)
    res = run_bass_kernel_spmd(nc, maps, core_ids=list(range(8)))
    TC = T // 4
    out = np.zeros((2, T, D), np.float32)
    for core in range(8):
        out[core // 4, (core % 4) * TC:(core % 4 + 1) * TC] = res.results[core]["out2"]
    return out


def run_fused(inp, T):
    nc, P = build_fused_program(T)
    m1 = _phase1_inputs(inp, T)
    m2 = _phase2_inputs(inp, None, T)
    maps = []
    for core in range(8):
        m = dict(m1[core])
        m.update(m2[core])
        qs = np.zeros((128, 8), np.float32)
        qs[:, core] = 1.0
        m["qsel"] = qs
        maps.append(m)
    res = run_bass_kernel_spmd(nc, maps, core_ids=list(range(8)))
    TC = T // 4
    out = np.zeros((2, T, D), np.float32)
    for core in range(8):
        out[core // 4, (core % 4) * TC:(core % 4 + 1) * TC] = res.results[core]["out2"]
    return out


def kernel(**inputs):
    return run_fused_nocc(inputs, T_FULL)
```

```python
from collections import defaultdict
from contextlib import ExitStack

import numpy as np
import concourse.bass as bass
import concourse.mybir as mybir
from concourse.bass_utils import run_bass_kernel_spmd

F32 = mybir.dt.float32
BF16 = mybir.dt.bfloat16
AF = mybir.ActivationFunctionType
ALU = mybir.AluOpType

D = 1024
NCH = 8
EPS = 1e-6
CH = 64
DK = 128
D_IN = 5640
D_FF = 2816
NEG = -30000.0


PSUM_PREFIXES = ("psl", "ptb", "ppj", "ps_tr", "aps_tr", "bps_tr", "pbig", "bpbig", "ps_o")


class _Rec:
    def __getattr__(self, name):
        def f(*a, **k):
            self.call = (name, a, k)
            return self
        return f


class Prog:
    ENGS = ("pe", "act", "dve", "pool", "sp")

    def __init__(self, nc):
        self.nc = nc
        self.streams = {e: [] for e in self.ENGS}
        self.count = defaultdict(int)
        self.lastw = {}
        self.readers = defaultdict(list)
        self.waited = defaultdict(int)
        self.nops = 0
        self.epoch = 0
        self.pool_hold = False
        import os
        self.cut = int(os.environ["PCUT"]) if "PCUT" in os.environ else None

    def _dep(self, eng, rec):
        semkey, val = rec[0], rec[1]
        if eng == "pool" and (semkey.startswith("dma_cc@") or self.pool_hold):
            return
        if self.waited[(eng, semkey)] < val:
            self.waited[(eng, semkey)] = val
            self.streams[eng].append(("wait", semkey, val))

    def op(self, eng, fn, reads=(), writes=(), chan=None, inc_override=None):
        if self.cut is not None and self.nops >= self.cut:
            return
        isdma = chan is not None
        for k in reads:
            w = self.lastw.get(k)
            if w is not None:
                self._dep(eng, w)
            if k.startswith(PSUM_PREFIXES):
                for r in self.readers[k]:
                    if r[2] != eng:
                        self._dep(eng, r)
        for k in writes:
            w = self.lastw.get(k)
            if w is not None:
                if not (w[2] == eng == "pe" and not w[3] and not isdma):
                    self._dep(eng, w)
            for r in self.readers[k]:
                if r[2] != eng or r[3] or isdma:
                    self._dep(eng, r)
        if isdma:
            semkey, inc = "dma_%s@%d" % (chan, self.epoch), (inc_override or 16)
        else:
            semkey, inc = "%s@%d" % (eng, self.epoch), 1
        self.count[semkey] += inc
        rec = (semkey, self.count[semkey], eng, isdma)
        rec_ = _Rec()
        fn(rec_)
        self.streams[eng].append(("op", rec_.call, semkey, inc))
        for k in writes:
            self.lastw[k] = rec
            self.readers[k] = []
        for k in reads:
            self.readers[k].append(rec)
        self.nops += 1

    def barrier(self):
        for e in self.ENGS:
            for semkey, val in list(self.count.items()):
                if val:
                    self._dep(e, (semkey, val))
        self.lastw.clear()
        self.readers.clear()
        self.epoch += 1

    def finish(self):
        for semkey, val in list(self.count.items()):
            if semkey.startswith("dma_"):
                self._dep("sp", (semkey, val))
        for semkey, val in list(self.count.items()):
            if not semkey.startswith("dma_") and val:
                self._dep("sp", (semkey, val))

    def emit(self, es):
        nc = self.nc
        sems = {}
        for i, k in enumerate(sorted(self.count)):
            sems[k] = es.enter_context(nc.semaphore("s%d" % i))
        block = es.enter_context(nc.Block())
        streams = self.streams

        def run(eng_handle, items):
            for it in items:
                if it[0] == "wait":
                    eng_handle.wait_ge(sems[it[1]], it[2])
                else:
                    name, a, k = it[1]
                    getattr(eng_handle, name)(*a, **k).then_inc(sems[it[2]], it[3])

        @block.tensor
        def _(e):
            run(e, streams["pe"])

        @block.scalar
        def _(e):
            run(e, streams["act"])

        @block.vector
        def _(e):
            run(e, streams["dve"])

        @block.gpsimd
        def _(e):
            run(e, streams["pool"])

        @block.sync
        def _(e):
            run(e, streams["sp"])


class Ctx:
    _uid = [0]

    def __init__(self, nc, es, P):
        self.nc, self.es, self.P = nc, es, P
        Ctx._uid[0] += 1
        self.n = Ctx._uid[0] * 1000

    def sb(self, shape, dt=F32, name=None):
        self.n += 1
        return self.es.enter_context(self.nc.sbuf_tensor("%s_%d" % (name or "t", self.n), list(shape), dt))

    def ps(self, shape, dt=F32, name=None):
        self.n += 1
        return self.es.enter_context(self.nc.psum_tensor("%s_%d" % (name or "p", self.n), list(shape), dt))


def chunk_consts():
    j = np.arange(128)
    same = (j[:, None] // CH) == (j[None, :] // CH)
    m1 = (same & (j[:, None] <= j[None, :])).astype(np.float32)
    m2 = (same & (j[:, None] > j[None, :])).astype(np.float32)
    ident = np.eye(128, dtype=np.float32)
    ones = np.ones((128, 128), np.float32)
    cind = np.zeros((128, 128), np.float32)
    cind[:64, 0] = 1.0
    cind[64:, 1] = 1.0
    return np.concatenate([m1, m2, ident, ones, cind], axis=1)


C_M1, C_M2, C_ID, C_ONES, C_CIND = 0, 128, 256, 384, 512
NCONST = 640


def make_epsc(P, A, eng="pool"):
    epsc = A.sb([128, 2], F32, "epsc")
    P.op(eng, lambda e: e.memset(epsc[:, 0:1], D * EPS), writes=["epsc0"])
    P.op(eng, lambda e: e.memset(epsc[:, 1:2], EPS), reads=["epsc0"], writes=["epsc"])
    return epsc


def norm_block(P, epsc, x_blk, xkey, ss, rs, sskey, junk, junkkey, xn, xnkey, ps_tr, pskey, idb, hT_dst, hTkey,
               wrow=None, wkey=None):
    P.op("act", lambda e: e.activation(out=junk, in_=x_blk, func=AF.Square, accum_out=ss),
         reads=[xkey], writes=[junkkey, sskey])
    P.op("act", lambda e: e.activation(out=rs, in_=ss, func=AF.Ln, bias=epsc[:, 0:1]),
         reads=[sskey, "epsc"], writes=[sskey + "r0"])
    P.op("act", lambda e: e.activation(out=rs, in_=rs, func=AF.Exp, scale=-0.5),
         reads=[sskey + "r0"], writes=[sskey + "r"])
    if wrow is None:
        P.op("dve", lambda e: e.tensor_scalar(xn, x_blk, rs, None, ALU.mult),
             reads=[xkey, sskey + "r"], writes=[xnkey])
    else:
        P.op("dve", lambda e: e.scalar_tensor_tensor(out=xn, in0=x_blk, scalar=rs, in1=wrow,
                                                      op0=ALU.mult, op1=ALU.mult),
             reads=[xkey, sskey + "r", wkey], writes=[xnkey])
    for c in range(NCH):
        P.op("pe", lambda e, c=c: e.transpose(ps_tr[:, c, :], xn[:, c * 128:(c + 1) * 128], idb),
             reads=[xnkey, "consts_b"], writes=[pskey])
    P.op("act", lambda e: e.copy(hT_dst, ps_tr[:, :, :]), reads=[pskey], writes=[hTkey])


def build_phase1(nc, es, P, A, T, x1, w1, cw1, sc1, nw1, cst, o_out):
    NT = T // 512
    sb, ps = A.sb, A.ps
    cf = sb([128, NCONST], F32, "cf")
    cb = sb([128, NCONST], BF16, "cb")
    P.op("sp", lambda e: e.dma_start(out=cf[:], in_=cst[:, :]), writes=["consts_f"], chan="cf")
    P.op("dve", lambda e: e.tensor_copy(cb[:], cf[:]), reads=["consts_f"], writes=["consts_b"])
    m1f, m2f = cf[:, C_M1:C_M1 + 128], cf[:, C_M2:C_M2 + 128]
    idf, onesf, cindf = cf[:, C_ID:C_ID + 128], cf[:, C_ONES:C_ONES + 128], cf[:, C_CIND:C_CIND + 2]
    idb, onesb = cb[:, C_ID:C_ID + 128], cb[:, C_ONES:C_ONES + 128]

    epsc = make_epsc(P, A)
    wf = sb([128, NCH, 386], F32, "wf")
    wb = sb([128, NCH, 386], BF16, "wb")
    nw = sb([128, NCH], F32, "nw")
    cw = sb([128, 12], F32, "cw")
    sc = sb([128, 2], F32, "sc")
    negA = sb([128, 1], F32, "negA")
    P.op("sp", lambda e: e.dma_start(out=wf[:], in_=w1.rearrange("(c p) n -> p c n", p=128)), writes=["wf"], chan="wf")
    P.op("sp", lambda e: e.dma_start(out=nw[:], in_=nw1[:, :]), writes=["nw"], chan="nw")
    P.op("sp", lambda e: e.dma_start(out=cw[:], in_=cw1[:, :]), writes=["cw"], chan="cw")
    P.op("sp", lambda e: e.dma_start(out=sc[:], in_=sc1[:, :]), writes=["sc"], chan="sc")
    for c in range(NCH):
        P.op("dve", lambda e, c=c: e.tensor_scalar(wb[:, c, :], wf[:, c, :], nw[:, c:c + 1], 32.0, ALU.mult, ALU.mult),
             reads=["wf", "nw"], writes=["wb"])
    P.op("act", lambda e: e.activation(out=negA[:], in_=sc[:, 0:1], func=AF.Exp), reads=["sc"], writes=["negA0"])
    P.op("dve", lambda e: e.tensor_scalar(negA[:], negA[:], -1.0, None, ALU.mult), reads=["negA0"], writes=["negA"])

    xt = [sb([128, 4, D], F32, "xt") for _ in range(2)]
    junk = sb([128, D], BF16, "junk")
    ss = sb([128, 8], F32, "ss")
    rs = sb([128, 8], F32, "rs")
    xn = [sb([128, D], BF16, "xn") for _ in range(2)]
    hT = [sb([128, NCH, 512], BF16, "hT") for _ in range(2)]
    cbuf = [sb([128, 3 + 512], F32, "cbuf") for _ in range(3)]
    acc = [sb([128, 512], F32, "acc") for _ in range(3)]
    sil = [sb([128, 512], F32, "sil") for _ in range(2)]
    sq = [sb([128, 512], BF16, "sq") for _ in range(2)]
    rn = [sb([128, 512], F32, "rn") for _ in range(2)]
    QT = [sb([128, 512], BF16, "QT") for _ in range(2)]
    KT = [sb([128, 512], BF16, "KT") for _ in range(2)]
    VT = [sb([128, 512], BF16, "VT") for _ in range(2)]
    bdt = [sb([128, 4, 2], F32, "bdt") for _ in range(2)]
    gsc = [sb([128, 8, 4], F32, "gsc") for _ in range(2)]
    def four(shape, dt, name):
        return [sb(shape, dt, name) for _ in range(4)]

    def eight(shape, dt, name):
        return [[sb(shape, dt, name) for _ in range(4)] for _ in range(2)]

    gM = four([128, 128], F32, "gM")
    rgc = four([128, 2], F32, "rgc")
    D1 = four([128, 128], F32, "D1")
    D2 = four([128, 128], F32, "D2")
    bg = four([128, 1], F32, "bg")
    bgK = four([128, 128], BF16, "bgK")
    Bm = four([128, 128], F32, "Bm")
    Bq = four([128, 128], F32, "Bq")
    Nq = four([128, 128], F32, "Nq")
    Rq = four([128, 128], F32, "Rq")
    Rt = four([128, 128], F32, "Rt")
    PTm = four([128, 128], F32, "PTm")
    smx = eight([128, 4], F32, "smx")
    KD = eight([128, 128], BF16, "KD")
    bV = eight([128, 128], BF16, "bV")
    TTb = eight([128, 128], BF16, "TTb")
    PT = eight([128, 128], BF16, "PT")
    nWT = eight([128, 128], BF16, "nWT")
    Ub = [sb([128, 128], BF16, "Ub") for _ in range(2)]
    pus = [sb([128, 128], F32, "pus") for _ in range(2)]
    Osb = [sb([128, 128], F32, "Osb") for _ in range(2)]
    Sf = [sb([128, 128], F32, "Sf") for _ in range(2)]
    Sb = [sb([128, 128], BF16, "Sb") for _ in range(2)]

    ps_tr = ps([128, NCH, 128], BF16, "ps_tr")
    ps_tb = ps([128, 8, 128], BF16, "ps_tb")
    ps_pj = [ps([128, 512], F32, "ps_pj") for _ in range(1)]
    ps_ch = ps([128, 4, 128], F32, "ps_ch")
    ps_sl = [ps([128, 4, 128], F32, "ps_sl") for _ in range(4)]
    pj_i = [0]

    def pjslot():
        i = pj_i[0] % len(ps_pj)
        pj_i[0] += 1
        return ps_pj[i], "ppj%d" % i


    P.op("pool", lambda e: e.memset(Sf[0][:], 0.0), writes=["Sf0"])
    P.op("pool", lambda e: e.memset(Sb[0][:], 0.0), writes=["Sb0"])
    for g in range(3):
        P.op("pool", lambda e, g=g: e.memset(cbuf[g][:, 0:3], 0.0), writes=["cbufh%d" % g])
    sidx = [0]
    chain_q = []

    def tile_level(ti):
        tp = ti % 2
        xk = "xt%d" % tp
        hk = "hT%d" % tp
        bk = "bdt%d" % tp
        qk, kk, vk = "QT%d" % tp, "KT%d" % tp, "VT%d" % tp
        G = gsc[tp]
        gk = "gsc%d" % tp
        xg, ax, ee, ll, sp_, gg, be, nbe = (G[:, i, :] for i in range(8))
        pieces = []

        def p_load():
            P.op("sp", lambda e, ti=ti, tp=tp: e.dma_start(
                out=xt[tp][:], in_=x1[ti * 512:(ti + 1) * 512, :].rearrange("(j p) d -> p j d", p=128)),
                writes=[xk], chan=xk)
        pieces.append(p_load)
        def p_norm(j):
            bp = j % 2
            norm_block(P, epsc, xt[tp][:, j, :], xk, ss[:, j + 4 * tp:j + 4 * tp + 1], rs[:, j + 4 * tp:j + 4 * tp + 1],
                       "ss%d_%d" % (tp, j), junk[:], "junk", xn[bp][:], "xn%d" % bp, ps_tr, "ps_tr", idb,
                       hT[tp][:, :, j * 128:(j + 1) * 128], hk)
        for j in range(4):
            pieces.append(lambda j=j: p_norm(j))
        def p_proj(g):
            pj, pjk = pjslot()
            for c in range(NCH):
                P.op("pe", lambda e, g=g, c=c, pj=pj: e.matmul(pj[:], lhsT=wb[:, c, g * 128:(g + 1) * 128],
                                                              rhs=hT[tp][:, c, :], start=(c == 0), stop=(c == NCH - 1)),
                     reads=["wb", hk], writes=[pjk])
            P.op("act", lambda e, g=g, pj=pj: e.copy(cbuf[g][:, 3:515], pj[:]), reads=[pjk], writes=["cbufm%d" % g])
        for g in range(3):
            pieces.append(lambda g=g: p_proj(g))
        def p_bd():
            pj, pjk = pjslot()
            for j in range(4):
                for c in range(NCH):
                    P.op("pe", lambda e, j=j, c=c, pj=pj: e.matmul(pj[:, 2 * j:2 * j + 2], lhsT=hT[tp][:, c, j * 128:(j + 1) * 128],
                                                                  rhs=wb[:, c, 384:386], start=(c == 0), stop=(c == NCH - 1)),
                         reads=["wb", hk], writes=[pjk])
            P.op("dve", lambda e, pj=pj: e.tensor_copy(bdt[tp][:].rearrange("p a b -> p (a b)"), pj[:, 0:8]), reads=[pjk], writes=[bk])
        pieces.append(p_bd)
        def p_conv(g):
            ck = ["cbufh%d" % g, "cbufm%d" % g]
            ak = "acc%d" % g
            P.op("dve", lambda e, g=g: e.tensor_scalar(acc[g][:], cbuf[g][:, 0:512], cw[:, 4 * g:4 * g + 1], None, ALU.mult),
                 reads=ck + ["cw"], writes=[ak])
            for k in range(1, 4):
                P.op("dve", lambda e, g=g, k=k: e.scalar_tensor_tensor(
                    out=acc[g][:], in0=cbuf[g][:, k:k + 512], scalar=cw[:, 4 * g + k:4 * g + k + 1], in1=acc[g][:],
                    op0=ALU.mult, op1=ALU.add), reads=ck + ["cw", ak], writes=[ak])
            P.op("pool", lambda e, g=g: e.tensor_copy(cbuf[g][:, 0:3], cbuf[g][:, 512:515]),
                 reads=["cbufm%d" % g, ak], writes=["cbufh%d" % g])
        for g in range(3):
            pieces.append(lambda g=g: p_conv(g))
        def p_qkv():
            P.op("act", lambda e: e.activation(out=VT[tp][:], in_=acc[2][:], func=AF.Silu), reads=["acc2"], writes=[vk])
            for g in range(2):
                P.op("act", lambda e, g=g: e.activation(out=sil[g][:], in_=acc[g][:], func=AF.Silu), reads=["acc%d" % g], writes=["sil%d" % g])
                P.op("act", lambda e, g=g: e.activation(out=sq[g][:], in_=sil[g][:], func=AF.Square), reads=["sil%d" % g], writes=["sq%d" % g])
                pj, pjk = pjslot()
                P.op("pe", lambda e, g=g, pj=pj: e.matmul(pj[:], lhsT=onesb, rhs=sq[g][:], start=True, stop=True),
                     reads=["consts_b", "sq%d" % g], writes=[pjk])
                P.op("act", lambda e, g=g, pj=pj: e.activation(out=rn[g][:], in_=pj[:], func=AF.Ln, bias=epsc[:, 1:2]),
                     reads=[pjk, "epsc"], writes=["rn%da" % g])
                P.op("act", lambda e, g=g: e.activation(out=rn[g][:], in_=rn[g][:], func=AF.Exp, scale=-0.5),
                     reads=["rn%da" % g], writes=["rn%d" % g])
            P.op("dve", lambda e: e.scalar_tensor_tensor(out=QT[tp][:], in0=sil[0][:], scalar=float(DK) ** -0.5, in1=rn[0][:],
                                                          op0=ALU.mult, op1=ALU.mult), reads=["sil0", "rn0"], writes=[qk])
            P.op("dve", lambda e: e.tensor_tensor(out=KT[tp][:], in0=sil[1][:], in1=rn[1][:], op=ALU.mult), reads=["sil1", "rn1"], writes=[kk])
        pieces.append(p_qkv)
        def p_gate():
            P.op("dve", lambda e: e.tensor_scalar(xg, bdt[tp][:, :, 1], sc[:, 1:2], None, ALU.add), reads=[bk, "sc"], writes=[gk + "a"])
            P.op("dve", lambda e: e.scalar_tensor_tensor(out=ax, in0=xg, scalar=-1.0, in1=xg, op0=ALU.mult, op1=ALU.max), reads=[gk + "a"], writes=[gk + "b"])
            P.op("act", lambda e: e.activation(out=ee, in_=ax, func=AF.Exp, scale=-1.0), reads=[gk + "b"], writes=[gk + "c"])
            P.op("act", lambda e: e.activation(out=ll, in_=ee, func=AF.Ln, bias=1.0), reads=[gk + "c"], writes=[gk + "d"])
            P.op("dve", lambda e: e.scalar_tensor_tensor(out=sp_, in0=xg, scalar=0.0, in1=ll, op0=ALU.max, op1=ALU.add),
                 reads=[gk + "a", gk + "d"], writes=[gk + "e"])
            P.op("dve", lambda e: e.tensor_scalar(gg, sp_, negA[:, 0:1], None, ALU.mult), reads=[gk + "e", "negA"], writes=[gk + "g"])
            P.op("act", lambda e: e.activation(out=be, in_=bdt[tp][:, :, 0], func=AF.Sigmoid), reads=[bk], writes=[gk + "be"])
            P.op("dve", lambda e: e.tensor_scalar(nbe, be, -1.0, None, ALU.mult), reads=[gk + "be"], writes=[gk + "nb"])


        pieces.append(p_gate)
        return pieces

    def block_level(ti):
        tp = ti % 2
        qk, kk, vk = "QT%d" % tp, "KT%d" % tp, "VT%d" % tp
        G = gsc[tp]
        gk = "gsc%d" % tp
        xg, ax, ee, ll, sp_, gg, be, nbe = (G[:, i, :] for i in range(8))
        def bk(j):
            return ps_sl[j], "psl%d" % j

        def stage_done():
            if chain_q:
                chain_q.pop(0)()
            if pre_q:
                pre_q.pop(0)()

        J = range(4)
        sfx = ["_%d" % j for j in J]
        csl = [slice(j * 128, (j + 1) * 128) for j in J]
        g_ = [gg[:, j:j + 1] for j in J]
        be_ = [be[:, j:j + 1] for j in J]
        nbe_ = [nbe[:, j:j + 1] for j in J]
        ck = ["_%d_%d" % (tp, j) for j in J]
        for j in J:
            P.op("dve", lambda e: e.tensor_scalar(gM[j][:], m1f, g_[j], None, ALU.mult), reads=["consts_f", gk + "g"], writes=["gM" + sfx[j]])
            P.op("dve", lambda e: e.tensor_scalar(rgc[j][:], cindf, g_[j], None, ALU.mult), reads=["consts_f", gk + "g"], writes=["rgc" + sfx[j]])
        for j in J:
            b_, bkk = bk(j)
            P.op("pe", lambda e: e.matmul(b_[:, 0, :], lhsT=gM[j][:], rhs=m2f, start=True, stop=True), reads=["gM" + sfx[j], "consts_f"], writes=[bkk])
            P.op("pe", lambda e: e.matmul(b_[:, 1, :], lhsT=m2f, rhs=gM[j][:], start=True, stop=True), reads=["gM" + sfx[j], "consts_f"], writes=[bkk])
            P.op("pe", lambda e: e.matmul(b_[:, 2, 0:1], lhsT=m1f, rhs=g_[j], start=True, stop=True), reads=[gk + "g", "consts_f"], writes=[bkk])
            P.op("pe", lambda e: e.matmul(b_[:, 2, 1:2], lhsT=m2f, rhs=g_[j], start=True, stop=True), reads=[gk + "g", "consts_f"], writes=[bkk])
            P.op("pe", lambda e: e.matmul(b_[:, 2, 2:4], lhsT=onesf, rhs=rgc[j][:], start=True, stop=True), reads=["rgc" + sfx[j], "consts_f"], writes=[bkk])
        for j in J:
            b_, bkk = bk(j)
            P.op("act", lambda e: e.activation(out=D1[j][:], in_=b_[:, 0, :], func=AF.Exp), reads=[bkk], writes=["D1" + sfx[j]])
            P.op("act", lambda e: e.activation(out=D2[j][:], in_=b_[:, 1, :], func=AF.Exp), reads=[bkk], writes=["D2" + sfx[j]])
            P.op("act", lambda e: e.activation(out=smx[tp][j][:], in_=b_[:, 2, 0:4], func=AF.Exp), reads=[bkk], writes=["smx" + ck[j]])
        for j in J:
            P.op("dve", lambda e: e.tensor_tensor(out=bg[j][:], in0=be_[j], in1=smx[tp][j][:, 0:1], op=ALU.mult),
                 reads=[gk + "be", "smx" + ck[j]], writes=["bg" + sfx[j]])
            P.op("pool", lambda e: e.tensor_tensor(out=PTm[j][:], in0=D2[j][:], in1=m1f, op=ALU.mult), reads=["D2" + sfx[j], "consts_f"], writes=["PTm" + sfx[j]])
        stage_done()
        for j in J:
            P.op("pe", lambda e: e.transpose(ps_tb[:, 2 * j, :], KT[tp][:, csl[j]], idb), reads=[kk, "consts_b"], writes=["ptb"])
            P.op("pe", lambda e: e.transpose(ps_tb[:, 2 * j + 1, :], VT[tp][:, csl[j]], idb), reads=[vk, "consts_b"], writes=["ptb"])
        for j in J:
            P.op("dve", lambda e: e.tensor_scalar(bgK[j][:], ps_tb[:, 2 * j, :], bg[j][:, 0:1], None, ALU.mult), reads=["ptb", "bg" + sfx[j]], writes=["bgK" + sfx[j]])
        for j in J:
            P.op("act", lambda e: e.activation(out=KD[tp][j][:], in_=ps_tb[:, 2 * j, :], func=AF.Copy, scale=smx[tp][j][:, 1:2]),
                 reads=["ptb", "smx" + ck[j]], writes=["KD" + ck[j]])
            P.op("act", lambda e: e.activation(out=bV[tp][j][:], in_=ps_tb[:, 2 * j + 1, :], func=AF.Copy, scale=be_[j]),
                 reads=["ptb", gk + "be"], writes=["bV" + ck[j]])
        stage_done()
        for j in J:
            b_, bkk = bk(j)
            P.op("pe", lambda e: e.matmul(b_[:, 0, :], lhsT=KT[tp][:, csl[j]], rhs=KT[tp][:, csl[j]], start=True, stop=True), reads=[kk], writes=[bkk])
            P.op("pe", lambda e: e.matmul(b_[:, 1, :], lhsT=KT[tp][:, csl[j]], rhs=QT[tp][:, csl[j]], start=True, stop=True), reads=[kk, qk], writes=[bkk])
        for j in J:
            b_, bkk = bk(j)
            P.op("dve", lambda e: e.tensor_tensor(out=Bm[j][:], in0=b_[:, 0, :], in1=D1[j][:], op=ALU.mult), reads=[bkk, "D1" + sfx[j]], writes=["Bm" + sfx[j]])
            P.op("dve", lambda e: e.tensor_tensor(out=PT[tp][j][:], in0=b_[:, 1, :], in1=PTm[j][:], op=ALU.mult), reads=[bkk, "PTm" + sfx[j]], writes=["PT" + ck[j]])
            P.op("dve", lambda e: e.scalar_tensor_tensor(out=Bq[j][:], in0=Bm[j][:], scalar=nbe_[j], in1=m2f, op0=ALU.mult, op1=ALU.mult),
                 reads=["Bm" + sfx[j], gk + "nb", "consts_f"], writes=["B" + sfx[j]])
        stage_done()
        for j in J:
            b_, bkk = bk(j)
            P.op("pe", lambda e: e.transpose(b_[:, 2, :], Bq[j][:], idf), reads=["B" + sfx[j], "consts_f"], writes=[bkk])
        for j in J:
            b_, bkk = bk(j)
            P.op("act", lambda e: e.copy(Nq[j][:], b_[:, 2, :]), reads=[bkk], writes=["N" + sfx[j]])
            P.op("pool", lambda e: e.tensor_tensor(out=Rt[j][:], in0=Bq[j][:], in1=idf, op=ALU.add), reads=["B" + sfx[j], "consts_f"], writes=["Rt" + sfx[j]])
        for j in J:
            P.op("dve", lambda e: e.tensor_tensor(out=Rq[j][:], in0=Nq[j][:], in1=idf, op=ALU.add), reads=["N" + sfx[j], "consts_f"], writes=["R" + sfx[j]])
        stage_done()
        for lvl in range(5):
            last = lvl == 4
            for j in J:
                b_, bkk = bk(j)
                P.op("pe", lambda e: e.matmul(b_[:, 0, :], lhsT=Bq[j][:], rhs=Nq[j][:], start=True, stop=True), reads=["B" + sfx[j], "N" + sfx[j]], writes=[bkk])
                if not last:
                    P.op("pe", lambda e: e.matmul(b_[:, 1, :], lhsT=Nq[j][:], rhs=Bq[j][:], start=True, stop=True), reads=["B" + sfx[j], "N" + sfx[j]], writes=[bkk])
            for j in J:
                b_, bkk = bk(j)
                P.op("act", lambda e: e.copy(Nq[j][:], b_[:, 0, :]), reads=[bkk], writes=["N" + sfx[j]])
                if not last:
                    P.op("act", lambda e: e.copy(Bq[j][:], b_[:, 1, :]), reads=[bkk], writes=["B" + sfx[j]])
            stage_done()
            for j in J:
                b_, bkk = bk(j)
                P.op("pe", lambda e: e.matmul(b_[:, 2, :], lhsT=Rt[j][:], rhs=Nq[j][:], start=True, stop=True), reads=["Rt" + sfx[j], "N" + sfx[j]], writes=[bkk])
                if not last:
                    P.op("pe", lambda e: e.matmul(b_[:, 3, :], lhsT=Rq[j][:], rhs=Bq[j][:], start=True, stop=True), reads=["R" + sfx[j], "B" + sfx[j]], writes=[bkk])
            for j in J:
                b_, bkk = bk(j)
                if not last:
                    P.op("dve", lambda e: e.tensor_tensor(out=Rq[j][:], in0=b_[:, 2, :], in1=Rq[j][:], op=ALU.add), reads=[bkk, "R" + sfx[j]], writes=["R" + sfx[j]])
                    P.op("dve", lambda e: e.tensor_tensor(out=Rt[j][:], in0=b_[:, 3, :], in1=Rt[j][:], op=ALU.add), reads=[bkk, "Rt" + sfx[j]], writes=["Rt" + sfx[j]])
                else:
                    P.op("dve", lambda e: e.tensor_tensor(out=TTb[tp][j][:], in0=b_[:, 2, :], in1=Rq[j][:], op=ALU.add), reads=[bkk, "R" + sfx[j]], writes=["TTb" + ck[j]])
            stage_done()
        for j in J:
            b_, bkk = bk(j)
            P.op("pe", lambda e: e.matmul(b_[:, 0, :], lhsT=bgK[j][:], rhs=TTb[tp][j][:], start=True, stop=True), reads=["bgK" + sfx[j], "TTb" + ck[j]], writes=[bkk])
        for j in J:
            b_, bkk = bk(j)
            P.op("act", lambda e: e.mul(nWT[tp][j][:], b_[:, 0, :], -1.0), reads=[bkk], writes=["nWT" + ck[j]])
        stage_done()
        while chain_q:
            chain_q.pop(0)()
        while pre_q:
            pre_q.pop(0)()

        def chunk_step(j, c, tp=tp, qk=qk, ck=ck, ti=ti, csl=csl):
            r = slice(64 * c, 64 * c + 64)
            si = sidx[0]
            so, sn_ = si % 2, (si + 1) % 2
            sidx[0] += 1
            o2 = j % 2
            u, qs, sn, pu = ps_ch[:, 0, :], ps_ch[:, 1, :], ps_ch[:, 2, :], ps_ch[:, 3, :]
            P.op("pe", lambda e: e.matmul(u, lhsT=TTb[tp][j][r, :], rhs=bV[tp][j][r, :], start=True, stop=False),
                 reads=["TTb" + ck[j], "bV" + ck[j]], writes=["ps_ch"])
            P.op("pe", lambda e: e.matmul(u, lhsT=nWT[tp][j][:], rhs=Sb[so][:], start=False, stop=True),
                 reads=["nWT" + ck[j], "Sb%d" % so], writes=["ps_ch"])
            P.op("pe", lambda e: e.matmul(qs, lhsT=QT[tp][:, csl[j]], rhs=Sb[so][:], start=True, stop=True),
                 reads=[qk, "Sb%d" % so], writes=["ps_ch"])
            P.op("dve", lambda e: e.tensor_copy(Ub[o2][r, :], u[r, :]), reads=["ps_ch"], writes=["Ub%d_%d" % (o2, c)])
            P.op("pe", lambda e: e.matmul(sn, lhsT=KD[tp][j][r, :], rhs=Ub[o2][r, :], start=True, stop=True),
                 reads=["KD" + ck[j], "Ub%d_%d" % (o2, c)], writes=["ps_ch"])
            P.op("pe", lambda e: e.matmul(pu, lhsT=PT[tp][j][r, :], rhs=Ub[o2][r, :], start=True, stop=True),
                 reads=["PT" + ck[j], "Ub%d_%d" % (o2, c)], writes=["ps_ch"])
            P.op("dve", lambda e: e.scalar_tensor_tensor(out=Sf[sn_][:], in0=Sf[so][:], scalar=smx[tp][j][:, 2 + c:3 + c], in1=sn,
                                                         op0=ALU.mult, op1=ALU.add), reads=["Sf%d" % so, "smx" + ck[j], "ps_ch"], writes=["Sf%d" % sn_])
            P.op("act", lambda e: e.copy(Sb[sn_][:], Sf[sn_][:]), reads=["Sf%d" % sn_], writes=["Sb%d" % sn_])
            P.op("dve", lambda e: e.tensor_copy(pus[o2][r, :], pu[r, :]), reads=["ps_ch"], writes=["pus%d_%d" % (o2, c)])
            P.op("dve", lambda e: e.scalar_tensor_tensor(out=Osb[o2][r, :], in0=qs[r, :], scalar=smx[tp][j][r, 0:1], in1=pus[o2][r, :],
                                                         op0=ALU.mult, op1=ALU.add), reads=["ps_ch", "smx" + ck[j], "pus%d_%d" % (o2, c)],
                 writes=["Osb%d_%d" % (o2, c)])
            if c == 1:
                blk = ti * 4 + j
                P.op("sp", lambda e: e.dma_start(out=o_out[blk * 128:(blk + 1) * 128, :], in_=Osb[o2][:]),
                     reads=["Osb%d_0" % o2, "Osb%d_1" % o2], chan="ost%d" % o2)

        for j in J:
            for c in range(2):
                chain_q.append(lambda j=j, c=c, f=chunk_step: f(j, c))


    pre_q = []
    for f in tile_level(0):
        f()
    for ti in range(NT):
        if ti + 1 < NT:
            pre_q.extend(tile_level(ti + 1))
        block_level(ti)
    while chain_q:
        chain_q.pop(0)()


def prep_weight(P, stg, stgkey, src2d, n_c, ncols, dst, dst_col0, scale_fn, dkey, skeys, cnt, dst_c0=0):
    pw = min(2048 // n_c, ncols)
    for col in range(0, ncols, pw):
        w = min(pw, ncols - col)
        b = cnt[0] % len(stg)
        cnt[0] += 1
        sv = stg[b][:, 0:n_c * w].rearrange("p (c n) -> p c n", c=n_c)
        k = stgkey + str(b)
        P.op("sp", lambda e: e.dma_start(out=sv, in_=src2d[:, col:col + w].rearrange("(c p) n -> p c n", p=128)),
             writes=[k], chan=k)
        dv = dst[:, dst_c0:dst_c0 + n_c, dst_col0 + col:dst_col0 + col + w]
        if scale_fn is None:
            eng = "act" if (cnt[0] % 2) else "dve"
            if eng == "act":
                P.op("act", lambda e: e.copy(dv, sv), reads=[k], writes=[dkey])
            else:
                P.op("dve", lambda e: e.tensor_copy(dv, sv), reads=[k], writes=[dkey])
        else:
            for c in range(n_c):
                sc_ = scale_fn(c)
                if c % 2:
                    P.op("act", lambda e: e.activation(out=dst[:, c, dst_col0 + col:dst_col0 + col + w], in_=sv[:, c, :],
                                                       func=AF.Copy, scale=sc_), reads=[k] + skeys, writes=[dkey])
                else:
                    P.op("dve", lambda e: e.tensor_scalar(dst[:, c, dst_col0 + col:dst_col0 + col + w], sv[:, c, :], sc_, None, ALU.mult),
                         reads=[k] + skeys, writes=[dkey])


TB = 2
TW = TB * 128


def load_consts(P, A, cst, pre):
    cf = A.sb([128, NCONST], F32, "cf")
    cb = A.sb([128, NCONST], BF16, "cb")
    P.op("sp", lambda e: e.dma_start(out=cf[:], in_=cst[:, :]), writes=[pre + "consts_f"], chan=pre + "cf")
    P.op("dve", lambda e: e.tensor_copy(cb[:], cf[:]), reads=[pre + "consts_f"], writes=["consts_b"])
    return cf, cb


def build_phase2a(nc, P, A, NTM, x2, oa2, validc, w_in, w_ba, w_bb, w_out, nwm_d, gnw_d, biasT_d, cst, xmid,
                  o_all=None, qsel_d=None, RT=None):
    NT2 = NTM + 3
    TC = NTM * TW
    sb, ps = A.sb, A.ps
    cf, cb = load_consts(P, A, cst, "a")
    idb = cb[:, C_ID:C_ID + 128]
    epsc = make_epsc(P, A, "dve")
    nwm = sb([128, NCH], F32, "nwm")
    gnw = sb([128, 1], F32, "gnw")
    P.op("sp", lambda e: e.dma_start(out=nwm[:], in_=nwm_d[:, :]), writes=["nwm0"], chan="nwm")
    P.op("sp", lambda e: e.dma_start(out=gnw[:], in_=gnw_d[:, :]), writes=["gnw"], chan="gnw")
    P.op("dve", lambda e: e.tensor_scalar(nwm[:], nwm[:], 32.0, None, ALU.mult), reads=["nwm0"], writes=["nwm"])
    Wi = sb([128, NCH, 4096], BF16, "Wi")
    WbA = sb([128, 4, 1024], BF16, "WbA")
    WbB = sb([128, 4, 1024], BF16, "WbB")
    Wo = sb([128, NCH, 1024], BF16, "Wo")
    es_stg = ExitStack()
    stg = [es_stg.enter_context(nc.sbuf_tensor("astg%d" % i, [128, 2048], F32)) for i in range(2)]
    cnt = [0]
    prep_weight(P, stg, "astg", w_in[:, 1536:2048], 8, 512, Wi, 0, lambda c: nwm[:, c:c + 1], "Wi", ["nwm"], cnt)
    prep_weight(P, stg, "astg", w_in[:, 2056:5640], 8, 3584, Wi, 512, lambda c: nwm[:, c:c + 1], "Wi", ["nwm"], cnt)
    prep_weight(P, stg, "astg", w_ba, 4, 1024, WbA, 0, lambda c: gnw[:, 0:1], "WbA", ["gnw"], cnt)
    prep_weight(P, stg, "astg", w_bb, 4, 1024, WbB, 0, None, "WbB", [], cnt)
    prep_weight(P, stg, "astg", w_out, 8, 1024, Wo, 0, None, "Wo", [], cnt)
    es_stg.close()
    P.barrier()
    biasT = sb([128, 8, 640], F32, "biasT")
    P.op("sp", lambda e: e.dma_start(out=biasT[:], in_=biasT_d[:, :, :]), writes=["biasT"], chan="biasT")
    valid = sb([128, NT2 * TB], F32, "valid")
    P.op("sp", lambda e: e.dma_start(out=valid[:], in_=validc[:, :]), writes=["valid"], chan="valid")
    ones8 = sb([128, 8, 1], F32, "ones8")
    P.op("dve", lambda e: e.memset(ones8[:], 1.0), writes=["ones8"])

    xt = [sb([128, TB, D], F32, "xt") for _ in range(2)]
    junk = sb([128, D], BF16, "junk")
    ss = sb([128, 8], F32, "ss")
    rs = sb([128, 8], F32, "rs")
    xn = [sb([128, D], BF16, "xn") for _ in range(2)]
    hT = sb([128, NCH, TW], BF16, "hT")
    KTb = sb([128, 4, 8 * 128], BF16, "KTb")
    Vaug = sb([128, 8, 8, 65], BF16, "Vaug")
    QTb = sb([128, 4, TW], BF16, "QTb")
    zs = sb([128, TB, 512], F32, "zs")
    oat = sb([128, TB, 512], F32, "oat")
    cands = None
    if o_all is not None:
        cand = sb([128, 4, 512], F32, "cand")
        if RT is None:
            cands = [(lambda r, q_=q_: o_all[q_ * TC + r:q_ * TC + r + 128, :].rearrange("p (h d) -> p h d", h=4)) for q_ in range(4)]
        else:
            o_alls, chunks, CR = o_all
            views = [a.rearrange("(r t) d -> t r d", r=8) for a in o_alls]

            def cand_ap(row, b_):
                i, off = row // CR, row % CR
                return views[i][off:off + 128, 4 * b_:4 * b_ + 4, :]
            cands = [(lambda r, b_=c_ // 4, q_=c_ % 4: cand_ap(q_ * TC + r, b_)) for c_ in range(8)]
        qsel = sb([128, len(cands)], F32, "qsel")
        P.op("sp", lambda e: e.dma_start(out=qsel[:], in_=qsel_d[:, :]), writes=["qsel"], chan="qsel")
    ssa = sb([128, 4], F32, "ssa")
    ra = sb([128, 4], F32, "ra")
    oan = sb([128, 512], BF16, "oan")
    oaT = sb([128, 4, TW], BF16, "oaT")
    ob = sb([128, 512], BF16, "ob")
    obT = sb([128, 4, TW], BF16, "obT")
    scs = [sb([128, 640], F32, "scs")] * 2
    PTb = [sb([128, 640], BF16, "PTb") for _ in range(2)]
    rden = sb([128, 8], F32, "rden")
    sg = [sb([128, 2 * TW], F32, "sg") for _ in range(2)]
    tt_ = [sb([128, 2 * TW], F32, "tt") for _ in range(2)]
    mixT = sb([128, NCH, TW], BF16, "mixT")

    ps_tr = ps([128, NCH, 128], BF16, "ps_tr")
    pbig = [ps([128, 512], F32, "pbig") for _ in range(5)]
    ps_o = [ps([128, 4, 65], F32, "ps_o") for _ in range(2)]
    bi = [0]

    def big():
        i = bi[0] % 5
        bi[0] += 1
        return pbig[i], "pbig%d" % i

    scale_q = 64.0 ** -0.5
    for tt in range(NT2):
        tp = tt % 2
        xk = "axt%d" % tp
        P.op("sp", lambda e: e.dma_start(out=xt[tp][:], in_=x2[tt * TW:(tt + 1) * TW, :].rearrange("(j p) d -> p j d", p=128)),
             writes=[xk], chan=xk)
        for j in range(TB):
            norm_block(P, epsc, xt[tp][:, j, :], xk, ss[:, j:j + 1], rs[:, j:j + 1], "ass%d" % j, junk[:], "ajunk",
                       xn[j % 2][:], "axn%d" % (j % 2), ps_tr, "aps_tr", idb, hT[:, :, j * 128:(j + 1) * 128], "ahT")
        ring0 = (tt * TB) % 8
        for m in range(4):
            pb_, pk = big()
            for c in range(NCH):
                P.op("pe", lambda e: e.matmul(pb_[:, 0:TW], lhsT=Wi[:, c, 1024 + m * 128:1024 + (m + 1) * 128], rhs=hT[:, c, :],
                                              start=(c == 0), stop=(c == NCH - 1)), reads=["Wi", "ahT"], writes=[pk])
            P.op("act", lambda e: e.copy(KTb[:, m, ring0 * 128:ring0 * 128 + TW], pb_[:, 0:TW]), reads=[pk], writes=["KTb"])
        for j in range(TB):
            slot = ring0 + j
            pb_, pk = big()
            for c in range(NCH):
                P.op("pe", lambda e: e.matmul(pb_[:, :], lhsT=hT[:, c, j * 128:(j + 1) * 128], rhs=Wi[:, c, 1536:2048],
                                              start=(c == 0), stop=(c == NCH - 1)), reads=["Wi", "ahT"], writes=[pk])
            P.op("dve", lambda e: e.tensor_copy(Vaug[:, slot, :, 0:64], pb_[:, :].rearrange("p (h d) -> p h d", h=8)),
                 reads=[pk], writes=["Vaug"])
            P.op("act", lambda e: e.activation(out=Vaug[:, slot, :, 64:65], in_=ones8[:], func=AF.Copy,
                                               scale=valid[:, tt * TB + j:tt * TB + j + 1]),
                 reads=["ones8", "valid"], writes=["Vaug"])
        if tt < 2:
            continue
        for m in range(4):
            pb_, pk = big()
            for c in range(NCH):
                P.op("pe", lambda e: e.matmul(pb_[:, 0:TW], lhsT=Wi[:, c, 512 + m * 128:512 + (m + 1) * 128], rhs=hT[:, c, :],
                                              start=(c == 0), stop=(c == NCH - 1)), reads=["Wi", "ahT"], writes=[pk])
            P.op("act", lambda e: e.mul(QTb[:, m, :], pb_[:, 0:TW], scale_q), reads=[pk], writes=["QTb"])
        for j in range(TB):
            pb_, pk = big()
            for c in range(NCH):
                P.op("pe", lambda e: e.matmul(pb_[:, :], lhsT=hT[:, c, j * 128:(j + 1) * 128], rhs=Wi[:, c, 0:512],
                                              start=(c == 0), stop=(c == NCH - 1)), reads=["Wi", "ahT"], writes=[pk])
            P.op("act", lambda e: e.activation(out=zs[:, j, :], in_=pb_[:, :], func=AF.Silu), reads=[pk], writes=["zs%d" % j])
        if o_all is None:
            P.op("sp", lambda e: e.dma_start(out=oat[:], in_=oa2[(tt - 2) * TW:(tt - 1) * TW, :].rearrange("(j p) d -> p j d", p=128)),
                 writes=["oat"], chan="oat")
        else:
            for j in range(TB):
                for cc_, cf_ in enumerate(cands):
                    k = cc_ % 4
                    ck_ = "cand_%d" % k
                    P.op("sp", lambda e: e.dma_start(out=cand[:, k, :].rearrange("p (h d) -> p h d", h=4),
                                                     in_=cf_((tt - 2) * TW + j * 128)),
                         reads=["oall"], writes=[ck_], chan=ck_)
                    if cc_ == 0:
                        P.op("dve", lambda e: e.tensor_scalar(oat[:, j, :], cand[:, k, :], qsel[:, cc_:cc_ + 1], None, ALU.mult),
                             reads=[ck_, "qsel"], writes=["oat"])
                    else:
                        P.op("dve", lambda e: e.scalar_tensor_tensor(out=oat[:, j, :], in0=cand[:, k, :], scalar=qsel[:, cc_:cc_ + 1],
                                                                     in1=oat[:, j, :], op0=ALU.mult, op1=ALU.add),
                             reads=[ck_, "qsel", "oat"], writes=["oat"])
        for j in range(TB):
            g = tt * TB + j
            for h in range(8):
                m, r = h // 2, slice(64 * (h % 2), 64 * (h % 2) + 64)
                p1, p1k = big()
                p2, p2k = big()
                for kb in range(5):
                    slot = (g - 4 + kb) % 8
                    dst = p1[:, kb * 128:(kb + 1) * 128] if kb < 4 else p2[:, 0:128]
                    P.op("pe", lambda e: e.matmul(dst, lhsT=KTb[r, m, slot * 128:(slot + 1) * 128], rhs=QTb[r, m, j * 128:(j + 1) * 128],
                                                  start=True, stop=True), reads=["KTb", "QTb"], writes=[p1k if kb < 4 else p2k])
                sp_ = h % 2
                P.op("dve", lambda e: e.tensor_tensor(out=scs[sp_][:, 0:512], in0=p1[:, :], in1=biasT[:, h, 0:512], op=ALU.add),
                     reads=[p1k, "biasT"], writes=["scsa"])
                P.op("dve", lambda e: e.tensor_tensor(out=scs[sp_][:, 512:640], in0=p2[:, 0:128], in1=biasT[:, h, 512:640], op=ALU.add),
                     reads=[p2k, "biasT"], writes=["scsb"])
                P.op("act", lambda e: e.activation(out=PTb[sp_][:], in_=scs[sp_][:], func=AF.Exp),
                     reads=["scsa", "scsb"], writes=["PTb%d" % sp_])
                for kb in range(5):
                    slot = (g - 4 + kb) % 8
                    P.op("pe", lambda e: e.matmul(ps_o[h // 4][:, h % 4, :], lhsT=PTb[sp_][:, kb * 128:(kb + 1) * 128],
                                                  rhs=Vaug[:, slot, h, :], start=(kb == 0), stop=(kb == 4)),
                         reads=["PTb%d" % sp_, "Vaug"], writes=["ps_o%d" % (h // 4)])
            for hg in range(2):
                P.op("dve", lambda e: e.tensor_scalar(rden[:, hg * 4:hg * 4 + 4], ps_o[hg][:, :, 64], 1e-30, None, ALU.add),
                     reads=["ps_o%d" % hg], writes=["rden%da" % hg])
                P.op("dve", lambda e: e.reciprocal(rden[:, hg * 4:hg * 4 + 4], rden[:, hg * 4:hg * 4 + 4]),
                     reads=["rden%da" % hg], writes=["rden%d" % hg])
            for h in range(8):
                P.op("act", lambda e: e.activation(out=ob[:, h * 64:(h + 1) * 64], in_=ps_o[h // 4][:, h % 4, 0:64], func=AF.Copy,
                                                   scale=rden[:, h:h + 1]), reads=["ps_o%d" % (h // 4), "rden%d" % (h // 4)], writes=["ob"])
            for c in range(4):
                P.op("pe", lambda e: e.transpose(ps_tr[:, c, :], ob[:, c * 128:(c + 1) * 128], idb), reads=["ob", "consts_b"], writes=["aps_tr"])
            P.op("act", lambda e: e.copy(obT[:, :, j * 128:(j + 1) * 128], ps_tr[:, 0:4, :]), reads=["aps_tr"], writes=["obT"])
            for hh in range(4):
                P.op("act", lambda e: e.activation(out=junk[:, 0:128], in_=oat[:, j, hh * 128:(hh + 1) * 128], func=AF.Square,
                                                   accum_out=ssa[:, hh:hh + 1]), reads=["oat"], writes=["ajunk", "ssa"])
            P.op("act", lambda e: e.activation(out=ra[:], in_=ssa[:], func=AF.Ln, scale=1.0 / 128.0, bias=epsc[:, 1:2]),
                 reads=["ssa", "epsc"], writes=["ra0"])
            P.op("act", lambda e: e.activation(out=ra[:], in_=ra[:], func=AF.Exp, scale=-0.5), reads=["ra0"], writes=["ra"])
            for hh in range(4):
                P.op("dve", lambda e: e.scalar_tensor_tensor(out=oan[:, hh * 128:(hh + 1) * 128], in0=oat[:, j, hh * 128:(hh + 1) * 128],
                                                             scalar=ra[:, hh:hh + 1], in1=zs[:, j, hh * 128:(hh + 1) * 128],
                                                             op0=ALU.mult, op1=ALU.mult), reads=["oat", "ra", "zs%d" % j], writes=["oan"])
            for c in range(4):
                P.op("pe", lambda e: e.transpose(ps_tr[:, 4 + c, :], oan[:, c * 128:(c + 1) * 128], idb), reads=["oan", "consts_b"], writes=["aps_tr"])
            P.op("act", lambda e: e.copy(oaT[:, :, j * 128:(j + 1) * 128], ps_tr[:, 4:8, :]), reads=["aps_tr"], writes=["oaT"])
        for mo in range(8):
            py, pyk = big()
            pg, pgk = big()
            for half, (Wb, src, skey) in enumerate(((WbA, oaT, "oaT"), (WbB, obT, "obT"))):
                for c in range(4):
                    P.op("pe", lambda e: e.matmul(py[:, half * TW:(half + 1) * TW], lhsT=Wb[:, c, mo * 128:(mo + 1) * 128], rhs=src[:, c, :],
                                                  start=(c == 0), stop=(c == 3)), reads=["WbA", "WbB", skey], writes=[pyk])
            for half in range(2):
                col0 = 2048 + half * 1024 + mo * 128
                for c in range(NCH):
                    P.op("pe", lambda e: e.matmul(pg[:, half * TW:(half + 1) * TW], lhsT=Wi[:, c, col0:col0 + 128], rhs=hT[:, c, :],
                                                  start=(c == 0), stop=(c == NCH - 1)), reads=["Wi", "ahT"], writes=[pgk])
            q2 = mo % 2
            P.op("act", lambda e: e.activation(out=sg[q2][:], in_=pg[:, :], func=AF.Sigmoid), reads=[pgk], writes=["sg%d" % q2])
            P.op("dve", lambda e: e.tensor_tensor(out=tt_[q2][:], in0=py[:, :], in1=sg[q2][:], op=ALU.mult),
                 reads=[pyk, "sg%d" % q2], writes=["tt%d" % q2])
            P.op("dve", lambda e: e.tensor_tensor(out=mixT[:, mo, :], in0=tt_[q2][:, 0:TW], in1=tt_[q2][:, TW:2 * TW], op=ALU.add),
                 reads=["tt%d" % q2], writes=["mixT"])
        for j in range(TB):
            for half in range(2):
                po, pok = big()
                for c in range(NCH):
                    P.op("pe", lambda e: e.matmul(po[:, :], lhsT=mixT[:, c, j * 128:(j + 1) * 128], rhs=Wo[:, c, half * 512:(half + 1) * 512],
                                                  start=(c == 0), stop=(c == NCH - 1)), reads=["mixT", "Wo"], writes=[pok])
                P.op("dve", lambda e: e.tensor_tensor(out=xt[tp][:, j, half * 512:(half + 1) * 512], in0=po[:, :],
                                                      in1=xt[tp][:, j, half * 512:(half + 1) * 512], op=ALU.add), reads=[pok, xk], writes=[xk])
        P.op("sp", lambda e: e.dma_start(out=xmid[(tt - 2) * TW:(tt - 1) * TW, :].rearrange("(j p) d -> p j d", p=128), in_=xt[tp][:]),
             reads=[xk], writes=["xmid_d"], chan="xmst%d" % tp)


def build_phase2b(nc, P, A, NTM, xmid, w_up, w_down, nwf_d, cfw_d, cfb_d, wfin_d, cst, out2):
    sb, ps = A.sb, A.ps
    cf, cb = load_consts(P, A, cst, "b")
    idb = cb[:, C_ID:C_ID + 128]
    epsc = make_epsc(P, A)
    nwf = sb([128, NCH], F32, "nwf")
    P.op("sp", lambda e: e.dma_start(out=nwf[:], in_=nwf_d[:, :]), writes=["nwf0"], chan="nwf")
    P.op("dve", lambda e: e.tensor_scalar(nwf[:], nwf[:], 32.0, None, ALU.mult), reads=["nwf0"], writes=["nwf"])
    cfw = sb([128, 44, 3], F32, "cfw")
    cfb = sb([128, 44], F32, "cfb")
    wfb = sb([128, D], F32, "wfb")
    P.op("sp", lambda e: e.dma_start(out=cfw[:], in_=cfw_d[:, :, :]), writes=["cfw"], chan="cfw")
    P.op("sp", lambda e: e.dma_start(out=cfb[:], in_=cfb_d[:, :]), writes=["cfb"], chan="cfb")
    P.op("sp", lambda e: e.dma_start(out=wfb[:], in_=wfin_d[:, :]), writes=["wfb0"], chan="wfb")
    P.op("pool", lambda e: e.tensor_scalar(wfb[:], wfb[:], 32.0, None, ALU.mult), reads=["wfb0"], writes=["wfb"])
    Wu = sb([128, NCH, 2 * D_FF], BF16, "Wu")
    Wd = sb([128, 22, D], BF16, "Wd")
    es_stg = ExitStack()
    stg = [es_stg.enter_context(nc.sbuf_tensor("bstg%d" % i, [128, 2048], F32)) for i in range(2)]
    cnt = [0]
    prep_weight(P, stg, "bstg", w_up, 8, 2 * D_FF, Wu, 0, lambda c: nwf[:, c:c + 1], "Wu", ["nwf"], cnt)
    prep_weight(P, stg, "bstg", w_down[0:1408, :], 11, D, Wd, 0, None, "Wd", [], cnt, dst_c0=0)
    prep_weight(P, stg, "bstg", w_down[1408:2816, :], 11, D, Wd, 0, None, "Wd", [], cnt, dst_c0=11)
    es_stg.close()
    P.barrier()

    xm = [sb([128, TB, D], F32, "xm") for _ in range(2)]
    junk = sb([128, D], BF16, "junk")
    ss = sb([128, 8], F32, "ss")
    rs = sb([128, 8], F32, "rs")
    xn = [sb([128, D], BF16, "xn") for _ in range(2)]
    h2T = sb([128, NCH, TW], BF16, "h2T")
    ubuf = [sb([128, 2, TW + 2], F32, "ubuf") for _ in range(2)]
    cv = [sb([128, 2, TW], F32, "cv") for _ in range(2)]
    sgt = [sb([128, TW], F32, "sgt") for _ in range(2)]
    uh = sb([128, 22, 2, 2], F32, "uh")
    actT = sb([128, 22, TW], BF16, "actT")
    outt = [sb([128, D], F32, "outt") for _ in range(2)]
    P.op("pool", lambda e: e.memset(uh[:], 0.0), writes=["uh"])

    ps_tr = ps([128, NCH, 128], BF16, "ps_tr")
    pbig = [ps([128, 512], F32, "pbig") for _ in range(6)]
    bi = [0]

    def big():
        i = bi[0] % 6
        bi[0] += 1
        return pbig[i], "bpbig%d" % i

    for u in range(NTM + 1):
        tp = u % 2
        xk = "bxm%d" % tp
        P.op("sp", lambda e: e.dma_start(out=xm[tp][:], in_=xmid[u * TW:(u + 1) * TW, :].rearrange("(j p) d -> p j d", p=128)),
             reads=["xmid_d"], writes=[xk], chan=xk)
        for j in range(TB):
            norm_block(P, epsc, xm[tp][:, j, :], xk, ss[:, j:j + 1], rs[:, j:j + 1], "bss%d" % j, junk[:], "bjunk",
                       xn[j % 2][:], "bxn%d" % (j % 2), ps_tr, "bps_tr", idb, h2T[:, :, j * 128:(j + 1) * 128], "h2T")
        for m in range(22):
            q2 = m % 2
            pg, pgk = big()
            for half in range(2):
                col0 = half * D_FF + m * 128
                for c in range(NCH):
                    P.op("pe", lambda e: e.matmul(pg[:, half * TW:(half + 1) * TW], lhsT=Wu[:, c, col0:col0 + 128], rhs=h2T[:, c, :],
                                                  start=(c == 0), stop=(c == NCH - 1)), reads=["Wu", "h2T"], writes=[pgk])
            uk = "ubuf%d" % q2
            P.op("pool", lambda e: e.tensor_copy(ubuf[q2][:, :, 0:2], uh[:, m, :, :]), reads=["uh"], writes=[uk + "h"])
            P.op("act", lambda e: e.copy(ubuf[q2][:, :, 2:TW + 2], pg[:, :].rearrange("p (s n) -> p s n", s=2)), reads=[pgk], writes=[uk])
            P.op("pool", lambda e: e.tensor_copy(uh[:, m, :, :], ubuf[q2][:, :, TW:TW + 2]), reads=[uk, uk + "h"], writes=["uh"])
            if u == 0:
                continue
            ck = "cv%d" % q2
            for s_ in range(2):
                ch = s_ * 22 + m
                eng = "dve"
                P.op("act", lambda e: e.activation(out=cv[q2][:, s_, :], in_=ubuf[q2][:, s_, 0:TW], func=AF.Identity,
                                                   scale=cfw[:, ch, 0:1], bias=cfb[:, ch:ch + 1]),
                     reads=[uk, uk + "h", "cfw", "cfb"], writes=[ck + str(s_)])
                for k in range(1, 3):
                    P.op(eng, lambda e: e.scalar_tensor_tensor(out=cv[q2][:, s_, :], in0=ubuf[q2][:, s_, k:k + TW], scalar=cfw[:, ch, k:k + 1],
                                                               in1=cv[q2][:, s_, :], op0=ALU.mult, op1=ALU.add),
                         reads=[uk, uk + "h", "cfw", ck + str(s_)], writes=[ck + str(s_)])
            P.op("act", lambda e: e.activation(out=sgt[q2][:], in_=cv[q2][:, 0, :], func=AF.Silu), reads=[ck + "0"], writes=["sgt%d" % q2])
            P.op("dve", lambda e: e.tensor_tensor(out=actT[:, m, :], in0=sgt[q2][:], in1=cv[q2][:, 1, :], op=ALU.mult),
                 reads=["sgt%d" % q2, ck + "1"], writes=["actT"])
        if u == 0:
            continue
        for j in range(TB):
            for half in range(2):
                po, pok = big()
                for m in range(22):
                    P.op("pe", lambda e: e.matmul(po[:, :], lhsT=actT[:, m, j * 128:(j + 1) * 128], rhs=Wd[:, m, half * 512:(half + 1) * 512],
                                                  start=(m == 0), stop=(m == 21)), reads=["actT", "Wd"], writes=[pok])
                P.op("dve", lambda e: e.tensor_tensor(out=xm[tp][:, j, half * 512:(half + 1) * 512], in0=po[:, :],
                                                      in1=xm[tp][:, j, half * 512:(half + 1) * 512], op=ALU.add), reads=[pok, xk], writes=[xk])
            o2 = j % 2
            P.op("act", lambda e: e.activation(out=junk[:], in_=xm[tp][:, j, :], func=AF.Square, accum_out=ss[:, 4 + j:5 + j]),
                 reads=[xk], writes=["bjunk", "fss%d" % j])
            P.op("act", lambda e: e.activation(out=rs[:, 4 + j:5 + j], in_=ss[:, 4 + j:5 + j], func=AF.Ln, bias=epsc[:, 0:1]),
                 reads=["fss%d" % j, "epsc"], writes=["frs%da" % j])
            P.op("act", lambda e: e.activation(out=rs[:, 4 + j:5 + j], in_=rs[:, 4 + j:5 + j], func=AF.Exp, scale=-0.5),
                 reads=["frs%da" % j], writes=["frs%d" % j])
            P.op("dve", lambda e: e.scalar_tensor_tensor(out=outt[o2][:], in0=xm[tp][:, j, :], scalar=rs[:, 4 + j:5 + j], in1=wfb[:],
                                                         op0=ALU.mult, op1=ALU.mult), reads=[xk, "frs%d" % j, "wfb"], writes=["outt%d" % o2])
            P.op("sp", lambda e: e.dma_start(out=out2[(u - 1) * TW + j * 128:(u - 1) * TW + (j + 1) * 128, :], in_=outt[o2][:]),
                 reads=["outt%d" % o2], chan="ost%d" % o2)


def _phase1_inputs(inp, T):
    x = np.asarray(inp["x"], np.float32)
    w_in = np.asarray(inp["w_in"], np.float32)[0]
    conv = np.asarray(inp["conv_qkv_w"], np.float32)[0]
    a_log = np.asarray(inp["a_log"], np.float32)[0]
    dtb = np.asarray(inp["dt_bias"], np.float32)[0]
    nw = np.asarray(inp["norm_mix_w"], np.float32)[0]
    cst = chunk_consts()
    maps = []
    for core in range(8):
        b, h = core // 4, core % 4
        cols = np.concatenate([np.arange(h * 128, (h + 1) * 128), 512 + np.arange(h * 128, (h + 1) * 128),
                               1024 + np.arange(h * 128, (h + 1) * 128), [2048 + h], [2052 + h]])
        w1 = np.ascontiguousarray(w_in[:, cols])
        cw = np.zeros((128, 12), np.float32)
        for g in range(3):
            cw[:, 4 * g:4 * g + 4] = conv[:, g * 512 + h * 128:g * 512 + (h + 1) * 128].T
        sc = np.zeros((128, 2), np.float32)
        sc[:, 0] = a_log[h]
        sc[:, 1] = dtb[h]
        maps.append({"x1": np.ascontiguousarray(x[b, :T]), "w1": w1, "cw1": cw, "sc1": sc,
                     "nw1": np.ascontiguousarray(nw.reshape(8, 128).T), "cst": cst})
    return maps


def build_p1_program(T):
    nc = bass.Bass("TRN2", target_bir_lowering=False)
    x1 = nc.dram_tensor("x1", [T, D], F32, kind="ExternalInput").ap()
    w1 = nc.dram_tensor("w1", [D, 386], F32, kind="ExternalInput").ap()
    cw1 = nc.dram_tensor("cw1", [128, 12], F32, kind="ExternalInput").ap()
    sc1 = nc.dram_tensor("sc1", [128, 2], F32, kind="ExternalInput").ap()
    nw1 = nc.dram_tensor("nw1", [128, 8], F32, kind="ExternalInput").ap()
    cst = nc.dram_tensor("cst", [128, NCONST], F32, kind="ExternalInput").ap()
    o_out = nc.dram_tensor("o1", [T, 128], F32, kind="ExternalOutput").ap()
    es = ExitStack()
    P = Prog(nc)
    A = Ctx(nc, es, P)
    build_phase1(nc, es, P, A, T, x1, w1, cw1, sc1, nw1, cst, o_out)
    P.finish()
    P.emit(es)
    es.close()
    return nc, P


def run_phase1(inp, T):
    nc, P = build_p1_program(T)
    maps = _phase1_inputs(inp, T)
    res = run_bass_kernel_spmd(nc, maps, core_ids=list(range(8)))
    o = np.zeros((2, T, 4, 128), np.float32)
    for core in range(8):
        o[core // 4, :, core % 4, :] = res.results[core]["o1"]
    return o


def _bias_tile(rel):
    ki = np.arange(128)[:, None]
    qi = np.arange(128)[None, :]
    out = np.zeros((128, 8, 640), np.float32)
    for kb in range(5):
        dist = qi - ki + (4 - kb) * 128
        idx = np.clip(dist, -128, 128) + 128
        cdiff = 2 * (4 - kb) + qi // 64 - ki // 64
        ok = (cdiff >= 0) & (cdiff <= 8)
        for h in range(8):
            out[:, h, kb * 128:(kb + 1) * 128] = np.where(ok, rel[h][idx], NEG)
    return out


def _phase2_inputs(inp, o1, T):
    TC = T // 4
    NTM = TC // TW
    x = np.asarray(inp["x"], np.float32)
    w_in = np.ascontiguousarray(np.asarray(inp["w_in"], np.float32)[0])
    cfw_ = np.asarray(inp["conv_ffn_w"], np.float32)[0]
    cfb_ = np.asarray(inp["conv_ffn_b"], np.float32)[0]
    shared = {
        "w_in": w_in,
        "w_ba": np.ascontiguousarray(np.asarray(inp["w_branch_a"], np.float32)[0]),
        "w_bb": np.ascontiguousarray(np.asarray(inp["w_branch_b"], np.float32)[0]),
        "w_out": np.ascontiguousarray(np.asarray(inp["w_out"], np.float32)[0]),
        "w_up": np.ascontiguousarray(np.asarray(inp["w_up"], np.float32)[0]),
        "w_down": np.ascontiguousarray(np.asarray(inp["w_down"], np.float32)[0]),
        "nwm": np.ascontiguousarray(np.asarray(inp["norm_mix_w"], np.float32)[0].reshape(8, 128).T),
        "nwf": np.ascontiguousarray(np.asarray(inp["norm_ffn_w"], np.float32)[0].reshape(8, 128).T),
        "gnw": np.ascontiguousarray(np.asarray(inp["gdn_norm_w"], np.float32)[0].reshape(128, 1)),
        "biasT": _bias_tile(np.asarray(inp["rel_bias"], np.float32)[0]),
        "cfw": np.ascontiguousarray(cfw_.reshape(3, 44, 128).transpose(2, 1, 0)),
        "cfb": np.ascontiguousarray(cfb_.reshape(44, 128).T),
        "wfin": np.ascontiguousarray(np.broadcast_to(np.asarray(inp["norm_final_w"], np.float32)[None, :], (128, D))),
        "cst2": chunk_consts(),
    }
    maps = []
    for core in range(8):
        b, q = core // 4, core % 4
        t0 = q * TC
        lo = t0 - 3 * TW
        x2 = np.zeros(((NTM + 3) * TW, D), np.float32)
        s0 = max(lo, 0)
        x2[s0 - lo:] = x[b, s0:t0 + TC]
        pos = lo + np.arange((NTM + 3) * TW)
        valid = (pos >= 0).astype(np.float32).reshape((NTM + 3) * TB, 128).T
        m = dict(shared)
        m["x2"] = x2
        m["validc"] = np.ascontiguousarray(valid)
        if o1 is not None:
            lo2 = t0 - TW
            oa2 = np.zeros(((NTM + 1) * TW, 512), np.float32)
            s1 = max(lo2, 0)
            oa2[s1 - lo2:] = o1[b, s1:t0 + TC].reshape(-1, 512)
            m["oa2"] = oa2
        maps.append(m)
    return maps


def _declare_p2(nc, NTM, with_oa):
    d = {}
    def inp(name, shape):
        d[name] = nc.dram_tensor(name, list(shape), F32, kind="ExternalInput").ap()
    inp("x2", [(NTM + 3) * TW, D])
    if with_oa:
        inp("oa2", [(NTM + 1) * TW, 512])
    inp("validc", [128, (NTM + 3) * TB])
    inp("w_in", [D, D_IN]); inp("w_ba", [512, D]); inp("w_bb", [512, D]); inp("w_out", [D, D])
    inp("w_up", [D, 2 * D_FF]); inp("w_down", [D_FF, D]); inp("nwm", [128, 8]); inp("nwf", [128, 8]); inp("gnw", [128, 1])
    inp("biasT", [128, 8, 640]); inp("cfw", [128, 44, 3]); inp("cfb", [128, 44]); inp("wfin", [128, D]); inp("cst2", [128, NCONST])
    d["xmid"] = nc.dram_tensor("xmid", [(NTM + 1) * TW, D], F32, kind="Internal").ap()
    d["out2"] = nc.dram_tensor("out2", [NTM * TW, D], F32, kind="ExternalOutput").ap()
    return d


def _emit_phase2(nc, P, d, NTM, oa_ap, o_all=None, qsel_d=None, RT=None):
    es_a = ExitStack()
    build_phase2a(nc, P, Ctx(nc, es_a, P), NTM, d["x2"], oa_ap, d["validc"], d["w_in"], d["w_ba"], d["w_bb"], d["w_out"],
                  d["nwm"], d["gnw"], d["biasT"], d["cst2"], d["xmid"], o_all=o_all, qsel_d=qsel_d, RT=RT)
    es_a.close()
    P.pool_hold = False
    P.barrier()
    es_b = ExitStack()
    build_phase2b(nc, P, Ctx(nc, es_b, P), NTM, d["xmid"], d["w_up"], d["w_down"], d["nwf"], d["cfw"], d["cfb"], d["wfin"],
                  d["cst2"], d["out2"])
    es_b.close()


def build_p2_program(T):
    NTM = (T // 4) // TW
    nc = bass.Bass("TRN2", target_bir_lowering=False)
    d = _declare_p2(nc, NTM, True)
    P = Prog(nc)
    _emit_phase2(nc, P, d, NTM, d["oa2"])
    P.finish()
    es = ExitStack()
    P.emit(es)
    es.close()
    return nc, P


def run_phase2(inp, o1, T):
    nc, P = build_p2_program(T)
    maps = _phase2_inputs(inp, o1, T)
    res = run_bass_kernel_spmd(nc, maps, core_ids=list(range(8)))
    TC = T // 4
    out = np.zeros((2, T, D), np.float32)
    for core in range(8):
        out[core // 4, (core % 4) * TC:(core % 4 + 1) * TC] = res.results[core]["out2"]
    return out


T_FULL = 16384


def build_fused_program(T):
    NTM = (T // 4) // TW
    RT = T + TW
    nc = bass.Bass("TRN2", target_bir_lowering=False)
    x1 = nc.dram_tensor("x1", [T, D], F32, kind="ExternalInput").ap()
    w1 = nc.dram_tensor("w1", [D, 386], F32, kind="ExternalInput").ap()
    cw1 = nc.dram_tensor("cw1", [128, 12], F32, kind="ExternalInput").ap()
    sc1 = nc.dram_tensor("sc1", [128, 2], F32, kind="ExternalInput").ap()
    nw1 = nc.dram_tensor("nw1", [128, 8], F32, kind="ExternalInput").ap()
    cst = nc.dram_tensor("cst", [128, NCONST], F32, kind="ExternalInput").ap()
    qsel_d = nc.dram_tensor("qsel", [128, 8], F32, kind="ExternalInput").ap()
    o_loc = nc.dram_tensor("o_loc", [RT, 128], F32, kind="Internal").ap()
    CR = RT
    chunks = [(r0, min(CR, RT - r0)) for r0 in range(0, RT, CR)]
    o_alls = [nc.dram_tensor("o_all%d" % i, [8 * n, 128], F32, kind="Internal").ap() for i, (r0, n) in enumerate(chunks)]
    d = _declare_p2(nc, NTM, False)
    P = Prog(nc)
    es1 = ExitStack()
    A1 = Ctx(nc, es1, P)
    zt = A1.sb([128, 128], F32, "zt")
    P.op("pool", lambda e: e.memset(zt[:], 0.0), writes=["zt"])
    for i in range(TW // 128):
        P.op("sp", lambda e: e.dma_start(out=o_loc[i * 128:(i + 1) * 128, :], in_=zt[:]), reads=["zt"], chan="zt")
    build_phase1(nc, es1, P, A1, T, x1, w1, cw1, sc1, nw1, cst, o_loc[TW:RT, :])
    es1.close()
    P.barrier()
    for i, (r0, n) in enumerate(chunks):
        P.op("pool", lambda e: e.collective_compute("AllGather", ALU.bypass, replica_groups=[list(range(8))],
                                                    ins=[o_loc[r0:r0 + n, :]], outs=[o_alls[i][:, :]]),
             writes=["oall"], chan="cc", inc_override=1)
    P.pool_hold = True
    _emit_phase2(nc, P, d, NTM, None, o_all=(o_alls, chunks, CR), qsel_d=qsel_d, RT=RT)
    P.finish()
    es = ExitStack()
    P.emit(es)
    es.close()
    return nc, P


def build_fused_nocc(T):
    NTM = (T // 4) // TW
    RT = T + TW
    nc = bass.Bass("TRN2", target_bir_lowering=False)
    x1 = nc.dram_tensor("x1", [T, D], F32, kind="ExternalInput").ap()
    w1a = nc.dram_tensor("w1a", [4, D, 386], F32, kind="ExternalInput").ap()
    cw1a = nc.dram_tensor("cw1a", [4, 128, 12], F32, kind="ExternalInput").ap()
    sc1a = nc.dram_tensor("sc1a", [4, 128, 2], F32, kind="ExternalInput").ap()
    nw1 = nc.dram_tensor("nw1", [128, 8], F32, kind="ExternalInput").ap()
    cst = nc.dram_tensor("cst", [128, NCONST], F32, kind="ExternalInput").ap()
    qsel_d = nc.dram_tensor("qsel", [128, 4], F32, kind="ExternalInput").ap()
    o_loc = nc.dram_tensor("o_loc", [RT, 512], F32, kind="Internal").ap()
    d = _declare_p2(nc, NTM, False)
    P = Prog(nc)
    for h in range(4):
        es1 = ExitStack()
        A1 = Ctx(nc, es1, P)
        if h == 0:
            zt = A1.sb([128, 512], F32, "zt")
            P.op("pool", lambda e: e.memset(zt[:], 0.0), writes=["zt"])
            for i in range(TW // 128):
                P.op("sp", lambda e: e.dma_start(out=o_loc[i * 128:(i + 1) * 128, :], in_=zt[:]), reads=["zt"], chan="zt")
        build_phase1(nc, es1, P, A1, T, x1, w1a[h], cw1a[h], sc1a[h], nw1, cst, o_loc[TW:RT, h * 128:(h + 1) * 128])
        es1.close()
        P.barrier()
    _emit_phase2(nc, P, d, NTM, None, o_all=o_loc, qsel_d=qsel_d, RT=None)
    P.finish()
    es = ExitStack()
    P.emit(es)
    es.close()
    return nc, P


def run_fused_nocc(inp, T):
    nc, P = build_fused_nocc(T)
    m1 = _phase1_inputs(inp, T)
    m2 = _phase2_inputs(inp, None, T)
    maps = []
    for core in range(8):
        b, q = core // 4, core % 4
        m = dict(m2[core])
        m["x1"] = m1[core]["x1"]
        m["nw1"] = m1[core]["nw1"]
        m["cst"] = m1[core]["cst"]
        m["w1a"] = np.stack([m1[4 * b + h]["w1"] for h in range(4)])
        m["cw1a"] = np.stack([m1[4 * b + h]["cw1"] for h in range(4)])
        m["sc1a"] = np.stack([m1[4 * b + h]["sc1"] for h in range(4)])
        qs = np.zeros((128, 4), np.float32)
        qs[:, q] = 1.0
        m["qsel"] = qs
        maps.append(m)
    res = run_bass_kernel_spmd(nc, maps, core_ids=list(range(8)))
    TC = T // 4
    out = np.zeros((2, T, D), np.float32)
    for core in range(8):
        out[core // 4, (core % 4) * TC:(core % 4 + 1) * TC] = res.results[core]["out2"]
    return out


def run_fused(inp, T):
    nc, P = build_fused_program(T)
    m1 = _phase1_inputs(inp, T)
    m2 = _phase2_inputs(inp, None, T)
    maps = []
    for core in range(8):
        m = dict(m1[core])
        m.update(m2[core])
        qs = np.zeros((128, 8), np.float32)
        qs[:, core] = 1.0
        m["qsel"] = qs
        maps.append(m)
    res = run_bass_kernel_spmd(nc, maps, core_ids=list(range(8)))
    TC = T // 4
    out = np.zeros((2, T, D), np.float32)
    for core in range(8):
        out[core // 4, (core % 4) * TC:(core % 4 + 1) * TC] = res.results[core]["out2"]
    return out


def kernel(**inputs):
    return run_fused_nocc(inputs, T_FULL)
```

```python
from collections import defaultdict
from contextlib import ExitStack

import numpy as np
import concourse.bass as bass
import concourse.mybir as mybir
from concourse.bass_utils import run_bass_kernel_spmd

F32 = mybir.dt.float32
BF16 = mybir.dt.bfloat16
AF = mybir.ActivationFunctionType
ALU = mybir.AluOpType

D = 1024
NCH = 8
EPS = 1e-6
CH = 64
DK = 128
D_IN = 5640
D_FF = 2816
NEG = -30000.0


PSUM_PREFIXES = ("psl", "ptb", "ppj", "ps_tr", "aps_tr", "bps_tr", "pbig", "bpbig", "ps_o")


class _Rec:
    def __getattr__(self, name):
        def f(*a, **k):
            self.call = (name, a, k)
            return self
        return f


class Prog:
    ENGS = ("pe", "act", "dve", "pool", "sp")

    def __init__(self, nc):
        self.nc = nc
        self.streams = {e: [] for e in self.ENGS}
        self.count = defaultdict(int)
        self.lastw = {}
        self.readers = defaultdict(list)
        self.waited = defaultdict(int)
        self.nops = 0
        self.epoch = 0
        self.pool_hold = False
        import os
        self.cut = int(os.environ["PCUT"]) if "PCUT" in os.environ else None

    def _dep(self, eng, rec):
        semkey, val = rec[0], rec[1]
        if eng == "pool" and (semkey.startswith("dma_cc@") or self.pool_hold):
            return
        if self.waited[(eng, semkey)] < val:
            self.waited[(eng, semkey)] = val
            self.streams[eng].append(("wait", semkey, val))

    def op(self, eng, fn, reads=(), writes=(), chan=None, inc_override=None):
        if self.cut is not None and self.nops >= self.cut:
            return
        isdma = chan is not None
        for k in reads:
            w = self.lastw.get(k)
            if w is not None:
                self._dep(eng, w)
            if k.startswith(PSUM_PREFIXES):
                for r in self.readers[k]:
                    if r[2] != eng:
                        self._dep(eng, r)
        for k in writes:
            w = self.lastw.get(k)
            if w is not None:
                if not (w[2] == eng == "pe" and not w[3] and not isdma):
                    self._dep(eng, w)
            for r in self.readers[k]:
                if r[2] != eng or r[3] or isdma:
                    self._dep(eng, r)
        if isdma:
            semkey, inc = "dma_%s@%d" % (chan, self.epoch), (inc_override or 16)
        else:
            semkey, inc = "%s@%d" % (eng, self.epoch), 1
        self.count[semkey] += inc
        rec = (semkey, self.count[semkey], eng, isdma)
        rec_ = _Rec()
        fn(rec_)
        self.streams[eng].append(("op", rec_.call, semkey, inc))
        for k in writes:
            self.lastw[k] = rec
            self.readers[k] = []
        for k in reads:
            self.readers[k].append(rec)
        self.nops += 1

    def barrier(self):
        for e in self.ENGS:
            for semkey, val in list(self.count.items()):
                if val:
                    self._dep(e, (semkey, val))
        self.lastw.clear()
        self.readers.clear()
        self.epoch += 1

    def finish(self):
        for semkey, val in list(self.count.items()):
            if semkey.startswith("dma_"):
                self._dep("sp", (semkey, val))
        for semkey, val in list(self.count.items()):
            if not semkey.startswith("dma_") and val:
                self._dep("sp", (semkey, val))

    def emit(self, es):
        nc = self.nc
        sems = {}
        for i, k in enumerate(sorted(self.count)):
            sems[k] = es.enter_context(nc.semaphore("s%d" % i))
        block = es.enter_context(nc.Block())
        streams = self.streams

        def run(eng_handle, items):
            for it in items:
                if it[0] == "wait":
                    eng_handle.wait_ge(sems[it[1]], it[2])
                else:
                    name, a, k = it[1]
                    getattr(eng_handle, name)(*a, **k).then_inc(sems[it[2]], it[3])

        @block.tensor
        def _(e):
            run(e, streams["pe"])

        @block.scalar
        def _(e):
            run(e, streams["act"])

        @block.vector
        def _(e):
            run(e, streams["dve"])

        @block.gpsimd
        def _(e):
            run(e, streams["pool"])

        @block.sync
        def _(e):
            run(e, streams["sp"])


class Ctx:
    _uid = [0]

    def __init__(self, nc, es, P):
        self.nc, self.es, self.P = nc, es, P
        Ctx._uid[0] += 1
        self.n = Ctx._uid[0] * 1000

    def sb(self, shape, dt=F32, name=None):
        self.n += 1
        return self.es.enter_context(self.nc.sbuf_tensor("%s_%d" % (name or "t", self.n), list(shape), dt))

    def ps(self, shape, dt=F32, name=None):
        self.n += 1
        return self.es.enter_context(self.nc.psum_tensor("%s_%d" % (name or "p", self.n), list(shape), dt))


def chunk_consts():
    j = np.arange(128)
    same = (j[:, None] // CH) == (j[None, :] // CH)
    m1 = (same & (j[:, None] <= j[None, :])).astype(np.float32)
    m2 = (same & (j[:, None] > j[None, :])).astype(np.float32)
    ident = np.eye(128, dtype=np.float32)
    ones = np.ones((128, 128), np.float32)
    cind = np.zeros((128, 128), np.float32)
    cind[:64, 0] = 1.0
    cind[64:, 1] = 1.0
    return np.concatenate([m1, m2, ident, ones, cind], axis=1)


C_M1, C_M2, C_ID, C_ONES, C_CIND = 0, 128, 256, 384, 512
NCONST = 640


def make_epsc(P, A, eng="pool"):
    epsc = A.sb([128, 2], F32, "epsc")
    P.op(eng, lambda e: e.memset(epsc[:, 0:1], D * EPS), writes=["epsc0"])
    P.op(eng, lambda e: e.memset(epsc[:, 1:2], EPS), reads=["epsc0"], writes=["epsc"])
    return epsc


def norm_block(P, epsc, x_blk, xkey, ss, rs, sskey, junk, junkkey, xn, xnkey, ps_tr, pskey, idb, hT_dst, hTkey,
               wrow=None, wkey=None):
    P.op("act", lambda e: e.activation(out=junk, in_=x_blk, func=AF.Square, accum_out=ss),
         reads=[xkey], writes=[junkkey, sskey])
    P.op("act", lambda e: e.activation(out=rs, in_=ss, func=AF.Ln, bias=epsc[:, 0:1]),
         reads=[sskey, "epsc"], writes=[sskey + "r0"])
    P.op("act", lambda e: e.activation(out=rs, in_=rs, func=AF.Exp, scale=-0.5),
         reads=[sskey + "r0"], writes=[sskey + "r"])
    if wrow is None:
        P.op("dve", lambda e: e.tensor_scalar(xn, x_blk, rs, None, ALU.mult),
             reads=[xkey, sskey + "r"], writes=[xnkey])
    else:
        P.op("dve", lambda e: e.scalar_tensor_tensor(out=xn, in0=x_blk, scalar=rs, in1=wrow,
                                                      op0=ALU.mult, op1=ALU.mult),
             reads=[xkey, sskey + "r", wkey], writes=[xnkey])
    for c in range(NCH):
        P.op("pe", lambda e, c=c: e.transpose(ps_tr[:, c, :], xn[:, c * 128:(c + 1) * 128], idb),
             reads=[xnkey, "consts_b"], writes=[pskey])
    P.op("act", lambda e: e.copy(hT_dst, ps_tr[:, :, :]), reads=[pskey], writes=[hTkey])


def build_phase1(nc, es, P, A, T, x1, w1, cw1, sc1, nw1, cst, o_out):
    NT = T // 512
    sb, ps = A.sb, A.ps
    cf = sb([128, NCONST], F32, "cf")
    cb = sb([128, NCONST], BF16, "cb")
    P.op("sp", lambda e: e.dma_start(out=cf[:], in_=cst[:, :]), writes=["consts_f"], chan="cf")
    P.op("dve", lambda e: e.tensor_copy(cb[:], cf[:]), reads=["consts_f"], writes=["consts_b"])
    m1f, m2f = cf[:, C_M1:C_M1 + 128], cf[:, C_M2:C_M2 + 128]
    idf, onesf, cindf = cf[:, C_ID:C_ID + 128], cf[:, C_ONES:C_ONES + 128], cf[:, C_CIND:C_CIND + 2]
    idb, onesb = cb[:, C_ID:C_ID + 128], cb[:, C_ONES:C_ONES + 128]

    epsc = make_epsc(P, A)
    wf = sb([128, NCH, 386], F32, "wf")
    wb = sb([128, NCH, 386], BF16, "wb")
    nw = sb([128, NCH], F32, "nw")
    cw = sb([128, 12], F32, "cw")
    sc = sb([128, 2], F32, "sc")
    negA = sb([128, 1], F32, "negA")
    P.op("sp", lambda e: e.dma_start(out=wf[:], in_=w1.rearrange("(c p) n -> p c n", p=128)), writes=["wf"], chan="wf")
    P.op("sp", lambda e: e.dma_start(out=nw[:], in_=nw1[:, :]), writes=["nw"], chan="nw")
    P.op("sp", lambda e: e.dma_start(out=cw[:], in_=cw1[:, :]), writes=["cw"], chan="cw")
    P.op("sp", lambda e: e.dma_start(out=sc[:], in_=sc1[:, :]), writes=["sc"], chan="sc")
    for c in range(NCH):
        P.op("dve", lambda e, c=c: e.tensor_scalar(wb[:, c, :], wf[:, c, :], nw[:, c:c + 1], 32.0, ALU.mult, ALU.mult),
             reads=["wf", "nw"], writes=["wb"])
    P.op("act", lambda e: e.activation(out=negA[:], in_=sc[:, 0:1], func=AF.Exp), reads=["sc"], writes=["negA0"])
    P.op("dve", lambda e: e.tensor_scalar(negA[:], negA[:], -1.0, None, ALU.mult), reads=["negA0"], writes=["negA"])

    xt = [sb([128, 4, D], F32, "xt") for _ in range(2)]
    junk = sb([128, D], BF16, "junk")
    ss = sb([128, 8], F32, "ss")
    rs = sb([128, 8], F32, "rs")
    xn = [sb([128, D], BF16, "xn") for _ in range(2)]
    hT = [sb([128, NCH, 512], BF16, "hT") for _ in range(2)]
    cbuf = [sb([128, 3 + 512], F32, "cbuf") for _ in range(3)]
    acc = [sb([128, 512], F32, "acc") for _ in range(3)]
    sil = [sb([128, 512], F32, "sil") for _ in range(2)]
    sq = [sb([128, 512], BF16, "sq") for _ in range(2)]
    rn = [sb([128, 512], F32, "rn") for _ in range(2)]
    QT = [sb([128, 512], BF16, "QT") for _ in range(2)]
    KT = [sb([128, 512], BF16, "KT") for _ in range(2)]
    VT = [sb([128, 512], BF16, "VT") for _ in range(2)]
    bdt = [sb([128, 4, 2], F32, "bdt") for _ in range(2)]
    gsc = [sb([128, 8, 4], F32, "gsc") for _ in range(2)]
    def four(shape, dt, name):
        return [sb(shape, dt, name) for _ in range(4)]

    def eight(shape, dt, name):
        return [[sb(shape, dt, name) for _ in range(4)] for _ in range(2)]

    gM = four([128, 128], F32, "gM")
    rgc = four([128, 2], F32, "rgc")
    D1 = four([128, 128], F32, "D1")
    D2 = four([128, 128], F32, "D2")
    bg = four([128, 1], F32, "bg")
    bgK = four([128, 128], BF16, "bgK")
    Bm = four([128, 128], F32, "Bm")
    Bq = four([128, 128], F32, "Bq")
    Nq = four([128, 128], F32, "Nq")
    Rq = four([128, 128], F32, "Rq")
    Rt = four([128, 128], F32, "Rt")
    PTm = four([128, 128], F32, "PTm")
    smx = eight([128, 4], F32, "smx")
    KD = eight([128, 128], BF16, "KD")
    bV = eight([128, 128], BF16, "bV")
    TTb = eight([128, 128], BF16, "TTb")
    PT = eight([128, 128], BF16, "PT")
    nWT = eight([128, 128], BF16, "nWT")
    Ub = [sb([128, 128], BF16, "Ub") for _ in range(2)]
    pus = [sb([128, 128], F32, "pus") for _ in range(2)]
    Osb = [sb([128, 128], F32, "Osb") for _ in range(2)]
    Sf = [sb([128, 128], F32, "Sf") for _ in range(2)]
    Sb = [sb([128, 128], BF16, "Sb") for _ in range(2)]

    ps_tr = ps([128, NCH, 128], BF16, "ps_tr")
    ps_tb = ps([128, 8, 128], BF16, "ps_tb")
    ps_pj = [ps([128, 512], F32, "ps_pj") for _ in range(1)]
    ps_ch = ps([128, 4, 128], F32, "ps_ch")
    ps_sl = [ps([128, 4, 128], F32, "ps_sl") for _ in range(4)]
    pj_i = [0]

    def pjslot():
        i = pj_i[0] % len(ps_pj)
        pj_i[0] += 1
        return ps_pj[i], "ppj%d" % i


    P.op("pool", lambda e: e.memset(Sf[0][:], 0.0), writes=["Sf0"])
    P.op("pool", lambda e: e.memset(Sb[0][:], 0.0), writes=["Sb0"])
    for g in range(3):
        P.op("pool", lambda e, g=g: e.memset(cbuf[g][:, 0:3], 0.0), writes=["cbufh%d" % g])
    sidx = [0]
    chain_q = []

    def tile_level(ti):
        tp = ti % 2
        xk = "xt%d" % tp
        hk = "hT%d" % tp
        bk = "bdt%d" % tp
        qk, kk, vk = "QT%d" % tp, "KT%d" % tp, "VT%d" % tp
        G = gsc[tp]
        gk = "gsc%d" % tp
        xg, ax, ee, ll, sp_, gg, be, nbe = (G[:, i, :] for i in range(8))
        pieces = []

        def p_load():
            P.op("sp", lambda e, ti=ti, tp=tp: e.dma_start(
                out=xt[tp][:], in_=x1[ti * 512:(ti + 1) * 512, :].rearrange("(j p) d -> p j d", p=128)),
                writes=[xk], chan=xk)
        pieces.append(p_load)
        def p_norm(j):
            bp = j % 2
            norm_block(P, epsc, xt[tp][:, j, :], xk, ss[:, j + 4 * tp:j + 4 * tp + 1], rs[:, j + 4 * tp:j + 4 * tp + 1],
                       "ss%d_%d" % (tp, j), junk[:], "junk", xn[bp][:], "xn%d" % bp, ps_tr, "ps_tr", idb,
                       hT[tp][:, :, j * 128:(j + 1) * 128], hk)
        for j in range(4):
            pieces.append(lambda j=j: p_norm(j))
        def p_proj(g):
            pj, pjk = pjslot()
            for c in range(NCH):
                P.op("pe", lambda e, g=g, c=c, pj=pj: e.matmul(pj[:], lhsT=wb[:, c, g * 128:(g + 1) * 128],
                                                              rhs=hT[tp][:, c, :], start=(c == 0), stop=(c == NCH - 1)),
                     reads=["wb", hk], writes=[pjk])
            P.op("act", lambda e, g=g, pj=pj: e.copy(cbuf[g][:, 3:515], pj[:]), reads=[pjk], writes=["cbufm%d" % g])
        for g in range(3):
            pieces.append(lambda g=g: p_proj(g))
        def p_bd():
            pj, pjk = pjslot()
            for j in range(4):
                for c in range(NCH):
                    P.op("pe", lambda e, j=j, c=c, pj=pj: e.matmul(pj[:, 2 * j:2 * j + 2], lhsT=hT[tp][:, c, j * 128:(j + 1) * 128],
                                                                  rhs=wb[:, c, 384:386], start=(c == 0), stop=(c == NCH - 1)),
                         reads=["wb", hk], writes=[pjk])
            P.op("dve", lambda e, pj=pj: e.tensor_copy(bdt[tp][:].rearrange("p a b -> p (a b)"), pj[:, 0:8]), reads=[pjk], writes=[bk])
        pieces.append(p_bd)
        def p_conv(g):
            ck = ["cbufh%d" % g, "cbufm%d" % g]
            ak = "acc%d" % g
            P.op("dve", lambda e, g=g: e.tensor_scalar(acc[g][:], cbuf[g][:, 0:512], cw[:, 4 * g:4 * g + 1], None, ALU.mult),
                 reads=ck + ["cw"], writes=[ak])
            for k in range(1, 4):
                P.op("dve", lambda e, g=g, k=k: e.scalar_tensor_tensor(
                    out=acc[g][:], in0=cbuf[g][:, k:k + 512], scalar=cw[:, 4 * g + k:4 * g + k + 1], in1=acc[g][:],
                    op0=ALU.mult, op1=ALU.add), reads=ck + ["cw", ak], writes=[ak])
            P.op("pool", lambda e, g=g: e.tensor_copy(cbuf[g][:, 0:3], cbuf[g][:, 512:515]),
                 reads=["cbufm%d" % g, ak], writes=["cbufh%d" % g])
        for g in range(3):
            pieces.append(lambda g=g: p_conv(g))
        def p_qkv():
            P.op("act", lambda e: e.activation(out=VT[tp][:], in_=acc[2][:], func=AF.Silu), reads=["acc2"], writes=[vk])
            for g in range(2):
                P.op("act", lambda e, g=g: e.activation(out=sil[g][:], in_=acc[g][:], func=AF.Silu), reads=["acc%d" % g], writes=["sil%d" % g])
                P.op("act", lambda e, g=g: e.activation(out=sq[g][:], in_=sil[g][:], func=AF.Square), reads=["sil%d" % g], writes=["sq%d" % g])
                pj, pjk = pjslot()
                P.op("pe", lambda e, g=g, pj=pj: e.matmul(pj[:], lhsT=onesb, rhs=sq[g][:], start=True, stop=True),
                     reads=["consts_b", "sq%d" % g], writes=[pjk])
                P.op("act", lambda e, g=g, pj=pj: e.activation(out=rn[g][:], in_=pj[:], func=AF.Ln, bias=epsc[:, 1:2]),
                     reads=[pjk, "epsc"], writes=["rn%da" % g])
                P.op("act", lambda e, g=g: e.activation(out=rn[g][:], in_=rn[g][:], func=AF.Exp, scale=-0.5),
                     reads=["rn%da" % g], writes=["rn%d" % g])
            P.op("dve", lambda e: e.scalar_tensor_tensor(out=QT[tp][:], in0=sil[0][:], scalar=float(DK) ** -0.5, in1=rn[0][:],
                                                          op0=ALU.mult, op1=ALU.mult), reads=["sil0", "rn0"], writes=[qk])
            P.op("dve", lambda e: e.tensor_tensor(out=KT[tp][:], in0=sil[1][:], in1=rn[1][:], op=ALU.mult), reads=["sil1", "rn1"], writes=[kk])
        pieces.append(p_qkv)
        def p_gate():
            P.op("dve", lambda e: e.tensor_scalar(xg, bdt[tp][:, :, 1], sc[:, 1:2], None, ALU.add), reads=[bk, "sc"], writes=[gk + "a"])
            P.op("dve", lambda e: e.scalar_tensor_tensor(out=ax, in0=xg, scalar=-1.0, in1=xg, op0=ALU.mult, op1=ALU.max), reads=[gk + "a"], writes=[gk + "b"])
            P.op("act", lambda e: e.activation(out=ee, in_=ax, func=AF.Exp, scale=-1.0), reads=[gk + "b"], writes=[gk + "c"])
            P.op("act", lambda e: e.activation(out=ll, in_=ee, func=AF.Ln, bias=1.0), reads=[gk + "c"], writes=[gk + "d"])
            P.op("dve", lambda e: e.scalar_tensor_tensor(out=sp_, in0=xg, scalar=0.0, in1=ll, op0=ALU.max, op1=ALU.add),
                 reads=[gk + "a", gk + "d"], writes=[gk + "e"])
            P.op("dve", lambda e: e.tensor_scalar(gg, sp_, negA[:, 0:1], None, ALU.mult), reads=[gk + "e", "negA"], writes=[gk + "g"])
            P.op("act", lambda e: e.activation(out=be, in_=bdt[tp][:, :, 0], func=AF.Sigmoid), reads=[bk], writes=[gk + "be"])
            P.op("dve", lambda e: e.tensor_scalar(nbe, be, -1.0, None, ALU.mult), reads=[gk + "be"], writes=[gk + "nb"])


        pieces.append(p_gate)
        return pieces

    def block_level(ti):
        tp = ti % 2
        qk, kk, vk = "QT%d" % tp, "KT%d" % tp, "VT%d" % tp
        G = gsc[tp]
        gk = "gsc%d" % tp
        xg, ax, ee, ll, sp_, gg, be, nbe = (G[:, i, :] for i in range(8))
        def bk(j):
            return ps_sl[j], "psl%d" % j

        def hop():
            if chain_q:
                chain_q.pop(0)()

        def stage_done():
            hop()
            if pre_q:
                pre_q.pop(0)()

        J = range(4)
        sfx = ["_%d" % j for j in J]
        csl = [slice(j * 128, (j + 1) * 128) for j in J]
        g_ = [gg[:, j:j + 1] for j in J]
        be_ = [be[:, j:j + 1] for j in J]
        nbe_ = [nbe[:, j:j + 1] for j in J]
        ck = ["_%d_%d" % (tp, j) for j in J]
        for j in J:
            P.op("dve", lambda e: e.tensor_scalar(gM[j][:], m1f, g_[j], None, ALU.mult), reads=["consts_f", gk + "g"], writes=["gM" + sfx[j]])
            P.op("dve", lambda e: e.tensor_scalar(rgc[j][:], cindf, g_[j], None, ALU.mult), reads=["consts_f", gk + "g"], writes=["rgc" + sfx[j]])
        hop()
        for j in J:
            b_, bkk = bk(j)
            P.op("pe", lambda e: e.matmul(b_[:, 0, :], lhsT=gM[j][:], rhs=m2f, start=True, stop=True), reads=["gM" + sfx[j], "consts_f"], writes=[bkk])
            P.op("pe", lambda e: e.matmul(b_[:, 1, :], lhsT=m2f, rhs=gM[j][:], start=True, stop=True), reads=["gM" + sfx[j], "consts_f"], writes=[bkk])
            P.op("pe", lambda e: e.matmul(b_[:, 2, 0:1], lhsT=m1f, rhs=g_[j], start=True, stop=True), reads=[gk + "g", "consts_f"], writes=[bkk])
            P.op("pe", lambda e: e.matmul(b_[:, 2, 1:2], lhsT=m2f, rhs=g_[j], start=True, stop=True), reads=[gk + "g", "consts_f"], writes=[bkk])
            P.op("pe", lambda e: e.matmul(b_[:, 2, 2:4], lhsT=onesf, rhs=rgc[j][:], start=True, stop=True), reads=["rgc" + sfx[j], "consts_f"], writes=[bkk])
        hop()
        for j in J:
            b_, bkk = bk(j)
            P.op("act", lambda e: e.activation(out=D1[j][:], in_=b_[:, 0, :], func=AF.Exp), reads=[bkk], writes=["D1" + sfx[j]])
            P.op("act", lambda e: e.activation(out=D2[j][:], in_=b_[:, 1, :], func=AF.Exp), reads=[bkk], writes=["D2" + sfx[j]])
            P.op("act", lambda e: e.activation(out=smx[tp][j][:], in_=b_[:, 2, 0:4], func=AF.Exp), reads=[bkk], writes=["smx" + ck[j]])
        hop()
        for j in J:
            P.op("dve", lambda e: e.tensor_tensor(out=bg[j][:], in0=be_[j], in1=smx[tp][j][:, 0:1], op=ALU.mult),
                 reads=[gk + "be", "smx" + ck[j]], writes=["bg" + sfx[j]])
            P.op("pool", lambda e: e.tensor_tensor(out=PTm[j][:], in0=D2[j][:], in1=m1f, op=ALU.mult), reads=["D2" + sfx[j], "consts_f"], writes=["PTm" + sfx[j]])
        hop()
        stage_done()
        for j in J:
            P.op("pe", lambda e: e.transpose(ps_tb[:, 2 * j, :], KT[tp][:, csl[j]], idb), reads=[kk, "consts_b"], writes=["ptb"])
            P.op("pe", lambda e: e.transpose(ps_tb[:, 2 * j + 1, :], VT[tp][:, csl[j]], idb), reads=[vk, "consts_b"], writes=["ptb"])
        hop()
        for j in J:
            P.op("dve", lambda e: e.tensor_scalar(bgK[j][:], ps_tb[:, 2 * j, :], bg[j][:, 0:1], None, ALU.mult), reads=["ptb", "bg" + sfx[j]], writes=["bgK" + sfx[j]])
        hop()
        for j in J:
            P.op("act", lambda e: e.activation(out=KD[tp][j][:], in_=ps_tb[:, 2 * j, :], func=AF.Copy, scale=smx[tp][j][:, 1:2]),
                 reads=["ptb", "smx" + ck[j]], writes=["KD" + ck[j]])
            P.op("act", lambda e: e.activation(out=bV[tp][j][:], in_=ps_tb[:, 2 * j + 1, :], func=AF.Copy, scale=be_[j]),
                 reads=["ptb", gk + "be"], writes=["bV" + ck[j]])
        hop()
        stage_done()
        for j in J:
            b_, bkk = bk(j)
            P.op("pe", lambda e: e.matmul(b_[:, 0, :], lhsT=KT[tp][:, csl[j]], rhs=KT[tp][:, csl[j]], start=True, stop=True), reads=[kk], writes=[bkk])
            P.op("pe", lambda e: e.matmul(b_[:, 1, :], lhsT=KT[tp][:, csl[j]], rhs=QT[tp][:, csl[j]], start=True, stop=True), reads=[kk, qk], writes=[bkk])
        hop()
        for j in J:
            b_, bkk = bk(j)
            P.op("dve", lambda e: e.tensor_tensor(out=Bm[j][:], in0=b_[:, 0, :], in1=D1[j][:], op=ALU.mult), reads=[bkk, "D1" + sfx[j]], writes=["Bm" + sfx[j]])
            P.op("dve", lambda e: e.tensor_tensor(out=PT[tp][j][:], in0=b_[:, 1, :], in1=PTm[j][:], op=ALU.mult), reads=[bkk, "PTm" + sfx[j]], writes=["PT" + ck[j]])
            P.op("dve", lambda e: e.scalar_tensor_tensor(out=Bq[j][:], in0=Bm[j][:], scalar=nbe_[j], in1=m2f, op0=ALU.mult, op1=ALU.mult),
                 reads=["Bm" + sfx[j], gk + "nb", "consts_f"], writes=["B" + sfx[j]])
        hop()
        stage_done()
        for j in J:
            b_, bkk = bk(j)
            P.op("pe", lambda e: e.transpose(b_[:, 2, :], Bq[j][:], idf), reads=["B" + sfx[j], "consts_f"], writes=[bkk])
        hop()
        for j in J:
            b_, bkk = bk(j)
            P.op("act", lambda e: e.copy(Nq[j][:], b_[:, 2, :]), reads=[bkk], writes=["N" + sfx[j]])
            P.op("pool", lambda e: e.tensor_tensor(out=Rt[j][:], in0=Bq[j][:], in1=idf, op=ALU.add), reads=["B" + sfx[j], "consts_f"], writes=["Rt" + sfx[j]])
        hop()
        for j in J:
            P.op("dve", lambda e: e.tensor_tensor(out=Rq[j][:], in0=Nq[j][:], in1=idf, op=ALU.add), reads=["N" + sfx[j], "consts_f"], writes=["R" + sfx[j]])
        hop()
        stage_done()
        for lvl in range(5):
            last = lvl == 4
            for j in J:
                b_, bkk = bk(j)
                P.op("pe", lambda e: e.matmul(b_[:, 0, :], lhsT=Bq[j][:], rhs=Nq[j][:], start=True, stop=True), reads=["B" + sfx[j], "N" + sfx[j]], writes=[bkk])
                if not last:
                    P.op("pe", lambda e: e.matmul(b_[:, 1, :], lhsT=Nq[j][:], rhs=Bq[j][:], start=True, stop=True), reads=["B" + sfx[j], "N" + sfx[j]], writes=[bkk])
            hop()
            for j in J:
                b_, bkk = bk(j)
                P.op("act", lambda e: e.copy(Nq[j][:], b_[:, 0, :]), reads=[bkk], writes=["N" + sfx[j]])
                if not last:
                    P.op("act", lambda e: e.copy(Bq[j][:], b_[:, 1, :]), reads=[bkk], writes=["B" + sfx[j]])
            hop()
            stage_done()
            for j in J:
                b_, bkk = bk(j)
                P.op("pe", lambda e: e.matmul(b_[:, 2, :], lhsT=Rt[j][:], rhs=Nq[j][:], start=True, stop=True), reads=["Rt" + sfx[j], "N" + sfx[j]], writes=[bkk])
                if not last:
                    P.op("pe", lambda e: e.matmul(b_[:, 3, :], lhsT=Rq[j][:], rhs=Bq[j][:], start=True, stop=True), reads=["R" + sfx[j], "B" + sfx[j]], writes=[bkk])
            hop()
            for j in J:
                b_, bkk = bk(j)
                if not last:
                    P.op("dve", lambda e: e.tensor_tensor(out=Rq[j][:], in0=b_[:, 2, :], in1=Rq[j][:], op=ALU.add), reads=[bkk, "R" + sfx[j]], writes=["R" + sfx[j]])
                    P.op("dve", lambda e: e.tensor_tensor(out=Rt[j][:], in0=b_[:, 3, :], in1=Rt[j][:], op=ALU.add), reads=[bkk, "Rt" + sfx[j]], writes=["Rt" + sfx[j]])
                else:
                    P.op("dve", lambda e: e.tensor_tensor(out=TTb[tp][j][:], in0=b_[:, 2, :], in1=Rq[j][:], op=ALU.add), reads=[bkk, "R" + sfx[j]], writes=["TTb" + ck[j]])
            hop()
            stage_done()
        for j in J:
            b_, bkk = bk(j)
            P.op("pe", lambda e: e.matmul(b_[:, 0, :], lhsT=bgK[j][:], rhs=TTb[tp][j][:], start=True, stop=True), reads=["bgK" + sfx[j], "TTb" + ck[j]], writes=[bkk])
        hop()
        for j in J:
            b_, bkk = bk(j)
            P.op("act", lambda e: e.mul(nWT[tp][j][:], b_[:, 0, :], -1.0), reads=[bkk], writes=["nWT" + ck[j]])
        hop()
        stage_done()
        while chain_q:
            chain_q.pop(0)()
        while pre_q:
            pre_q.pop(0)()

        def chunk_hops(j, c, tp=tp, qk=qk, ck=ck, ti=ti, csl=csl):
            r = slice(64 * c, 64 * c + 64)
            si = sidx[0]
            so, sn_ = si % 2, (si + 1) % 2
            sidx[0] += 1
            o2 = j % 2
            u, qs, sn, pu = ps_ch[:, 0, :], ps_ch[:, 1, :], ps_ch[:, 2, :], ps_ch[:, 3, :]

            def h1():
                P.op("pe", lambda e: e.matmul(u, lhsT=TTb[tp][j][r, :], rhs=bV[tp][j][r, :], start=True, stop=False),
                     reads=["TTb" + ck[j], "bV" + ck[j]], writes=["ps_ch"])
                P.op("pe", lambda e: e.matmul(u, lhsT=nWT[tp][j][:], rhs=Sb[so][:], start=False, stop=True),
                     reads=["nWT" + ck[j], "Sb%d" % so], writes=["ps_ch"])
                P.op("pe", lambda e: e.matmul(qs, lhsT=QT[tp][:, csl[j]], rhs=Sb[so][:], start=True, stop=True),
                     reads=[qk, "Sb%d" % so], writes=["ps_ch"])

            def h2():
                P.op("dve", lambda e: e.tensor_copy(Ub[o2][r, :], u[r, :]), reads=["ps_ch"], writes=["Ub%d_%d" % (o2, c)])

            def h3():
                P.op("pe", lambda e: e.matmul(sn, lhsT=KD[tp][j][r, :], rhs=Ub[o2][r, :], start=True, stop=True),
                     reads=["KD" + ck[j], "Ub%d_%d" % (o2, c)], writes=["ps_ch"])
                P.op("pe", lambda e: e.matmul(pu, lhsT=PT[tp][j][r, :], rhs=Ub[o2][r, :], start=True, stop=True),
                     reads=["PT" + ck[j], "Ub%d_%d" % (o2, c)], writes=["ps_ch"])

            def h4():
                P.op("dve", lambda e: e.scalar_tensor_tensor(out=Sf[sn_][:], in0=Sf[so][:], scalar=smx[tp][j][:, 2 + c:3 + c], in1=sn,
                                                             op0=ALU.mult, op1=ALU.add), reads=["Sf%d" % so, "smx" + ck[j], "ps_ch"], writes=["Sf%d" % sn_])

            def h5():
                P.op("act", lambda e: e.copy(Sb[sn_][:], Sf[sn_][:]), reads=["Sf%d" % sn_], writes=["Sb%d" % sn_])
                P.op("dve", lambda e: e.tensor_copy(pus[o2][r, :], pu[r, :]), reads=["ps_ch"], writes=["pus%d_%d" % (o2, c)])
                P.op("dve", lambda e: e.scalar_tensor_tensor(out=Osb[o2][r, :], in0=qs[r, :], scalar=smx[tp][j][r, 0:1], in1=pus[o2][r, :],
                                                             op0=ALU.mult, op1=ALU.add), reads=["ps_ch", "smx" + ck[j], "pus%d_%d" % (o2, c)],
                     writes=["Osb%d_%d" % (o2, c)])
                if c == 1:
                    blk = ti * 4 + j
                    P.op("sp", lambda e: e.dma_start(out=o_out[blk * 128:(blk + 1) * 128, :], in_=Osb[o2][:]),
                         reads=["Osb%d_0" % o2, "Osb%d_1" % o2], chan="ost%d" % o2)
            return [h1, h2, h3, h4, h5]

        for j in J:
            for c in range(2):
                chain_q.extend(chunk_hops(j, c))
    pre_q = []
    for f in tile_level(0):
        f()
    for ti in range(NT):
        if ti + 1 < NT:
            pre_q.extend(tile_level(ti + 1))
        block_level(ti)
    while chain_q:
        chain_q.pop(0)()


def prep_weight(P, stg, stgkey, src2d, n_c, ncols, dst, dst_col0, scale_fn, dkey, skeys, cnt, dst_c0=0):
    pw = min(2048 // n_c, ncols)
    for col in range(0, ncols, pw):
        w = min(pw, ncols - col)
        b = cnt[0] % len(stg)
        cnt[0] += 1
        sv = stg[b][:, 0:n_c * w].rearrange("p (c n) -> p c n", c=n_c)
        k = stgkey + str(b)
        P.op("sp", lambda e: e.dma_start(out=sv, in_=src2d[:, col:col + w].rearrange("(c p) n -> p c n", p=128)),
             writes=[k], chan=k)
        dv = dst[:, dst_c0:dst_c0 + n_c, dst_col0 + col:dst_col0 + col + w]
        if scale_fn is None:
            eng = "act" if (cnt[0] % 2) else "dve"
            if eng == "act":
                P.op("act", lambda e: e.copy(dv, sv), reads=[k], writes=[dkey])
            else:
                P.op("dve", lambda e: e.tensor_copy(dv, sv), reads=[k], writes=[dkey])
        else:
            for c in range(n_c):
                sc_ = scale_fn(c)
                if c % 2:
                    P.op("act", lambda e: e.activation(out=dst[:, c, dst_col0 + col:dst_col0 + col + w], in_=sv[:, c, :],
                                                       func=AF.Copy, scale=sc_), reads=[k] + skeys, writes=[dkey])
                else:
                    P.op("dve", lambda e: e.tensor_scalar(dst[:, c, dst_col0 + col:dst_col0 + col + w], sv[:, c, :], sc_, None, ALU.mult),
                         reads=[k] + skeys, writes=[dkey])


TB = 2
TW = TB * 128


def load_consts(P, A, cst, pre):
    cf = A.sb([128, NCONST], F32, "cf")
    cb = A.sb([128, NCONST], BF16, "cb")
    P.op("sp", lambda e: e.dma_start(out=cf[:], in_=cst[:, :]), writes=[pre + "consts_f"], chan=pre + "cf")
    P.op("dve", lambda e: e.tensor_copy(cb[:], cf[:]), reads=[pre + "consts_f"], writes=["consts_b"])
    return cf, cb


def build_phase2a(nc, P, A, NTM, x2, oa2, validc, w_in, w_ba, w_bb, w_out, nwm_d, gnw_d, biasT_d, cst, xmid,
                  o_all=None, qsel_d=None, RT=None):
    NT2 = NTM + 3
    TC = NTM * TW
    sb, ps = A.sb, A.ps
    cf, cb = load_consts(P, A, cst, "a")
    idb = cb[:, C_ID:C_ID + 128]
    epsc = make_epsc(P, A, "dve")
    nwm = sb([128, NCH], F32, "nwm")
    gnw = sb([128, 1], F32, "gnw")
    P.op("sp", lambda e: e.dma_start(out=nwm[:], in_=nwm_d[:, :]), writes=["nwm0"], chan="nwm")
    P.op("sp", lambda e: e.dma_start(out=gnw[:], in_=gnw_d[:, :]), writes=["gnw"], chan="gnw")
    P.op("dve", lambda e: e.tensor_scalar(nwm[:], nwm[:], 32.0, None, ALU.mult), reads=["nwm0"], writes=["nwm"])
    Wi = sb([128, NCH, 4096], BF16, "Wi")
    WbA = sb([128, 4, 1024], BF16, "WbA")
    WbB = sb([128, 4, 1024], BF16, "WbB")
    Wo = sb([128, NCH, 1024], BF16, "Wo")
    es_stg = ExitStack()
    stg = [es_stg.enter_context(nc.sbuf_tensor("astg%d" % i, [128, 2048], F32)) for i in range(2)]
    cnt = [0]
    prep_weight(P, stg, "astg", w_in[:, 1536:2048], 8, 512, Wi, 0, lambda c: nwm[:, c:c + 1], "Wi", ["nwm"], cnt)
    prep_weight(P, stg, "astg", w_in[:, 2056:5640], 8, 3584, Wi, 512, lambda c: nwm[:, c:c + 1], "Wi", ["nwm"], cnt)
    prep_weight(P, stg, "astg", w_ba, 4, 1024, WbA, 0, lambda c: gnw[:, 0:1], "WbA", ["gnw"], cnt)
    prep_weight(P, stg, "astg", w_bb, 4, 1024, WbB, 0, None, "WbB", [], cnt)
    prep_weight(P, stg, "astg", w_out, 8, 1024, Wo, 0, None, "Wo", [], cnt)
    es_stg.close()
    P.barrier()
    biasT = sb([128, 8, 640], F32, "biasT")
    P.op("sp", lambda e: e.dma_start(out=biasT[:], in_=biasT_d[:, :, :]), writes=["biasT"], chan="biasT")
    valid = sb([128, NT2 * TB], F32, "valid")
    P.op("sp", lambda e: e.dma_start(out=valid[:], in_=validc[:, :]), writes=["valid"], chan="valid")
    ones8 = sb([128, 8, 1], F32, "ones8")
    P.op("dve", lambda e: e.memset(ones8[:], 1.0), writes=["ones8"])

    xt = [sb([128, TB, D], F32, "xt") for _ in range(2)]
    junk = sb([128, D], BF16, "junk")
    ss = sb([128, 8], F32, "ss")
    rs = sb([128, 8], F32, "rs")
    xn = [sb([128, D], BF16, "xn") for _ in range(2)]
    hT = sb([128, NCH, TW], BF16, "hT")
    KTb = sb([128, 4, 8 * 128], BF16, "KTb")
    Vaug = sb([128, 8, 8, 65], BF16, "Vaug")
    QTb = sb([128, 4, TW], BF16, "QTb")
    zs = sb([128, TB, 512], F32, "zs")
    oat = sb([128, TB, 512], F32, "oat")
    cands = None
    if o_all is not None:
        cand = sb([128, 4, 512], F32, "cand")
        if RT is None:
            cands = [(lambda r, q_=q_: o_all[q_ * TC + r:q_ * TC + r + 128, :].rearrange("p (h d) -> p h d", h=4)) for q_ in range(4)]
        else:
            o_alls, chunks, CR = o_all
            views = [a.rearrange("(r t) d -> t r d", r=8) for a in o_alls]

            def cand_ap(row, b_):
                i, off = row // CR, row % CR
                return views[i][off:off + 128, 4 * b_:4 * b_ + 4, :]
            cands = [(lambda r, b_=c_ // 4, q_=c_ % 4: cand_ap(q_ * TC + r, b_)) for c_ in range(8)]
        qsel = sb([128, len(cands)], F32, "qsel")
        P.op("sp", lambda e: e.dma_start(out=qsel[:], in_=qsel_d[:, :]), writes=["qsel"], chan="qsel")
    ssa = sb([128, 4], F32, "ssa")
    ra = sb([128, 4], F32, "ra")
    oan = sb([128, 512], BF16, "oan")
    oaT = sb([128, 4, TW], BF16, "oaT")
    ob = sb([128, 512], BF16, "ob")
    obT = sb([128, 4, TW], BF16, "obT")
    scs = [sb([128, 640], F32, "scs")] * 2
    PTb = [sb([128, 640], BF16, "PTb") for _ in range(2)]
    rden = sb([128, 8], F32, "rden")
    sg = [sb([128, 2 * TW], F32, "sg") for _ in range(2)]
    tt_ = [sb([128, 2 * TW], F32, "tt") for _ in range(2)]
    mixT = sb([128, NCH, TW], BF16, "mixT")

    ps_tr = ps([128, NCH, 128], BF16, "ps_tr")
    pbig = [ps([128, 512], F32, "pbig") for _ in range(5)]
    ps_o = [ps([128, 4, 65], F32, "ps_o") for _ in range(2)]
    bi = [0]

    def big():
        i = bi[0] % 5
        bi[0] += 1
        return pbig[i], "pbig%d" % i

    scale_q = 64.0 ** -0.5
    for tt in range(NT2):
        tp = tt % 2
        xk = "axt%d" % tp
        P.op("sp", lambda e: e.dma_start(out=xt[tp][:], in_=x2[tt * TW:(tt + 1) * TW, :].rearrange("(j p) d -> p j d", p=128)),
             writes=[xk], chan=xk)
        for j in range(TB):
            norm_block(P, epsc, xt[tp][:, j, :], xk, ss[:, j:j + 1], rs[:, j:j + 1], "ass%d" % j, junk[:], "ajunk",
                       xn[j % 2][:], "axn%d" % (j % 2), ps_tr, "aps_tr", idb, hT[:, :, j * 128:(j + 1) * 128], "ahT")
        ring0 = (tt * TB) % 8
        for m in range(4):
            pb_, pk = big()
            for c in range(NCH):
                P.op("pe", lambda e: e.matmul(pb_[:, 0:TW], lhsT=Wi[:, c, 1024 + m * 128:1024 + (m + 1) * 128], rhs=hT[:, c, :],
                                              start=(c == 0), stop=(c == NCH - 1)), reads=["Wi", "ahT"], writes=[pk])
            P.op("act", lambda e: e.copy(KTb[:, m, ring0 * 128:ring0 * 128 + TW], pb_[:, 0:TW]), reads=[pk], writes=["KTb"])
        for j in range(TB):
            slot = ring0 + j
            pb_, pk = big()
            for c in range(NCH):
                P.op("pe", lambda e: e.matmul(pb_[:, :], lhsT=hT[:, c, j * 128:(j + 1) * 128], rhs=Wi[:, c, 1536:2048],
                                              start=(c == 0), stop=(c == NCH - 1)), reads=["Wi", "ahT"], writes=[pk])
            P.op("dve", lambda e: e.tensor_copy(Vaug[:, slot, :, 0:64], pb_[:, :].rearrange("p (h d) -> p h d", h=8)),
                 reads=[pk], writes=["Vaug"])
            P.op("act", lambda e: e.activation(out=Vaug[:, slot, :, 64:65], in_=ones8[:], func=AF.Copy,
                                               scale=valid[:, tt * TB + j:tt * TB + j + 1]),
                 reads=["ones8", "valid"], writes=["Vaug"])
        if tt < 2:
            continue
        for m in range(4):
            pb_, pk = big()
            for c in range(NCH):
                P.op("pe", lambda e: e.matmul(pb_[:, 0:TW], lhsT=Wi[:, c, 512 + m * 128:512 + (m + 1) * 128], rhs=hT[:, c, :],
                                              start=(c == 0), stop=(c == NCH - 1)), reads=["Wi", "ahT"], writes=[pk])
            P.op("act", lambda e: e.mul(QTb[:, m, :], pb_[:, 0:TW], scale_q), reads=[pk], writes=["QTb"])
        for j in range(TB):
            pb_, pk = big()
            for c in range(NCH):
                P.op("pe", lambda e: e.matmul(pb_[:, :], lhsT=hT[:, c, j * 128:(j + 1) * 128], rhs=Wi[:, c, 0:512],
                                              start=(c == 0), stop=(c == NCH - 1)), reads=["Wi", "ahT"], writes=[pk])
            P.op("act", lambda e: e.activation(out=zs[:, j, :], in_=pb_[:, :], func=AF.Silu), reads=[pk], writes=["zs%d" % j])
        if o_all is None:
            P.op("sp", lambda e: e.dma_start(out=oat[:], in_=oa2[(tt - 2) * TW:(tt - 1) * TW, :].rearrange("(j p) d -> p j d", p=128)),
                 writes=["oat"], chan="oat")
        else:
            for j in range(TB):
                for cc_, cf_ in enumerate(cands):
                    k = cc_ % 4
                    ck_ = "cand_%d" % k
                    P.op("sp", lambda e: e.dma_start(out=cand[:, k, :].rearrange("p (h d) -> p h d", h=4),
                                                     in_=cf_((tt - 2) * TW + j * 128)),
                         reads=["oall"], writes=[ck_], chan=ck_)
                    if cc_ == 0:
                        P.op("dve", lambda e: e.tensor_scalar(oat[:, j, :], cand[:, k, :], qsel[:, cc_:cc_ + 1], None, ALU.mult),
                             reads=[ck_, "qsel"], writes=["oat"])
                    else:
                        P.op("dve", lambda e: e.scalar_tensor_tensor(out=oat[:, j, :], in0=cand[:, k, :], scalar=qsel[:, cc_:cc_ + 1],
                                                                     in1=oat[:, j, :], op0=ALU.mult, op1=ALU.add),
                             reads=[ck_, "qsel", "oat"], writes=["oat"])
        for j in range(TB):
            g = tt * TB + j
            for h in range(8):
                m, r = h // 2, slice(64 * (h % 2), 64 * (h % 2) + 64)
                p1, p1k = big()
                p2, p2k = big()
                for kb in range(5):
                    slot = (g - 4 + kb) % 8
                    dst = p1[:, kb * 128:(kb + 1) * 128] if kb < 4 else p2[:, 0:128]
                    P.op("pe", lambda e: e.matmul(dst, lhsT=KTb[r, m, slot * 128:(slot + 1) * 128], rhs=QTb[r, m, j * 128:(j + 1) * 128],
                                                  start=True, stop=True), reads=["KTb", "QTb"], writes=[p1k if kb < 4 else p2k])
                sp_ = h % 2
                P.op("dve", lambda e: e.tensor_tensor(out=scs[sp_][:, 0:512], in0=p1[:, :], in1=biasT[:, h, 0:512], op=ALU.add),
                     reads=[p1k, "biasT"], writes=["scsa"])
                P.op("dve", lambda e: e.tensor_tensor(out=scs[sp_][:, 512:640], in0=p2[:, 0:128], in1=biasT[:, h, 512:640], op=ALU.add),
                     reads=[p2k, "biasT"], writes=["scsb"])
                P.op("act", lambda e: e.activation(out=PTb[sp_][:], in_=scs[sp_][:], func=AF.Exp),
                     reads=["scsa", "scsb"], writes=["PTb%d" % sp_])
                for kb in range(5):
                    slot = (g - 4 + kb) % 8
                    P.op("pe", lambda e: e.matmul(ps_o[h // 4][:, h % 4, :], lhsT=PTb[sp_][:, kb * 128:(kb + 1) * 128],
                                                  rhs=Vaug[:, slot, h, :], start=(kb == 0), stop=(kb == 4)),
                         reads=["PTb%d" % sp_, "Vaug"], writes=["ps_o%d" % (h // 4)])
            for hg in range(2):
                P.op("dve", lambda e: e.tensor_scalar(rden[:, hg * 4:hg * 4 + 4], ps_o[hg][:, :, 64], 1e-30, None, ALU.add),
                     reads=["ps_o%d" % hg], writes=["rden%da" % hg])
                P.op("dve", lambda e: e.reciprocal(rden[:, hg * 4:hg * 4 + 4], rden[:, hg * 4:hg * 4 + 4]),
                     reads=["rden%da" % hg], writes=["rden%d" % hg])
            for h in range(8):
                P.op("act", lambda e: e.activation(out=ob[:, h * 64:(h + 1) * 64], in_=ps_o[h // 4][:, h % 4, 0:64], func=AF.Copy,
                                                   scale=rden[:, h:h + 1]), reads=["ps_o%d" % (h // 4), "rden%d" % (h // 4)], writes=["ob"])
            for c in range(4):
                P.op("pe", lambda e: e.transpose(ps_tr[:, c, :], ob[:, c * 128:(c + 1) * 128], idb), reads=["ob", "consts_b"], writes=["aps_tr"])
            P.op("act", lambda e: e.copy(obT[:, :, j * 128:(j + 1) * 128], ps_tr[:, 0:4, :]), reads=["aps_tr"], writes=["obT"])
            for hh in range(4):
                P.op("act", lambda e: e.activation(out=junk[:, 0:128], in_=oat[:, j, hh * 128:(hh + 1) * 128], func=AF.Square,
                                                   accum_out=ssa[:, hh:hh + 1]), reads=["oat"], writes=["ajunk", "ssa"])
            P.op("act", lambda e: e.activation(out=ra[:], in_=ssa[:], func=AF.Ln, scale=1.0 / 128.0, bias=epsc[:, 1:2]),
                 reads=["ssa", "epsc"], writes=["ra0"])
            P.op("act", lambda e: e.activation(out=ra[:], in_=ra[:], func=AF.Exp, scale=-0.5), reads=["ra0"], writes=["ra"])
            for hh in range(4):
                P.op("dve", lambda e: e.scalar_tensor_tensor(out=oan[:, hh * 128:(hh + 1) * 128], in0=oat[:, j, hh * 128:(hh + 1) * 128],
                                                             scalar=ra[:, hh:hh + 1], in1=zs[:, j, hh * 128:(hh + 1) * 128],
                                                             op0=ALU.mult, op1=ALU.mult), reads=["oat", "ra", "zs%d" % j], writes=["oan"])
            for c in range(4):
                P.op("pe", lambda e: e.transpose(ps_tr[:, 4 + c, :], oan[:, c * 128:(c + 1) * 128], idb), reads=["oan", "consts_b"], writes=["aps_tr"])
            P.op("act", lambda e: e.copy(oaT[:, :, j * 128:(j + 1) * 128], ps_tr[:, 4:8, :]), reads=["aps_tr"], writes=["oaT"])
        for mo in range(8):
            py, pyk = big()
            pg, pgk = big()
            for half, (Wb, src, skey) in enumerate(((WbA, oaT, "oaT"), (WbB, obT, "obT"))):
                for c in range(4):
                    P.op("pe", lambda e: e.matmul(py[:, half * TW:(half + 1) * TW], lhsT=Wb[:, c, mo * 128:(mo + 1) * 128], rhs=src[:, c, :],
                                                  start=(c == 0), stop=(c == 3)), reads=["WbA", "WbB", skey], writes=[pyk])
            for half in range(2):
                col0 = 2048 + half * 1024 + mo * 128
                for c in range(NCH):
                    P.op("pe", lambda e: e.matmul(pg[:, half * TW:(half + 1) * TW], lhsT=Wi[:, c, col0:col0 + 128], rhs=hT[:, c, :],
                                                  start=(c == 0), stop=(c == NCH - 1)), reads=["Wi", "ahT"], writes=[pgk])
            q2 = mo % 2
            P.op("act", lambda e: e.activation(out=sg[q2][:], in_=pg[:, :], func=AF.Sigmoid), reads=[pgk], writes=["sg%d" % q2])
            P.op("dve", lambda e: e.tensor_tensor(out=tt_[q2][:], in0=py[:, :], in1=sg[q2][:], op=ALU.mult),
                 reads=[pyk, "sg%d" % q2], writes=["tt%d" % q2])
            P.op("dve", lambda e: e.tensor_tensor(out=mixT[:, mo, :], in0=tt_[q2][:, 0:TW], in1=tt_[q2][:, TW:2 * TW], op=ALU.add),
                 reads=["tt%d" % q2], writes=["mixT"])
        for j in range(TB):
            for half in range(2):
                po, pok = big()
                for c in range(NCH):
                    P.op("pe", lambda e: e.matmul(po[:, :], lhsT=mixT[:, c, j * 128:(j + 1) * 128], rhs=Wo[:, c, half * 512:(half + 1) * 512],
                                                  start=(c == 0), stop=(c == NCH - 1)), reads=["mixT", "Wo"], writes=[pok])
                P.op("dve", lambda e: e.tensor_tensor(out=xt[tp][:, j, half * 512:(half + 1) * 512], in0=po[:, :],
                                                      in1=xt[tp][:, j, half * 512:(half + 1) * 512], op=ALU.add), reads=[pok, xk], writes=[xk])
        P.op("sp", lambda e: e.dma_start(out=xmid[(tt - 2) * TW:(tt - 1) * TW, :].rearrange("(j p) d -> p j d", p=128), in_=xt[tp][:]),
             reads=[xk], writes=["xmid_d"], chan="xmst%d" % tp)


def build_phase2b(nc, P, A, NTM, xmid, w_up, w_down, nwf_d, cfw_d, cfb_d, wfin_d, cst, out2):
    sb, ps = A.sb, A.ps
    cf, cb = load_consts(P, A, cst, "b")
    idb = cb[:, C_ID:C_ID + 128]
    epsc = make_epsc(P, A)
    nwf = sb([128, NCH], F32, "nwf")
    P.op("sp", lambda e: e.dma_start(out=nwf[:], in_=nwf_d[:, :]), writes=["nwf0"], chan="nwf")
    P.op("dve", lambda e: e.tensor_scalar(nwf[:], nwf[:], 32.0, None, ALU.mult), reads=["nwf0"], writes=["nwf"])
    cfw = sb([128, 44, 3], F32, "cfw")
    cfb = sb([128, 44], F32, "cfb")
    wfb = sb([128, D], F32, "wfb")
    P.op("sp", lambda e: e.dma_start(out=cfw[:], in_=cfw_d[:, :, :]), writes=["cfw"], chan="cfw")
    P.op("sp", lambda e: e.dma_start(out=cfb[:], in_=cfb_d[:, :]), writes=["cfb"], chan="cfb")
    P.op("sp", lambda e: e.dma_start(out=wfb[:], in_=wfin_d[:, :]), writes=["wfb0"], chan="wfb")
    P.op("pool", lambda e: e.tensor_scalar(wfb[:], wfb[:], 32.0, None, ALU.mult), reads=["wfb0"], writes=["wfb"])
    Wu = sb([128, NCH, 2 * D_FF], BF16, "Wu")
    Wd = sb([128, 22, D], BF16, "Wd")
    es_stg = ExitStack()
    stg = [es_stg.enter_context(nc.sbuf_tensor("bstg%d" % i, [128, 2048], F32)) for i in range(2)]
    cnt = [0]
    prep_weight(P, stg, "bstg", w_up, 8, 2 * D_FF, Wu, 0, lambda c: nwf[:, c:c + 1], "Wu", ["nwf"], cnt)
    prep_weight(P, stg, "bstg", w_down[0:1408, :], 11, D, Wd, 0, None, "Wd", [], cnt, dst_c0=0)
    prep_weight(P, stg, "bstg", w_down[1408:2816, :], 11, D, Wd, 0, None, "Wd", [], cnt, dst_c0=11)
    es_stg.close()
    P.barrier()

    xm = [sb([128, TB, D], F32, "xm") for _ in range(2)]
    junk = sb([128, D], BF16, "junk")
    ss = sb([128, 8], F32, "ss")
    rs = sb([128, 8], F32, "rs")
    xn = [sb([128, D], BF16, "xn") for _ in range(2)]
    h2T = sb([128, NCH, TW], BF16, "h2T")
    ubuf = [sb([128, 2, TW + 2], F32, "ubuf") for _ in range(2)]
    cv = [sb([128, 2, TW], F32, "cv") for _ in range(2)]
    sgt = [sb([128, TW], F32, "sgt") for _ in range(2)]
    uh = sb([128, 22, 2, 2], F32, "uh")
    actT = sb([128, 22, TW], BF16, "actT")
    outt = [sb([128, D], F32, "outt") for _ in range(2)]
    P.op("pool", lambda e: e.memset(uh[:], 0.0), writes=["uh"])

    ps_tr = ps([128, NCH, 128], BF16, "ps_tr")
    pbig = [ps([128, 512], F32, "pbig") for _ in range(6)]
    bi = [0]

    def big():
        i = bi[0] % 6
        bi[0] += 1
        return pbig[i], "bpbig%d" % i

    for u in range(NTM + 1):
        tp = u % 2
        xk = "bxm%d" % tp
        P.op("sp", lambda e: e.dma_start(out=xm[tp][:], in_=xmid[u * TW:(u + 1) * TW, :].rearrange("(j p) d -> p j d", p=128)),
             reads=["xmid_d"], writes=[xk], chan=xk)
        for j in range(TB):
            norm_block(P, epsc, xm[tp][:, j, :], xk, ss[:, j:j + 1], rs[:, j:j + 1], "bss%d" % j, junk[:], "bjunk",
                       xn[j % 2][:], "bxn%d" % (j % 2), ps_tr, "bps_tr", idb, h2T[:, :, j * 128:(j + 1) * 128], "h2T")
        for m in range(22):
            q2 = m % 2
            pg, pgk = big()
            for half in range(2):
                col0 = half * D_FF + m * 128
                for c in range(NCH):
                    P.op("pe", lambda e: e.matmul(pg[:, half * TW:(half + 1) * TW], lhsT=Wu[:, c, col0:col0 + 128], rhs=h2T[:, c, :],
                                                  start=(c == 0), stop=(c == NCH - 1)), reads=["Wu", "h2T"], writes=[pgk])
            uk = "ubuf%d" % q2
            P.op("pool", lambda e: e.tensor_copy(ubuf[q2][:, :, 0:2], uh[:, m, :, :]), reads=["uh"], writes=[uk + "h"])
            P.op("act", lambda e: e.copy(ubuf[q2][:, :, 2:TW + 2], pg[:, :].rearrange("p (s n) -> p s n", s=2)), reads=[pgk], writes=[uk])
            P.op("pool", lambda e: e.tensor_copy(uh[:, m, :, :], ubuf[q2][:, :, TW:TW + 2]), reads=[uk, uk + "h"], writes=["uh"])
            if u == 0:
                continue
            ck = "cv%d" % q2
            for s_ in range(2):
                ch = s_ * 22 + m
                eng = "dve"
                P.op("act", lambda e: e.activation(out=cv[q2][:, s_, :], in_=ubuf[q2][:, s_, 0:TW], func=AF.Identity,
                                                   scale=cfw[:, ch, 0:1], bias=cfb[:, ch:ch + 1]),
                     reads=[uk, uk + "h", "cfw", "cfb"], writes=[ck + str(s_)])
                for k in range(1, 3):
                    P.op(eng, lambda e: e.scalar_tensor_tensor(out=cv[q2][:, s_, :], in0=ubuf[q2][:, s_, k:k + TW], scalar=cfw[:, ch, k:k + 1],
                                                               in1=cv[q2][:, s_, :], op0=ALU.mult, op1=ALU.add),
                         reads=[uk, uk + "h", "cfw", ck + str(s_)], writes=[ck + str(s_)])
            P.op("act", lambda e: e.activation(out=sgt[q2][:], in_=cv[q2][:, 0, :], func=AF.Silu), reads=[ck + "0"], writes=["sgt%d" % q2])
            P.op("dve", lambda e: e.tensor_tensor(out=actT[:, m, :], in0=sgt[q2][:], in1=cv[q2][:, 1, :], op=ALU.mult),
                 reads=["sgt%d" % q2, ck + "1"], writes=["actT"])
        if u == 0:
            continue
        for j in range(TB):
            for half in range(2):
                po, pok = big()
                for m in range(22):
                    P.op("pe", lambda e: e.matmul(po[:, :], lhsT=actT[:, m, j * 128:(j + 1) * 128], rhs=Wd[:, m, half * 512:(half + 1) * 512],
                                                  start=(m == 0), stop=(m == 21)), reads=["actT", "Wd"], writes=[pok])
                P.op("dve", lambda e: e.tensor_tensor(out=xm[tp][:, j, half * 512:(half + 1) * 512], in0=po[:, :],
                                                      in1=xm[tp][:, j, half * 512:(half + 1) * 512], op=ALU.add), reads=[pok, xk], writes=[xk])
            o2 = j % 2
            P.op("act", lambda e: e.activation(out=junk[:], in_=xm[tp][:, j, :], func=AF.Square, accum_out=ss[:, 4 + j:5 + j]),
                 reads=[xk], writes=["bjunk", "fss%d" % j])
            P.op("act", lambda e: e.activation(out=rs[:, 4 + j:5 + j], in_=ss[:, 4 + j:5 + j], func=AF.Ln, bias=epsc[:, 0:1]),
                 reads=["fss%d" % j, "epsc"], writes=["frs%da" % j])
            P.op("act", lambda e: e.activation(out=rs[:, 4 + j:5 + j], in_=rs[:, 4 + j:5 + j], func=AF.Exp, scale=-0.5),
                 reads=["frs%da" % j], writes=["frs%d" % j])
            P.op("dve", lambda e: e.scalar_tensor_tensor(out=outt[o2][:], in0=xm[tp][:, j, :], scalar=rs[:, 4 + j:5 + j], in1=wfb[:],
                                                         op0=ALU.mult, op1=ALU.mult), reads=[xk, "frs%d" % j, "wfb"], writes=["outt%d" % o2])
            P.op("sp", lambda e: e.dma_start(out=out2[(u - 1) * TW + j * 128:(u - 1) * TW + (j + 1) * 128, :], in_=outt[o2][:]),
                 reads=["outt%d" % o2], chan="ost%d" % o2)


def _phase1_inputs(inp, T):
    x = np.asarray(inp["x"], np.float32)
    w_in = np.asarray(inp["w_in"], np.float32)[0]
    conv = np.asarray(inp["conv_qkv_w"], np.float32)[0]
    a_log = np.asarray(inp["a_log"], np.float32)[0]
    dtb = np.asarray(inp["dt_bias"], np.float32)[0]
    nw = np.asarray(inp["norm_mix_w"], np.float32)[0]
    cst = chunk_consts()
    maps = []
    for core in range(8):
        b, h = core // 4, core % 4
        cols = np.concatenate([np.arange(h * 128, (h + 1) * 128), 512 + np.arange(h * 128, (h + 1) * 128),
                               1024 + np.arange(h * 128, (h + 1) * 128), [2048 + h], [2052 + h]])
        w1 = np.ascontiguousarray(w_in[:, cols])
        cw = np.zeros((128, 12), np.float32)
        for g in range(3):
            cw[:, 4 * g:4 * g + 4] = conv[:, g * 512 + h * 128:g * 512 + (h + 1) * 128].T
        sc = np.zeros((128, 2), np.float32)
        sc[:, 0] = a_log[h]
        sc[:, 1] = dtb[h]
        maps.append({"x1": np.ascontiguousarray(x[b, :T]), "w1": w1, "cw1": cw, "sc1": sc,
                     "nw1": np.ascontiguousarray(nw.reshape(8, 128).T), "cst": cst})
    return maps


def build_p1_program(T):
    nc = bass.Bass("TRN2", target_bir_lowering=False)
    x1 = nc.dram_tensor("x1", [T, D], F32, kind="ExternalInput").ap()
    w1 = nc.dram_tensor("w1", [D, 386], F32, kind="ExternalInput").ap()
    cw1 = nc.dram_tensor("cw1", [128, 12], F32, kind="ExternalInput").ap()
    sc1 = nc.dram_tensor("sc1", [128, 2], F32, kind="ExternalInput").ap()
    nw1 = nc.dram_tensor("nw1", [128, 8], F32, kind="ExternalInput").ap()
    cst = nc.dram_tensor("cst", [128, NCONST], F32, kind="ExternalInput").ap()
    o_out = nc.dram_tensor("o1", [T, 128], F32, kind="ExternalOutput").ap()
    es = ExitStack()
    P = Prog(nc)
    A = Ctx(nc, es, P)
    build_phase1(nc, es, P, A, T, x1, w1, cw1, sc1, nw1, cst, o_out)
    P.finish()
    P.emit(es)
    es.close()
    return nc, P


def run_phase1(inp, T):
    nc, P = build_p1_program(T)
    maps = _phase1_inputs(inp, T)
    res = run_bass_kernel_spmd(nc, maps, core_ids=list(range(8)))
    o = np.zeros((2, T, 4, 128), np.float32)
    for core in range(8):
        o[core // 4, :, core % 4, :] = res.results[core]["o1"]
    return o


def _bias_tile(rel):
    ki = np.arange(128)[:, None]
    qi = np.arange(128)[None, :]
    out = np.zeros((128, 8, 640), np.float32)
    for kb in range(5):
        dist = qi - ki + (4 - kb) * 128
        idx = np.clip(dist, -128, 128) + 128
        cdiff = 2 * (4 - kb) + qi // 64 - ki // 64
        ok = (cdiff >= 0) & (cdiff <= 8)
        for h in range(8):
            out[:, h, kb * 128:(kb + 1) * 128] = np.where(ok, rel[h][idx], NEG)
    return out


def _phase2_inputs(inp, o1, T):
    TC = T // 4
    NTM = TC // TW
    x = np.asarray(inp["x"], np.float32)
    w_in = np.ascontiguousarray(np.asarray(inp["w_in"], np.float32)[0])
    cfw_ = np.asarray(inp["conv_ffn_w"], np.float32)[0]
    cfb_ = np.asarray(inp["conv_ffn_b"], np.float32)[0]
    shared = {
        "w_in": w_in,
        "w_ba": np.ascontiguousarray(np.asarray(inp["w_branch_a"], np.float32)[0]),
        "w_bb": np.ascontiguousarray(np.asarray(inp["w_branch_b"], np.float32)[0]),
        "w_out": np.ascontiguousarray(np.asarray(inp["w_out"], np.float32)[0]),
        "w_up": np.ascontiguousarray(np.asarray(inp["w_up"], np.float32)[0]),
        "w_down": np.ascontiguousarray(np.asarray(inp["w_down"], np.float32)[0]),
        "nwm": np.ascontiguousarray(np.asarray(inp["norm_mix_w"], np.float32)[0].reshape(8, 128).T),
        "nwf": np.ascontiguousarray(np.asarray(inp["norm_ffn_w"], np.float32)[0].reshape(8, 128).T),
        "gnw": np.ascontiguousarray(np.asarray(inp["gdn_norm_w"], np.float32)[0].reshape(128, 1)),
        "biasT": _bias_tile(np.asarray(inp["rel_bias"], np.float32)[0]),
        "cfw": np.ascontiguousarray(cfw_.reshape(3, 44, 128).transpose(2, 1, 0)),
        "cfb": np.ascontiguousarray(cfb_.reshape(44, 128).T),
        "wfin": np.ascontiguousarray(np.broadcast_to(np.asarray(inp["norm_final_w"], np.float32)[None, :], (128, D))),
        "cst2": chunk_consts(),
    }
    maps = []
    for core in range(8):
        b, q = core // 4, core % 4
        t0 = q * TC
        lo = t0 - 3 * TW
        x2 = np.zeros(((NTM + 3) * TW, D), np.float32)
        s0 = max(lo, 0)
        x2[s0 - lo:] = x[b, s0:t0 + TC]
        pos = lo + np.arange((NTM + 3) * TW)
        valid = (pos >= 0).astype(np.float32).reshape((NTM + 3) * TB, 128).T
        m = dict(shared)
        m["x2"] = x2
        m["validc"] = np.ascontiguousarray(valid)
        if o1 is not None:
            lo2 = t0 - TW
            oa2 = np.zeros(((NTM + 1) * TW, 512), np.float32)
            s1 = max(lo2, 0)
            oa2[s1 - lo2:] = o1[b, s1:t0 + TC].reshape(-1, 512)
            m["oa2"] = oa2
        maps.append(m)
    return maps


def _declare_p2(nc, NTM, with_oa):
    d = {}
    def inp(name, shape):
        d[name] = nc.dram_tensor(name, list(shape), F32, kind="ExternalInput").ap()
    inp("x2", [(NTM + 3) * TW, D])
    if with_oa:
        inp("oa2", [(NTM + 1) * TW, 512])
    inp("validc", [128, (NTM + 3) * TB])
    inp("w_in", [D, D_IN]); inp("w_ba", [512, D]); inp("w_bb", [512, D]); inp("w_out", [D, D])
    inp("w_up", [D, 2 * D_FF]); inp("w_down", [D_FF, D]); inp("nwm", [128, 8]); inp("nwf", [128, 8]); inp("gnw", [128, 1])
    inp("biasT", [128, 8, 640]); inp("cfw", [128, 44, 3]); inp("cfb", [128, 44]); inp("wfin", [128, D]); inp("cst2", [128, NCONST])
    d["xmid"] = nc.dram_tensor("xmid", [(NTM + 1) * TW, D], F32, kind="Internal").ap()
    d["out2"] = nc.dram_tensor("out2", [NTM * TW, D], F32, kind="ExternalOutput").ap()
    return d


def _emit_phase2(nc, P, d, NTM, oa_ap, o_all=None, qsel_d=None, RT=None):
    es_a = ExitStack()
    build_phase2a(nc, P, Ctx(nc, es_a, P), NTM, d["x2"], oa_ap, d["validc"], d["w_in"], d["w_ba"], d["w_bb"], d["w_out"],
                  d["nwm"], d["gnw"], d["biasT"], d["cst2"], d["xmid"], o_all=o_all, qsel_d=qsel_d, RT=RT)
    es_a.close()
    P.pool_hold = False
    P.barrier()
    es_b = ExitStack()
    build_phase2b(nc, P, Ctx(nc, es_b, P), NTM, d["xmid"], d["w_up"], d["w_down"], d["nwf"], d["cfw"], d["cfb"], d["wfin"],
                  d["cst2"], d["out2"])
    es_b.close()


def build_p2_program(T):
    NTM = (T // 4) // TW
    nc = bass.Bass("TRN2", target_bir_lowering=False)
    d = _declare_p2(nc, NTM, True)
    P = Prog(nc)
    _emit_phase2(nc, P, d, NTM, d["oa2"])
    P.finish()
    es = ExitStack()
    P.emit(es)
    es.close()
    return nc, P


def run_phase2(inp, o1, T):
    nc, P = build_p2_program(T)
    maps = _phase2_inputs(inp, o1, T)
    res = run_bass_kernel_spmd(nc, maps, core_ids=list(range(8)))
    TC = T // 4
    out = np.zeros((2, T, D), np.float32)
    for core in range(8):
        out[core // 4, (core % 4) * TC:(core % 4 + 1) * TC] = res.results[core]["out2"]
    return out


T_FULL = 16384


def build_fused_program(T):
    NTM = (T // 4) // TW
    RT = T + TW
    nc = bass.Bass("TRN2", target_bir_lowering=False)
    x1 = nc.dram_tensor("x1", [T, D], F32, kind="ExternalInput").ap()
    w1 = nc.dram_tensor("w1", [D, 386], F32, kind="ExternalInput").ap()
    cw1 = nc.dram_tensor("cw1", [128, 12], F32, kind="ExternalInput").ap()
    sc1 = nc.dram_tensor("sc1", [128, 2], F32, kind="ExternalInput").ap()
    nw1 = nc.dram_tensor("nw1", [128, 8], F32, kind="ExternalInput").ap()
    cst = nc.dram_tensor("cst", [128, NCONST], F32, kind="ExternalInput").ap()
    qsel_d = nc.dram_tensor("qsel", [128, 8], F32, kind="ExternalInput").ap()
    o_loc = nc.dram_tensor("o_loc", [RT, 128], F32, kind="Internal").ap()
    CR = RT
    chunks = [(r0, min(CR, RT - r0)) for r0 in range(0, RT, CR)]
    o_alls = [nc.dram_tensor("o_all%d" % i, [8 * n, 128], F32, kind="Internal").ap() for i, (r0, n) in enumerate(chunks)]
    d = _declare_p2(nc, NTM, False)
    P = Prog(nc)
    es1 = ExitStack()
    A1 = Ctx(nc, es1, P)
    zt = A1.sb([128, 128], F32, "zt")
    P.op("pool", lambda e: e.memset(zt[:], 0.0), writes=["zt"])
    for i in range(TW // 128):
        P.op("sp", lambda e: e.dma_start(out=o_loc[i * 128:(i + 1) * 128, :], in_=zt[:]), reads=["zt"], chan="zt")
    build_phase1(nc, es1, P, A1, T, x1, w1, cw1, sc1, nw1, cst, o_loc[TW:RT, :])
    es1.close()
    P.barrier()
    for i, (r0, n) in enumerate(chunks):
        P.op("pool", lambda e: e.collective_compute("AllGather", ALU.bypass, replica_groups=[list(range(8))],
                                                    ins=[o_loc[r0:r0 + n, :]], outs=[o_alls[i][:, :]]),
             writes=["oall"], chan="cc", inc_override=1)
    P.pool_hold = True
    _emit_phase2(nc, P, d, NTM, None, o_all=(o_alls, chunks, CR), qsel_d=qsel_d, RT=RT)
    P.finish()
    es = ExitStack()
    P.emit(es)
    es.close()
    return nc, P


def build_fused_nocc(T):
    NTM = (T // 4) // TW
    RT = T + TW
    nc = bass.Bass("TRN2", target_bir_lowering=False)
    x1 = nc.dram_tensor("x1", [T, D], F32, kind="ExternalInput").ap()
    w1a = nc.dram_tensor("w1a", [4, D, 386], F32, kind="ExternalInput").ap()
    cw1a = nc.dram_tensor("cw1a", [4, 128, 12], F32, kind="ExternalInput").ap()
    sc1a = nc.dram_tensor("sc1a", [4, 128, 2], F32, kind="ExternalInput").ap()
    nw1 = nc.dram_tensor("nw1", [128, 8], F32, kind="ExternalInput").ap()
    cst = nc.dram_tensor("cst", [128, NCONST], F32, kind="ExternalInput").ap()
    qsel_d = nc.dram_tensor("qsel", [128, 4], F32, kind="ExternalInput").ap()
    o_loc = nc.dram_tensor("o_loc", [RT, 512], F32, kind="Internal").ap()
    d = _declare_p2(nc, NTM, False)
    P = Prog(nc)
    for h in range(4):
        es1 = ExitStack()
        A1 = Ctx(nc, es1, P)
        if h == 0:
            zt = A1.sb([128, 512], F32, "zt")
            P.op("pool", lambda e: e.memset(zt[:], 0.0), writes=["zt"])
            for i in range(TW // 128):
                P.op("sp", lambda e: e.dma_start(out=o_loc[i * 128:(i + 1) * 128, :], in_=zt[:]), reads=["zt"], chan="zt")
        build_phase1(nc, es1, P, A1, T, x1, w1a[h], cw1a[h], sc1a[h], nw1, cst, o_loc[TW:RT, h * 128:(h + 1) * 128])
        es1.close()
        P.barrier()
    _emit_phase2(nc, P, d, NTM, None, o_all=o_loc, qsel_d=qsel_d, RT=None)
    P.finish()
    es = ExitStack()
    P.emit(es)
    es.close()
    return nc, P


def run_fused_nocc(inp, T):
    nc, P = build_fused_nocc(T)
    m1 = _phase1_inputs(inp, T)
    m2 = _phase2_inputs(inp, None, T)
    maps = []
    for core in range(8):
        b, q = core // 4, core % 4
        m = dict(m2[core])
        m["x1"] = m1[core]["x1"]
        m["nw1"] = m1[core]["nw1"]
        m["cst"] = m1[core]["cst"]
        m["w1a"] = np.stack([m1[4 * b + h]["w1"] for h in range(4)])
        m["cw1a"] = np.stack([m1[4 * b + h]["cw1"] for h in range(4)])
        m["sc1a"] = np.stack([m1[4 * b + h]["sc1"] for h in range(4)])
        qs = np.zeros((128, 4), np.float32)
        qs[:, q] = 1.0
        m["qsel"] = qs
        maps.append(m)
    res = run_bass_kernel_spmd(nc, maps, core_ids=list(range(8)))
    TC = T // 4
    out = np.zeros((2, T, D), np.float32)
    for core in range(8):
        out[core // 4, (core % 4) * TC:(core % 4 + 1) * TC] = res.results[core]["out2"]
    return out


def run_fused(inp, T):
    nc, P = build_fused_program(T)
    m1 = _phase1_inputs(inp, T)
    m2 = _phase2_inputs(inp, None, T)
    maps = []
    for core in range(8):
        m = dict(m1[core])
        m.update(m2[core])
        qs = np.zeros((128, 8), np.float32)
        qs[:, core] = 1.0
        m["qsel"] = qs
        maps.append(m)
    res = run_bass_kernel_spmd(nc, maps, core_ids=list(range(8)))
    TC = T // 4
    out = np.zeros((2, T, D), np.float32)
    for core in range(8):
        out[core // 4, (core % 4) * TC:(core % 4 + 1) * TC] = res.results[core]["out2"]
    return out


def kernel(**inputs):
    return run_fused_nocc(inputs, T_FULL)
```

```python
from collections import defaultdict
from contextlib import ExitStack

import numpy as np
import concourse.bass as bass
import concourse.mybir as mybir
from concourse.bass_utils import run_bass_kernel_spmd

F32 = mybir.dt.float32
BF16 = mybir.dt.bfloat16
AF = mybir.ActivationFunctionType
ALU = mybir.AluOpType

D = 1024
NCH = 8
EPS = 1e-6
CH = 64
DK = 128
D_IN = 5640
D_FF = 2816
NEG = -30000.0


PSUM_PREFIXES = ("psl", "ptb", "ppj", "ps_tr", "aps_tr", "bps_tr", "pbig", "bpbig", "ps_o")


class _Rec:
    def __getattr__(self, name):
        def f(*a, **k):
            self.call = (name, a, k)
            return self
        return f


class Prog:
    ENGS = ("pe", "act", "dve", "pool", "sp")

    def __init__(self, nc):
        self.nc = nc
        self.streams = {e: [] for e in self.ENGS}
        self.count = defaultdict(int)
        self.lastw = {}
        self.readers = defaultdict(list)
        self.waited = defaultdict(int)
        self.nops = 0
        self.epoch = 0
        self.pool_hold = False
        import os
        self.cut = int(os.environ["PCUT"]) if "PCUT" in os.environ else None

    def _dep(self, eng, rec):
        semkey, val = rec[0], rec[1]
        if eng == "pool" and (semkey.startswith("dma_cc@") or self.pool_hold):
            return
        if self.waited[(eng, semkey)] < val:
            self.waited[(eng, semkey)] = val
            self.streams[eng].append(("wait", semkey, val))

    def op(self, eng, fn, reads=(), writes=(), chan=None, inc_override=None):
        if self.cut is not None and self.nops >= self.cut:
            return
        isdma = chan is not None
        for k in reads:
            w = self.lastw.get(k)
            if w is not None:
                self._dep(eng, w)
            if k.startswith(PSUM_PREFIXES):
                for r in self.readers[k]:
                    if r[2] != eng:
                        self._dep(eng, r)
        for k in writes:
            w = self.lastw.get(k)
            if w is not None:
                if not (w[2] == eng == "pe" and not w[3] and not isdma):
                    self._dep(eng, w)
            for r in self.readers[k]:
                if r[2] != eng or r[3] or isdma:
                    self._dep(eng, r)
        if isdma:
            semkey, inc = "dma_%s@%d" % (chan, self.epoch), (inc_override or 16)
        else:
            semkey, inc = "%s@%d" % (eng, self.epoch), 1
        self.count[semkey] += inc
        rec = (semkey, self.count[semkey], eng, isdma)
        rec_ = _Rec()
        fn(rec_)
        self.streams[eng].append(("op", rec_.call, semkey, inc))
        for k in writes:
            self.lastw[k] = rec
            self.readers[k] = []
        for k in reads:
            self.readers[k].append(rec)
        self.nops += 1

    def barrier(self):
        for e in self.ENGS:
            for semkey, val in list(self.count.items()):
                if val:
                    self._dep(e, (semkey, val))
        self.lastw.clear()
        self.readers.clear()
        self.epoch += 1

    def finish(self):
        for semkey, val in list(self.count.items()):
            if semkey.startswith("dma_"):
                self._dep("sp", (semkey, val))
        for semkey, val in list(self.count.items()):
            if not semkey.startswith("dma_") and val:
                self._dep("sp", (semkey, val))

    def emit(self, es):
        nc = self.nc
        sems = {}
        for i, k in enumerate(sorted(self.count)):
            sems[k] = es.enter_context(nc.semaphore("s%d" % i))
        block = es.enter_context(nc.Block())
        streams = self.streams

        def run(eng_handle, items):
            for it in items:
                if it[0] == "wait":
                    eng_handle.wait_ge(sems[it[1]], it[2])
                else:
                    name, a, k = it[1]
                    getattr(eng_handle, name)(*a, **k).then_inc(sems[it[2]], it[3])

        @block.tensor
        def _(e):
            run(e, streams["pe"])

        @block.scalar
        def _(e):
            run(e, streams["act"])

        @block.vector
        def _(e):
            run(e, streams["dve"])

        @block.gpsimd
        def _(e):
            run(e, streams["pool"])

        @block.sync
        def _(e):
            run(e, streams["sp"])


class Ctx:
    _uid = [0]

    def __init__(self, nc, es, P):
        self.nc, self.es, self.P = nc, es, P
        Ctx._uid[0] += 1
        self.n = Ctx._uid[0] * 1000

    def sb(self, shape, dt=F32, name=None):
        self.n += 1
        return self.es.enter_context(self.nc.sbuf_tensor("%s_%d" % (name or "t", self.n), list(shape), dt))

    def ps(self, shape, dt=F32, name=None):
        self.n += 1
        return self.es.enter_context(self.nc.psum_tensor("%s_%d" % (name or "p", self.n), list(shape), dt))


def chunk_consts():
    j = np.arange(128)
    same = (j[:, None] // CH) == (j[None, :] // CH)
    m1 = (same & (j[:, None] <= j[None, :])).astype(np.float32)
    m2 = (same & (j[:, None] > j[None, :])).astype(np.float32)
    ident = np.eye(128, dtype=np.float32)
    ones = np.ones((128, 128), np.float32)
    cind = np.zeros((128, 128), np.float32)
    cind[:64, 0] = 1.0
    cind[64:, 1] = 1.0
    return np.concatenate([m1, m2, ident, ones, cind], axis=1)


C_M1, C_M2, C_ID, C_ONES, C_CIND = 0, 128, 256, 384, 512
NCONST = 640


def make_epsc(P, A, eng="pool"):
    epsc = A.sb([128, 2], F32, "epsc")
    P.op(eng, lambda e: e.memset(epsc[:, 0:1], D * EPS), writes=["epsc0"])
    P.op(eng, lambda e: e.memset(epsc[:, 1:2], EPS), reads=["epsc0"], writes=["epsc"])
    return epsc


def norm_block(P, epsc, x_blk, xkey, ss, rs, sskey, junk, junkkey, xn, xnkey, ps_tr, pskey, idb, hT_dst, hTkey,
               wrow=None, wkey=None):
    P.op("act", lambda e: e.activation(out=junk, in_=x_blk, func=AF.Square, accum_out=ss),
         reads=[xkey], writes=[junkkey, sskey])
    P.op("act", lambda e: e.activation(out=rs, in_=ss, func=AF.Ln, bias=epsc[:, 0:1]),
         reads=[sskey, "epsc"], writes=[sskey + "r0"])
    P.op("act", lambda e: e.activation(out=rs, in_=rs, func=AF.Exp, scale=-0.5),
         reads=[sskey + "r0"], writes=[sskey + "r"])
    if wrow is None:
        P.op("dve", lambda e: e.tensor_scalar(xn, x_blk, rs, None, ALU.mult),
             reads=[xkey, sskey + "r"], writes=[xnkey])
    else:
        P.op("dve", lambda e: e.scalar_tensor_tensor(out=xn, in0=x_blk, scalar=rs, in1=wrow,
                                                      op0=ALU.mult, op1=ALU.mult),
             reads=[xkey, sskey + "r", wkey], writes=[xnkey])
    for c in range(NCH):
        P.op("pe", lambda e, c=c: e.transpose(ps_tr[:, c, :], xn[:, c * 128:(c + 1) * 128], idb),
             reads=[xnkey, "consts_b"], writes=[pskey])
    P.op("act", lambda e: e.copy(hT_dst, ps_tr[:, :, :]), reads=[pskey], writes=[hTkey])


def build_phase1(nc, es, P, A, T, x1, w1, cw1, sc1, nw1, cst, o_out, hts=None, hts_mode=None):
    NT = T // 512
    sb, ps = A.sb, A.ps
    cf = sb([128, NCONST], F32, "cf")
    cb = sb([128, NCONST], BF16, "cb")
    P.op("sp", lambda e: e.dma_start(out=cf[:], in_=cst[:, :]), writes=["consts_f"], chan="cf")
    P.op("dve", lambda e: e.tensor_copy(cb[:], cf[:]), reads=["consts_f"], writes=["consts_b"])
    m1f, m2f = cf[:, C_M1:C_M1 + 128], cf[:, C_M2:C_M2 + 128]
    idf, onesf, cindf = cf[:, C_ID:C_ID + 128], cf[:, C_ONES:C_ONES + 128], cf[:, C_CIND:C_CIND + 2]
    idb, onesb = cb[:, C_ID:C_ID + 128], cb[:, C_ONES:C_ONES + 128]

    epsc = make_epsc(P, A)
    wf = sb([128, NCH, 386], F32, "wf")
    wb = sb([128, NCH, 386], BF16, "wb")
    nw = sb([128, NCH], F32, "nw")
    cw = sb([128, 12], F32, "cw")
    sc = sb([128, 2], F32, "sc")
    negA = sb([128, 1], F32, "negA")
    P.op("sp", lambda e: e.dma_start(out=wf[:], in_=w1.rearrange("(c p) n -> p c n", p=128)), writes=["wf"], chan="wf")
    P.op("sp", lambda e: e.dma_start(out=nw[:], in_=nw1[:, :]), writes=["nw"], chan="nw")
    P.op("sp", lambda e: e.dma_start(out=cw[:], in_=cw1[:, :]), writes=["cw"], chan="cw")
    P.op("sp", lambda e: e.dma_start(out=sc[:], in_=sc1[:, :]), writes=["sc"], chan="sc")
    for c in range(NCH):
        P.op("dve", lambda e, c=c: e.tensor_scalar(wb[:, c, :], wf[:, c, :], nw[:, c:c + 1], 32.0, ALU.mult, ALU.mult),
             reads=["wf", "nw"], writes=["wb"])
    P.op("act", lambda e: e.activation(out=negA[:], in_=sc[:, 0:1], func=AF.Exp), reads=["sc"], writes=["negA0"])
    P.op("dve", lambda e: e.tensor_scalar(negA[:], negA[:], -1.0, None, ALU.mult), reads=["negA0"], writes=["negA"])

    xt = [sb([128, 4, D], F32, "xt") for _ in range(2)]
    junk = sb([128, D], BF16, "junk")
    ss = sb([128, 8], F32, "ss")
    rs = sb([128, 8], F32, "rs")
    xn = [sb([128, D], BF16, "xn") for _ in range(2)]
    hT = [sb([128, NCH, 512], BF16, "hT") for _ in range(2)]
    cbuf = [sb([128, 3 + 512], F32, "cbuf") for _ in range(3)]
    acc = [sb([128, 512], F32, "acc") for _ in range(3)]
    sil = [sb([128, 512], F32, "sil") for _ in range(2)]
    sq = [sb([128, 512], BF16, "sq") for _ in range(2)]
    rn = [sb([128, 512], F32, "rn") for _ in range(2)]
    QT = [sb([128, 512], BF16, "QT") for _ in range(2)]
    KT = [sb([128, 512], BF16, "KT") for _ in range(2)]
    VT = [sb([128, 512], BF16, "VT") for _ in range(2)]
    bdt = [sb([128, 4, 2], F32, "bdt") for _ in range(2)]
    gsc = [sb([128, 8, 4], F32, "gsc") for _ in range(2)]
    def four(shape, dt, name):
        return [sb(shape, dt, name) for _ in range(4)]

    def eight(shape, dt, name):
        return [[sb(shape, dt, name) for _ in range(4)] for _ in range(2)]

    gM = four([128, 128], F32, "gM")
    rgc = four([128, 2], F32, "rgc")
    D1 = four([128, 128], F32, "D1")
    D2 = four([128, 128], F32, "D2")
    bg = four([128, 1], F32, "bg")
    bgK = four([128, 128], BF16, "bgK")
    Bm = four([128, 128], F32, "Bm")
    Bq = four([128, 128], F32, "Bq")
    Nq = four([128, 128], F32, "Nq")
    Rq = four([128, 128], F32, "Rq")
    Rt = four([128, 128], F32, "Rt")
    PTm = four([128, 128], F32, "PTm")
    smx = eight([128, 4], F32, "smx")
    KD = eight([128, 128], BF16, "KD")
    bV = eight([128, 128], BF16, "bV")
    TTb = eight([128, 128], BF16, "TTb")
    PT = eight([128, 128], BF16, "PT")
    nWT = eight([128, 128], BF16, "nWT")
    Ub = [sb([128, 128], BF16, "Ub") for _ in range(2)]
    pus = [sb([128, 128], F32, "pus") for _ in range(2)]
    Osb = [sb([128, 128], F32, "Osb") for _ in range(2)]
    Sf = [sb([128, 128], F32, "Sf") for _ in range(2)]
    Sb = [sb([128, 128], BF16, "Sb") for _ in range(2)]

    ps_tr = ps([128, NCH, 128], BF16, "ps_tr")
    ps_tb = ps([128, 8, 128], BF16, "ps_tb")
    ps_pj = [ps([128, 512], F32, "ps_pj") for _ in range(1)]
    ps_ch = ps([128, 4, 128], F32, "ps_ch")
    ps_sl = [ps([128, 4, 128], F32, "ps_sl") for _ in range(4)]
    pj_i = [0]

    def pjslot():
        i = pj_i[0] % len(ps_pj)
        pj_i[0] += 1
        return ps_pj[i], "ppj%d" % i


    P.op("pool", lambda e: e.memset(Sf[0][:], 0.0), writes=["Sf0"])
    P.op("pool", lambda e: e.memset(Sb[0][:], 0.0), writes=["Sb0"])
    for g in range(3):
        P.op("pool", lambda e, g=g: e.memset(cbuf[g][:, 0:3], 0.0), writes=["cbufh%d" % g])
    sidx = [0]
    chain_q = []

    def tile_level(ti):
        tp = ti % 2
        xk = "xt%d" % tp
        hk = "hT%d" % tp
        bk = "bdt%d" % tp
        qk, kk, vk = "QT%d" % tp, "KT%d" % tp, "VT%d" % tp
        G = gsc[tp]
        gk = "gsc%d" % tp
        xg, ax, ee, ll, sp_, gg, be, nbe = (G[:, i, :] for i in range(8))
        pieces = []

        def p_load():
            P.op("sp", lambda e, ti=ti, tp=tp: e.dma_start(
                out=xt[tp][:], in_=x1[ti * 512:(ti + 1) * 512, :].rearrange("(j p) d -> p j d", p=128)),
                writes=[xk], chan=xk)
        def p_norm(j):
            bp = j % 2
            norm_block(P, epsc, xt[tp][:, j, :], xk, ss[:, j + 4 * tp:j + 4 * tp + 1], rs[:, j + 4 * tp:j + 4 * tp + 1],
                       "ss%d_%d" % (tp, j), junk[:], "junk", xn[bp][:], "xn%d" % bp, ps_tr, "ps_tr", idb,
                       hT[tp][:, :, j * 128:(j + 1) * 128], hk)

        def p_hload():
            P.op("sp", lambda e: e.dma_start(out=hT[tp][:], in_=hts[ti]), writes=[hk], chan=hk)

        def p_hsave():
            P.op("sp", lambda e: e.dma_start(out=hts[ti], in_=hT[tp][:]), reads=[hk], chan="hsv%d" % tp)

        if hts_mode == "load":
            pieces.append(p_hload)
            for _ in range(5):
                pieces.append(lambda: None)
        else:
            pieces.append(p_load)
            for j in range(4):
                pieces.append(lambda j=j: p_norm(j))
            if hts_mode == "save":
                pieces.append(p_hsave)
        def p_proj(g):
            pj, pjk = pjslot()
            for c in range(NCH):
                P.op("pe", lambda e, g=g, c=c, pj=pj: e.matmul(pj[:], lhsT=wb[:, c, g * 128:(g + 1) * 128],
                                                              rhs=hT[tp][:, c, :], start=(c == 0), stop=(c == NCH - 1)),
                     reads=["wb", hk], writes=[pjk])
            P.op("act", lambda e, g=g, pj=pj: e.copy(cbuf[g][:, 3:515], pj[:]), reads=[pjk], writes=["cbufm%d" % g])
        for g in range(3):
            pieces.append(lambda g=g: p_proj(g))
        def p_bd():
            pj, pjk = pjslot()
            for j in range(4):
                for c in range(NCH):
                    P.op("pe", lambda e, j=j, c=c, pj=pj: e.matmul(pj[:, 2 * j:2 * j + 2], lhsT=hT[tp][:, c, j * 128:(j + 1) * 128],
                                                                  rhs=wb[:, c, 384:386], start=(c == 0), stop=(c == NCH - 1)),
                         reads=["wb", hk], writes=[pjk])
            P.op("dve", lambda e, pj=pj: e.tensor_copy(bdt[tp][:].rearrange("p a b -> p (a b)"), pj[:, 0:8]), reads=[pjk], writes=[bk])
        pieces.append(p_bd)
        def p_conv(g):
            ck = ["cbufh%d" % g, "cbufm%d" % g]
            ak = "acc%d" % g
            P.op("dve", lambda e, g=g: e.tensor_scalar(acc[g][:], cbuf[g][:, 0:512], cw[:, 4 * g:4 * g + 1], None, ALU.mult),
                 reads=ck + ["cw"], writes=[ak])
            for k in range(1, 4):
                P.op("dve", lambda e, g=g, k=k: e.scalar_tensor_tensor(
                    out=acc[g][:], in0=cbuf[g][:, k:k + 512], scalar=cw[:, 4 * g + k:4 * g + k + 1], in1=acc[g][:],
                    op0=ALU.mult, op1=ALU.add), reads=ck + ["cw", ak], writes=[ak])
            P.op("pool", lambda e, g=g: e.tensor_copy(cbuf[g][:, 0:3], cbuf[g][:, 512:515]),
                 reads=["cbufm%d" % g, ak], writes=["cbufh%d" % g])
        for g in range(3):
            pieces.append(lambda g=g: p_conv(g))
        def p_qkv():
            while chain_q:
                chain_q.pop(0)()
            P.op("act", lambda e: e.activation(out=VT[tp][:], in_=acc[2][:], func=AF.Silu), reads=["acc2"], writes=[vk])
            for g in range(2):
                P.op("act", lambda e, g=g: e.activation(out=sil[g][:], in_=acc[g][:], func=AF.Silu), reads=["acc%d" % g], writes=["sil%d" % g])
                P.op("act", lambda e, g=g: e.activation(out=sq[g][:], in_=sil[g][:], func=AF.Square), reads=["sil%d" % g], writes=["sq%d" % g])
                pj, pjk = pjslot()
                P.op("pe", lambda e, g=g, pj=pj: e.matmul(pj[:], lhsT=onesb, rhs=sq[g][:], start=True, stop=True),
                     reads=["consts_b", "sq%d" % g], writes=[pjk])
                P.op("act", lambda e, g=g, pj=pj: e.activation(out=rn[g][:], in_=pj[:], func=AF.Ln, bias=epsc[:, 1:2]),
                     reads=[pjk, "epsc"], writes=["rn%da" % g])
                P.op("act", lambda e, g=g: e.activation(out=rn[g][:], in_=rn[g][:], func=AF.Exp, scale=-0.5),
                     reads=["rn%da" % g], writes=["rn%d" % g])
            P.op("dve", lambda e: e.scalar_tensor_tensor(out=QT[tp][:], in0=sil[0][:], scalar=float(DK) ** -0.5, in1=rn[0][:],
                                                          op0=ALU.mult, op1=ALU.mult), reads=["sil0", "rn0"], writes=[qk])
            P.op("dve", lambda e: e.tensor_tensor(out=KT[tp][:], in0=sil[1][:], in1=rn[1][:], op=ALU.mult), reads=["sil1", "rn1"], writes=[kk])
        pieces.append(p_qkv)
        def p_gate():
            P.op("dve", lambda e: e.tensor_scalar(xg, bdt[tp][:, :, 1], sc[:, 1:2], None, ALU.add), reads=[bk, "sc"], writes=[gk + "a"])
            P.op("dve", lambda e: e.scalar_tensor_tensor(out=ax, in0=xg, scalar=-1.0, in1=xg, op0=ALU.mult, op1=ALU.max), reads=[gk + "a"], writes=[gk + "b"])
            P.op("act", lambda e: e.activation(out=ee, in_=ax, func=AF.Exp, scale=-1.0), reads=[gk + "b"], writes=[gk + "c"])
            P.op("act", lambda e: e.activation(out=ll, in_=ee, func=AF.Ln, bias=1.0), reads=[gk + "c"], writes=[gk + "d"])
            P.op("dve", lambda e: e.scalar_tensor_tensor(out=sp_, in0=xg, scalar=0.0, in1=ll, op0=ALU.max, op1=ALU.add),
                 reads=[gk + "a", gk + "d"], writes=[gk + "e"])
            P.op("dve", lambda e: e.tensor_scalar(gg, sp_, negA[:, 0:1], None, ALU.mult), reads=[gk + "e", "negA"], writes=[gk + "g"])
            P.op("act", lambda e: e.activation(out=be, in_=bdt[tp][:, :, 0], func=AF.Sigmoid), reads=[bk], writes=[gk + "be"])
            P.op("dve", lambda e: e.tensor_scalar(nbe, be, -1.0, None, ALU.mult), reads=[gk + "be"], writes=[gk + "nb"])


        pieces.append(p_gate)
        return pieces

    def block_level(ti):
        tp = ti % 2
        qk, kk, vk = "QT%d" % tp, "KT%d" % tp, "VT%d" % tp
        G = gsc[tp]
        gk = "gsc%d" % tp
        xg, ax, ee, ll, sp_, gg, be, nbe = (G[:, i, :] for i in range(8))
        def bk(j):
            return ps_sl[j], "psl%d" % j

        def hop():
            if chain_q:
                chain_q.pop(0)()

        def stage_done():
            hop()
            if pre_q:
                pre_q.pop(0)()

        J = range(4)
        sfx = ["_%d" % j for j in J]
        csl = [slice(j * 128, (j + 1) * 128) for j in J]
        g_ = [gg[:, j:j + 1] for j in J]
        be_ = [be[:, j:j + 1] for j in J]
        nbe_ = [nbe[:, j:j + 1] for j in J]
        ck = ["_%d_%d" % (tp, j) for j in J]
        for j in J:
            P.op("dve", lambda e: e.tensor_scalar(gM[j][:], m1f, g_[j], None, ALU.mult), reads=["consts_f", gk + "g"], writes=["gM" + sfx[j]])
            P.op("dve", lambda e: e.tensor_scalar(rgc[j][:], cindf, g_[j], None, ALU.mult), reads=["consts_f", gk + "g"], writes=["rgc" + sfx[j]])
        hop()
        for j in J:
            b_, bkk = bk(j)
            P.op("pe", lambda e: e.matmul(b_[:, 0, :], lhsT=gM[j][:], rhs=m2f, start=True, stop=True), reads=["gM" + sfx[j], "consts_f"], writes=[bkk])
            P.op("pe", lambda e: e.matmul(b_[:, 1, :], lhsT=m2f, rhs=gM[j][:], start=True, stop=True), reads=["gM" + sfx[j], "consts_f"], writes=[bkk])
            P.op("pe", lambda e: e.matmul(b_[:, 2, 0:1], lhsT=m1f, rhs=g_[j], start=True, stop=True), reads=[gk + "g", "consts_f"], writes=[bkk])
            P.op("pe", lambda e: e.matmul(b_[:, 2, 1:2], lhsT=m2f, rhs=g_[j], start=True, stop=True), reads=[gk + "g", "consts_f"], writes=[bkk])
            P.op("pe", lambda e: e.matmul(b_[:, 2, 2:4], lhsT=onesf, rhs=rgc[j][:], start=True, stop=True), reads=["rgc" + sfx[j], "consts_f"], writes=[bkk])
        hop()
        for j in J:
            b_, bkk = bk(j)
            P.op("act", lambda e: e.activation(out=D1[j][:], in_=b_[:, 0, :], func=AF.Exp), reads=[bkk], writes=["D1" + sfx[j]])
            P.op("act", lambda e: e.activation(out=D2[j][:], in_=b_[:, 1, :], func=AF.Exp), reads=[bkk], writes=["D2" + sfx[j]])
            P.op("act", lambda e: e.activation(out=smx[tp][j][:], in_=b_[:, 2, 0:4], func=AF.Exp), reads=[bkk], writes=["smx" + ck[j]])
        hop()
        for j in J:
            P.op("dve", lambda e: e.tensor_tensor(out=bg[j][:], in0=be_[j], in1=smx[tp][j][:, 0:1], op=ALU.mult),
                 reads=[gk + "be", "smx" + ck[j]], writes=["bg" + sfx[j]])
            P.op("pool", lambda e: e.tensor_tensor(out=PTm[j][:], in0=D2[j][:], in1=m1f, op=ALU.mult), reads=["D2" + sfx[j], "consts_f"], writes=["PTm" + sfx[j]])
        hop()
        stage_done()
        for j in J:
            P.op("pe", lambda e: e.transpose(ps_tb[:, 2 * j, :], KT[tp][:, csl[j]], idb), reads=[kk, "consts_b"], writes=["ptb"])
            P.op("pe", lambda e: e.transpose(ps_tb[:, 2 * j + 1, :], VT[tp][:, csl[j]], idb), reads=[vk, "consts_b"], writes=["ptb"])
        hop()
        for j in J:
            P.op("dve", lambda e: e.tensor_scalar(bgK[j][:], ps_tb[:, 2 * j, :], bg[j][:, 0:1], None, ALU.mult), reads=["ptb", "bg" + sfx[j]], writes=["bgK" + sfx[j]])
        hop()
        for j in J:
            P.op("act", lambda e: e.activation(out=KD[tp][j][:], in_=ps_tb[:, 2 * j, :], func=AF.Copy, scale=smx[tp][j][:, 1:2]),
                 reads=["ptb", "smx" + ck[j]], writes=["KD" + ck[j]])
            P.op("act", lambda e: e.activation(out=bV[tp][j][:], in_=ps_tb[:, 2 * j + 1, :], func=AF.Copy, scale=be_[j]),
                 reads=["ptb", gk + "be"], writes=["bV" + ck[j]])
        hop()
        stage_done()
        for j in J:
            b_, bkk = bk(j)
            P.op("pe", lambda e: e.matmul(b_[:, 0, :], lhsT=KT[tp][:, csl[j]], rhs=KT[tp][:, csl[j]], start=True, stop=True), reads=[kk], writes=[bkk])
            P.op("pe", lambda e: e.matmul(b_[:, 1, :], lhsT=KT[tp][:, csl[j]], rhs=QT[tp][:, csl[j]], start=True, stop=True), reads=[kk, qk], writes=[bkk])
        hop()
        for j in J:
            b_, bkk = bk(j)
            P.op("dve", lambda e: e.tensor_tensor(out=Bm[j][:], in0=b_[:, 0, :], in1=D1[j][:], op=ALU.mult), reads=[bkk, "D1" + sfx[j]], writes=["Bm" + sfx[j]])
            P.op("dve", lambda e: e.tensor_tensor(out=PT[tp][j][:], in0=b_[:, 1, :], in1=PTm[j][:], op=ALU.mult), reads=[bkk, "PTm" + sfx[j]], writes=["PT" + ck[j]])
            P.op("dve", lambda e: e.scalar_tensor_tensor(out=Bq[j][:], in0=Bm[j][:], scalar=nbe_[j], in1=m2f, op0=ALU.mult, op1=ALU.mult),
                 reads=["Bm" + sfx[j], gk + "nb", "consts_f"], writes=["B" + sfx[j]])
        hop()
        stage_done()
        for j in J:
            b_, bkk = bk(j)
            P.op("pe", lambda e: e.transpose(b_[:, 2, :], Bq[j][:], idf), reads=["B" + sfx[j], "consts_f"], writes=[bkk])
        hop()
        for j in J:
            b_, bkk = bk(j)
            P.op("act", lambda e: e.copy(Nq[j][:], b_[:, 2, :]), reads=[bkk], writes=["N" + sfx[j]])
            P.op("pool", lambda e: e.tensor_tensor(out=Rt[j][:], in0=Bq[j][:], in1=idf, op=ALU.add), reads=["B" + sfx[j], "consts_f"], writes=["Rt" + sfx[j]])
        hop()
        for j in J:
            P.op("dve", lambda e: e.tensor_tensor(out=Rq[j][:], in0=Nq[j][:], in1=idf, op=ALU.add), reads=["N" + sfx[j], "consts_f"], writes=["R" + sfx[j]])
        hop()
        stage_done()
        for lvl in range(5):
            last = lvl == 4
            for j in J:
                b_, bkk = bk(j)
                P.op("pe", lambda e: e.matmul(b_[:, 0, :], lhsT=Bq[j][:], rhs=Nq[j][:], start=True, stop=True), reads=["B" + sfx[j], "N" + sfx[j]], writes=[bkk])
                if not last:
                    P.op("pe", lambda e: e.matmul(b_[:, 1, :], lhsT=Nq[j][:], rhs=Bq[j][:], start=True, stop=True), reads=["B" + sfx[j], "N" + sfx[j]], writes=[bkk])
            hop()
            for j in J:
                b_, bkk = bk(j)
                P.op("act", lambda e: e.copy(Nq[j][:], b_[:, 0, :]), reads=[bkk], writes=["N" + sfx[j]])
                if not last:
                    P.op("act", lambda e: e.copy(Bq[j][:], b_[:, 1, :]), reads=[bkk], writes=["B" + sfx[j]])
            hop()
            stage_done()
            for j in J:
                b_, bkk = bk(j)
                P.op("pe", lambda e: e.matmul(b_[:, 2, :], lhsT=Rt[j][:], rhs=Nq[j][:], start=True, stop=True), reads=["Rt" + sfx[j], "N" + sfx[j]], writes=[bkk])
                if not last:
                    P.op("pe", lambda e: e.matmul(b_[:, 3, :], lhsT=Rq[j][:], rhs=Bq[j][:], start=True, stop=True), reads=["R" + sfx[j], "B" + sfx[j]], writes=[bkk])
            hop()
            for j in J:
                b_, bkk = bk(j)
                if not last:
                    P.op("dve", lambda e: e.tensor_tensor(out=Rq[j][:], in0=b_[:, 2, :], in1=Rq[j][:], op=ALU.add), reads=[bkk, "R" + sfx[j]], writes=["R" + sfx[j]])
                    P.op("dve", lambda e: e.tensor_tensor(out=Rt[j][:], in0=b_[:, 3, :], in1=Rt[j][:], op=ALU.add), reads=[bkk, "Rt" + sfx[j]], writes=["Rt" + sfx[j]])
                else:
                    P.op("dve", lambda e: e.tensor_tensor(out=TTb[tp][j][:], in0=b_[:, 2, :], in1=Rq[j][:], op=ALU.add), reads=[bkk, "R" + sfx[j]], writes=["TTb" + ck[j]])
            hop()
            stage_done()
        for j in J:
            b_, bkk = bk(j)
            P.op("pe", lambda e: e.matmul(b_[:, 0, :], lhsT=bgK[j][:], rhs=TTb[tp][j][:], start=True, stop=True), reads=["bgK" + sfx[j], "TTb" + ck[j]], writes=[bkk])
        hop()
        for j in J:
            b_, bkk = bk(j)
            P.op("act", lambda e: e.mul(nWT[tp][j][:], b_[:, 0, :], -1.0), reads=[bkk], writes=["nWT" + ck[j]])
        hop()
        stage_done()
        while chain_q:
            chain_q.pop(0)()
        while pre_q:
            pre_q.pop(0)()

        def chunk_hops(j, c, tp=tp, qk=qk, ck=ck, ti=ti, csl=csl):
            r = slice(64 * c, 64 * c + 64)
            si = sidx[0]
            so, sn_ = si % 2, (si + 1) % 2
            sidx[0] += 1
            o2 = j % 2
            u, qs, sn, pu = ps_ch[:, 0, :], ps_ch[:, 1, :], ps_ch[:, 2, :], ps_ch[:, 3, :]

            def h1():
                P.op("pe", lambda e: e.matmul(u, lhsT=TTb[tp][j][r, :], rhs=bV[tp][j][r, :], start=True, stop=False),
                     reads=["TTb" + ck[j], "bV" + ck[j]], writes=["ps_ch"])
                P.op("pe", lambda e: e.matmul(u, lhsT=nWT[tp][j][:], rhs=Sb[so][:], start=False, stop=True),
                     reads=["nWT" + ck[j], "Sb%d" % so], writes=["ps_ch"])
                P.op("pe", lambda e: e.matmul(qs, lhsT=QT[tp][:, csl[j]], rhs=Sb[so][:], start=True, stop=True),
                     reads=[qk, "Sb%d" % so], writes=["ps_ch"])

            def h2():
                P.op("dve", lambda e: e.tensor_copy(Ub[o2][r, :], u[r, :]), reads=["ps_ch"], writes=["Ub%d_%d" % (o2, c)])

            def h3():
                P.op("pe", lambda e: e.matmul(sn, lhsT=KD[tp][j][r, :], rhs=Ub[o2][r, :], start=True, stop=True),
                     reads=["KD" + ck[j], "Ub%d_%d" % (o2, c)], writes=["ps_ch"])
                P.op("pe", lambda e: e.matmul(pu, lhsT=PT[tp][j][r, :], rhs=Ub[o2][r, :], start=True, stop=True),
                     reads=["PT" + ck[j], "Ub%d_%d" % (o2, c)], writes=["ps_ch"])

            def h4():
                P.op("dve", lambda e: e.scalar_tensor_tensor(out=Sf[sn_][:], in0=Sf[so][:], scalar=smx[tp][j][:, 2 + c:3 + c], in1=sn,
                                                             op0=ALU.mult, op1=ALU.add), reads=["Sf%d" % so, "smx" + ck[j], "ps_ch"], writes=["Sf%d" % sn_])

            def h5():
                P.op("act", lambda e: e.copy(Sb[sn_][:], Sf[sn_][:]), reads=["Sf%d" % sn_], writes=["Sb%d" % sn_])
                P.op("dve", lambda e: e.tensor_copy(pus[o2][r, :], pu[r, :]), reads=["ps_ch"], writes=["pus%d_%d" % (o2, c)])
                P.op("dve", lambda e: e.scalar_tensor_tensor(out=Osb[o2][r, :], in0=qs[r, :], scalar=smx[tp][j][r, 0:1], in1=pus[o2][r, :],
                                                             op0=ALU.mult, op1=ALU.add), reads=["ps_ch", "smx" + ck[j], "pus%d_%d" % (o2, c)],
                     writes=["Osb%d_%d" % (o2, c)])
                if c == 1:
                    blk = ti * 4 + j
                    P.op("sp", lambda e: e.dma_start(out=o_out[blk * 128:(blk + 1) * 128, :], in_=Osb[o2][:]),
                         reads=["Osb%d_0" % o2, "Osb%d_1" % o2], chan="ost%d" % o2)
            return [h1, h2, h3, h4, h5]

        for j in J:
            for c in range(2):
                chain_q.extend(chunk_hops(j, c))
    pre_q = []
    for f in tile_level(0):
        f()
    for ti in range(NT):
        if ti + 1 < NT:
            pre_q.extend(tile_level(ti + 1))
        block_level(ti)
    while chain_q:
        chain_q.pop(0)()


def prep_weight(P, stg, stgkey, src2d, n_c, ncols, dst, dst_col0, scale_fn, dkey, skeys, cnt, dst_c0=0):
    pw = min(2048 // n_c, ncols)
    for col in range(0, ncols, pw):
        w = min(pw, ncols - col)
        b = cnt[0] % len(stg)
        cnt[0] += 1
        sv = stg[b][:, 0:n_c * w].rearrange("p (c n) -> p c n", c=n_c)
        k = stgkey + str(b)
        P.op("sp", lambda e: e.dma_start(out=sv, in_=src2d[:, col:col + w].rearrange("(c p) n -> p c n", p=128)),
             writes=[k], chan=k)
        dv = dst[:, dst_c0:dst_c0 + n_c, dst_col0 + col:dst_col0 + col + w]
        if scale_fn is None:
            eng = "act" if (cnt[0] % 2) else "dve"
            if eng == "act":
                P.op("act", lambda e: e.copy(dv, sv), reads=[k], writes=[dkey])
            else:
                P.op("dve", lambda e: e.tensor_copy(dv, sv), reads=[k], writes=[dkey])
        else:
            for c in range(n_c):
                sc_ = scale_fn(c)
                if c % 2:
                    P.op("act", lambda e: e.activation(out=dst[:, c, dst_col0 + col:dst_col0 + col + w], in_=sv[:, c, :],
                                                       func=AF.Copy, scale=sc_), reads=[k] + skeys, writes=[dkey])
                else:
                    P.op("dve", lambda e: e.tensor_scalar(dst[:, c, dst_col0 + col:dst_col0 + col + w], sv[:, c, :], sc_, None, ALU.mult),
                         reads=[k] + skeys, writes=[dkey])


TB = 2
TW = TB * 128


def load_consts(P, A, cst, pre):
    cf = A.sb([128, NCONST], F32, "cf")
    cb = A.sb([128, NCONST], BF16, "cb")
    P.op("sp", lambda e: e.dma_start(out=cf[:], in_=cst[:, :]), writes=[pre + "consts_f"], chan=pre + "cf")
    P.op("dve", lambda e: e.tensor_copy(cb[:], cf[:]), reads=[pre + "consts_f"], writes=["consts_b"])
    return cf, cb


def build_phase2a(nc, P, A, NTM, x2, oa2, validc, w_in, w_ba, w_bb, w_out, nwm_d, gnw_d, biasT_d, cst, xmid,
                  o_all=None, qsel_d=None, RT=None):
    NT2 = NTM + 3
    TC = NTM * TW
    sb, ps = A.sb, A.ps
    cf, cb = load_consts(P, A, cst, "a")
    idb = cb[:, C_ID:C_ID + 128]
    epsc = make_epsc(P, A, "dve")
    nwm = sb([128, NCH], F32, "nwm")
    gnw = sb([128, 1], F32, "gnw")
    P.op("sp", lambda e: e.dma_start(out=nwm[:], in_=nwm_d[:, :]), writes=["nwm0"], chan="nwm")
    P.op("sp", lambda e: e.dma_start(out=gnw[:], in_=gnw_d[:, :]), writes=["gnw"], chan="gnw")
    P.op("dve", lambda e: e.tensor_scalar(nwm[:], nwm[:], 32.0, None, ALU.mult), reads=["nwm0"], writes=["nwm"])
    Wi = sb([128, NCH, 4096], BF16, "Wi")
    WbA = sb([128, 4, 1024], BF16, "WbA")
    WbB = sb([128, 4, 1024], BF16, "WbB")
    Wo = sb([128, NCH, 1024], BF16, "Wo")
    es_stg = ExitStack()
    stg = [es_stg.enter_context(nc.sbuf_tensor("astg%d" % i, [128, 2048], F32)) for i in range(2)]
    cnt = [0]
    prep_weight(P, stg, "astg", w_in[:, 1536:2048], 8, 512, Wi, 0, lambda c: nwm[:, c:c + 1], "Wi", ["nwm"], cnt)
    prep_weight(P, stg, "astg", w_in[:, 2056:5640], 8, 3584, Wi, 512, lambda c: nwm[:, c:c + 1], "Wi", ["nwm"], cnt)
    prep_weight(P, stg, "astg", w_ba, 4, 1024, WbA, 0, lambda c: gnw[:, 0:1], "WbA", ["gnw"], cnt)
    prep_weight(P, stg, "astg", w_bb, 4, 1024, WbB, 0, None, "WbB", [], cnt)
    prep_weight(P, stg, "astg", w_out, 8, 1024, Wo, 0, None, "Wo", [], cnt)
    es_stg.close()
    P.barrier()
    biasT = sb([128, 8, 640], F32, "biasT")
    P.op("sp", lambda e: e.dma_start(out=biasT[:], in_=biasT_d[:, :, :]), writes=["biasT"], chan="biasT")
    valid = sb([128, NT2 * TB], F32, "valid")
    P.op("sp", lambda e: e.dma_start(out=valid[:], in_=validc[:, :]), writes=["valid"], chan="valid")
    ones8 = sb([128, 8, 1], F32, "ones8")
    P.op("dve", lambda e: e.memset(ones8[:], 1.0), writes=["ones8"])

    xt = [sb([128, TB, D], F32, "xt") for _ in range(2)]
    junk = sb([128, D], BF16, "junk")
    ss = sb([128, 8], F32, "ss")
    rs = sb([128, 8], F32, "rs")
    xn = [sb([128, D], BF16, "xn") for _ in range(2)]
    hT = sb([128, NCH, TW], BF16, "hT")
    KTb = sb([128, 4, 8 * 128], BF16, "KTb")
    Vaug = sb([128, 8, 8, 65], BF16, "Vaug")
    QTb = sb([128, 4, TW], BF16, "QTb")
    zs = sb([128, TB, 512], F32, "zs")
    oat = sb([128, TB, 512], F32, "oat")
    cands = None
    if o_all is not None:
        cand = sb([128, 4, 512], F32, "cand")
        if RT is None:
            cands = [(lambda r, q_=q_: o_all[q_ * TC + r:q_ * TC + r + 128, :].rearrange("p (h d) -> p h d", h=4)) for q_ in range(4)]
        else:
            o_alls, chunks, CR = o_all
            views = [a.rearrange("(r t) d -> t r d", r=8) for a in o_alls]

            def cand_ap(row, b_):
                i, off = row // CR, row % CR
                return views[i][off:off + 128, 4 * b_:4 * b_ + 4, :]
            cands = [(lambda r, b_=c_ // 4, q_=c_ % 4: cand_ap(q_ * TC + r, b_)) for c_ in range(8)]
        qsel = sb([128, len(cands)], F32, "qsel")
        P.op("sp", lambda e: e.dma_start(out=qsel[:], in_=qsel_d[:, :]), writes=["qsel"], chan="qsel")
    ssa = sb([128, 4], F32, "ssa")
    ra = sb([128, 4], F32, "ra")
    oan = sb([128, 512], BF16, "oan")
    oaT = sb([128, 4, TW], BF16, "oaT")
    ob = sb([128, 512], BF16, "ob")
    obT = sb([128, 4, TW], BF16, "obT")
    scs = [sb([128, 640], F32, "scs")] * 2
    PTb = [sb([128, 640], BF16, "PTb") for _ in range(2)]
    rden = sb([128, 8], F32, "rden")
    sg = [sb([128, 2 * TW], F32, "sg") for _ in range(2)]
    tt_ = [sb([128, 2 * TW], F32, "tt") for _ in range(2)]
    mixT = sb([128, NCH, TW], BF16, "mixT")

    ps_tr = ps([128, NCH, 128], BF16, "ps_tr")
    pbig = [ps([128, 512], F32, "pbig") for _ in range(5)]
    ps_o = [ps([128, 4, 65], F32, "ps_o") for _ in range(2)]
    bi = [0]

    def big():
        i = bi[0] % 5
        bi[0] += 1
        return pbig[i], "pbig%d" % i

    scale_q = 64.0 ** -0.5
    for tt in range(NT2):
        tp = tt % 2
        xk = "axt%d" % tp
        P.op("sp", lambda e: e.dma_start(out=xt[tp][:], in_=x2[tt * TW:(tt + 1) * TW, :].rearrange("(j p) d -> p j d", p=128)),
             writes=[xk], chan=xk)
        for j in range(TB):
            norm_block(P, epsc, xt[tp][:, j, :], xk, ss[:, j:j + 1], rs[:, j:j + 1], "ass%d" % j, junk[:], "ajunk",
                       xn[j % 2][:], "axn%d" % (j % 2), ps_tr, "aps_tr", idb, hT[:, :, j * 128:(j + 1) * 128], "ahT")
        ring0 = (tt * TB) % 8
        for m in range(4):
            pb_, pk = big()
            for c in range(NCH):
                P.op("pe", lambda e: e.matmul(pb_[:, 0:TW], lhsT=Wi[:, c, 1024 + m * 128:1024 + (m + 1) * 128], rhs=hT[:, c, :],
                                              start=(c == 0), stop=(c == NCH - 1)), reads=["Wi", "ahT"], writes=[pk])
            P.op("act", lambda e: e.copy(KTb[:, m, ring0 * 128:ring0 * 128 + TW], pb_[:, 0:TW]), reads=[pk], writes=["KTb"])
        for j in range(TB):
            slot = ring0 + j
            pb_, pk = big()
            for c in range(NCH):
                P.op("pe", lambda e: e.matmul(pb_[:, :], lhsT=hT[:, c, j * 128:(j + 1) * 128], rhs=Wi[:, c, 1536:2048],
                                              start=(c == 0), stop=(c == NCH - 1)), reads=["Wi", "ahT"], writes=[pk])
            P.op("dve", lambda e: e.tensor_copy(Vaug[:, slot, :, 0:64], pb_[:, :].rearrange("p (h d) -> p h d", h=8)),
                 reads=[pk], writes=["Vaug"])
            P.op("act", lambda e: e.activation(out=Vaug[:, slot, :, 64:65], in_=ones8[:], func=AF.Copy,
                                               scale=valid[:, tt * TB + j:tt * TB + j + 1]),
                 reads=["ones8", "valid"], writes=["Vaug"])
        if tt < 2:
            continue
        for m in range(4):
            pb_, pk = big()
            for c in range(NCH):
                P.op("pe", lambda e: e.matmul(pb_[:, 0:TW], lhsT=Wi[:, c, 512 + m * 128:512 + (m + 1) * 128], rhs=hT[:, c, :],
                                              start=(c == 0), stop=(c == NCH - 1)), reads=["Wi", "ahT"], writes=[pk])
            P.op("act", lambda e: e.mul(QTb[:, m, :], pb_[:, 0:TW], scale_q), reads=[pk], writes=["QTb"])
        for j in range(TB):
            pb_, pk = big()
            for c in range(NCH):
                P.op("pe", lambda e: e.matmul(pb_[:, :], lhsT=hT[:, c, j * 128:(j + 1) * 128], rhs=Wi[:, c, 0:512],
                                              start=(c == 0), stop=(c == NCH - 1)), reads=["Wi", "ahT"], writes=[pk])
            P.op("act", lambda e: e.activation(out=zs[:, j, :], in_=pb_[:, :], func=AF.Silu), reads=[pk], writes=["zs%d" % j])
        if o_all is None:
            P.op("sp", lambda e: e.dma_start(out=oat[:], in_=oa2[(tt - 2) * TW:(tt - 1) * TW, :].rearrange("(j p) d -> p j d", p=128)),
                 writes=["oat"], chan="oat")
        else:
            for j in range(TB):
                for cc_, cf_ in enumerate(cands):
                    k = cc_ % 4
                    ck_ = "cand_%d" % k
                    P.op("sp", lambda e: e.dma_start(out=cand[:, k, :].rearrange("p (h d) -> p h d", h=4),
                                                     in_=cf_((tt - 2) * TW + j * 128)),
                         reads=["oall"], writes=[ck_], chan=ck_)
                    if cc_ == 0:
                        P.op("dve", lambda e: e.tensor_scalar(oat[:, j, :], cand[:, k, :], qsel[:, cc_:cc_ + 1], None, ALU.mult),
                             reads=[ck_, "qsel"], writes=["oat"])
                    else:
                        P.op("dve", lambda e: e.scalar_tensor_tensor(out=oat[:, j, :], in0=cand[:, k, :], scalar=qsel[:, cc_:cc_ + 1],
                                                                     in1=oat[:, j, :], op0=ALU.mult, op1=ALU.add),
                             reads=[ck_, "qsel", "oat"], writes=["oat"])
        for j in range(TB):
            g = tt * TB + j
            for h in range(8):
                m, r = h // 2, slice(64 * (h % 2), 64 * (h % 2) + 64)
                p1, p1k = big()
                p2, p2k = big()
                for kb in range(5):
                    slot = (g - 4 + kb) % 8
                    dst = p1[:, kb * 128:(kb + 1) * 128] if kb < 4 else p2[:, 0:128]
                    P.op("pe", lambda e: e.matmul(dst, lhsT=KTb[r, m, slot * 128:(slot + 1) * 128], rhs=QTb[r, m, j * 128:(j + 1) * 128],
                                                  start=True, stop=True), reads=["KTb", "QTb"], writes=[p1k if kb < 4 else p2k])
                sp_ = h % 2
                P.op("dve", lambda e: e.tensor_tensor(out=scs[sp_][:, 0:512], in0=p1[:, :], in1=biasT[:, h, 0:512], op=ALU.add),
                     reads=[p1k, "biasT"], writes=["scsa"])
                P.op("dve", lambda e: e.tensor_tensor(out=scs[sp_][:, 512:640], in0=p2[:, 0:128], in1=biasT[:, h, 512:640], op=ALU.add),
                     reads=[p2k, "biasT"], writes=["scsb"])
                P.op("act", lambda e: e.activation(out=PTb[sp_][:], in_=scs[sp_][:], func=AF.Exp),
                     reads=["scsa", "scsb"], writes=["PTb%d" % sp_])
                for kb in range(5):
                    slot = (g - 4 + kb) % 8
                    P.op("pe", lambda e: e.matmul(ps_o[h // 4][:, h % 4, :], lhsT=PTb[sp_][:, kb * 128:(kb + 1) * 128],
                                                  rhs=Vaug[:, slot, h, :], start=(kb == 0), stop=(kb == 4)),
                         reads=["PTb%d" % sp_, "Vaug"], writes=["ps_o%d" % (h // 4)])
            for hg in range(2):
                P.op("dve", lambda e: e.tensor_scalar(rden[:, hg * 4:hg * 4 + 4], ps_o[hg][:, :, 64], 1e-30, None, ALU.add),
                     reads=["ps_o%d" % hg], writes=["rden%da" % hg])
                P.op("dve", lambda e: e.reciprocal(rden[:, hg * 4:hg * 4 + 4], rden[:, hg * 4:hg * 4 + 4]),
                     reads=["rden%da" % hg], writes=["rden%d" % hg])
            for h in range(8):
                P.op("act", lambda e: e.activation(out=ob[:, h * 64:(h + 1) * 64], in_=ps_o[h // 4][:, h % 4, 0:64], func=AF.Copy,
                                                   scale=rden[:, h:h + 1]), reads=["ps_o%d" % (h // 4), "rden%d" % (h // 4)], writes=["ob"])
            for c in range(4):
                P.op("pe", lambda e: e.transpose(ps_tr[:, c, :], ob[:, c * 128:(c + 1) * 128], idb), reads=["ob", "consts_b"], writes=["aps_tr"])
            P.op("act", lambda e: e.copy(obT[:, :, j * 128:(j + 1) * 128], ps_tr[:, 0:4, :]), reads=["aps_tr"], writes=["obT"])
            for hh in range(4):
                P.op("act", lambda e: e.activation(out=junk[:, 0:128], in_=oat[:, j, hh * 128:(hh + 1) * 128], func=AF.Square,
                                                   accum_out=ssa[:, hh:hh + 1]), reads=["oat"], writes=["ajunk", "ssa"])
            P.op("act", lambda e: e.activation(out=ra[:], in_=ssa[:], func=AF.Ln, scale=1.0 / 128.0, bias=epsc[:, 1:2]),
                 reads=["ssa", "epsc"], writes=["ra0"])
            P.op("act", lambda e: e.activation(out=ra[:], in_=ra[:], func=AF.Exp, scale=-0.5), reads=["ra0"], writes=["ra"])
            for hh in range(4):
                P.op("dve", lambda e: e.scalar_tensor_tensor(out=oan[:, hh * 128:(hh + 1) * 128], in0=oat[:, j, hh * 128:(hh + 1) * 128],
                                                             scalar=ra[:, hh:hh + 1], in1=zs[:, j, hh * 128:(hh + 1) * 128],
                                                             op0=ALU.mult, op1=ALU.mult), reads=["oat", "ra", "zs%d" % j], writes=["oan"])
            for c in range(4):
                P.op("pe", lambda e: e.transpose(ps_tr[:, 4 + c, :], oan[:, c * 128:(c + 1) * 128], idb), reads=["oan", "consts_b"], writes=["aps_tr"])
            P.op("act", lambda e: e.copy(oaT[:, :, j * 128:(j + 1) * 128], ps_tr[:, 4:8, :]), reads=["aps_tr"], writes=["oaT"])
        for mo in range(8):
            py, pyk = big()
            pg, pgk = big()
            for half, (Wb, src, skey) in enumerate(((WbA, oaT, "oaT"), (WbB, obT, "obT"))):
                for c in range(4):
                    P.op("pe", lambda e: e.matmul(py[:, half * TW:(half + 1) * TW], lhsT=Wb[:, c, mo * 128:(mo + 1) * 128], rhs=src[:, c, :],
                                                  start=(c == 0), stop=(c == 3)), reads=["WbA", "WbB", skey], writes=[pyk])
            for half in range(2):
                col0 = 2048 + half * 1024 + mo * 128
                for c in range(NCH):
                    P.op("pe", lambda e: e.matmul(pg[:, half * TW:(half + 1) * TW], lhsT=Wi[:, c, col0:col0 + 128], rhs=hT[:, c, :],
                                                  start=(c == 0), stop=(c == NCH - 1)), reads=["Wi", "ahT"], writes=[pgk])
            q2 = mo % 2
            P.op("act", lambda e: e.activation(out=sg[q2][:], in_=pg[:, :], func=AF.Sigmoid), reads=[pgk], writes=["sg%d" % q2])
            P.op("dve", lambda e: e.tensor_tensor(out=tt_[q2][:], in0=py[:, :], in1=sg[q2][:], op=ALU.mult),
                 reads=[pyk, "sg%d" % q2], writes=["tt%d" % q2])
            P.op("dve", lambda e: e.tensor_tensor(out=mixT[:, mo, :], in0=tt_[q2][:, 0:TW], in1=tt_[q2][:, TW:2 * TW], op=ALU.add),
                 reads=["tt%d" % q2], writes=["mixT"])
        for j in range(TB):
            for half in range(2):
                po, pok = big()
                for c in range(NCH):
                    P.op("pe", lambda e: e.matmul(po[:, :], lhsT=mixT[:, c, j * 128:(j + 1) * 128], rhs=Wo[:, c, half * 512:(half + 1) * 512],
                                                  start=(c == 0), stop=(c == NCH - 1)), reads=["mixT", "Wo"], writes=[pok])
                P.op("dve", lambda e: e.tensor_tensor(out=xt[tp][:, j, half * 512:(half + 1) * 512], in0=po[:, :],
                                                      in1=xt[tp][:, j, half * 512:(half + 1) * 512], op=ALU.add), reads=[pok, xk], writes=[xk])
        P.op("sp", lambda e: e.dma_start(out=xmid[(tt - 2) * TW:(tt - 1) * TW, :].rearrange("(j p) d -> p j d", p=128), in_=xt[tp][:]),
             reads=[xk], writes=["xmid_d"], chan="xmst%d" % tp)


def build_phase2b(nc, P, A, NTM, xmid, w_up, w_down, nwf_d, cfw_d, cfb_d, wfin_d, cst, out2):
    sb, ps = A.sb, A.ps
    cf, cb = load_consts(P, A, cst, "b")
    idb = cb[:, C_ID:C_ID + 128]
    epsc = make_epsc(P, A)
    nwf = sb([128, NCH], F32, "nwf")
    P.op("sp", lambda e: e.dma_start(out=nwf[:], in_=nwf_d[:, :]), writes=["nwf0"], chan="nwf")
    P.op("dve", lambda e: e.tensor_scalar(nwf[:], nwf[:], 32.0, None, ALU.mult), reads=["nwf0"], writes=["nwf"])
    cfw = sb([128, 44, 3], F32, "cfw")
    cfb = sb([128, 44], F32, "cfb")
    wfb = sb([128, D], F32, "wfb")
    P.op("sp", lambda e: e.dma_start(out=cfw[:], in_=cfw_d[:, :, :]), writes=["cfw"], chan="cfw")
    P.op("sp", lambda e: e.dma_start(out=cfb[:], in_=cfb_d[:, :]), writes=["cfb"], chan="cfb")
    P.op("sp", lambda e: e.dma_start(out=wfb[:], in_=wfin_d[:, :]), writes=["wfb0"], chan="wfb")
    P.op("pool", lambda e: e.tensor_scalar(wfb[:], wfb[:], 32.0, None, ALU.mult), reads=["wfb0"], writes=["wfb"])
    Wu = sb([128, NCH, 2 * D_FF], BF16, "Wu")
    Wd = sb([128, 22, D], BF16, "Wd")
    es_stg = ExitStack()
    stg = [es_stg.enter_context(nc.sbuf_tensor("bstg%d" % i, [128, 2048], F32)) for i in range(2)]
    cnt = [0]
    prep_weight(P, stg, "bstg", w_up, 8, 2 * D_FF, Wu, 0, lambda c: nwf[:, c:c + 1], "Wu", ["nwf"], cnt)
    prep_weight(P, stg, "bstg", w_down[0:1408, :], 11, D, Wd, 0, None, "Wd", [], cnt, dst_c0=0)
    prep_weight(P, stg, "bstg", w_down[1408:2816, :], 11, D, Wd, 0, None, "Wd", [], cnt, dst_c0=11)
    es_stg.close()
    P.barrier()

    xm = [sb([128, TB, D], F32, "xm") for _ in range(2)]
    junk = sb([128, D], BF16, "junk")
    ss = sb([128, 8], F32, "ss")
    rs = sb([128, 8], F32, "rs")
    xn = [sb([128, D], BF16, "xn") for _ in range(2)]
    h2T = sb([128, NCH, TW], BF16, "h2T")
    ubuf = [sb([128, 2, TW + 2], F32, "ubuf") for _ in range(2)]
    cv = [sb([128, 2, TW], F32, "cv") for _ in range(2)]
    sgt = [sb([128, TW], F32, "sgt") for _ in range(2)]
    uh = sb([128, 22, 2, 2], F32, "uh")
    actT = sb([128, 22, TW], BF16, "actT")
    outt = [sb([128, D], F32, "outt") for _ in range(2)]
    P.op("pool", lambda e: e.memset(uh[:], 0.0), writes=["uh"])

    ps_tr = ps([128, NCH, 128], BF16, "ps_tr")
    pbig = [ps([128, 512], F32, "pbig") for _ in range(6)]
    bi = [0]

    def big():
        i = bi[0] % 6
        bi[0] += 1
        return pbig[i], "bpbig%d" % i

    for u in range(NTM + 1):
        tp = u % 2
        xk = "bxm%d" % tp
        P.op("sp", lambda e: e.dma_start(out=xm[tp][:], in_=xmid[u * TW:(u + 1) * TW, :].rearrange("(j p) d -> p j d", p=128)),
             reads=["xmid_d"], writes=[xk], chan=xk)
        for j in range(TB):
            norm_block(P, epsc, xm[tp][:, j, :], xk, ss[:, j:j + 1], rs[:, j:j + 1], "bss%d" % j, junk[:], "bjunk",
                       xn[j % 2][:], "bxn%d" % (j % 2), ps_tr, "bps_tr", idb, h2T[:, :, j * 128:(j + 1) * 128], "h2T")
        for m in range(22):
            q2 = m % 2
            pg, pgk = big()
            for half in range(2):
                col0 = half * D_FF + m * 128
                for c in range(NCH):
                    P.op("pe", lambda e: e.matmul(pg[:, half * TW:(half + 1) * TW], lhsT=Wu[:, c, col0:col0 + 128], rhs=h2T[:, c, :],
                                                  start=(c == 0), stop=(c == NCH - 1)), reads=["Wu", "h2T"], writes=[pgk])
            uk = "ubuf%d" % q2
            P.op("pool", lambda e: e.tensor_copy(ubuf[q2][:, :, 0:2], uh[:, m, :, :]), reads=["uh"], writes=[uk + "h"])
            P.op("act", lambda e: e.copy(ubuf[q2][:, :, 2:TW + 2], pg[:, :].rearrange("p (s n) -> p s n", s=2)), reads=[pgk], writes=[uk])
            P.op("pool", lambda e: e.tensor_copy(uh[:, m, :, :], ubuf[q2][:, :, TW:TW + 2]), reads=[uk, uk + "h"], writes=["uh"])
            if u == 0:
                continue
            ck = "cv%d" % q2
            for s_ in range(2):
                ch = s_ * 22 + m
                eng = "dve"
                P.op("act", lambda e: e.activation(out=cv[q2][:, s_, :], in_=ubuf[q2][:, s_, 0:TW], func=AF.Identity,
                                                   scale=cfw[:, ch, 0:1], bias=cfb[:, ch:ch + 1]),
                     reads=[uk, uk + "h", "cfw", "cfb"], writes=[ck + str(s_)])
                for k in range(1, 3):
                    P.op(eng, lambda e: e.scalar_tensor_tensor(out=cv[q2][:, s_, :], in0=ubuf[q2][:, s_, k:k + TW], scalar=cfw[:, ch, k:k + 1],
                                                               in1=cv[q2][:, s_, :], op0=ALU.mult, op1=ALU.add),
                         reads=[uk, uk + "h", "cfw", ck + str(s_)], writes=[ck + str(s_)])
            P.op("act", lambda e: e.activation(out=sgt[q2][:], in_=cv[q2][:, 0, :], func=AF.Silu), reads=[ck + "0"], writes=["sgt%d" % q2])
            P.op("dve", lambda e: e.tensor_tensor(out=actT[:, m, :], in0=sgt[q2][:], in1=cv[q2][:, 1, :], op=ALU.mult),
                 reads=["sgt%d" % q2, ck + "1"], writes=["actT"])
        if u == 0:
            continue
        for j in range(TB):
            for half in range(2):
                po, pok = big()
                for m in range(22):
                    P.op("pe", lambda e: e.matmul(po[:, :], lhsT=actT[:, m, j * 128:(j + 1) * 128], rhs=Wd[:, m, half * 512:(half + 1) * 512],
                                                  start=(m == 0), stop=(m == 21)), reads=["actT", "Wd"], writes=[pok])
                P.op("dve", lambda e: e.tensor_tensor(out=xm[tp][:, j, half * 512:(half + 1) * 512], in0=po[:, :],
                                                      in1=xm[tp][:, j, half * 512:(half + 1) * 512], op=ALU.add), reads=[pok, xk], writes=[xk])
            o2 = j % 2
            P.op("act", lambda e: e.activation(out=junk[:], in_=xm[tp][:, j, :], func=AF.Square, accum_out=ss[:, 4 + j:5 + j]),
                 reads=[xk], writes=["bjunk", "fss%d" % j])
            P.op("act", lambda e: e.activation(out=rs[:, 4 + j:5 + j], in_=ss[:, 4 + j:5 + j], func=AF.Ln, bias=epsc[:, 0:1]),
                 reads=["fss%d" % j, "epsc"], writes=["frs%da" % j])
            P.op("act", lambda e: e.activation(out=rs[:, 4 + j:5 + j], in_=rs[:, 4 + j:5 + j], func=AF.Exp, scale=-0.5),
                 reads=["frs%da" % j], writes=["frs%d" % j])
            P.op("dve", lambda e: e.scalar_tensor_tensor(out=outt[o2][:], in0=xm[tp][:, j, :], scalar=rs[:, 4 + j:5 + j], in1=wfb[:],
                                                         op0=ALU.mult, op1=ALU.mult), reads=[xk, "frs%d" % j, "wfb"], writes=["outt%d" % o2])
            P.op("sp", lambda e: e.dma_start(out=out2[(u - 1) * TW + j * 128:(u - 1) * TW + (j + 1) * 128, :], in_=outt[o2][:]),
                 reads=["outt%d" % o2], chan="ost%d" % o2)


def _phase1_inputs(inp, T):
    x = np.asarray(inp["x"], np.float32)
    w_in = np.asarray(inp["w_in"], np.float32)[0]
    conv = np.asarray(inp["conv_qkv_w"], np.float32)[0]
    a_log = np.asarray(inp["a_log"], np.float32)[0]
    dtb = np.asarray(inp["dt_bias"], np.float32)[0]
    nw = np.asarray(inp["norm_mix_w"], np.float32)[0]
    cst = chunk_consts()
    maps = []
    for core in range(8):
        b, h = core // 4, core % 4
        cols = np.concatenate([np.arange(h * 128, (h + 1) * 128), 512 + np.arange(h * 128, (h + 1) * 128),
                               1024 + np.arange(h * 128, (h + 1) * 128), [2048 + h], [2052 + h]])
        w1 = np.ascontiguousarray(w_in[:, cols])
        cw = np.zeros((128, 12), np.float32)
        for g in range(3):
            cw[:, 4 * g:4 * g + 4] = conv[:, g * 512 + h * 128:g * 512 + (h + 1) * 128].T
        sc = np.zeros((128, 2), np.float32)
        sc[:, 0] = a_log[h]
        sc[:, 1] = dtb[h]
        maps.append({"x1": np.ascontiguousarray(x[b, :T]), "w1": w1, "cw1": cw, "sc1": sc,
                     "nw1": np.ascontiguousarray(nw.reshape(8, 128).T), "cst": cst})
    return maps


def build_p1_program(T):
    nc = bass.Bass("TRN2", target_bir_lowering=False)
    x1 = nc.dram_tensor("x1", [T, D], F32, kind="ExternalInput").ap()
    w1 = nc.dram_tensor("w1", [D, 386], F32, kind="ExternalInput").ap()
    cw1 = nc.dram_tensor("cw1", [128, 12], F32, kind="ExternalInput").ap()
    sc1 = nc.dram_tensor("sc1", [128, 2], F32, kind="ExternalInput").ap()
    nw1 = nc.dram_tensor("nw1", [128, 8], F32, kind="ExternalInput").ap()
    cst = nc.dram_tensor("cst", [128, NCONST], F32, kind="ExternalInput").ap()
    o_out = nc.dram_tensor("o1", [T, 128], F32, kind="ExternalOutput").ap()
    es = ExitStack()
    P = Prog(nc)
    A = Ctx(nc, es, P)
    build_phase1(nc, es, P, A, T, x1, w1, cw1, sc1, nw1, cst, o_out)
    P.finish()
    P.emit(es)
    es.close()
    return nc, P


def run_phase1(inp, T):
    nc, P = build_p1_program(T)
    maps = _phase1_inputs(inp, T)
    res = run_bass_kernel_spmd(nc, maps, core_ids=list(range(8)))
    o = np.zeros((2, T, 4, 128), np.float32)
    for core in range(8):
        o[core // 4, :, core % 4, :] = res.results[core]["o1"]
    return o


def _bias_tile(rel):
    ki = np.arange(128)[:, None]
    qi = np.arange(128)[None, :]
    out = np.zeros((128, 8, 640), np.float32)
    for kb in range(5):
        dist = qi - ki + (4 - kb) * 128
        idx = np.clip(dist, -128, 128) + 128
        cdiff = 2 * (4 - kb) + qi // 64 - ki // 64
        ok = (cdiff >= 0) & (cdiff <= 8)
        for h in range(8):
            out[:, h, kb * 128:(kb + 1) * 128] = np.where(ok, rel[h][idx], NEG)
    return out


def _phase2_inputs(inp, o1, T):
    TC = T // 4
    NTM = TC // TW
    x = np.asarray(inp["x"], np.float32)
    w_in = np.ascontiguousarray(np.asarray(inp["w_in"], np.float32)[0])
    cfw_ = np.asarray(inp["conv_ffn_w"], np.float32)[0]
    cfb_ = np.asarray(inp["conv_ffn_b"], np.float32)[0]
    shared = {
        "w_in": w_in,
        "w_ba": np.ascontiguousarray(np.asarray(inp["w_branch_a"], np.float32)[0]),
        "w_bb": np.ascontiguousarray(np.asarray(inp["w_branch_b"], np.float32)[0]),
        "w_out": np.ascontiguousarray(np.asarray(inp["w_out"], np.float32)[0]),
        "w_up": np.ascontiguousarray(np.asarray(inp["w_up"], np.float32)[0]),
        "w_down": np.ascontiguousarray(np.asarray(inp["w_down"], np.float32)[0]),
        "nwm": np.ascontiguousarray(np.asarray(inp["norm_mix_w"], np.float32)[0].reshape(8, 128).T),
        "nwf": np.ascontiguousarray(np.asarray(inp["norm_ffn_w"], np.float32)[0].reshape(8, 128).T),
        "gnw": np.ascontiguousarray(np.asarray(inp["gdn_norm_w"], np.float32)[0].reshape(128, 1)),
        "biasT": _bias_tile(np.asarray(inp["rel_bias"], np.float32)[0]),
        "cfw": np.ascontiguousarray(cfw_.reshape(3, 44, 128).transpose(2, 1, 0)),
        "cfb": np.ascontiguousarray(cfb_.reshape(44, 128).T),
        "wfin": np.ascontiguousarray(np.broadcast_to(np.asarray(inp["norm_final_w"], np.float32)[None, :], (128, D))),
        "cst2": chunk_consts(),
    }
    maps = []
    for core in range(8):
        b, q = core // 4, core % 4
        t0 = q * TC
        lo = t0 - 3 * TW
        x2 = np.zeros(((NTM + 3) * TW, D), np.float32)
        s0 = max(lo, 0)
        x2[s0 - lo:] = x[b, s0:t0 + TC]
        pos = lo + np.arange((NTM + 3) * TW)
        valid = (pos >= 0).astype(np.float32).reshape((NTM + 3) * TB, 128).T
        m = dict(shared)
        m["x2"] = x2
        m["validc"] = np.ascontiguousarray(valid)
        if o1 is not None:
            lo2 = t0 - TW
            oa2 = np.zeros(((NTM + 1) * TW, 512), np.float32)
            s1 = max(lo2, 0)
            oa2[s1 - lo2:] = o1[b, s1:t0 + TC].reshape(-1, 512)
            m["oa2"] = oa2
        maps.append(m)
    return maps


def _declare_p2(nc, NTM, with_oa):
    d = {}
    def inp(name, shape):
        d[name] = nc.dram_tensor(name, list(shape), F32, kind="ExternalInput").ap()
    inp("x2", [(NTM + 3) * TW, D])
    if with_oa:
        inp("oa2", [(NTM + 1) * TW, 512])
    inp("validc", [128, (NTM + 3) * TB])
    inp("w_in", [D, D_IN]); inp("w_ba", [512, D]); inp("w_bb", [512, D]); inp("w_out", [D, D])
    inp("w_up", [D, 2 * D_FF]); inp("w_down", [D_FF, D]); inp("nwm", [128, 8]); inp("nwf", [128, 8]); inp("gnw", [128, 1])
    inp("biasT", [128, 8, 640]); inp("cfw", [128, 44, 3]); inp("cfb", [128, 44]); inp("wfin", [128, D]); inp("cst2", [128, NCONST])
    d["xmid"] = nc.dram_tensor("xmid", [(NTM + 1) * TW, D], F32, kind="Internal").ap()
    d["out2"] = nc.dram_tensor("out2", [NTM * TW, D], F32, kind="ExternalOutput").ap()
    return d


def _emit_phase2(nc, P, d, NTM, oa_ap, o_all=None, qsel_d=None, RT=None):
    es_a = ExitStack()
    build_phase2a(nc, P, Ctx(nc, es_a, P), NTM, d["x2"], oa_ap, d["validc"], d["w_in"], d["w_ba"], d["w_bb"], d["w_out"],
                  d["nwm"], d["gnw"], d["biasT"], d["cst2"], d["xmid"], o_all=o_all, qsel_d=qsel_d, RT=RT)
    es_a.close()
    P.pool_hold = False
    P.barrier()
    es_b = ExitStack()
    build_phase2b(nc, P, Ctx(nc, es_b, P), NTM, d["xmid"], d["w_up"], d["w_down"], d["nwf"], d["cfw"], d["cfb"], d["wfin"],
                  d["cst2"], d["out2"])
    es_b.close()


def build_p2_program(T):
    NTM = (T // 4) // TW
    nc = bass.Bass("TRN2", target_bir_lowering=False)
    d = _declare_p2(nc, NTM, True)
    P = Prog(nc)
    _emit_phase2(nc, P, d, NTM, d["oa2"])
    P.finish()
    es = ExitStack()
    P.emit(es)
    es.close()
    return nc, P


def run_phase2(inp, o1, T):
    nc, P = build_p2_program(T)
    maps = _phase2_inputs(inp, o1, T)
    res = run_bass_kernel_spmd(nc, maps, core_ids=list(range(8)))
    TC = T // 4
    out = np.zeros((2, T, D), np.float32)
    for core in range(8):
        out[core // 4, (core % 4) * TC:(core % 4 + 1) * TC] = res.results[core]["out2"]
    return out


T_FULL = 16384


def build_fused_program(T):
    NTM = (T // 4) // TW
    RT = T + TW
    nc = bass.Bass("TRN2", target_bir_lowering=False)
    x1 = nc.dram_tensor("x1", [T, D], F32, kind="ExternalInput").ap()
    w1 = nc.dram_tensor("w1", [D, 386], F32, kind="ExternalInput").ap()
    cw1 = nc.dram_tensor("cw1", [128, 12], F32, kind="ExternalInput").ap()
    sc1 = nc.dram_tensor("sc1", [128, 2], F32, kind="ExternalInput").ap()
    nw1 = nc.dram_tensor("nw1", [128, 8], F32, kind="ExternalInput").ap()
    cst = nc.dram_tensor("cst", [128, NCONST], F32, kind="ExternalInput").ap()
    qsel_d = nc.dram_tensor("qsel", [128, 8], F32, kind="ExternalInput").ap()
    o_loc = nc.dram_tensor("o_loc", [RT, 128], F32, kind="Internal").ap()
    CR = RT
    chunks = [(r0, min(CR, RT - r0)) for r0 in range(0, RT, CR)]
    o_alls = [nc.dram_tensor("o_all%d" % i, [8 * n, 128], F32, kind="Internal").ap() for i, (r0, n) in enumerate(chunks)]
    d = _declare_p2(nc, NTM, False)
    P = Prog(nc)
    es1 = ExitStack()
    A1 = Ctx(nc, es1, P)
    zt = A1.sb([128, 128], F32, "zt")
    P.op("pool", lambda e: e.memset(zt[:], 0.0), writes=["zt"])
    for i in range(TW // 128):
        P.op("sp", lambda e: e.dma_start(out=o_loc[i * 128:(i + 1) * 128, :], in_=zt[:]), reads=["zt"], chan="zt")
    build_phase1(nc, es1, P, A1, T, x1, w1, cw1, sc1, nw1, cst, o_loc[TW:RT, :])
    es1.close()
    P.barrier()
    for i, (r0, n) in enumerate(chunks):
        P.op("pool", lambda e: e.collective_compute("AllGather", ALU.bypass, replica_groups=[list(range(8))],
                                                    ins=[o_loc[r0:r0 + n, :]], outs=[o_alls[i][:, :]]),
             writes=["oall"], chan="cc", inc_override=1)
    P.pool_hold = True
    _emit_phase2(nc, P, d, NTM, None, o_all=(o_alls, chunks, CR), qsel_d=qsel_d, RT=RT)
    P.finish()
    es = ExitStack()
    P.emit(es)
    es.close()
    return nc, P


def build_fused_nocc(T):
    NTM = (T // 4) // TW
    RT = T + TW
    nc = bass.Bass("TRN2", target_bir_lowering=False)
    x1 = nc.dram_tensor("x1", [T, D], F32, kind="ExternalInput").ap()
    w1a = nc.dram_tensor("w1a", [4, D, 386], F32, kind="ExternalInput").ap()
    cw1a = nc.dram_tensor("cw1a", [4, 128, 12], F32, kind="ExternalInput").ap()
    sc1a = nc.dram_tensor("sc1a", [4, 128, 2], F32, kind="ExternalInput").ap()
    nw1 = nc.dram_tensor("nw1", [128, 8], F32, kind="ExternalInput").ap()
    cst = nc.dram_tensor("cst", [128, NCONST], F32, kind="ExternalInput").ap()
    qsel_d = nc.dram_tensor("qsel", [128, 4], F32, kind="ExternalInput").ap()
    o_loc = nc.dram_tensor("o_loc", [RT, 512], F32, kind="Internal").ap()
    hts = nc.dram_tensor("hts", [T // 512, 128, NCH, 512], BF16, kind="Internal").ap()
    d = _declare_p2(nc, NTM, False)
    P = Prog(nc)
    for h in range(4):
        es1 = ExitStack()
        A1 = Ctx(nc, es1, P)
        if h == 0:
            zt = A1.sb([128, 512], F32, "zt")
            P.op("pool", lambda e: e.memset(zt[:], 0.0), writes=["zt"])
            for i in range(TW // 128):
                P.op("sp", lambda e: e.dma_start(out=o_loc[i * 128:(i + 1) * 128, :], in_=zt[:]), reads=["zt"], chan="zt")
        build_phase1(nc, es1, P, A1, T, x1, w1a[h], cw1a[h], sc1a[h], nw1, cst, o_loc[TW:RT, h * 128:(h + 1) * 128],
                     hts=hts, hts_mode=("save" if h == 0 else "load"))
        es1.close()
        P.barrier()
    _emit_phase2(nc, P, d, NTM, None, o_all=o_loc, qsel_d=qsel_d, RT=None)
    P.finish()
    es = ExitStack()
    P.emit(es)
    es.close()
    return nc, P


def run_fused_nocc(inp, T):
    nc, P = build_fused_nocc(T)
    m1 = _phase1_inputs(inp, T)
    m2 = _phase2_inputs(inp, None, T)
    maps = []
    for core in range(8):
        b, q = core // 4, core % 4
        m = dict(m2[core])
        m["x1"] = m1[core]["x1"]
        m["nw1"] = m1[core]["nw1"]
        m["cst"] = m1[core]["cst"]
        m["w1a"] = np.stack([m1[4 * b + h]["w1"] for h in range(4)])
        m["cw1a"] = np.stack([m1[4 * b + h]["cw1"] for h in range(4)])
        m["sc1a"] = np.stack([m1[4 * b + h]["sc1"] for h in range(4)])
        qs = np.zeros((128, 4), np.float32)
        qs[:, q] = 1.0
        m["qsel"] = qs
        maps.append(m)
    res = run_bass_kernel_spmd(nc, maps, core_ids=list(range(8)))
    TC = T // 4
    out = np.zeros((2, T, D), np.float32)
    for core in range(8):
        out[core // 4, (core % 4) * TC:(core % 4 + 1) * TC] = res.results[core]["out2"]
    return out


def run_fused(inp, T):
    nc, P = build_fused_program(T)
    m1 = _phase1_inputs(inp, T)
    m2 = _phase2_inputs(inp, None, T)
    maps = []
    for core in range(8):
        m = dict(m1[core])
        m.update(m2[core])
        qs = np.zeros((128, 8), np.float32)
        qs[:, core] = 1.0
        m["qsel"] = qs
        maps.append(m)
    res = run_bass_kernel_spmd(nc, maps, core_ids=list(range(8)))
    TC = T // 4
    out = np.zeros((2, T, D), np.float32)
    for core in range(8):
        out[core // 4, (core % 4) * TC:(core % 4 + 1) * TC] = res.results[core]["out2"]
    return out


def kernel(**inputs):
    return run_fused_nocc(inputs, T_FULL)
```

```python
from collections import defaultdict
from contextlib import ExitStack

import numpy as np
import concourse.bass as bass
import concourse.mybir as mybir
from concourse.bass_utils import run_bass_kernel_spmd

F32 = mybir.dt.float32
BF16 = mybir.dt.bfloat16
AF = mybir.ActivationFunctionType
ALU = mybir.AluOpType

D = 1024
NCH = 8
EPS = 1e-6
CH = 64
DK = 128
D_IN = 5640
D_FF = 2816
NEG = -30000.0


PSUM_PREFIXES = ("psl", "ptb", "ppj", "ps_tr", "aps_tr", "bps_tr", "pbig", "bpbig", "ps_o")


class _Rec:
    def __getattr__(self, name):
        def f(*a, **k):
            self.call = (name, a, k)
            return self
        return f


class Prog:
    ENGS = ("pe", "act", "dve", "pool", "sp")

    def __init__(self, nc):
        self.nc = nc
        self.streams = {e: [] for e in self.ENGS}
        self.count = defaultdict(int)
        self.lastw = {}
        self.readers = defaultdict(list)
        self.waited = defaultdict(int)
        self.nops = 0
        self.epoch = 0
        self.pool_hold = False
        import os
        self.cut = int(os.environ["PCUT"]) if "PCUT" in os.environ else None

    def _dep(self, eng, rec):
        semkey, val = rec[0], rec[1]
        if eng == "pool" and (semkey.startswith("dma_cc@") or self.pool_hold):
            return
        if self.waited[(eng, semkey)] < val:
            self.waited[(eng, semkey)] = val
            self.streams[eng].append(("wait", semkey, val))

    def op(self, eng, fn, reads=(), writes=(), chan=None, inc_override=None):
        if self.cut is not None and self.nops >= self.cut:
            return
        isdma = chan is not None
        for k in reads:
            w = self.lastw.get(k)
            if w is not None:
                self._dep(eng, w)
            if k.startswith(PSUM_PREFIXES):
                for r in self.readers[k]:
                    if r[2] != eng:
                        self._dep(eng, r)
        for k in writes:
            w = self.lastw.get(k)
            if w is not None:
                if not (w[2] == eng == "pe" and not w[3] and not isdma):
                    self._dep(eng, w)
            for r in self.readers[k]:
                if r[2] != eng or r[3] or isdma:
                    self._dep(eng, r)
        if isdma:
            semkey, inc = "dma_%s@%d" % (chan, self.epoch), (inc_override or 16)
        else:
            semkey, inc = "%s@%d" % (eng, self.epoch), 1
        self.count[semkey] += inc
        rec = (semkey, self.count[semkey], eng, isdma)
        rec_ = _Rec()
        fn(rec_)
        self.streams[eng].append(("op", rec_.call, semkey, inc))
        for k in writes:
            self.lastw[k] = rec
            self.readers[k] = []
        for k in reads:
            self.readers[k].append(rec)
        self.nops += 1

    def barrier(self):
        for e in self.ENGS:
            for semkey, val in list(self.count.items()):
                if val:
                    self._dep(e, (semkey, val))
        self.lastw.clear()
        self.readers.clear()
        self.epoch += 1

    def finish(self):
        for semkey, val in list(self.count.items()):
            if semkey.startswith("dma_"):
                self._dep("sp", (semkey, val))
        for semkey, val in list(self.count.items()):
            if not semkey.startswith("dma_") and val:
                self._dep("sp", (semkey, val))

    def emit(self, es):
        nc = self.nc
        sems = {}
        for i, k in enumerate(sorted(self.count)):
            sems[k] = es.enter_context(nc.semaphore("s%d" % i))
        block = es.enter_context(nc.Block())
        streams = self.streams

        def run(eng_handle, items):
            for it in items:
                if it[0] == "wait":
                    eng_handle.wait_ge(sems[it[1]], it[2])
                else:
                    name, a, k = it[1]
                    getattr(eng_handle, name)(*a, **k).then_inc(sems[it[2]], it[3])

        @block.tensor
        def _(e):
            run(e, streams["pe"])

        @block.scalar
        def _(e):
            run(e, streams["act"])

        @block.vector
        def _(e):
            run(e, streams["dve"])

        @block.gpsimd
        def _(e):
            run(e, streams["pool"])

        @block.sync
        def _(e):
            run(e, streams["sp"])


class Ctx:
    _uid = [0]

    def __init__(self, nc, es, P):
        self.nc, self.es, self.P = nc, es, P
        Ctx._uid[0] += 1
        self.n = Ctx._uid[0] * 1000

    def sb(self, shape, dt=F32, name=None):
        self.n += 1
        return self.es.enter_context(self.nc.sbuf_tensor("%s_%d" % (name or "t", self.n), list(shape), dt))

    def ps(self, shape, dt=F32, name=None):
        self.n += 1
        return self.es.enter_context(self.nc.psum_tensor("%s_%d" % (name or "p", self.n), list(shape), dt))


def chunk_consts():
    j = np.arange(128)
    same = (j[:, None] // CH) == (j[None, :] // CH)
    m1 = (same & (j[:, None] <= j[None, :])).astype(np.float32)
    m2 = (same & (j[:, None] > j[None, :])).astype(np.float32)
    ident = np.eye(128, dtype=np.float32)
    ones = np.ones((128, 128), np.float32)
    cind = np.zeros((128, 128), np.float32)
    cind[:64, 0] = 1.0
    cind[64:, 1] = 1.0
    return np.concatenate([m1, m2, ident, ones, cind], axis=1)


C_M1, C_M2, C_ID, C_ONES, C_CIND = 0, 128, 256, 384, 512
NCONST = 640


def make_epsc(P, A, eng="pool"):
    epsc = A.sb([128, 2], F32, "epsc")
    P.op(eng, lambda e: e.memset(epsc[:, 0:1], D * EPS), writes=["epsc0"])
    P.op(eng, lambda e: e.memset(epsc[:, 1:2], EPS), reads=["epsc0"], writes=["epsc"])
    return epsc


def norm_block(P, epsc, x_blk, xkey, ss, rs, sskey, junk, junkkey, xn, xnkey, ps_tr, pskey, idb, hT_dst, hTkey,
               wrow=None, wkey=None):
    P.op("act", lambda e: e.activation(out=junk, in_=x_blk, func=AF.Square, accum_out=ss),
         reads=[xkey], writes=[junkkey, sskey])
    P.op("act", lambda e: e.activation(out=rs, in_=ss, func=AF.Ln, bias=epsc[:, 0:1]),
         reads=[sskey, "epsc"], writes=[sskey + "r0"])
    P.op("act", lambda e: e.activation(out=rs, in_=rs, func=AF.Exp, scale=-0.5),
         reads=[sskey + "r0"], writes=[sskey + "r"])
    if wrow is None:
        P.op("dve", lambda e: e.tensor_scalar(xn, x_blk, rs, None, ALU.mult),
             reads=[xkey, sskey + "r"], writes=[xnkey])
    else:
        P.op("dve", lambda e: e.scalar_tensor_tensor(out=xn, in0=x_blk, scalar=rs, in1=wrow,
                                                      op0=ALU.mult, op1=ALU.mult),
             reads=[xkey, sskey + "r", wkey], writes=[xnkey])
    for c in range(NCH):
        P.op("pe", lambda e, c=c: e.transpose(ps_tr[:, c, :], xn[:, c * 128:(c + 1) * 128], idb),
             reads=[xnkey, "consts_b"], writes=[pskey])
    P.op("act", lambda e: e.copy(hT_dst, ps_tr[:, :, :]), reads=[pskey], writes=[hTkey])


def build_phase1(nc, es, P, A, T, x1, w1, cw1, sc1, nw1, cst, o_out, hts=None, hts_mode=None):
    NT = T // 512
    sb, ps = A.sb, A.ps
    cf = sb([128, NCONST], F32, "cf")
    cb = sb([128, NCONST], BF16, "cb")
    P.op("sp", lambda e: e.dma_start(out=cf[:], in_=cst[:, :]), writes=["consts_f"], chan="cf")
    P.op("dve", lambda e: e.tensor_copy(cb[:], cf[:]), reads=["consts_f"], writes=["consts_b"])
    m1f, m2f = cf[:, C_M1:C_M1 + 128], cf[:, C_M2:C_M2 + 128]
    idf, onesf, cindf = cf[:, C_ID:C_ID + 128], cf[:, C_ONES:C_ONES + 128], cf[:, C_CIND:C_CIND + 2]
    idb, onesb = cb[:, C_ID:C_ID + 128], cb[:, C_ONES:C_ONES + 128]

    epsc = make_epsc(P, A)
    wf = sb([128, NCH, 386], F32, "wf")
    wb = sb([128, NCH, 386], BF16, "wb")
    nw = sb([128, NCH], F32, "nw")
    cw = sb([128, 12], F32, "cw")
    sc = sb([128, 2], F32, "sc")
    negA = sb([128, 1], F32, "negA")
    P.op("sp", lambda e: e.dma_start(out=wf[:], in_=w1.rearrange("(c p) n -> p c n", p=128)), writes=["wf"], chan="wf")
    P.op("sp", lambda e: e.dma_start(out=nw[:], in_=nw1[:, :]), writes=["nw"], chan="nw")
    P.op("sp", lambda e: e.dma_start(out=cw[:], in_=cw1[:, :]), writes=["cw"], chan="cw")
    P.op("sp", lambda e: e.dma_start(out=sc[:], in_=sc1[:, :]), writes=["sc"], chan="sc")
    for c in range(NCH):
        P.op("dve", lambda e, c=c: e.tensor_scalar(wb[:, c, :], wf[:, c, :], nw[:, c:c + 1], 32.0, ALU.mult, ALU.mult),
             reads=["wf", "nw"], writes=["wb"])
    P.op("act", lambda e: e.activation(out=negA[:], in_=sc[:, 0:1], func=AF.Exp), reads=["sc"], writes=["negA0"])
    P.op("dve", lambda e: e.tensor_scalar(negA[:], negA[:], -1.0, None, ALU.mult), reads=["negA0"], writes=["negA"])

    xt = [sb([128, 4, D], F32, "xt") for _ in range(2)]
    junk = sb([128, D], BF16, "junk")
    ss = sb([128, 8], F32, "ss")
    rs = sb([128, 8], F32, "rs")
    xn = [sb([128, D], BF16, "xn") for _ in range(2)]
    hT = [sb([128, NCH, 512], BF16, "hT") for _ in range(2)]
    cbuf = [sb([128, 3 + 512], F32, "cbuf") for _ in range(3)]
    acc = [sb([128, 512], F32, "acc") for _ in range(3)]
    sil = [sb([128, 512], F32, "sil") for _ in range(2)]
    sq = [sb([128, 512], BF16, "sq") for _ in range(2)]
    rn = [sb([128, 512], F32, "rn") for _ in range(2)]
    QT = [sb([128, 512], BF16, "QT") for _ in range(3)]
    KT = [sb([128, 512], BF16, "KT") for _ in range(2)]
    VT = [sb([128, 512], BF16, "VT") for _ in range(2)]
    bdt = [sb([128, 4, 2], F32, "bdt") for _ in range(2)]
    gsc = [sb([128, 8, 4], F32, "gsc") for _ in range(2)]
    def four(shape, dt, name):
        return [sb(shape, dt, name) for _ in range(4)]

    def eight(shape, dt, name):
        return [[sb(shape, dt, name) for _ in range(4)] for _ in range(2)]

    gM = four([128, 128], F32, "gM")
    rgc = four([128, 2], F32, "rgc")
    D1 = four([128, 128], F32, "D1")
    D2 = four([128, 128], F32, "D2")
    bg = four([128, 1], F32, "bg")
    bgK = four([128, 128], BF16, "bgK")
    Bm = four([128, 128], F32, "Bm")
    Bq = four([128, 128], F32, "Bq")
    Nq = four([128, 128], F32, "Nq")
    Rq = four([128, 128], F32, "Rq")
    Rt = four([128, 128], F32, "Rt")
    PTm = four([128, 128], F32, "PTm")
    smx = eight([128, 4], F32, "smx")
    KD = eight([128, 128], BF16, "KD")
    bV = eight([128, 128], BF16, "bV")
    TTb = eight([128, 128], BF16, "TTb")
    PT = eight([128, 128], BF16, "PT")
    nWT = eight([128, 128], BF16, "nWT")
    Ub = [sb([128, 128], BF16, "Ub") for _ in range(2)]
    pus = [sb([128, 128], F32, "pus") for _ in range(2)]
    Osb = [sb([128, 128], F32, "Osb") for _ in range(2)]
    Sf = [sb([128, 128], F32, "Sf") for _ in range(2)]
    Sb = [sb([128, 128], BF16, "Sb") for _ in range(2)]

    ps_tr = ps([128, NCH, 128], BF16, "ps_tr")
    ps_tb = ps([128, 8, 128], BF16, "ps_tb")
    ps_pj = [ps([128, 512], F32, "ps_pj") for _ in range(1)]
    ps_ch = ps([128, 4, 128], F32, "ps_ch")
    ps_sl = [ps([128, 4, 128], F32, "ps_sl") for _ in range(4)]
    pj_i = [0]

    def pjslot():
        i = pj_i[0] % len(ps_pj)
        pj_i[0] += 1
        return ps_pj[i], "ppj%d" % i


    P.op("pool", lambda e: e.memset(Sf[0][:], 0.0), writes=["Sf0"])
    P.op("pool", lambda e: e.memset(Sb[0][:], 0.0), writes=["Sb0"])
    for g in range(3):
        P.op("pool", lambda e, g=g: e.memset(cbuf[g][:, 0:3], 0.0), writes=["cbufh%d" % g])
    sidx = [0]
    chain_q = []

    def tile_level(ti):
        tp = ti % 2
        xk = "xt%d" % tp
        hk = "hT%d" % tp
        bk = "bdt%d" % tp
        cq = ti % 3
        qk, kk, vk = "QT%d" % cq, "KT%d" % tp, "VT%d" % tp
        G = gsc[tp]
        gk = "gsc%d" % tp
        xg, ax, ee, ll, sp_, gg, be, nbe = (G[:, i, :] for i in range(8))
        pieces = []

        def p_load():
            P.op("sp", lambda e, ti=ti, tp=tp: e.dma_start(
                out=xt[tp][:], in_=x1[ti * 512:(ti + 1) * 512, :].rearrange("(j p) d -> p j d", p=128)),
                writes=[xk], chan=xk)
        def p_norm(j):
            bp = j % 2
            norm_block(P, epsc, xt[tp][:, j, :], xk, ss[:, j + 4 * tp:j + 4 * tp + 1], rs[:, j + 4 * tp:j + 4 * tp + 1],
                       "ss%d_%d" % (tp, j), junk[:], "junk", xn[bp][:], "xn%d" % bp, ps_tr, "ps_tr", idb,
                       hT[tp][:, :, j * 128:(j + 1) * 128], hk)

        def p_hload():
            P.op("sp", lambda e: e.dma_start(out=hT[tp][:], in_=hts[ti]), writes=[hk], chan=hk)

        def p_hsave():
            P.op("sp", lambda e: e.dma_start(out=hts[ti], in_=hT[tp][:]), reads=[hk], chan="hsv%d" % tp)

        if hts_mode == "load":
            pieces.append(p_hload)
        else:
            pieces.append(p_load)
            for j in range(4):
                pieces.append(lambda j=j: p_norm(j))
            if hts_mode == "save":
                pieces.append(p_hsave)
        def p_proj(g):
            pj, pjk = pjslot()
            for c in range(NCH):
                P.op("pe", lambda e, g=g, c=c, pj=pj: e.matmul(pj[:], lhsT=wb[:, c, g * 128:(g + 1) * 128],
                                                              rhs=hT[tp][:, c, :], start=(c == 0), stop=(c == NCH - 1)),
                     reads=["wb", hk], writes=[pjk])
            P.op("act", lambda e, g=g, pj=pj: e.copy(cbuf[g][:, 3:515], pj[:]), reads=[pjk], writes=["cbufm%d" % g])
        for g in range(3):
            pieces.append(lambda g=g: p_proj(g))
        def p_bd():
            pj, pjk = pjslot()
            for j in range(4):
                for c in range(NCH):
                    P.op("pe", lambda e, j=j, c=c, pj=pj: e.matmul(pj[:, 2 * j:2 * j + 2], lhsT=hT[tp][:, c, j * 128:(j + 1) * 128],
                                                                  rhs=wb[:, c, 384:386], start=(c == 0), stop=(c == NCH - 1)),
                         reads=["wb", hk], writes=[pjk])
            P.op("dve", lambda e, pj=pj: e.tensor_copy(bdt[tp][:].rearrange("p a b -> p (a b)"), pj[:, 0:8]), reads=[pjk], writes=[bk])
        pieces.append(p_bd)
        def p_conv_a(g):
            ck = ["cbufh%d" % g, "cbufm%d" % g]
            ak = "acc%d" % g
            P.op("dve", lambda e: e.tensor_scalar(acc[g][:], cbuf[g][:, 0:512], cw[:, 4 * g:4 * g + 1], None, ALU.mult),
                 reads=ck + ["cw"], writes=[ak])
            P.op("dve", lambda e: e.scalar_tensor_tensor(
                out=acc[g][:], in0=cbuf[g][:, 1:513], scalar=cw[:, 4 * g + 1:4 * g + 2], in1=acc[g][:],
                op0=ALU.mult, op1=ALU.add), reads=ck + ["cw", ak], writes=[ak])

        def p_conv_b(g):
            ck = ["cbufh%d" % g, "cbufm%d" % g]
            ak = "acc%d" % g
            for k in range(2, 4):
                P.op("dve", lambda e: e.scalar_tensor_tensor(
                    out=acc[g][:], in0=cbuf[g][:, k:k + 512], scalar=cw[:, 4 * g + k:4 * g + k + 1], in1=acc[g][:],
                    op0=ALU.mult, op1=ALU.add), reads=ck + ["cw", ak], writes=[ak])
            P.op("pool", lambda e: e.tensor_copy(cbuf[g][:, 0:3], cbuf[g][:, 512:515]),
                 reads=["cbufm%d" % g, ak], writes=["cbufh%d" % g])

        def p_silu_v():
            P.op("act", lambda e: e.activation(out=VT[tp][:], in_=acc[2][:], func=AF.Silu), reads=["acc2"], writes=[vk])

        def p_l2_a(g):
            P.op("act", lambda e: e.activation(out=sil[g][:], in_=acc[g][:], func=AF.Silu), reads=["acc%d" % g], writes=["sil%d" % g])
            P.op("act", lambda e: e.activation(out=sq[g][:], in_=sil[g][:], func=AF.Square), reads=["sil%d" % g], writes=["sq%d" % g])

        def p_l2_b(g):
            pj, pjk = pjslot()
            P.op("pe", lambda e: e.matmul(pj[:], lhsT=onesb, rhs=sq[g][:], start=True, stop=True),
                 reads=["consts_b", "sq%d" % g], writes=[pjk])
            P.op("act", lambda e: e.activation(out=rn[g][:], in_=pj[:], func=AF.Ln, bias=epsc[:, 1:2]),
                 reads=[pjk, "epsc"], writes=["rn%da" % g])
            P.op("act", lambda e: e.activation(out=rn[g][:], in_=rn[g][:], func=AF.Exp, scale=-0.5),
                 reads=["rn%da" % g], writes=["rn%d" % g])

        def p_qk_out():
            P.op("dve", lambda e: e.scalar_tensor_tensor(out=QT[cq][:], in0=sil[0][:], scalar=float(DK) ** -0.5, in1=rn[0][:],
                                                          op0=ALU.mult, op1=ALU.mult), reads=["sil0", "rn0"], writes=[qk])
            P.op("dve", lambda e: e.tensor_tensor(out=KT[tp][:], in0=sil[1][:], in1=rn[1][:], op=ALU.mult), reads=["sil1", "rn1"], writes=[kk])

        def p_gate_a():
            P.op("dve", lambda e: e.tensor_scalar(xg, bdt[tp][:, :, 1], sc[:, 1:2], None, ALU.add), reads=[bk, "sc"], writes=[gk + "a"])
            P.op("dve", lambda e: e.scalar_tensor_tensor(out=ax, in0=xg, scalar=-1.0, in1=xg, op0=ALU.mult, op1=ALU.max), reads=[gk + "a"], writes=[gk + "b"])
            P.op("act", lambda e: e.activation(out=ee, in_=ax, func=AF.Exp, scale=-1.0), reads=[gk + "b"], writes=[gk + "c"])
            P.op("act", lambda e: e.activation(out=ll, in_=ee, func=AF.Ln, bias=1.0), reads=[gk + "c"], writes=[gk + "d"])

        def p_gate_b():
            P.op("dve", lambda e: e.scalar_tensor_tensor(out=sp_, in0=xg, scalar=0.0, in1=ll, op0=ALU.max, op1=ALU.add),
                 reads=[gk + "a", gk + "d"], writes=[gk + "e"])
            P.op("dve", lambda e: e.tensor_scalar(gg, sp_, negA[:, 0:1], None, ALU.mult), reads=[gk + "e", "negA"], writes=[gk + "g"])
            P.op("act", lambda e: e.activation(out=be, in_=bdt[tp][:, :, 0], func=AF.Sigmoid), reads=[bk], writes=[gk + "be"])
            P.op("dve", lambda e: e.tensor_scalar(nbe, be, -1.0, None, ALU.mult), reads=[gk + "be"], writes=[gk + "nb"])

        pieces.append(p_gate_a)
        for g in (2, 0, 1):
            pieces.append(lambda g=g: p_conv_a(g))
            pieces.append(lambda g=g: p_conv_b(g))
            if g == 2:
                pieces.append(p_silu_v)
                pieces.append(p_gate_b)
            else:
                pieces.append(lambda g=g: p_l2_a(g))
                pieces.append(lambda g=g: p_l2_b(g))
        pieces.append(p_qk_out)
        return pieces

    def block_level(ti):
        tp = ti % 2
        cq = ti % 3
        qk, kk, vk = "QT%d" % cq, "KT%d" % tp, "VT%d" % tp
        G = gsc[tp]
        gk = "gsc%d" % tp
        xg, ax, ee, ll, sp_, gg, be, nbe = (G[:, i, :] for i in range(8))
        def bk(j):
            return ps_sl[j], "psl%d" % j

        hopn = [0]

        def hop():
            if chain_q:
                chain_q.pop(0)()
            hopn[0] += 1
            if hopn[0] % 3 == 0 and pre_q:
                pre_q.pop(0)()

        def stage_done():
            hop()
            if pre_q:
                pre_q.pop(0)()

        J = range(4)
        sfx = ["_%d" % j for j in J]
        csl = [slice(j * 128, (j + 1) * 128) for j in J]
        g_ = [gg[:, j:j + 1] for j in J]
        be_ = [be[:, j:j + 1] for j in J]
        nbe_ = [nbe[:, j:j + 1] for j in J]
        ck = ["_%d_%d" % (tp, j) for j in J]
        for j in J:
            P.op("dve", lambda e: e.tensor_scalar(gM[j][:], m1f, g_[j], None, ALU.mult), reads=["consts_f", gk + "g"], writes=["gM" + sfx[j]])
            P.op("dve", lambda e: e.tensor_scalar(rgc[j][:], cindf, g_[j], None, ALU.mult), reads=["consts_f", gk + "g"], writes=["rgc" + sfx[j]])
        hop()
        for j in J:
            b_, bkk = bk(j)
            P.op("pe", lambda e: e.matmul(b_[:, 0, :], lhsT=gM[j][:], rhs=m2f, start=True, stop=True), reads=["gM" + sfx[j], "consts_f"], writes=[bkk])
            P.op("pe", lambda e: e.matmul(b_[:, 1, :], lhsT=m2f, rhs=gM[j][:], start=True, stop=True), reads=["gM" + sfx[j], "consts_f"], writes=[bkk])
            P.op("pe", lambda e: e.matmul(b_[:, 2, 0:1], lhsT=m1f, rhs=g_[j], start=True, stop=True), reads=[gk + "g", "consts_f"], writes=[bkk])
            P.op("pe", lambda e: e.matmul(b_[:, 2, 1:2], lhsT=m2f, rhs=g_[j], start=True, stop=True), reads=[gk + "g", "consts_f"], writes=[bkk])
            P.op("pe", lambda e: e.matmul(b_[:, 2, 2:4], lhsT=onesf, rhs=rgc[j][:], start=True, stop=True), reads=["rgc" + sfx[j], "consts_f"], writes=[bkk])
        hop()
        for j in J:
            b_, bkk = bk(j)
            P.op("act", lambda e: e.activation(out=D1[j][:], in_=b_[:, 0, :], func=AF.Exp), reads=[bkk], writes=["D1" + sfx[j]])
            P.op("act", lambda e: e.activation(out=D2[j][:], in_=b_[:, 1, :], func=AF.Exp), reads=[bkk], writes=["D2" + sfx[j]])
            P.op("act", lambda e: e.activation(out=smx[tp][j][:], in_=b_[:, 2, 0:4], func=AF.Exp), reads=[bkk], writes=["smx" + ck[j]])
        hop()
        for j in J:
            P.op("dve", lambda e: e.tensor_tensor(out=bg[j][:], in0=be_[j], in1=smx[tp][j][:, 0:1], op=ALU.mult),
                 reads=[gk + "be", "smx" + ck[j]], writes=["bg" + sfx[j]])
            P.op("pool", lambda e: e.tensor_tensor(out=PTm[j][:], in0=D2[j][:], in1=m1f, op=ALU.mult), reads=["D2" + sfx[j], "consts_f"], writes=["PTm" + sfx[j]])
        hop()
        stage_done()
        for j in J:
            P.op("pe", lambda e: e.transpose(ps_tb[:, 2 * j, :], KT[tp][:, csl[j]], idb), reads=[kk, "consts_b"], writes=["ptb"])
            P.op("pe", lambda e: e.transpose(ps_tb[:, 2 * j + 1, :], VT[tp][:, csl[j]], idb), reads=[vk, "consts_b"], writes=["ptb"])
        hop()
        for j in J:
            P.op("dve", lambda e: e.tensor_scalar(bgK[j][:], ps_tb[:, 2 * j, :], bg[j][:, 0:1], None, ALU.mult), reads=["ptb", "bg" + sfx[j]], writes=["bgK" + sfx[j]])
        hop()
        for j in J:
            P.op("act", lambda e: e.activation(out=KD[tp][j][:], in_=ps_tb[:, 2 * j, :], func=AF.Copy, scale=smx[tp][j][:, 1:2]),
                 reads=["ptb", "smx" + ck[j]], writes=["KD" + ck[j]])
            P.op("act", lambda e: e.activation(out=bV[tp][j][:], in_=ps_tb[:, 2 * j + 1, :], func=AF.Copy, scale=be_[j]),
                 reads=["ptb", gk + "be"], writes=["bV" + ck[j]])
        hop()
        stage_done()
        for j in J:
            b_, bkk = bk(j)
            P.op("pe", lambda e: e.matmul(b_[:, 0, :], lhsT=KT[tp][:, csl[j]], rhs=KT[tp][:, csl[j]], start=True, stop=True), reads=[kk], writes=[bkk])
            P.op("pe", lambda e: e.matmul(b_[:, 1, :], lhsT=KT[tp][:, csl[j]], rhs=QT[cq][:, csl[j]], start=True, stop=True), reads=[kk, qk], writes=[bkk])
        hop()
        for j in J:
            b_, bkk = bk(j)
            P.op("dve", lambda e: e.tensor_tensor(out=Bm[j][:], in0=b_[:, 0, :], in1=D1[j][:], op=ALU.mult), reads=[bkk, "D1" + sfx[j]], writes=["Bm" + sfx[j]])
            P.op("dve", lambda e: e.tensor_tensor(out=PT[tp][j][:], in0=b_[:, 1, :], in1=PTm[j][:], op=ALU.mult), reads=[bkk, "PTm" + sfx[j]], writes=["PT" + ck[j]])
            P.op("dve", lambda e: e.scalar_tensor_tensor(out=Bq[j][:], in0=Bm[j][:], scalar=nbe_[j], in1=m2f, op0=ALU.mult, op1=ALU.mult),
                 reads=["Bm" + sfx[j], gk + "nb", "consts_f"], writes=["B" + sfx[j]])
        hop()
        stage_done()
        for j in J:
            b_, bkk = bk(j)
            P.op("pe", lambda e: e.transpose(b_[:, 2, :], Bq[j][:], idf), reads=["B" + sfx[j], "consts_f"], writes=[bkk])
        hop()
        for j in J:
            b_, bkk = bk(j)
            P.op("act", lambda e: e.copy(Nq[j][:], b_[:, 2, :]), reads=[bkk], writes=["N" + sfx[j]])
            P.op("pool", lambda e: e.tensor_tensor(out=Rt[j][:], in0=Bq[j][:], in1=idf, op=ALU.add), reads=["B" + sfx[j], "consts_f"], writes=["Rt" + sfx[j]])
        hop()
        for j in J:
            P.op("dve", lambda e: e.tensor_tensor(out=Rq[j][:], in0=Nq[j][:], in1=idf, op=ALU.add), reads=["N" + sfx[j], "consts_f"], writes=["R" + sfx[j]])
        hop()
        stage_done()
        for lvl in range(5):
            last = lvl == 4
            for j in J:
                b_, bkk = bk(j)
                P.op("pe", lambda e: e.matmul(b_[:, 0, :], lhsT=Bq[j][:], rhs=Nq[j][:], start=True, stop=True), reads=["B" + sfx[j], "N" + sfx[j]], writes=[bkk])
                if not last:
                    P.op("pe", lambda e: e.matmul(b_[:, 1, :], lhsT=Nq[j][:], rhs=Bq[j][:], start=True, stop=True), reads=["B" + sfx[j], "N" + sfx[j]], writes=[bkk])
            hop()
            for j in J:
                b_, bkk = bk(j)
                P.op("act", lambda e: e.copy(Nq[j][:], b_[:, 0, :]), reads=[bkk], writes=["N" + sfx[j]])
                if not last:
                    P.op("act", lambda e: e.copy(Bq[j][:], b_[:, 1, :]), reads=[bkk], writes=["B" + sfx[j]])
            hop()
            stage_done()
            for j in J:
                b_, bkk = bk(j)
                P.op("pe", lambda e: e.matmul(b_[:, 2, :], lhsT=Rt[j][:], rhs=Nq[j][:], start=True, stop=True), reads=["Rt" + sfx[j], "N" + sfx[j]], writes=[bkk])
                if not last:
                    P.op("pe", lambda e: e.matmul(b_[:, 3, :], lhsT=Rq[j][:], rhs=Bq[j][:], start=True, stop=True), reads=["R" + sfx[j], "B" + sfx[j]], writes=[bkk])
            hop()
            for j in J:
                b_, bkk = bk(j)
                if not last:
                    P.op("dve", lambda e: e.tensor_tensor(out=Rq[j][:], in0=b_[:, 2, :], in1=Rq[j][:], op=ALU.add), reads=[bkk, "R" + sfx[j]], writes=["R" + sfx[j]])
                    P.op("dve", lambda e: e.tensor_tensor(out=Rt[j][:], in0=b_[:, 3, :], in1=Rt[j][:], op=ALU.add), reads=[bkk, "Rt" + sfx[j]], writes=["Rt" + sfx[j]])
                else:
                    P.op("dve", lambda e: e.tensor_tensor(out=TTb[tp][j][:], in0=b_[:, 2, :], in1=Rq[j][:], op=ALU.add), reads=[bkk, "R" + sfx[j]], writes=["TTb" + ck[j]])
            hop()
            stage_done()
        for j in J:
            b_, bkk = bk(j)
            P.op("pe", lambda e: e.matmul(b_[:, 0, :], lhsT=bgK[j][:], rhs=TTb[tp][j][:], start=True, stop=True), reads=["bgK" + sfx[j], "TTb" + ck[j]], writes=[bkk])
        hop()
        for j in J:
            b_, bkk = bk(j)
            P.op("act", lambda e: e.mul(nWT[tp][j][:], b_[:, 0, :], -1.0), reads=[bkk], writes=["nWT" + ck[j]])
        hop()
        stage_done()
        while chain_q:
            chain_q.pop(0)()
        while pre_q:
            pre_q.pop(0)()

        def chunk_hops(j, c, tp=tp, cq=cq, qk=qk, ck=ck, ti=ti, csl=csl):
            r = slice(64 * c, 64 * c + 64)
            si = sidx[0]
            so, sn_ = si % 2, (si + 1) % 2
            sidx[0] += 1
            o2 = j % 2
            u, qs, sn, pu = ps_ch[:, 0, :], ps_ch[:, 1, :], ps_ch[:, 2, :], ps_ch[:, 3, :]

            def h1():
                P.op("pe", lambda e: e.matmul(u, lhsT=TTb[tp][j][r, :], rhs=bV[tp][j][r, :], start=True, stop=False),
                     reads=["TTb" + ck[j], "bV" + ck[j]], writes=["ps_ch"])
                P.op("pe", lambda e: e.matmul(u, lhsT=nWT[tp][j][:], rhs=Sb[so][:], start=False, stop=True),
                     reads=["nWT" + ck[j], "Sb%d" % so], writes=["ps_ch"])
                P.op("pe", lambda e: e.matmul(qs, lhsT=QT[cq][:, csl[j]], rhs=Sb[so][:], start=True, stop=True),
                     reads=[qk, "Sb%d" % so], writes=["ps_ch"])

            def h2():
                P.op("dve", lambda e: e.tensor_copy(Ub[o2][r, :], u[r, :]), reads=["ps_ch"], writes=["Ub%d_%d" % (o2, c)])

            def h3():
                P.op("pe", lambda e: e.matmul(sn, lhsT=KD[tp][j][r, :], rhs=Ub[o2][r, :], start=True, stop=True),
                     reads=["KD" + ck[j], "Ub%d_%d" % (o2, c)], writes=["ps_ch"])
                P.op("pe", lambda e: e.matmul(pu, lhsT=PT[tp][j][r, :], rhs=Ub[o2][r, :], start=True, stop=True),
                     reads=["PT" + ck[j], "Ub%d_%d" % (o2, c)], writes=["ps_ch"])

            def h4():
                P.op("dve", lambda e: e.scalar_tensor_tensor(out=Sf[sn_][:], in0=Sf[so][:], scalar=smx[tp][j][:, 2 + c:3 + c], in1=sn,
                                                             op0=ALU.mult, op1=ALU.add), reads=["Sf%d" % so, "smx" + ck[j], "ps_ch"], writes=["Sf%d" % sn_])

            def h5():
                P.op("act", lambda e: e.copy(Sb[sn_][:], Sf[sn_][:]), reads=["Sf%d" % sn_], writes=["Sb%d" % sn_])
                P.op("dve", lambda e: e.tensor_copy(pus[o2][r, :], pu[r, :]), reads=["ps_ch"], writes=["pus%d_%d" % (o2, c)])
                P.op("dve", lambda e: e.scalar_tensor_tensor(out=Osb[o2][r, :], in0=qs[r, :], scalar=smx[tp][j][r, 0:1], in1=pus[o2][r, :],
                                                             op0=ALU.mult, op1=ALU.add), reads=["ps_ch", "smx" + ck[j], "pus%d_%d" % (o2, c)],
                     writes=["Osb%d_%d" % (o2, c)])
                if c == 1:
                    blk = ti * 4 + j
                    P.op("sp", lambda e: e.dma_start(out=o_out[blk * 128:(blk + 1) * 128, :], in_=Osb[o2][:]),
                         reads=["Osb%d_0" % o2, "Osb%d_1" % o2], chan="ost%d" % o2)
            return [h1, h2, h3, h4, h5]

        for j in J:
            for c in range(2):
                chain_q.extend(chunk_hops(j, c))
    pre_q = []
    for f in tile_level(0):
        f()
    for ti in range(NT):
        if ti + 1 < NT:
            pre_q.extend(tile_level(ti + 1))
        block_level(ti)
    while chain_q:
        chain_q.pop(0)()


def prep_weight(P, stg, stgkey, src2d, n_c, ncols, dst, dst_col0, scale_fn, dkey, skeys, cnt, dst_c0=0):
    pw = min(2048 // n_c, ncols)
    for col in range(0, ncols, pw):
        w = min(pw, ncols - col)
        b = cnt[0] % len(stg)
        cnt[0] += 1
        sv = stg[b][:, 0:n_c * w].rearrange("p (c n) -> p c n", c=n_c)
        k = stgkey + str(b)
        P.op("sp", lambda e: e.dma_start(out=sv, in_=src2d[:, col:col + w].rearrange("(c p) n -> p c n", p=128)),
             writes=[k], chan=k)
        dv = dst[:, dst_c0:dst_c0 + n_c, dst_col0 + col:dst_col0 + col + w]
        if scale_fn is None:
            eng = "act" if (cnt[0] % 2) else "dve"
            if eng == "act":
                P.op("act", lambda e: e.copy(dv, sv), reads=[k], writes=[dkey])
            else:
                P.op("dve", lambda e: e.tensor_copy(dv, sv), reads=[k], writes=[dkey])
        else:
            for c in range(n_c):
                sc_ = scale_fn(c)
                if c % 2:
                    P.op("act", lambda e: e.activation(out=dst[:, c, dst_col0 + col:dst_col0 + col + w], in_=sv[:, c, :],
                                                       func=AF.Copy, scale=sc_), reads=[k] + skeys, writes=[dkey])
                else:
                    P.op("dve", lambda e: e.tensor_scalar(dst[:, c, dst_col0 + col:dst_col0 + col + w], sv[:, c, :], sc_, None, ALU.mult),
                         reads=[k] + skeys, writes=[dkey])


TB = 2
TW = TB * 128


def load_consts(P, A, cst, pre):
    cf = A.sb([128, NCONST], F32, "cf")
    cb = A.sb([128, NCONST], BF16, "cb")
    P.op("sp", lambda e: e.dma_start(out=cf[:], in_=cst[:, :]), writes=[pre + "consts_f"], chan=pre + "cf")
    P.op("dve", lambda e: e.tensor_copy(cb[:], cf[:]), reads=[pre + "consts_f"], writes=["consts_b"])
    return cf, cb


def build_phase2a(nc, P, A, NTM, x2, oa2, validc, w_in, w_ba, w_bb, w_out, nwm_d, gnw_d, biasT_d, cst, xmid,
                  o_all=None, qsel_d=None, RT=None):
    NT2 = NTM + 3
    TC = NTM * TW
    sb, ps = A.sb, A.ps
    cf, cb = load_consts(P, A, cst, "a")
    idb = cb[:, C_ID:C_ID + 128]
    epsc = make_epsc(P, A, "dve")
    nwm = sb([128, NCH], F32, "nwm")
    gnw = sb([128, 1], F32, "gnw")
    P.op("sp", lambda e: e.dma_start(out=nwm[:], in_=nwm_d[:, :]), writes=["nwm0"], chan="nwm")
    P.op("sp", lambda e: e.dma_start(out=gnw[:], in_=gnw_d[:, :]), writes=["gnw"], chan="gnw")
    P.op("dve", lambda e: e.tensor_scalar(nwm[:], nwm[:], 32.0, None, ALU.mult), reads=["nwm0"], writes=["nwm"])
    Wi = sb([128, NCH, 4096], BF16, "Wi")
    WbA = sb([128, 4, 1024], BF16, "WbA")
    WbB = sb([128, 4, 1024], BF16, "WbB")
    Wo = sb([128, NCH, 1024], BF16, "Wo")
    es_stg = ExitStack()
    stg = [es_stg.enter_context(nc.sbuf_tensor("astg%d" % i, [128, 2048], F32)) for i in range(2)]
    cnt = [0]
    prep_weight(P, stg, "astg", w_in[:, 1536:2048], 8, 512, Wi, 0, lambda c: nwm[:, c:c + 1], "Wi", ["nwm"], cnt)
    prep_weight(P, stg, "astg", w_in[:, 2056:5640], 8, 3584, Wi, 512, lambda c: nwm[:, c:c + 1], "Wi", ["nwm"], cnt)
    prep_weight(P, stg, "astg", w_ba, 4, 1024, WbA, 0, lambda c: gnw[:, 0:1], "WbA", ["gnw"], cnt)
    prep_weight(P, stg, "astg", w_bb, 4, 1024, WbB, 0, None, "WbB", [], cnt)
    prep_weight(P, stg, "astg", w_out, 8, 1024, Wo, 0, None, "Wo", [], cnt)
    es_stg.close()
    P.barrier()
    biasT = sb([128, 8, 640], F32, "biasT")
    P.op("sp", lambda e: e.dma_start(out=biasT[:], in_=biasT_d[:, :, :]), writes=["biasT"], chan="biasT")
    valid = sb([128, NT2 * TB], F32, "valid")
    P.op("sp", lambda e: e.dma_start(out=valid[:], in_=validc[:, :]), writes=["valid"], chan="valid")
    ones8 = sb([128, 8, 1], F32, "ones8")
    P.op("dve", lambda e: e.memset(ones8[:], 1.0), writes=["ones8"])

    xt = [sb([128, TB, D], F32, "xt") for _ in range(2)]
    junk = sb([128, D], BF16, "junk")
    ss = sb([128, 8], F32, "ss")
    rs = sb([128, 8], F32, "rs")
    xn = [sb([128, D], BF16, "xn") for _ in range(2)]
    hT = sb([128, NCH, TW], BF16, "hT")
    KTb = sb([128, 4, 8 * 128], BF16, "KTb")
    Vaug = sb([128, 8, 8, 65], BF16, "Vaug")
    QTb = sb([128, 4, TW], BF16, "QTb")
    zs = sb([128, TB, 512], F32, "zs")
    oat = sb([128, TB, 512], F32, "oat")
    cands = None
    if o_all is not None:
        cand = sb([128, 4, 512], F32, "cand")
        if RT is None:
            cands = [(lambda r, q_=q_: o_all[q_ * TC + r:q_ * TC + r + 128, :].rearrange("p (h d) -> p h d", h=4)) for q_ in range(4)]
        else:
            o_alls, chunks, CR = o_all
            views = [a.rearrange("(r t) d -> t r d", r=8) for a in o_alls]

            def cand_ap(row, b_):
                i, off = row // CR, row % CR
                return views[i][off:off + 128, 4 * b_:4 * b_ + 4, :]
            cands = [(lambda r, b_=c_ // 4, q_=c_ % 4: cand_ap(q_ * TC + r, b_)) for c_ in range(8)]
        qsel = sb([128, len(cands)], F32, "qsel")
        P.op("sp", lambda e: e.dma_start(out=qsel[:], in_=qsel_d[:, :]), writes=["qsel"], chan="qsel")
    ssa = sb([128, 4], F32, "ssa")
    ra = sb([128, 4], F32, "ra")
    oan = sb([128, 512], BF16, "oan")
    oaT = sb([128, 4, TW], BF16, "oaT")
    ob = sb([128, 512], BF16, "ob")
    obT = sb([128, 4, TW], BF16, "obT")
    scs = [sb([128, 640], F32, "scs")] * 2
    PTb = [sb([128, 640], BF16, "PTb") for _ in range(2)]
    rden = sb([128, 8], F32, "rden")
    sg = [sb([128, 2 * TW], F32, "sg") for _ in range(2)]
    tt_ = [sb([128, 2 * TW], F32, "tt") for _ in range(2)]
    mixT = sb([128, NCH, TW], BF16, "mixT")

    ps_tr = ps([128, NCH, 128], BF16, "ps_tr")
    pbig = [ps([128, 512], F32, "pbig") for _ in range(5)]
    ps_o = [ps([128, 4, 65], F32, "ps_o") for _ in range(2)]
    bi = [0]

    def big():
        i = bi[0] % 5
        bi[0] += 1
        return pbig[i], "pbig%d" % i

    scale_q = 64.0 ** -0.5
    for tt in range(NT2):
        tp = tt % 2
        xk = "axt%d" % tp
        P.op("sp", lambda e: e.dma_start(out=xt[tp][:], in_=x2[tt * TW:(tt + 1) * TW, :].rearrange("(j p) d -> p j d", p=128)),
             writes=[xk], chan=xk)
        for j in range(TB):
            norm_block(P, epsc, xt[tp][:, j, :], xk, ss[:, j:j + 1], rs[:, j:j + 1], "ass%d" % j, junk[:], "ajunk",
                       xn[j % 2][:], "axn%d" % (j % 2), ps_tr, "aps_tr", idb, hT[:, :, j * 128:(j + 1) * 128], "ahT")
        ring0 = (tt * TB) % 8
        for m in range(4):
            pb_, pk = big()
            for c in range(NCH):
                P.op("pe", lambda e: e.matmul(pb_[:, 0:TW], lhsT=Wi[:, c, 1024 + m * 128:1024 + (m + 1) * 128], rhs=hT[:, c, :],
                                              start=(c == 0), stop=(c == NCH - 1)), reads=["Wi", "ahT"], writes=[pk])
            P.op("act", lambda e: e.copy(KTb[:, m, ring0 * 128:ring0 * 128 + TW], pb_[:, 0:TW]), reads=[pk], writes=["KTb"])
        for j in range(TB):
            slot = ring0 + j
            pb_, pk = big()
            for c in range(NCH):
                P.op("pe", lambda e: e.matmul(pb_[:, :], lhsT=hT[:, c, j * 128:(j + 1) * 128], rhs=Wi[:, c, 1536:2048],
                                              start=(c == 0), stop=(c == NCH - 1)), reads=["Wi", "ahT"], writes=[pk])
            P.op("dve", lambda e: e.tensor_copy(Vaug[:, slot, :, 0:64], pb_[:, :].rearrange("p (h d) -> p h d", h=8)),
                 reads=[pk], writes=["Vaug"])
            P.op("act", lambda e: e.activation(out=Vaug[:, slot, :, 64:65], in_=ones8[:], func=AF.Copy,
                                               scale=valid[:, tt * TB + j:tt * TB + j + 1]),
                 reads=["ones8", "valid"], writes=["Vaug"])
        if tt < 2:
            continue
        for m in range(4):
            pb_, pk = big()
            for c in range(NCH):
                P.op("pe", lambda e: e.matmul(pb_[:, 0:TW], lhsT=Wi[:, c, 512 + m * 128:512 + (m + 1) * 128], rhs=hT[:, c, :],
                                              start=(c == 0), stop=(c == NCH - 1)), reads=["Wi", "ahT"], writes=[pk])
            P.op("act", lambda e: e.mul(QTb[:, m, :], pb_[:, 0:TW], scale_q), reads=[pk], writes=["QTb"])
        for j in range(TB):
            pb_, pk = big()
            for c in range(NCH):
                P.op("pe", lambda e: e.matmul(pb_[:, :], lhsT=hT[:, c, j * 128:(j + 1) * 128], rhs=Wi[:, c, 0:512],
                                              start=(c == 0), stop=(c == NCH - 1)), reads=["Wi", "ahT"], writes=[pk])
            P.op("act", lambda e: e.activation(out=zs[:, j, :], in_=pb_[:, :], func=AF.Silu), reads=[pk], writes=["zs%d" % j])
        if o_all is None:
            P.op("sp", lambda e: e.dma_start(out=oat[:], in_=oa2[(tt - 2) * TW:(tt - 1) * TW, :].rearrange("(j p) d -> p j d", p=128)),
                 writes=["oat"], chan="oat")
        else:
            for j in range(TB):
                for cc_, cf_ in enumerate(cands):
                    k = cc_ % 4
                    ck_ = "cand_%d" % k
                    P.op("sp", lambda e: e.dma_start(out=cand[:, k, :].rearrange("p (h d) -> p h d", h=4),
                                                     in_=cf_((tt - 2) * TW + j * 128)),
                         reads=["oall"], writes=[ck_], chan=ck_)
                    if cc_ == 0:
                        P.op("dve", lambda e: e.tensor_scalar(oat[:, j, :], cand[:, k, :], qsel[:, cc_:cc_ + 1], None, ALU.mult),
                             reads=[ck_, "qsel"], writes=["oat"])
                    else:
                        P.op("dve", lambda e: e.scalar_tensor_tensor(out=oat[:, j, :], in0=cand[:, k, :], scalar=qsel[:, cc_:cc_ + 1],
                                                                     in1=oat[:, j, :], op0=ALU.mult, op1=ALU.add),
                             reads=[ck_, "qsel", "oat"], writes=["oat"])
        for j in range(TB):
            g = tt * TB + j
            for h in range(8):
                m, r = h // 2, slice(64 * (h % 2), 64 * (h % 2) + 64)
                p1, p1k = big()
                p2, p2k = big()
                for kb in range(5):
                    slot = (g - 4 + kb) % 8
                    dst = p1[:, kb * 128:(kb + 1) * 128] if kb < 4 else p2[:, 0:128]
                    P.op("pe", lambda e: e.matmul(dst, lhsT=KTb[r, m, slot * 128:(slot + 1) * 128], rhs=QTb[r, m, j * 128:(j + 1) * 128],
                                                  start=True, stop=True), reads=["KTb", "QTb"], writes=[p1k if kb < 4 else p2k])
                sp_ = h % 2
                P.op("dve", lambda e: e.tensor_tensor(out=scs[sp_][:, 0:512], in0=p1[:, :], in1=biasT[:, h, 0:512], op=ALU.add),
                     reads=[p1k, "biasT"], writes=["scsa"])
                P.op("dve", lambda e: e.tensor_tensor(out=scs[sp_][:, 512:640], in0=p2[:, 0:128], in1=biasT[:, h, 512:640], op=ALU.add),
                     reads=[p2k, "biasT"], writes=["scsb"])
                P.op("act", lambda e: e.activation(out=PTb[sp_][:], in_=scs[sp_][:], func=AF.Exp),
                     reads=["scsa", "scsb"], writes=["PTb%d" % sp_])
                for kb in range(5):
                    slot = (g - 4 + kb) % 8
                    P.op("pe", lambda e: e.matmul(ps_o[h // 4][:, h % 4, :], lhsT=PTb[sp_][:, kb * 128:(kb + 1) * 128],
                                                  rhs=Vaug[:, slot, h, :], start=(kb == 0), stop=(kb == 4)),
                         reads=["PTb%d" % sp_, "Vaug"], writes=["ps_o%d" % (h // 4)])
            for hg in range(2):
                P.op("dve", lambda e: e.tensor_scalar(rden[:, hg * 4:hg * 4 + 4], ps_o[hg][:, :, 64], 1e-30, None, ALU.add),
                     reads=["ps_o%d" % hg], writes=["rden%da" % hg])
                P.op("dve", lambda e: e.reciprocal(rden[:, hg * 4:hg * 4 + 4], rden[:, hg * 4:hg * 4 + 4]),
                     reads=["rden%da" % hg], writes=["rden%d" % hg])
            for h in range(8):
                P.op("act", lambda e: e.activation(out=ob[:, h * 64:(h + 1) * 64], in_=ps_o[h // 4][:, h % 4, 0:64], func=AF.Copy,
                                                   scale=rden[:, h:h + 1]), reads=["ps_o%d" % (h // 4), "rden%d" % (h // 4)], writes=["ob"])
            for c in range(4):
                P.op("pe", lambda e: e.transpose(ps_tr[:, c, :], ob[:, c * 128:(c + 1) * 128], idb), reads=["ob", "consts_b"], writes=["aps_tr"])
            P.op("act", lambda e: e.copy(obT[:, :, j * 128:(j + 1) * 128], ps_tr[:, 0:4, :]), reads=["aps_tr"], writes=["obT"])
            for hh in range(4):
                P.op("act", lambda e: e.activation(out=junk[:, 0:128], in_=oat[:, j, hh * 128:(hh + 1) * 128], func=AF.Square,
                                                   accum_out=ssa[:, hh:hh + 1]), reads=["oat"], writes=["ajunk", "ssa"])
            P.op("act", lambda e: e.activation(out=ra[:], in_=ssa[:], func=AF.Ln, scale=1.0 / 128.0, bias=epsc[:, 1:2]),
                 reads=["ssa", "epsc"], writes=["ra0"])
            P.op("act", lambda e: e.activation(out=ra[:], in_=ra[:], func=AF.Exp, scale=-0.5), reads=["ra0"], writes=["ra"])
            for hh in range(4):
                P.op("dve", lambda e: e.scalar_tensor_tensor(out=oan[:, hh * 128:(hh + 1) * 128], in0=oat[:, j, hh * 128:(hh + 1) * 128],
                                                             scalar=ra[:, hh:hh + 1], in1=zs[:, j, hh * 128:(hh + 1) * 128],
                                                             op0=ALU.mult, op1=ALU.mult), reads=["oat", "ra", "zs%d" % j], writes=["oan"])
            for c in range(4):
                P.op("pe", lambda e: e.transpose(ps_tr[:, 4 + c, :], oan[:, c * 128:(c + 1) * 128], idb), reads=["oan", "consts_b"], writes=["aps_tr"])
            P.op("act", lambda e: e.copy(oaT[:, :, j * 128:(j + 1) * 128], ps_tr[:, 4:8, :]), reads=["aps_tr"], writes=["oaT"])
        for mo in range(8):
            py, pyk = big()
            pg, pgk = big()
            for half, (Wb, src, skey) in enumerate(((WbA, oaT, "oaT"), (WbB, obT, "obT"))):
                for c in range(4):
                    P.op("pe", lambda e: e.matmul(py[:, half * TW:(half + 1) * TW], lhsT=Wb[:, c, mo * 128:(mo + 1) * 128], rhs=src[:, c, :],
                                                  start=(c == 0), stop=(c == 3)), reads=["WbA", "WbB", skey], writes=[pyk])
            for half in range(2):
                col0 = 2048 + half * 1024 + mo * 128
                for c in range(NCH):
                    P.op("pe", lambda e: e.matmul(pg[:, half * TW:(half + 1) * TW], lhsT=Wi[:, c, col0:col0 + 128], rhs=hT[:, c, :],
                                                  start=(c == 0), stop=(c == NCH - 1)), reads=["Wi", "ahT"], writes=[pgk])
            q2 = mo % 2
            P.op("act", lambda e: e.activation(out=sg[q2][:], in_=pg[:, :], func=AF.Sigmoid), reads=[pgk], writes=["sg%d" % q2])
            P.op("dve", lambda e: e.tensor_tensor(out=tt_[q2][:], in0=py[:, :], in1=sg[q2][:], op=ALU.mult),
                 reads=[pyk, "sg%d" % q2], writes=["tt%d" % q2])
            P.op("dve", lambda e: e.tensor_tensor(out=mixT[:, mo, :], in0=tt_[q2][:, 0:TW], in1=tt_[q2][:, TW:2 * TW], op=ALU.add),
                 reads=["tt%d" % q2], writes=["mixT"])
        for j in range(TB):
            for half in range(2):
                po, pok = big()
                for c in range(NCH):
                    P.op("pe", lambda e: e.matmul(po[:, :], lhsT=mixT[:, c, j * 128:(j + 1) * 128], rhs=Wo[:, c, half * 512:(half + 1) * 512],
                                                  start=(c == 0), stop=(c == NCH - 1)), reads=["mixT", "Wo"], writes=[pok])
                P.op("dve", lambda e: e.tensor_tensor(out=xt[tp][:, j, half * 512:(half + 1) * 512], in0=po[:, :],
                                                      in1=xt[tp][:, j, half * 512:(half + 1) * 512], op=ALU.add), reads=[pok, xk], writes=[xk])
        P.op("sp", lambda e: e.dma_start(out=xmid[(tt - 2) * TW:(tt - 1) * TW, :].rearrange("(j p) d -> p j d", p=128), in_=xt[tp][:]),
             reads=[xk], writes=["xmid_d"], chan="xmst%d" % tp)


def build_phase2b(nc, P, A, NTM, xmid, w_up, w_down, nwf_d, cfw_d, cfb_d, wfin_d, cst, out2):
    sb, ps = A.sb, A.ps
    cf, cb = load_consts(P, A, cst, "b")
    idb = cb[:, C_ID:C_ID + 128]
    epsc = make_epsc(P, A)
    nwf = sb([128, NCH], F32, "nwf")
    P.op("sp", lambda e: e.dma_start(out=nwf[:], in_=nwf_d[:, :]), writes=["nwf0"], chan="nwf")
    P.op("dve", lambda e: e.tensor_scalar(nwf[:], nwf[:], 32.0, None, ALU.mult), reads=["nwf0"], writes=["nwf"])
    cfw = sb([128, 44, 3], F32, "cfw")
    cfb = sb([128, 44], F32, "cfb")
    wfb = sb([128, D], F32, "wfb")
    P.op("sp", lambda e: e.dma_start(out=cfw[:], in_=cfw_d[:, :, :]), writes=["cfw"], chan="cfw")
    P.op("sp", lambda e: e.dma_start(out=cfb[:], in_=cfb_d[:, :]), writes=["cfb"], chan="cfb")
    P.op("sp", lambda e: e.dma_start(out=wfb[:], in_=wfin_d[:, :]), writes=["wfb0"], chan="wfb")
    P.op("pool", lambda e: e.tensor_scalar(wfb[:], wfb[:], 32.0, None, ALU.mult), reads=["wfb0"], writes=["wfb"])
    Wu = sb([128, NCH, 2 * D_FF], BF16, "Wu")
    Wd = sb([128, 22, D], BF16, "Wd")
    es_stg = ExitStack()
    stg = [es_stg.enter_context(nc.sbuf_tensor("bstg%d" % i, [128, 2048], F32)) for i in range(2)]
    cnt = [0]
    prep_weight(P, stg, "bstg", w_up, 8, 2 * D_FF, Wu, 0, lambda c: nwf[:, c:c + 1], "Wu", ["nwf"], cnt)
    prep_weight(P, stg, "bstg", w_down[0:1408, :], 11, D, Wd, 0, None, "Wd", [], cnt, dst_c0=0)
    prep_weight(P, stg, "bstg", w_down[1408:2816, :], 11, D, Wd, 0, None, "Wd", [], cnt, dst_c0=11)
    es_stg.close()
    P.barrier()

    xm = [sb([128, TB, D], F32, "xm") for _ in range(2)]
    junk = sb([128, D], BF16, "junk")
    ss = sb([128, 8], F32, "ss")
    rs = sb([128, 8], F32, "rs")
    xn = [sb([128, D], BF16, "xn") for _ in range(2)]
    h2T = sb([128, NCH, TW], BF16, "h2T")
    ubuf = [sb([128, 2, TW + 2], F32, "ubuf") for _ in range(2)]
    cv = [sb([128, 2, TW], F32, "cv") for _ in range(2)]
    sgt = [sb([128, TW], F32, "sgt") for _ in range(2)]
    uh = sb([128, 22, 2, 2], F32, "uh")
    actT = sb([128, 22, TW], BF16, "actT")
    outt = [sb([128, D], F32, "outt") for _ in range(2)]
    P.op("pool", lambda e: e.memset(uh[:], 0.0), writes=["uh"])

    ps_tr = ps([128, NCH, 128], BF16, "ps_tr")
    pbig = [ps([128, 512], F32, "pbig") for _ in range(6)]
    bi = [0]

    def big():
        i = bi[0] % 6
        bi[0] += 1
        return pbig[i], "bpbig%d" % i

    for u in range(NTM + 1):
        tp = u % 2
        xk = "bxm%d" % tp
        P.op("sp", lambda e: e.dma_start(out=xm[tp][:], in_=xmid[u * TW:(u + 1) * TW, :].rearrange("(j p) d -> p j d", p=128)),
             reads=["xmid_d"], writes=[xk], chan=xk)
        for j in range(TB):
            norm_block(P, epsc, xm[tp][:, j, :], xk, ss[:, j:j + 1], rs[:, j:j + 1], "bss%d" % j, junk[:], "bjunk",
                       xn[j % 2][:], "bxn%d" % (j % 2), ps_tr, "bps_tr", idb, h2T[:, :, j * 128:(j + 1) * 128], "h2T")
        for m in range(22):
            q2 = m % 2
            pg, pgk = big()
            for half in range(2):
                col0 = half * D_FF + m * 128
                for c in range(NCH):
                    P.op("pe", lambda e: e.matmul(pg[:, half * TW:(half + 1) * TW], lhsT=Wu[:, c, col0:col0 + 128], rhs=h2T[:, c, :],
                                                  start=(c == 0), stop=(c == NCH - 1)), reads=["Wu", "h2T"], writes=[pgk])
            uk = "ubuf%d" % q2
            P.op("pool", lambda e: e.tensor_copy(ubuf[q2][:, :, 0:2], uh[:, m, :, :]), reads=["uh"], writes=[uk + "h"])
            P.op("act", lambda e: e.copy(ubuf[q2][:, :, 2:TW + 2], pg[:, :].rearrange("p (s n) -> p s n", s=2)), reads=[pgk], writes=[uk])
            P.op("pool", lambda e: e.tensor_copy(uh[:, m, :, :], ubuf[q2][:, :, TW:TW + 2]), reads=[uk, uk + "h"], writes=["uh"])
            if u == 0:
                continue
            ck = "cv%d" % q2
            for s_ in range(2):
                ch = s_ * 22 + m
                eng = "dve"
                P.op("act", lambda e: e.activation(out=cv[q2][:, s_, :], in_=ubuf[q2][:, s_, 0:TW], func=AF.Identity,
                                                   scale=cfw[:, ch, 0:1], bias=cfb[:, ch:ch + 1]),
                     reads=[uk, uk + "h", "cfw", "cfb"], writes=[ck + str(s_)])
                for k in range(1, 3):
                    P.op(eng, lambda e: e.scalar_tensor_tensor(out=cv[q2][:, s_, :], in0=ubuf[q2][:, s_, k:k + TW], scalar=cfw[:, ch, k:k + 1],
                                                               in1=cv[q2][:, s_, :], op0=ALU.mult, op1=ALU.add),
                         reads=[uk, uk + "h", "cfw", ck + str(s_)], writes=[ck + str(s_)])
            P.op("act", lambda e: e.activation(out=sgt[q2][:], in_=cv[q2][:, 0, :], func=AF.Silu), reads=[ck + "0"], writes=["sgt%d" % q2])
            P.op("dve", lambda e: e.tensor_tensor(out=actT[:, m, :], in0=sgt[q2][:], in1=cv[q2][:, 1, :], op=ALU.mult),
                 reads=["sgt%d" % q2, ck + "1"], writes=["actT"])
        if u == 0:
            continue
        for j in range(TB):
            for half in range(2):
                po, pok = big()
                for m in range(22):
                    P.op("pe", lambda e: e.matmul(po[:, :], lhsT=actT[:, m, j * 128:(j + 1) * 128], rhs=Wd[:, m, half * 512:(half + 1) * 512],
                                                  start=(m == 0), stop=(m == 21)), reads=["actT", "Wd"], writes=[pok])
                P.op("dve", lambda e: e.tensor_tensor(out=xm[tp][:, j, half * 512:(half + 1) * 512], in0=po[:, :],
                                                      in1=xm[tp][:, j, half * 512:(half + 1) * 512], op=ALU.add), reads=[pok, xk], writes=[xk])
            o2 = j % 2
            P.op("act", lambda e: e.activation(out=junk[:], in_=xm[tp][:, j, :], func=AF.Square, accum_out=ss[:, 4 + j:5 + j]),
                 reads=[xk], writes=["bjunk", "fss%d" % j])
            P.op("act", lambda e: e.activation(out=rs[:, 4 + j:5 + j], in_=ss[:, 4 + j:5 + j], func=AF.Ln, bias=epsc[:, 0:1]),
                 reads=["fss%d" % j, "epsc"], writes=["frs%da" % j])
            P.op("act", lambda e: e.activation(out=rs[:, 4 + j:5 + j], in_=rs[:, 4 + j:5 + j], func=AF.Exp, scale=-0.5),
                 reads=["frs%da" % j], writes=["frs%d" % j])
            P.op("dve", lambda e: e.scalar_tensor_tensor(out=outt[o2][:], in0=xm[tp][:, j, :], scalar=rs[:, 4 + j:5 + j], in1=wfb[:],
                                                         op0=ALU.mult, op1=ALU.mult), reads=[xk, "frs%d" % j, "wfb"], writes=["outt%d" % o2])
            P.op("sp", lambda e: e.dma_start(out=out2[(u - 1) * TW + j * 128:(u - 1) * TW + (j + 1) * 128, :], in_=outt[o2][:]),
                 reads=["outt%d" % o2], chan="ost%d" % o2)


def _phase1_inputs(inp, T):
    x = np.asarray(inp["x"], np.float32)
    w_in = np.asarray(inp["w_in"], np.float32)[0]
    conv = np.asarray(inp["conv_qkv_w"], np.float32)[0]
    a_log = np.asarray(inp["a_log"], np.float32)[0]
    dtb = np.asarray(inp["dt_bias"], np.float32)[0]
    nw = np.asarray(inp["norm_mix_w"], np.float32)[0]
    cst = chunk_consts()
    maps = []
    for core in range(8):
        b, h = core // 4, core % 4
        cols = np.concatenate([np.arange(h * 128, (h + 1) * 128), 512 + np.arange(h * 128, (h + 1) * 128),
                               1024 + np.arange(h * 128, (h + 1) * 128), [2048 + h], [2052 + h]])
        w1 = np.ascontiguousarray(w_in[:, cols])
        cw = np.zeros((128, 12), np.float32)
        for g in range(3):
            cw[:, 4 * g:4 * g + 4] = conv[:, g * 512 + h * 128:g * 512 + (h + 1) * 128].T
        sc = np.zeros((128, 2), np.float32)
        sc[:, 0] = a_log[h]
        sc[:, 1] = dtb[h]
        maps.append({"x1": np.ascontiguousarray(x[b, :T]), "w1": w1, "cw1": cw, "sc1": sc,
                     "nw1": np.ascontiguousarray(nw.reshape(8, 128).T), "cst": cst})
    return maps


def build_p1_program(T):
    nc = bass.Bass("TRN2", target_bir_lowering=False)
    x1 = nc.dram_tensor("x1", [T, D], F32, kind="ExternalInput").ap()
    w1 = nc.dram_tensor("w1", [D, 386], F32, kind="ExternalInput").ap()
    cw1 = nc.dram_tensor("cw1", [128, 12], F32, kind="ExternalInput").ap()
    sc1 = nc.dram_tensor("sc1", [128, 2], F32, kind="ExternalInput").ap()
    nw1 = nc.dram_tensor("nw1", [128, 8], F32, kind="ExternalInput").ap()
    cst = nc.dram_tensor("cst", [128, NCONST], F32, kind="ExternalInput").ap()
    o_out = nc.dram_tensor("o1", [T, 128], F32, kind="ExternalOutput").ap()
    es = ExitStack()
    P = Prog(nc)
    A = Ctx(nc, es, P)
    build_phase1(nc, es, P, A, T, x1, w1, cw1, sc1, nw1, cst, o_out)
    P.finish()
    P.emit(es)
    es.close()
    return nc, P


def run_phase1(inp, T):
    nc, P = build_p1_program(T)
    maps = _phase1_inputs(inp, T)
    res = run_bass_kernel_spmd(nc, maps, core_ids=list(range(8)))
    o = np.zeros((2, T, 4, 128), np.float32)
    for core in range(8):
        o[core // 4, :, core % 4, :] = res.results[core]["o1"]
    return o


def _bias_tile(rel):
    ki = np.arange(128)[:, None]
    qi = np.arange(128)[None, :]
    out = np.zeros((128, 8, 640), np.float32)
    for kb in range(5):
        dist = qi - ki + (4 - kb) * 128
        idx = np.clip(dist, -128, 128) + 128
        cdiff = 2 * (4 - kb) + qi // 64 - ki // 64
        ok = (cdiff >= 0) & (cdiff <= 8)
        for h in range(8):
            out[:, h, kb * 128:(kb + 1) * 128] = np.where(ok, rel[h][idx], NEG)
    return out


def _phase2_inputs(inp, o1, T):
    TC = T // 4
    NTM = TC // TW
    x = np.asarray(inp["x"], np.float32)
    w_in = np.ascontiguousarray(np.asarray(inp["w_in"], np.float32)[0])
    cfw_ = np.asarray(inp["conv_ffn_w"], np.float32)[0]
    cfb_ = np.asarray(inp["conv_ffn_b"], np.float32)[0]
    shared = {
        "w_in": w_in,
        "w_ba": np.ascontiguousarray(np.asarray(inp["w_branch_a"], np.float32)[0]),
        "w_bb": np.ascontiguousarray(np.asarray(inp["w_branch_b"], np.float32)[0]),
        "w_out": np.ascontiguousarray(np.asarray(inp["w_out"], np.float32)[0]),
        "w_up": np.ascontiguousarray(np.asarray(inp["w_up"], np.float32)[0]),
        "w_down": np.ascontiguousarray(np.asarray(inp["w_down"], np.float32)[0]),
        "nwm": np.ascontiguousarray(np.asarray(inp["norm_mix_w"], np.float32)[0].reshape(8, 128).T),
        "nwf": np.ascontiguousarray(np.asarray(inp["norm_ffn_w"], np.float32)[0].reshape(8, 128).T),
        "gnw": np.ascontiguousarray(np.asarray(inp["gdn_norm_w"], np.float32)[0].reshape(128, 1)),
        "biasT": _bias_tile(np.asarray(inp["rel_bias"], np.float32)[0]),
        "cfw": np.ascontiguousarray(cfw_.reshape(3, 44, 128).transpose(2, 1, 0)),
        "cfb": np.ascontiguousarray(cfb_.reshape(44, 128).T),
        "wfin": np.ascontiguousarray(np.broadcast_to(np.asarray(inp["norm_final_w"], np.float32)[None, :], (128, D))),
        "cst2": chunk_consts(),
    }
    maps = []
    for core in range(8):
        b, q = core // 4, core % 4
        t0 = q * TC
        lo = t0 - 3 * TW
        x2 = np.zeros(((NTM + 3) * TW, D), np.float32)
        s0 = max(lo, 0)
        x2[s0 - lo:] = x[b, s0:t0 + TC]
        pos = lo + np.arange((NTM + 3) * TW)
        valid = (pos >= 0).astype(np.float32).reshape((NTM + 3) * TB, 128).T
        m = dict(shared)
        m["x2"] = x2
        m["validc"] = np.ascontiguousarray(valid)
        if o1 is not None:
            lo2 = t0 - TW
            oa2 = np.zeros(((NTM + 1) * TW, 512), np.float32)
            s1 = max(lo2, 0)
            oa2[s1 - lo2:] = o1[b, s1:t0 + TC].reshape(-1, 512)
            m["oa2"] = oa2
        maps.append(m)
    return maps


def _declare_p2(nc, NTM, with_oa):
    d = {}
    def inp(name, shape):
        d[name] = nc.dram_tensor(name, list(shape), F32, kind="ExternalInput").ap()
    inp("x2", [(NTM + 3) * TW, D])
    if with_oa:
        inp("oa2", [(NTM + 1) * TW, 512])
    inp("validc", [128, (NTM + 3) * TB])
    inp("w_in", [D, D_IN]); inp("w_ba", [512, D]); inp("w_bb", [512, D]); inp("w_out", [D, D])
    inp("w_up", [D, 2 * D_FF]); inp("w_down", [D_FF, D]); inp("nwm", [128, 8]); inp("nwf", [128, 8]); inp("gnw", [128, 1])
    inp("biasT", [128, 8, 640]); inp("cfw", [128, 44, 3]); inp("cfb", [128, 44]); inp("wfin", [128, D]); inp("cst2", [128, NCONST])
    d["xmid"] = nc.dram_tensor("xmid", [(NTM + 1) * TW, D], F32, kind="Internal").ap()
    d["out2"] = nc.dram_tensor("out2", [NTM * TW, D], F32, kind="ExternalOutput").ap()
    return d


def _emit_phase2(nc, P, d, NTM, oa_ap, o_all=None, qsel_d=None, RT=None):
    es_a = ExitStack()
    build_phase2a(nc, P, Ctx(nc, es_a, P), NTM, d["x2"], oa_ap, d["validc"], d["w_in"], d["w_ba"], d["w_bb"], d["w_out"],
                  d["nwm"], d["gnw"], d["biasT"], d["cst2"], d["xmid"], o_all=o_all, qsel_d=qsel_d, RT=RT)
    es_a.close()
    P.pool_hold = False
    P.barrier()
    es_b = ExitStack()
    build_phase2b(nc, P, Ctx(nc, es_b, P), NTM, d["xmid"], d["w_up"], d["w_down"], d["nwf"], d["cfw"], d["cfb"], d["wfin"],
                  d["cst2"], d["out2"])
    es_b.close()


def build_p2_program(T):
    NTM = (T // 4) // TW
    nc = bass.Bass("TRN2", target_bir_lowering=False)
    d = _declare_p2(nc, NTM, True)
    P = Prog(nc)
    _emit_phase2(nc, P, d, NTM, d["oa2"])
    P.finish()
    es = ExitStack()
    P.emit(es)
    es.close()
    return nc, P


def run_phase2(inp, o1, T):
    nc, P = build_p2_program(T)
    maps = _phase2_inputs(inp, o1, T)
    res = run_bass_kernel_spmd(nc, maps, core_ids=list(range(8)))
    TC = T // 4
    out = np.zeros((2, T, D), np.float32)
    for core in range(8):
        out[core // 4, (core % 4) * TC:(core % 4 + 1) * TC] = res.results[core]["out2"]
    return out


T_FULL = 16384


def build_fused_program(T):
    NTM = (T // 4) // TW
    RT = T + TW
    nc = bass.Bass("TRN2", target_bir_lowering=False)
    x1 = nc.dram_tensor("x1", [T, D], F32, kind="ExternalInput").ap()
    w1 = nc.dram_tensor("w1", [D, 386], F32, kind="ExternalInput").ap()
    cw1 = nc.dram_tensor("cw1", [128, 12], F32, kind="ExternalInput").ap()
    sc1 = nc.dram_tensor("sc1", [128, 2], F32, kind="ExternalInput").ap()
    nw1 = nc.dram_tensor("nw1", [128, 8], F32, kind="ExternalInput").ap()
    cst = nc.dram_tensor("cst", [128, NCONST], F32, kind="ExternalInput").ap()
    qsel_d = nc.dram_tensor("qsel", [128, 8], F32, kind="ExternalInput").ap()
    o_loc = nc.dram_tensor("o_loc", [RT, 128], F32, kind="Internal").ap()
    CR = RT
    chunks = [(r0, min(CR, RT - r0)) for r0 in range(0, RT, CR)]
    o_alls = [nc.dram_tensor("o_all%d" % i, [8 * n, 128], F32, kind="Internal").ap() for i, (r0, n) in enumerate(chunks)]
    d = _declare_p2(nc, NTM, False)
    P = Prog(nc)
    es1 = ExitStack()
    A1 = Ctx(nc, es1, P)
    zt = A1.sb([128, 128], F32, "zt")
    P.op("pool", lambda e: e.memset(zt[:], 0.0), writes=["zt"])
    for i in range(TW // 128):
        P.op("sp", lambda e: e.dma_start(out=o_loc[i * 128:(i + 1) * 128, :], in_=zt[:]), reads=["zt"], chan="zt")
    build_phase1(nc, es1, P, A1, T, x1, w1, cw1, sc1, nw1, cst, o_loc[TW:RT, :])
    es1.close()
    P.barrier()
    for i, (r0, n) in enumerate(chunks):
        P.op("pool", lambda e: e.collective_compute("AllGather", ALU.bypass, replica_groups=[list(range(8))],
                                                    ins=[o_loc[r0:r0 + n, :]], outs=[o_alls[i][:, :]]),
             writes=["oall"], chan="cc", inc_override=1)
    P.pool_hold = True
    _emit_phase2(nc, P, d, NTM, None, o_all=(o_alls, chunks, CR), qsel_d=qsel_d, RT=RT)
    P.finish()
    es = ExitStack()
    P.emit(es)
    es.close()
    return nc, P


def build_fused_nocc(T):
    NTM = (T // 4) // TW
    RT = T + TW
    nc = bass.Bass("TRN2", target_bir_lowering=False)
    x1 = nc.dram_tensor("x1", [T, D], F32, kind="ExternalInput").ap()
    w1a = nc.dram_tensor("w1a", [4, D, 386], F32, kind="ExternalInput").ap()
    cw1a = nc.dram_tensor("cw1a", [4, 128, 12], F32, kind="ExternalInput").ap()
    sc1a = nc.dram_tensor("sc1a", [4, 128, 2], F32, kind="ExternalInput").ap()
    nw1 = nc.dram_tensor("nw1", [128, 8], F32, kind="ExternalInput").ap()
    cst = nc.dram_tensor("cst", [128, NCONST], F32, kind="ExternalInput").ap()
    qsel_d = nc.dram_tensor("qsel", [128, 4], F32, kind="ExternalInput").ap()
    o_loc = nc.dram_tensor("o_loc", [RT, 512], F32, kind="Internal").ap()
    hts = nc.dram_tensor("hts", [T // 512, 128, NCH, 512], BF16, kind="Internal").ap()
    d = _declare_p2(nc, NTM, False)
    P = Prog(nc)
    for h in range(4):
        es1 = ExitStack()
        A1 = Ctx(nc, es1, P)
        if h == 0:
            zt = A1.sb([128, 512], F32, "zt")
            P.op("pool", lambda e: e.memset(zt[:], 0.0), writes=["zt"])
            for i in range(TW // 128):
                P.op("sp", lambda e: e.dma_start(out=o_loc[i * 128:(i + 1) * 128, :], in_=zt[:]), reads=["zt"], chan="zt")
        build_phase1(nc, es1, P, A1, T, x1, w1a[h], cw1a[h], sc1a[h], nw1, cst, o_loc[TW:RT, h * 128:(h + 1) * 128],
                     hts=hts, hts_mode=("save" if h == 0 else "load"))
        es1.close()
        P.barrier()
    _emit_phase2(nc, P, d, NTM, None, o_all=o_loc, qsel_d=qsel_d, RT=None)
    P.finish()
    es = ExitStack()
    P.emit(es)
    es.close()
    return nc, P


def run_fused_nocc(inp, T):
    nc, P = build_fused_nocc(T)
    m1 = _phase1_inputs(inp, T)
    m2 = _phase2_inputs(inp, None, T)
    maps = []
    for core in range(8):
        b, q = core // 4, core % 4
        m = dict(m2[core])
        m["x1"] = m1[core]["x1"]
        m["nw1"] = m1[core]["nw1"]
        m["cst"] = m1[core]["cst"]
        m["w1a"] = np.stack([m1[4 * b + h]["w1"] for h in range(4)])
        m["cw1a"] = np.stack([m1[4 * b + h]["cw1"] for h in range(4)])
        m["sc1a"] = np.stack([m1[4 * b + h]["sc1"] for h in range(4)])
        qs = np.zeros((128, 4), np.float32)
        qs[:, q] = 1.0
        m["qsel"] = qs
        maps.append(m)
    res = run_bass_kernel_spmd(nc, maps, core_ids=list(range(8)))
    TC = T // 4
    out = np.zeros((2, T, D), np.float32)
    for core in range(8):
        out[core // 4, (core % 4) * TC:(core % 4 + 1) * TC] = res.results[core]["out2"]
    return out


def run_fused(inp, T):
    nc, P = build_fused_program(T)
    m1 = _phase1_inputs(inp, T)
    m2 = _phase2_inputs(inp, None, T)
    maps = []
    for core in range(8):
        m = dict(m1[core])
        m.update(m2[core])
        qs = np.zeros((128, 8), np.float32)
        qs[:, core] = 1.0
        m["qsel"] = qs
        maps.append(m)
    res = run_bass_kernel_spmd(nc, maps, core_ids=list(range(8)))
    TC = T // 4
    out = np.zeros((2, T, D), np.float32)
    for core in range(8):
        out[core // 4, (core % 4) * TC:(core % 4 + 1) * TC] = res.results[core]["out2"]
    return out


def kernel(**inputs):
    return run_fused_nocc(inputs, T_FULL)
```

```python
from collections import defaultdict
from contextlib import ExitStack

import numpy as np
import concourse.bass as bass
import concourse.mybir as mybir
from concourse.bass_utils import run_bass_kernel_spmd

F32 = mybir.dt.float32
BF16 = mybir.dt.bfloat16
AF = mybir.ActivationFunctionType
ALU = mybir.AluOpType

D = 1024
NCH = 8
EPS = 1e-6
CH = 64
DK = 128
D_IN = 5640
D_FF = 2816
NEG = -30000.0


PSUM_PREFIXES = ("psl", "ptb", "ppj", "ps_tr", "aps_tr", "bps_tr", "pbig", "bpbig", "ps_o")


class _Rec:
    def __getattr__(self, name):
        def f(*a, **k):
            self.call = (name, a, k)
            return self
        return f


class Prog:
    ENGS = ("pe", "act", "dve", "pool", "sp")

    def __init__(self, nc):
        self.nc = nc
        self.streams = {e: [] for e in self.ENGS}
        self.count = defaultdict(int)
        self.lastw = {}
        self.readers = defaultdict(list)
        self.waited = defaultdict(int)
        self.nops = 0
        self.epoch = 0
        self.pool_hold = False
        import os
        self.cut = int(os.environ["PCUT"]) if "PCUT" in os.environ else None

    def _dep(self, eng, rec):
        semkey, val = rec[0], rec[1]
        if eng == "pool" and (semkey.startswith("dma_cc@") or self.pool_hold):
            return
        if self.waited[(eng, semkey)] < val:
            self.waited[(eng, semkey)] = val
            self.streams[eng].append(("wait", semkey, val))

    def op(self, eng, fn, reads=(), writes=(), chan=None, inc_override=None):
        if self.cut is not None and self.nops >= self.cut:
            return
        isdma = chan is not None
        for k in reads:
            w = self.lastw.get(k)
            if w is not None:
                self._dep(eng, w)
            if k.startswith(PSUM_PREFIXES):
                for r in self.readers[k]:
                    if r[2] != eng:
                        self._dep(eng, r)
        for k in writes:
            w = self.lastw.get(k)
            if w is not None:
                if not (w[2] == eng == "pe" and not w[3] and not isdma):
                    self._dep(eng, w)
            for r in self.readers[k]:
                if r[2] != eng or r[3] or isdma:
                    self._dep(eng, r)
        if isdma:
            semkey, inc = "dma_%s@%d" % (chan, self.epoch), (inc_override or 16)
        else:
            semkey, inc = "%s@%d" % (eng, self.epoch), 1
        self.count[semkey] += inc
        rec = (semkey, self.count[semkey], eng, isdma)
        rec_ = _Rec()
        fn(rec_)
        self.streams[eng].append(("op", rec_.call, semkey, inc))
        for k in writes:
            self.lastw[k] = rec
            self.readers[k] = []
        for k in reads:
            self.readers[k].append(rec)
        self.nops += 1

    def barrier(self):
        for e in self.ENGS:
            for semkey, val in list(self.count.items()):
                if val:
                    self._dep(e, (semkey, val))
        self.lastw.clear()
        self.readers.clear()
        self.epoch += 1

    def finish(self):
        for semkey, val in list(self.count.items()):
            if semkey.startswith("dma_"):
                self._dep("sp", (semkey, val))
        for semkey, val in list(self.count.items()):
            if not semkey.startswith("dma_") and val:
                self._dep("sp", (semkey, val))

    def emit(self, es):
        nc = self.nc
        sems = {}
        for i, k in enumerate(sorted(self.count)):
            sems[k] = es.enter_context(nc.semaphore("s%d" % i))
        block = es.enter_context(nc.Block())
        streams = self.streams

        def run(eng_handle, items):
            for it in items:
                if it[0] == "wait":
                    eng_handle.wait_ge(sems[it[1]], it[2])
                else:
                    name, a, k = it[1]
                    getattr(eng_handle, name)(*a, **k).then_inc(sems[it[2]], it[3])

        @block.tensor
        def _(e):
            run(e, streams["pe"])

        @block.scalar
        def _(e):
            run(e, streams["act"])

        @block.vector
        def _(e):
            run(e, streams["dve"])

        @block.gpsimd
        def _(e):
            run(e, streams["pool"])

        @block.sync
        def _(e):
            run(e, streams["sp"])


class Ctx:
    _uid = [0]

    def __init__(self, nc, es, P):
        self.nc, self.es, self.P = nc, es, P
        Ctx._uid[0] += 1
        self.n = Ctx._uid[0] * 1000

    def sb(self, shape, dt=F32, name=None):
        self.n += 1
        return self.es.enter_context(self.nc.sbuf_tensor("%s_%d" % (name or "t", self.n), list(shape), dt))

    def ps(self, shape, dt=F32, name=None):
        self.n += 1
        return self.es.enter_context(self.nc.psum_tensor("%s_%d" % (name or "p", self.n), list(shape), dt))


def chunk_consts():
    j = np.arange(128)
    same = (j[:, None] // CH) == (j[None, :] // CH)
    m1 = (same & (j[:, None] <= j[None, :])).astype(np.float32)
    m2 = (same & (j[:, None] > j[None, :])).astype(np.float32)
    ident = np.eye(128, dtype=np.float32)
    ones = np.ones((128, 128), np.float32)
    cind = np.zeros((128, 128), np.float32)
    cind[:64, 0] = 1.0
    cind[64:, 1] = 1.0
    return np.concatenate([m1, m2, ident, ones, cind], axis=1)


C_M1, C_M2, C_ID, C_ONES, C_CIND = 0, 128, 256, 384, 512
NCONST = 640


def make_epsc(P, A, eng="pool"):
    epsc = A.sb([128, 2], F32, "epsc")
    P.op(eng, lambda e: e.memset(epsc[:, 0:1], D * EPS), writes=["epsc0"])
    P.op(eng, lambda e: e.memset(epsc[:, 1:2], EPS), reads=["epsc0"], writes=["epsc"])
    return epsc


def norm_block(P, epsc, x_blk, xkey, ss, rs, sskey, junk, junkkey, xn, xnkey, ps_tr, pskey, idb, hT_dst, hTkey,
               wrow=None, wkey=None):
    P.op("act", lambda e: e.activation(out=junk, in_=x_blk, func=AF.Square, accum_out=ss),
         reads=[xkey], writes=[junkkey, sskey])
    P.op("act", lambda e: e.activation(out=rs, in_=ss, func=AF.Ln, bias=epsc[:, 0:1]),
         reads=[sskey, "epsc"], writes=[sskey + "r0"])
    P.op("act", lambda e: e.activation(out=rs, in_=rs, func=AF.Exp, scale=-0.5),
         reads=[sskey + "r0"], writes=[sskey + "r"])
    if wrow is None:
        P.op("dve", lambda e: e.tensor_scalar(xn, x_blk, rs, None, ALU.mult),
             reads=[xkey, sskey + "r"], writes=[xnkey])
    else:
        P.op("dve", lambda e: e.scalar_tensor_tensor(out=xn, in0=x_blk, scalar=rs, in1=wrow,
                                                      op0=ALU.mult, op1=ALU.mult),
             reads=[xkey, sskey + "r", wkey], writes=[xnkey])
    for c in range(NCH):
        P.op("pe", lambda e, c=c: e.transpose(ps_tr[:, c, :], xn[:, c * 128:(c + 1) * 128], idb),
             reads=[xnkey, "consts_b"], writes=[pskey])
    P.op("act", lambda e: e.copy(hT_dst, ps_tr[:, :, :]), reads=[pskey], writes=[hTkey])


def build_phase1(nc, es, P, A, T, x1, w1, cw1, sc1, nw1, cst, o_out, hts=None, hts_mode=None):
    NT = T // 512
    sb, ps = A.sb, A.ps
    cf = sb([128, NCONST], F32, "cf")
    cb = sb([128, NCONST], BF16, "cb")
    P.op("sp", lambda e: e.dma_start(out=cf[:], in_=cst[:, :]), writes=["consts_f"], chan="cf")
    P.op("dve", lambda e: e.tensor_copy(cb[:], cf[:]), reads=["consts_f"], writes=["consts_b"])
    m1f, m2f = cf[:, C_M1:C_M1 + 128], cf[:, C_M2:C_M2 + 128]
    idf, onesf, cindf = cf[:, C_ID:C_ID + 128], cf[:, C_ONES:C_ONES + 128], cf[:, C_CIND:C_CIND + 2]
    idb, onesb = cb[:, C_ID:C_ID + 128], cb[:, C_ONES:C_ONES + 128]

    epsc = make_epsc(P, A)
    wf = sb([128, NCH, 386], F32, "wf")
    wb = sb([128, NCH, 386], BF16, "wb")
    nw = sb([128, NCH], F32, "nw")
    cw = sb([128, 12], F32, "cw")
    sc = sb([128, 2], F32, "sc")
    negA = sb([128, 1], F32, "negA")
    P.op("sp", lambda e: e.dma_start(out=wf[:], in_=w1.rearrange("(c p) n -> p c n", p=128)), writes=["wf"], chan="wf")
    P.op("sp", lambda e: e.dma_start(out=nw[:], in_=nw1[:, :]), writes=["nw"], chan="nw")
    P.op("sp", lambda e: e.dma_start(out=cw[:], in_=cw1[:, :]), writes=["cw"], chan="cw")
    P.op("sp", lambda e: e.dma_start(out=sc[:], in_=sc1[:, :]), writes=["sc"], chan="sc")
    for c in range(NCH):
        P.op("dve", lambda e, c=c: e.tensor_scalar(wb[:, c, :], wf[:, c, :], nw[:, c:c + 1], 32.0, ALU.mult, ALU.mult),
             reads=["wf", "nw"], writes=["wb"])
    P.op("act", lambda e: e.activation(out=negA[:], in_=sc[:, 0:1], func=AF.Exp), reads=["sc"], writes=["negA0"])
    P.op("dve", lambda e: e.tensor_scalar(negA[:], negA[:], -1.0, None, ALU.mult), reads=["negA0"], writes=["negA"])

    xt = [sb([128, 4, D], F32, "xt") for _ in range(2)]
    junk = sb([128, D], BF16, "junk")
    ss = sb([128, 8], F32, "ss")
    rs = sb([128, 8], F32, "rs")
    xn = [sb([128, D], BF16, "xn") for _ in range(2)]
    hT = [sb([128, NCH, 512], BF16, "hT") for _ in range(2)]
    cbuf = [sb([128, 3 + 512], F32, "cbuf") for _ in range(3)]
    acc = [sb([128, 512], F32, "acc") for _ in range(3)]
    sil = [sb([128, 512], F32, "sil") for _ in range(2)]
    sq = [sb([128, 512], BF16, "sq") for _ in range(2)]
    rn = [sb([128, 512], F32, "rn") for _ in range(2)]
    QT = [sb([128, 512], BF16, "QT") for _ in range(3)]
    KT = [sb([128, 512], BF16, "KT") for _ in range(2)]
    VT = [sb([128, 512], BF16, "VT") for _ in range(2)]
    bdt = [sb([128, 4, 2], F32, "bdt") for _ in range(2)]
    gsc = [sb([128, 8, 4], F32, "gsc") for _ in range(2)]
    def four(shape, dt, name):
        return [sb(shape, dt, name) for _ in range(4)]

    def eight(shape, dt, name):
        return [[sb(shape, dt, name) for _ in range(4)] for _ in range(2)]

    gM = four([128, 128], F32, "gM")
    rgc = four([128, 2], F32, "rgc")
    D1 = four([128, 128], F32, "D1")
    D2 = four([128, 128], F32, "D2")
    bg = four([128, 1], F32, "bg")
    bgK = four([128, 128], BF16, "bgK")
    Bm = four([128, 128], F32, "Bm")
    Bq = four([128, 128], F32, "Bq")
    Nq = four([128, 128], F32, "Nq")
    Rq = four([128, 128], F32, "Rq")
    Rt = four([128, 128], F32, "Rt")
    PTm = four([128, 128], F32, "PTm")
    smx = eight([128, 4], F32, "smx")
    KD = eight([128, 128], BF16, "KD")
    bV = eight([128, 128], BF16, "bV")
    TTb = eight([128, 128], BF16, "TTb")
    PT = eight([128, 128], BF16, "PT")
    nWT = eight([128, 128], BF16, "nWT")
    Ub = [sb([128, 128], BF16, "Ub") for _ in range(2)]
    pus = [sb([128, 128], F32, "pus") for _ in range(2)]
    Osb = [sb([128, 128], F32, "Osb") for _ in range(2)]
    Sf = [sb([128, 128], F32, "Sf") for _ in range(2)]
    Sb = [sb([128, 128], BF16, "Sb") for _ in range(2)]

    ps_tr = ps([128, NCH, 128], BF16, "ps_tr")
    ps_tb = ps([128, 8, 128], BF16, "ps_tb")
    ps_pj = [ps([128, 512], F32, "ps_pj") for _ in range(1)]
    ps_ch = ps([128, 4, 128], F32, "ps_ch")
    ps_sl = [ps([128, 4, 128], F32, "ps_sl") for _ in range(4)]
    pj_i = [0]

    def pjslot():
        i = pj_i[0] % len(ps_pj)
        pj_i[0] += 1
        return ps_pj[i], "ppj%d" % i


    P.op("pool", lambda e: e.memset(Sf[0][:], 0.0), writes=["Sf0"])
    P.op("pool", lambda e: e.memset(Sb[0][:], 0.0), writes=["Sb0"])
    for g in range(3):
        P.op("pool", lambda e, g=g: e.memset(cbuf[g][:, 0:3], 0.0), writes=["cbufh%d" % g])
    sidx = [0]
    chain_q = []

    def tile_level(ti):
        tp = ti % 2
        xk = "xt%d" % tp
        hk = "hT%d" % tp
        bk = "bdt%d" % tp
        cq = ti % 3
        qk, kk, vk = "QT%d" % cq, "KT%d" % tp, "VT%d" % tp
        G = gsc[tp]
        gk = "gsc%d" % tp
        xg, ax, ee, ll, sp_, gg, be, nbe = (G[:, i, :] for i in range(8))
        pieces = []

        def p_load():
            P.op("sp", lambda e, ti=ti, tp=tp: e.dma_start(
                out=xt[tp][:], in_=x1[ti * 512:(ti + 1) * 512, :].rearrange("(j p) d -> p j d", p=128)),
                writes=[xk], chan=xk)
        def p_norm(j):
            bp = j % 2
            norm_block(P, epsc, xt[tp][:, j, :], xk, ss[:, j + 4 * tp:j + 4 * tp + 1], rs[:, j + 4 * tp:j + 4 * tp + 1],
                       "ss%d_%d" % (tp, j), junk[:], "junk", xn[bp][:], "xn%d" % bp, ps_tr, "ps_tr", idb,
                       hT[tp][:, :, j * 128:(j + 1) * 128], hk)

        def p_hload():
            P.op("sp", lambda e: e.dma_start(out=hT[tp][:], in_=hts[ti]), writes=[hk], chan=hk)

        def p_hsave():
            P.op("sp", lambda e: e.dma_start(out=hts[ti], in_=hT[tp][:]), reads=[hk], chan="hsv%d" % tp)

        if hts_mode == "load":
            pieces.append(p_hload)
        else:
            pieces.append(p_load)
            for j in range(4):
                pieces.append(lambda j=j: p_norm(j))
            if hts_mode == "save":
                pieces.append(p_hsave)
        def p_proj(g):
            pj, pjk = pjslot()
            for c in range(NCH):
                P.op("pe", lambda e, g=g, c=c, pj=pj: e.matmul(pj[:], lhsT=wb[:, c, g * 128:(g + 1) * 128],
                                                              rhs=hT[tp][:, c, :], start=(c == 0), stop=(c == NCH - 1)),
                     reads=["wb", hk], writes=[pjk])
            P.op("act", lambda e, g=g, pj=pj: e.copy(cbuf[g][:, 3:515], pj[:]), reads=[pjk], writes=["cbufm%d" % g])
        for g in range(3):
            pieces.append(lambda g=g: p_proj(g))
        def p_bd():
            pj, pjk = pjslot()
            for j in range(4):
                for c in range(NCH):
                    P.op("pe", lambda e, j=j, c=c, pj=pj: e.matmul(pj[:, 2 * j:2 * j + 2], lhsT=hT[tp][:, c, j * 128:(j + 1) * 128],
                                                                  rhs=wb[:, c, 384:386], start=(c == 0), stop=(c == NCH - 1)),
                         reads=["wb", hk], writes=[pjk])
            P.op("dve", lambda e, pj=pj: e.tensor_copy(bdt[tp][:].rearrange("p a b -> p (a b)"), pj[:, 0:8]), reads=[pjk], writes=[bk])
        pieces.append(p_bd)
        def p_conv_a(g):
            ck = ["cbufh%d" % g, "cbufm%d" % g]
            ak = "acc%d" % g
            P.op("dve", lambda e: e.tensor_scalar(acc[g][:], cbuf[g][:, 0:512], cw[:, 4 * g:4 * g + 1], None, ALU.mult),
                 reads=ck + ["cw"], writes=[ak])
            P.op("dve", lambda e: e.scalar_tensor_tensor(
                out=acc[g][:], in0=cbuf[g][:, 1:513], scalar=cw[:, 4 * g + 1:4 * g + 2], in1=acc[g][:],
                op0=ALU.mult, op1=ALU.add), reads=ck + ["cw", ak], writes=[ak])

        def p_conv_b(g):
            ck = ["cbufh%d" % g, "cbufm%d" % g]
            ak = "acc%d" % g
            for k in range(2, 4):
                P.op("dve", lambda e: e.scalar_tensor_tensor(
                    out=acc[g][:], in0=cbuf[g][:, k:k + 512], scalar=cw[:, 4 * g + k:4 * g + k + 1], in1=acc[g][:],
                    op0=ALU.mult, op1=ALU.add), reads=ck + ["cw", ak], writes=[ak])
            P.op("pool", lambda e: e.tensor_copy(cbuf[g][:, 0:3], cbuf[g][:, 512:515]),
                 reads=["cbufm%d" % g, ak], writes=["cbufh%d" % g])

        def p_silu_v():
            P.op("act", lambda e: e.activation(out=VT[tp][:], in_=acc[2][:], func=AF.Silu), reads=["acc2"], writes=[vk])

        def p_l2_a(g):
            P.op("act", lambda e: e.activation(out=sil[g][:], in_=acc[g][:], func=AF.Silu), reads=["acc%d" % g], writes=["sil%d" % g])
            P.op("act", lambda e: e.activation(out=sq[g][:], in_=sil[g][:], func=AF.Square), reads=["sil%d" % g], writes=["sq%d" % g])

        def p_l2_b(g):
            pj, pjk = pjslot()
            P.op("pe", lambda e: e.matmul(pj[:], lhsT=onesb, rhs=sq[g][:], start=True, stop=True),
                 reads=["consts_b", "sq%d" % g], writes=[pjk])
            P.op("act", lambda e: e.activation(out=rn[g][:], in_=pj[:], func=AF.Ln, bias=epsc[:, 1:2]),
                 reads=[pjk, "epsc"], writes=["rn%da" % g])
            P.op("act", lambda e: e.activation(out=rn[g][:], in_=rn[g][:], func=AF.Exp, scale=-0.5),
                 reads=["rn%da" % g], writes=["rn%d" % g])

        def p_qk_out():
            P.op("dve", lambda e: e.scalar_tensor_tensor(out=QT[cq][:], in0=sil[0][:], scalar=float(DK) ** -0.5, in1=rn[0][:],
                                                          op0=ALU.mult, op1=ALU.mult), reads=["sil0", "rn0"], writes=[qk])
            P.op("dve", lambda e: e.tensor_tensor(out=KT[tp][:], in0=sil[1][:], in1=rn[1][:], op=ALU.mult), reads=["sil1", "rn1"], writes=[kk])

        def p_gate_a():
            P.op("dve", lambda e: e.tensor_scalar(xg, bdt[tp][:, :, 1], sc[:, 1:2], None, ALU.add), reads=[bk, "sc"], writes=[gk + "a"])
            P.op("dve", lambda e: e.scalar_tensor_tensor(out=ax, in0=xg, scalar=-1.0, in1=xg, op0=ALU.mult, op1=ALU.max), reads=[gk + "a"], writes=[gk + "b"])
            P.op("act", lambda e: e.activation(out=ee, in_=ax, func=AF.Exp, scale=-1.0), reads=[gk + "b"], writes=[gk + "c"])
            P.op("act", lambda e: e.activation(out=ll, in_=ee, func=AF.Ln, bias=1.0), reads=[gk + "c"], writes=[gk + "d"])

        def p_gate_b():
            P.op("dve", lambda e: e.scalar_tensor_tensor(out=sp_, in0=xg, scalar=0.0, in1=ll, op0=ALU.max, op1=ALU.add),
                 reads=[gk + "a", gk + "d"], writes=[gk + "e"])
            P.op("dve", lambda e: e.tensor_scalar(gg, sp_, negA[:, 0:1], None, ALU.mult), reads=[gk + "e", "negA"], writes=[gk + "g"])
            P.op("act", lambda e: e.activation(out=be, in_=bdt[tp][:, :, 0], func=AF.Sigmoid), reads=[bk], writes=[gk + "be"])
            P.op("dve", lambda e: e.tensor_scalar(nbe, be, -1.0, None, ALU.mult), reads=[gk + "be"], writes=[gk + "nb"])

        pieces.append(p_gate_a)
        for g in (2, 0, 1):
            pieces.append(lambda g=g: p_conv_a(g))
            pieces.append(lambda g=g: p_conv_b(g))
            if g == 2:
                pieces.append(p_silu_v)
                pieces.append(p_gate_b)
            else:
                pieces.append(lambda g=g: p_l2_a(g))
                pieces.append(lambda g=g: p_l2_b(g))
        pieces.append(p_qk_out)
        return pieces

    def block_level(ti):
        tp = ti % 2
        cq = ti % 3
        qk, kk, vk = "QT%d" % cq, "KT%d" % tp, "VT%d" % tp
        G = gsc[tp]
        gk = "gsc%d" % tp
        xg, ax, ee, ll, sp_, gg, be, nbe = (G[:, i, :] for i in range(8))
        def bk(j):
            return ps_sl[j], "psl%d" % j

        hopn = [0]

        def hop():
            if chain_q:
                chain_q.pop(0)()
            hopn[0] += 1
            if hopn[0] % 3 == 0 and pre_q:
                pre_q.pop(0)()

        def stage_done():
            hop()
            if pre_q:
                pre_q.pop(0)()

        J = range(4)
        sfx = ["_%d" % j for j in J]
        csl = [slice(j * 128, (j + 1) * 128) for j in J]
        g_ = [gg[:, j:j + 1] for j in J]
        be_ = [be[:, j:j + 1] for j in J]
        nbe_ = [nbe[:, j:j + 1] for j in J]
        ck = ["_%d_%d" % (tp, j) for j in J]
        for j in J:
            P.op("dve", lambda e: e.tensor_scalar(gM[j][:], m1f, g_[j], None, ALU.mult), reads=["consts_f", gk + "g"], writes=["gM" + sfx[j]])
            P.op("dve", lambda e: e.tensor_scalar(rgc[j][:], cindf, g_[j], None, ALU.mult), reads=["consts_f", gk + "g"], writes=["rgc" + sfx[j]])
        hop()
        for j in J:
            b_, bkk = bk(j)
            P.op("pe", lambda e: e.matmul(b_[:, 0, :], lhsT=gM[j][:], rhs=m2f, start=True, stop=True), reads=["gM" + sfx[j], "consts_f"], writes=[bkk])
            P.op("pe", lambda e: e.matmul(b_[:, 1, :], lhsT=m2f, rhs=gM[j][:], start=True, stop=True), reads=["gM" + sfx[j], "consts_f"], writes=[bkk])
            P.op("pe", lambda e: e.matmul(b_[:, 2, 0:1], lhsT=m1f, rhs=g_[j], start=True, stop=True), reads=[gk + "g", "consts_f"], writes=[bkk])
            P.op("pe", lambda e: e.matmul(b_[:, 2, 1:2], lhsT=m2f, rhs=g_[j], start=True, stop=True), reads=[gk + "g", "consts_f"], writes=[bkk])
            P.op("pe", lambda e: e.matmul(b_[:, 2, 2:4], lhsT=onesf, rhs=rgc[j][:], start=True, stop=True), reads=["rgc" + sfx[j], "consts_f"], writes=[bkk])
        hop()
        for j in J:
            b_, bkk = bk(j)
            P.op("act", lambda e: e.activation(out=D1[j][:], in_=b_[:, 0, :], func=AF.Exp), reads=[bkk], writes=["D1" + sfx[j]])
            P.op("act", lambda e: e.activation(out=D2[j][:], in_=b_[:, 1, :], func=AF.Exp), reads=[bkk], writes=["D2" + sfx[j]])
            P.op("act", lambda e: e.activation(out=smx[tp][j][:], in_=b_[:, 2, 0:4], func=AF.Exp), reads=[bkk], writes=["smx" + ck[j]])
        hop()
        for j in J:
            P.op("dve", lambda e: e.tensor_tensor(out=bg[j][:], in0=be_[j], in1=smx[tp][j][:, 0:1], op=ALU.mult),
                 reads=[gk + "be", "smx" + ck[j]], writes=["bg" + sfx[j]])
            P.op("pool", lambda e: e.tensor_tensor(out=PTm[j][:], in0=D2[j][:], in1=m1f, op=ALU.mult), reads=["D2" + sfx[j], "consts_f"], writes=["PTm" + sfx[j]])
        hop()
        stage_done()
        for j in J:
            P.op("pe", lambda e: e.transpose(ps_tb[:, 2 * j, :], KT[tp][:, csl[j]], idb), reads=[kk, "consts_b"], writes=["ptb"])
            P.op("pe", lambda e: e.transpose(ps_tb[:, 2 * j + 1, :], VT[tp][:, csl[j]], idb), reads=[vk, "consts_b"], writes=["ptb"])
        hop()
        for j in J:
            P.op("dve", lambda e: e.tensor_scalar(bgK[j][:], ps_tb[:, 2 * j, :], bg[j][:, 0:1], None, ALU.mult), reads=["ptb", "bg" + sfx[j]], writes=["bgK" + sfx[j]])
        hop()
        for j in J:
            P.op("act", lambda e: e.activation(out=KD[tp][j][:], in_=ps_tb[:, 2 * j, :], func=AF.Copy, scale=smx[tp][j][:, 1:2]),
                 reads=["ptb", "smx" + ck[j]], writes=["KD" + ck[j]])
            P.op("act", lambda e: e.activation(out=bV[tp][j][:], in_=ps_tb[:, 2 * j + 1, :], func=AF.Copy, scale=be_[j]),
                 reads=["ptb", gk + "be"], writes=["bV" + ck[j]])
        hop()
        stage_done()
        for j in J:
            b_, bkk = bk(j)
            P.op("pe", lambda e: e.matmul(b_[:, 0, :], lhsT=KT[tp][:, csl[j]], rhs=KT[tp][:, csl[j]], start=True, stop=True), reads=[kk], writes=[bkk])
            P.op("pe", lambda e: e.matmul(b_[:, 1, :], lhsT=KT[tp][:, csl[j]], rhs=QT[cq][:, csl[j]], start=True, stop=True), reads=[kk, qk], writes=[bkk])
        hop()
        for j in J:
            b_, bkk = bk(j)
            P.op("dve", lambda e: e.tensor_tensor(out=Bm[j][:], in0=b_[:, 0, :], in1=D1[j][:], op=ALU.mult), reads=[bkk, "D1" + sfx[j]], writes=["Bm" + sfx[j]])
            P.op("dve", lambda e: e.tensor_tensor(out=PT[tp][j][:], in0=b_[:, 1, :], in1=PTm[j][:], op=ALU.mult), reads=[bkk, "PTm" + sfx[j]], writes=["PT" + ck[j]])
            P.op("dve", lambda e: e.scalar_tensor_tensor(out=Bq[j][:], in0=Bm[j][:], scalar=nbe_[j], in1=m2f, op0=ALU.mult, op1=ALU.mult),
                 reads=["Bm" + sfx[j], gk + "nb", "consts_f"], writes=["B" + sfx[j]])
        hop()
        stage_done()
        for j in J:
            b_, bkk = bk(j)
            P.op("pe", lambda e: e.transpose(b_[:, 2, :], Bq[j][:], idf), reads=["B" + sfx[j], "consts_f"], writes=[bkk])
        hop()
        for j in J:
            b_, bkk = bk(j)
            P.op("act", lambda e: e.copy(Nq[j][:], b_[:, 2, :]), reads=[bkk], writes=["N" + sfx[j]])
            P.op("pool", lambda e: e.tensor_tensor(out=Rt[j][:], in0=Bq[j][:], in1=idf, op=ALU.add), reads=["B" + sfx[j], "consts_f"], writes=["Rt" + sfx[j]])
        hop()
        for j in J:
            P.op("dve", lambda e: e.tensor_tensor(out=Rq[j][:], in0=Nq[j][:], in1=idf, op=ALU.add), reads=["N" + sfx[j], "consts_f"], writes=["R" + sfx[j]])
        hop()
        stage_done()
        for lvl in range(5):
            last = lvl == 4
            for j in J:
                b_, bkk = bk(j)
                P.op("pe", lambda e: e.matmul(b_[:, 0, :], lhsT=Bq[j][:], rhs=Nq[j][:], start=True, stop=True), reads=["B" + sfx[j], "N" + sfx[j]], writes=[bkk])
                if not last:
                    P.op("pe", lambda e: e.matmul(b_[:, 1, :], lhsT=Nq[j][:], rhs=Bq[j][:], start=True, stop=True), reads=["B" + sfx[j], "N" + sfx[j]], writes=[bkk])
            hop()
            for j in J:
                b_, bkk = bk(j)
                P.op("act", lambda e: e.copy(Nq[j][:], b_[:, 0, :]), reads=[bkk], writes=["N" + sfx[j]])
                if not last:
                    P.op("act", lambda e: e.copy(Bq[j][:], b_[:, 1, :]), reads=[bkk], writes=["B" + sfx[j]])
            hop()
            stage_done()
            for j in J:
                b_, bkk = bk(j)
                P.op("pe", lambda e: e.matmul(b_[:, 2, :], lhsT=Rt[j][:], rhs=Nq[j][:], start=True, stop=True), reads=["Rt" + sfx[j], "N" + sfx[j]], writes=[bkk])
                if not last:
                    P.op("pe", lambda e: e.matmul(b_[:, 3, :], lhsT=Rq[j][:], rhs=Bq[j][:], start=True, stop=True), reads=["R" + sfx[j], "B" + sfx[j]], writes=[bkk])
            hop()
            for j in J:
                b_, bkk = bk(j)
                if not last:
                    P.op("dve", lambda e: e.tensor_tensor(out=Rq[j][:], in0=b_[:, 2, :], in1=Rq[j][:], op=ALU.add), reads=[bkk, "R" + sfx[j]], writes=["R" + sfx[j]])
                    P.op("dve", lambda e: e.tensor_tensor(out=Rt[j][:], in0=b_[:, 3, :], in1=Rt[j][:], op=ALU.add), reads=[bkk, "Rt" + sfx[j]], writes=["Rt" + sfx[j]])
                else:
                    P.op("dve", lambda e: e.tensor_tensor(out=TTb[tp][j][:], in0=b_[:, 2, :], in1=Rq[j][:], op=ALU.add), reads=[bkk, "R" + sfx[j]], writes=["TTb" + ck[j]])
            hop()
            stage_done()
        for j in J:
            b_, bkk = bk(j)
            P.op("pe", lambda e: e.matmul(b_[:, 0, :], lhsT=bgK[j][:], rhs=TTb[tp][j][:], start=True, stop=True), reads=["bgK" + sfx[j], "TTb" + ck[j]], writes=[bkk])
        hop()
        for j in J:
            b_, bkk = bk(j)
            P.op("act", lambda e: e.mul(nWT[tp][j][:], b_[:, 0, :], -1.0), reads=[bkk], writes=["nWT" + ck[j]])
        hop()
        stage_done()
        while chain_q:
            chain_q.pop(0)()
        while pre_q:
            pre_q.pop(0)()

        def chunk_hops(j, c, tp=tp, cq=cq, qk=qk, ck=ck, ti=ti, csl=csl):
            r = slice(64 * c, 64 * c + 64)
            si = sidx[0]
            so, sn_ = si % 2, (si + 1) % 2
            sidx[0] += 1
            o2 = j % 2
            u, qs, sn, pu = ps_ch[:, 0, :], ps_ch[:, 1, :], ps_ch[:, 2, :], ps_ch[:, 3, :]

            def h1():
                P.op("pe", lambda e: e.matmul(u, lhsT=TTb[tp][j][r, :], rhs=bV[tp][j][r, :], start=True, stop=False),
                     reads=["TTb" + ck[j], "bV" + ck[j]], writes=["ps_ch"])
                P.op("pe", lambda e: e.matmul(u, lhsT=nWT[tp][j][:], rhs=Sb[so][:], start=False, stop=True),
                     reads=["nWT" + ck[j], "Sb%d" % so], writes=["ps_ch"])
                P.op("pe", lambda e: e.matmul(qs, lhsT=QT[cq][:, csl[j]], rhs=Sb[so][:], start=True, stop=True),
                     reads=[qk, "Sb%d" % so], writes=["ps_ch"])

            def h2():
                P.op("dve", lambda e: e.tensor_copy(Ub[o2][r, :], u[r, :]), reads=["ps_ch"], writes=["Ub%d_%d" % (o2, c)])

            def h3():
                P.op("pe", lambda e: e.matmul(sn, lhsT=KD[tp][j][r, :], rhs=Ub[o2][r, :], start=True, stop=True),
                     reads=["KD" + ck[j], "Ub%d_%d" % (o2, c)], writes=["ps_ch"])
                P.op("pe", lambda e: e.matmul(pu, lhsT=PT[tp][j][r, :], rhs=Ub[o2][r, :], start=True, stop=True),
                     reads=["PT" + ck[j], "Ub%d_%d" % (o2, c)], writes=["ps_ch"])

            def h4():
                P.op("dve", lambda e: e.scalar_tensor_tensor(out=Sb[sn_][:], in0=Sf[so][:], scalar=smx[tp][j][:, 2 + c:3 + c], in1=sn,
                                                             op0=ALU.mult, op1=ALU.add), reads=["Sf%d" % so, "smx" + ck[j], "ps_ch"], writes=["Sb%d" % sn_])
                P.op("dve", lambda e: e.scalar_tensor_tensor(out=Sf[sn_][:], in0=Sf[so][:], scalar=smx[tp][j][:, 2 + c:3 + c], in1=sn,
                                                             op0=ALU.mult, op1=ALU.add), reads=["Sf%d" % so, "smx" + ck[j], "ps_ch"], writes=["Sf%d" % sn_])

            def h5():
                P.op("dve", lambda e: e.tensor_copy(pus[o2][r, :], pu[r, :]), reads=["ps_ch"], writes=["pus%d_%d" % (o2, c)])
                P.op("dve", lambda e: e.scalar_tensor_tensor(out=Osb[o2][r, :], in0=qs[r, :], scalar=smx[tp][j][r, 0:1], in1=pus[o2][r, :],
                                                             op0=ALU.mult, op1=ALU.add), reads=["ps_ch", "smx" + ck[j], "pus%d_%d" % (o2, c)],
                     writes=["Osb%d_%d" % (o2, c)])
                if c == 1:
                    blk = ti * 4 + j
                    P.op("sp", lambda e: e.dma_start(out=o_out[blk * 128:(blk + 1) * 128, :], in_=Osb[o2][:]),
                         reads=["Osb%d_0" % o2, "Osb%d_1" % o2], chan="ost%d" % o2)
            return [h1, h2, h3, h4, h5]

        for j in J:
            for c in range(2):
                chain_q.extend(chunk_hops(j, c))
    pre_q = []
    for f in tile_level(0):
        f()
    for ti in range(NT):
        if ti + 1 < NT:
            pre_q.extend(tile_level(ti + 1))
        block_level(ti)
    while chain_q:
        chain_q.pop(0)()


def prep_weight(P, stg, stgkey, src2d, n_c, ncols, dst, dst_col0, scale_fn, dkey, skeys, cnt, dst_c0=0):
    pw = min(2048 // n_c, ncols)
    for col in range(0, ncols, pw):
        w = min(pw, ncols - col)
        b = cnt[0] % len(stg)
        cnt[0] += 1
        sv = stg[b][:, 0:n_c * w].rearrange("p (c n) -> p c n", c=n_c)
        k = stgkey + str(b)
        P.op("sp", lambda e: e.dma_start(out=sv, in_=src2d[:, col:col + w].rearrange("(c p) n -> p c n", p=128)),
             writes=[k], chan=k)
        dv = dst[:, dst_c0:dst_c0 + n_c, dst_col0 + col:dst_col0 + col + w]
        if scale_fn is None:
            eng = "act" if (cnt[0] % 2) else "dve"
            if eng == "act":
                P.op("act", lambda e: e.copy(dv, sv), reads=[k], writes=[dkey])
            else:
                P.op("dve", lambda e: e.tensor_copy(dv, sv), reads=[k], writes=[dkey])
        else:
            for c in range(n_c):
                sc_ = scale_fn(c)
                if c % 2:
                    P.op("act", lambda e: e.activation(out=dst[:, c, dst_col0 + col:dst_col0 + col + w], in_=sv[:, c, :],
                                                       func=AF.Copy, scale=sc_), reads=[k] + skeys, writes=[dkey])
                else:
                    P.op("dve", lambda e: e.tensor_scalar(dst[:, c, dst_col0 + col:dst_col0 + col + w], sv[:, c, :], sc_, None, ALU.mult),
                         reads=[k] + skeys, writes=[dkey])


TB = 2
TW = TB * 128


def load_consts(P, A, cst, pre):
    cf = A.sb([128, NCONST], F32, "cf")
    cb = A.sb([128, NCONST], BF16, "cb")
    P.op("sp", lambda e: e.dma_start(out=cf[:], in_=cst[:, :]), writes=[pre + "consts_f"], chan=pre + "cf")
    P.op("dve", lambda e: e.tensor_copy(cb[:], cf[:]), reads=[pre + "consts_f"], writes=["consts_b"])
    return cf, cb


def build_phase2a(nc, P, A, NTM, x2, oa2, validc, w_in, w_ba, w_bb, w_out, nwm_d, gnw_d, biasT_d, cst, xmid,
                  o_all=None, qsel_d=None, RT=None):
    NT2 = NTM + 3
    TC = NTM * TW
    sb, ps = A.sb, A.ps
    cf, cb = load_consts(P, A, cst, "a")
    idb = cb[:, C_ID:C_ID + 128]
    epsc = make_epsc(P, A, "dve")
    nwm = sb([128, NCH], F32, "nwm")
    gnw = sb([128, 1], F32, "gnw")
    P.op("sp", lambda e: e.dma_start(out=nwm[:], in_=nwm_d[:, :]), writes=["nwm0"], chan="nwm")
    P.op("sp", lambda e: e.dma_start(out=gnw[:], in_=gnw_d[:, :]), writes=["gnw"], chan="gnw")
    P.op("dve", lambda e: e.tensor_scalar(nwm[:], nwm[:], 32.0, None, ALU.mult), reads=["nwm0"], writes=["nwm"])
    Wi = sb([128, NCH, 4096], BF16, "Wi")
    WbA = sb([128, 4, 1024], BF16, "WbA")
    WbB = sb([128, 4, 1024], BF16, "WbB")
    Wo = sb([128, NCH, 1024], BF16, "Wo")
    es_stg = ExitStack()
    stg = [es_stg.enter_context(nc.sbuf_tensor("astg%d" % i, [128, 2048], F32)) for i in range(2)]
    cnt = [0]
    prep_weight(P, stg, "astg", w_in[:, 1536:2048], 8, 512, Wi, 0, lambda c: nwm[:, c:c + 1], "Wi", ["nwm"], cnt)
    prep_weight(P, stg, "astg", w_in[:, 2056:5640], 8, 3584, Wi, 512, lambda c: nwm[:, c:c + 1], "Wi", ["nwm"], cnt)
    prep_weight(P, stg, "astg", w_ba, 4, 1024, WbA, 0, lambda c: gnw[:, 0:1], "WbA", ["gnw"], cnt)
    prep_weight(P, stg, "astg", w_bb, 4, 1024, WbB, 0, None, "WbB", [], cnt)
    prep_weight(P, stg, "astg", w_out, 8, 1024, Wo, 0, None, "Wo", [], cnt)
    es_stg.close()
    P.barrier()
    biasT = sb([128, 8, 640], F32, "biasT")
    P.op("sp", lambda e: e.dma_start(out=biasT[:], in_=biasT_d[:, :, :]), writes=["biasT"], chan="biasT")
    valid = sb([128, NT2 * TB], F32, "valid")
    P.op("sp", lambda e: e.dma_start(out=valid[:], in_=validc[:, :]), writes=["valid"], chan="valid")
    ones8 = sb([128, 8, 1], F32, "ones8")
    P.op("dve", lambda e: e.memset(ones8[:], 1.0), writes=["ones8"])

    xt = [sb([128, TB, D], F32, "xt") for _ in range(2)]
    junk = sb([128, D], BF16, "junk")
    ss = sb([128, 8], F32, "ss")
    rs = sb([128, 8], F32, "rs")
    xn = [sb([128, D], BF16, "xn") for _ in range(2)]
    hT = sb([128, NCH, TW], BF16, "hT")
    KTb = sb([128, 4, 8 * 128], BF16, "KTb")
    Vaug = sb([128, 8, 8, 65], BF16, "Vaug")
    QTb = sb([128, 4, TW], BF16, "QTb")
    zs = sb([128, TB, 512], F32, "zs")
    oat = sb([128, TB, 512], F32, "oat")
    cands = None
    if o_all is not None:
        cand = sb([128, 4, 512], F32, "cand")
        if RT is None:
            cands = [(lambda r, q_=q_: o_all[q_ * TC + r:q_ * TC + r + 128, :].rearrange("p (h d) -> p h d", h=4)) for q_ in range(4)]
        else:
            o_alls, chunks, CR = o_all
            views = [a.rearrange("(r t) d -> t r d", r=8) for a in o_alls]

            def cand_ap(row, b_):
                i, off = row // CR, row % CR
                return views[i][off:off + 128, 4 * b_:4 * b_ + 4, :]
            cands = [(lambda r, b_=c_ // 4, q_=c_ % 4: cand_ap(q_ * TC + r, b_)) for c_ in range(8)]
        qsel = sb([128, len(cands)], F32, "qsel")
        P.op("sp", lambda e: e.dma_start(out=qsel[:], in_=qsel_d[:, :]), writes=["qsel"], chan="qsel")
    ssa = sb([128, 4], F32, "ssa")
    ra = sb([128, 4], F32, "ra")
    oan = sb([128, 512], BF16, "oan")
    oaT = sb([128, 4, TW], BF16, "oaT")
    ob = sb([128, 512], BF16, "ob")
    obT = sb([128, 4, TW], BF16, "obT")
    scs = [sb([128, 640], F32, "scs")] * 2
    PTb = [sb([128, 640], BF16, "PTb") for _ in range(2)]
    rden = sb([128, 8], F32, "rden")
    sg = [sb([128, 2 * TW], F32, "sg") for _ in range(2)]
    tt_ = [sb([128, 2 * TW], F32, "tt") for _ in range(2)]
    mixT = sb([128, NCH, TW], BF16, "mixT")

    ps_tr = ps([128, NCH, 128], BF16, "ps_tr")
    pbig = [ps([128, 512], F32, "pbig") for _ in range(5)]
    ps_o = [ps([128, 4, 65], F32, "ps_o") for _ in range(2)]
    bi = [0]

    def big():
        i = bi[0] % 5
        bi[0] += 1
        return pbig[i], "pbig%d" % i

    scale_q = 64.0 ** -0.5
    for tt in range(NT2):
        tp = tt % 2
        xk = "axt%d" % tp
        P.op("sp", lambda e: e.dma_start(out=xt[tp][:], in_=x2[tt * TW:(tt + 1) * TW, :].rearrange("(j p) d -> p j d", p=128)),
             writes=[xk], chan=xk)
        for j in range(TB):
            norm_block(P, epsc, xt[tp][:, j, :], xk, ss[:, j:j + 1], rs[:, j:j + 1], "ass%d" % j, junk[:], "ajunk",
                       xn[j % 2][:], "axn%d" % (j % 2), ps_tr, "aps_tr", idb, hT[:, :, j * 128:(j + 1) * 128], "ahT")
        ring0 = (tt * TB) % 8
        for m in range(4):
            pb_, pk = big()
            for c in range(NCH):
                P.op("pe", lambda e: e.matmul(pb_[:, 0:TW], lhsT=Wi[:, c, 1024 + m * 128:1024 + (m + 1) * 128], rhs=hT[:, c, :],
                                              start=(c == 0), stop=(c == NCH - 1)), reads=["Wi", "ahT"], writes=[pk])
            P.op("act", lambda e: e.copy(KTb[:, m, ring0 * 128:ring0 * 128 + TW], pb_[:, 0:TW]), reads=[pk], writes=["KTb"])
        for j in range(TB):
            slot = ring0 + j
            pb_, pk = big()
            for c in range(NCH):
                P.op("pe", lambda e: e.matmul(pb_[:, :], lhsT=hT[:, c, j * 128:(j + 1) * 128], rhs=Wi[:, c, 1536:2048],
                                              start=(c == 0), stop=(c == NCH - 1)), reads=["Wi", "ahT"], writes=[pk])
            P.op("dve", lambda e: e.tensor_copy(Vaug[:, slot, :, 0:64], pb_[:, :].rearrange("p (h d) -> p h d", h=8)),
                 reads=[pk], writes=["Vaug"])
            P.op("act", lambda e: e.activation(out=Vaug[:, slot, :, 64:65], in_=ones8[:], func=AF.Copy,
                                               scale=valid[:, tt * TB + j:tt * TB + j + 1]),
                 reads=["ones8", "valid"], writes=["Vaug"])
        if tt < 2:
            continue
        for m in range(4):
            pb_, pk = big()
            for c in range(NCH):
                P.op("pe", lambda e: e.matmul(pb_[:, 0:TW], lhsT=Wi[:, c, 512 + m * 128:512 + (m + 1) * 128], rhs=hT[:, c, :],
                                              start=(c == 0), stop=(c == NCH - 1)), reads=["Wi", "ahT"], writes=[pk])
            P.op("act", lambda e: e.mul(QTb[:, m, :], pb_[:, 0:TW], scale_q), reads=[pk], writes=["QTb"])
        for j in range(TB):
            pb_, pk = big()
            for c in range(NCH):
                P.op("pe", lambda e: e.matmul(pb_[:, :], lhsT=hT[:, c, j * 128:(j + 1) * 128], rhs=Wi[:, c, 0:512],
                                              start=(c == 0), stop=(c == NCH - 1)), reads=["Wi", "ahT"], writes=[pk])
            P.op("act", lambda e: e.activation(out=zs[:, j, :], in_=pb_[:, :], func=AF.Silu), reads=[pk], writes=["zs%d" % j])
        if o_all is None:
            P.op("sp", lambda e: e.dma_start(out=oat[:], in_=oa2[(tt - 2) * TW:(tt - 1) * TW, :].rearrange("(j p) d -> p j d", p=128)),
                 writes=["oat"], chan="oat")
        else:
            for j in range(TB):
                for cc_, cf_ in enumerate(cands):
                    k = cc_ % 4
                    ck_ = "cand_%d" % k
                    P.op("sp", lambda e: e.dma_start(out=cand[:, k, :].rearrange("p (h d) -> p h d", h=4),
                                                     in_=cf_((tt - 2) * TW + j * 128)),
                         reads=["oall"], writes=[ck_], chan=ck_)
                    if cc_ == 0:
                        P.op("dve", lambda e: e.tensor_scalar(oat[:, j, :], cand[:, k, :], qsel[:, cc_:cc_ + 1], None, ALU.mult),
                             reads=[ck_, "qsel"], writes=["oat"])
                    else:
                        P.op("dve", lambda e: e.scalar_tensor_tensor(out=oat[:, j, :], in0=cand[:, k, :], scalar=qsel[:, cc_:cc_ + 1],
                                                                     in1=oat[:, j, :], op0=ALU.mult, op1=ALU.add),
                             reads=[ck_, "qsel", "oat"], writes=["oat"])
        for j in range(TB):
            g = tt * TB + j
            for h in range(8):
                m, r = h // 2, slice(64 * (h % 2), 64 * (h % 2) + 64)
                p1, p1k = big()
                p2, p2k = big()
                for kb in range(5):
                    slot = (g - 4 + kb) % 8
                    dst = p1[:, kb * 128:(kb + 1) * 128] if kb < 4 else p2[:, 0:128]
                    P.op("pe", lambda e: e.matmul(dst, lhsT=KTb[r, m, slot * 128:(slot + 1) * 128], rhs=QTb[r, m, j * 128:(j + 1) * 128],
                                                  start=True, stop=True), reads=["KTb", "QTb"], writes=[p1k if kb < 4 else p2k])
                sp_ = h % 2
                P.op("dve", lambda e: e.tensor_tensor(out=scs[sp_][:, 0:512], in0=p1[:, :], in1=biasT[:, h, 0:512], op=ALU.add),
                     reads=[p1k, "biasT"], writes=["scsa"])
                P.op("dve", lambda e: e.tensor_tensor(out=scs[sp_][:, 512:640], in0=p2[:, 0:128], in1=biasT[:, h, 512:640], op=ALU.add),
                     reads=[p2k, "biasT"], writes=["scsb"])
                P.op("act", lambda e: e.activation(out=PTb[sp_][:], in_=scs[sp_][:], func=AF.Exp),
                     reads=["scsa", "scsb"], writes=["PTb%d" % sp_])
                for kb in range(5):
                    slot = (g - 4 + kb) % 8
                    P.op("pe", lambda e: e.matmul(ps_o[h // 4][:, h % 4, :], lhsT=PTb[sp_][:, kb * 128:(kb + 1) * 128],
                                                  rhs=Vaug[:, slot, h, :], start=(kb == 0), stop=(kb == 4)),
                         reads=["PTb%d" % sp_, "Vaug"], writes=["ps_o%d" % (h // 4)])
            for hg in range(2):
                P.op("dve", lambda e: e.tensor_scalar(rden[:, hg * 4:hg * 4 + 4], ps_o[hg][:, :, 64], 1e-30, None, ALU.add),
                     reads=["ps_o%d" % hg], writes=["rden%da" % hg])
                P.op("dve", lambda e: e.reciprocal(rden[:, hg * 4:hg * 4 + 4], rden[:, hg * 4:hg * 4 + 4]),
                     reads=["rden%da" % hg], writes=["rden%d" % hg])
            for h in range(8):
                P.op("act", lambda e: e.activation(out=ob[:, h * 64:(h + 1) * 64], in_=ps_o[h // 4][:, h % 4, 0:64], func=AF.Copy,
                                                   scale=rden[:, h:h + 1]), reads=["ps_o%d" % (h // 4), "rden%d" % (h // 4)], writes=["ob"])
            for c in range(4):
                P.op("pe", lambda e: e.transpose(ps_tr[:, c, :], ob[:, c * 128:(c + 1) * 128], idb), reads=["ob", "consts_b"], writes=["aps_tr"])
            P.op("act", lambda e: e.copy(obT[:, :, j * 128:(j + 1) * 128], ps_tr[:, 0:4, :]), reads=["aps_tr"], writes=["obT"])
            for hh in range(4):
                P.op("act", lambda e: e.activation(out=junk[:, 0:128], in_=oat[:, j, hh * 128:(hh + 1) * 128], func=AF.Square,
                                                   accum_out=ssa[:, hh:hh + 1]), reads=["oat"], writes=["ajunk", "ssa"])
            P.op("act", lambda e: e.activation(out=ra[:], in_=ssa[:], func=AF.Ln, scale=1.0 / 128.0, bias=epsc[:, 1:2]),
                 reads=["ssa", "epsc"], writes=["ra0"])
            P.op("act", lambda e: e.activation(out=ra[:], in_=ra[:], func=AF.Exp, scale=-0.5), reads=["ra0"], writes=["ra"])
            for hh in range(4):
                P.op("dve", lambda e: e.scalar_tensor_tensor(out=oan[:, hh * 128:(hh + 1) * 128], in0=oat[:, j, hh * 128:(hh + 1) * 128],
                                                             scalar=ra[:, hh:hh + 1], in1=zs[:, j, hh * 128:(hh + 1) * 128],
                                                             op0=ALU.mult, op1=ALU.mult), reads=["oat", "ra", "zs%d" % j], writes=["oan"])
            for c in range(4):
                P.op("pe", lambda e: e.transpose(ps_tr[:, 4 + c, :], oan[:, c * 128:(c + 1) * 128], idb), reads=["oan", "consts_b"], writes=["aps_tr"])
            P.op("act", lambda e: e.copy(oaT[:, :, j * 128:(j + 1) * 128], ps_tr[:, 4:8, :]), reads=["aps_tr"], writes=["oaT"])
        for mo in range(8):
            py, pyk = big()
            pg, pgk = big()
            for half, (Wb, src, skey) in enumerate(((WbA, oaT, "oaT"), (WbB, obT, "obT"))):
                for c in range(4):
                    P.op("pe", lambda e: e.matmul(py[:, half * TW:(half + 1) * TW], lhsT=Wb[:, c, mo * 128:(mo + 1) * 128], rhs=src[:, c, :],
                                                  start=(c == 0), stop=(c == 3)), reads=["WbA", "WbB", skey], writes=[pyk])
            for half in range(2):
                col0 = 2048 + half * 1024 + mo * 128
                for c in range(NCH):
                    P.op("pe", lambda e: e.matmul(pg[:, half * TW:(half + 1) * TW], lhsT=Wi[:, c, col0:col0 + 128], rhs=hT[:, c, :],
                                                  start=(c == 0), stop=(c == NCH - 1)), reads=["Wi", "ahT"], writes=[pgk])
            q2 = mo % 2
            P.op("act", lambda e: e.activation(out=sg[q2][:], in_=pg[:, :], func=AF.Sigmoid), reads=[pgk], writes=["sg%d" % q2])
            P.op("dve", lambda e: e.tensor_tensor(out=tt_[q2][:], in0=py[:, :], in1=sg[q2][:], op=ALU.mult),
                 reads=[pyk, "sg%d" % q2], writes=["tt%d" % q2])
            P.op("dve", lambda e: e.tensor_tensor(out=mixT[:, mo, :], in0=tt_[q2][:, 0:TW], in1=tt_[q2][:, TW:2 * TW], op=ALU.add),
                 reads=["tt%d" % q2], writes=["mixT"])
        for j in range(TB):
            for half in range(2):
                po, pok = big()
                for c in range(NCH):
                    P.op("pe", lambda e: e.matmul(po[:, :], lhsT=mixT[:, c, j * 128:(j + 1) * 128], rhs=Wo[:, c, half * 512:(half + 1) * 512],
                                                  start=(c == 0), stop=(c == NCH - 1)), reads=["mixT", "Wo"], writes=[pok])
                P.op("dve", lambda e: e.tensor_tensor(out=xt[tp][:, j, half * 512:(half + 1) * 512], in0=po[:, :],
                                                      in1=xt[tp][:, j, half * 512:(half + 1) * 512], op=ALU.add), reads=[pok, xk], writes=[xk])
        P.op("sp", lambda e: e.dma_start(out=xmid[(tt - 2) * TW:(tt - 1) * TW, :].rearrange("(j p) d -> p j d", p=128), in_=xt[tp][:]),
             reads=[xk], writes=["xmid_d"], chan="xmst%d" % tp)


def build_phase2b(nc, P, A, NTM, xmid, w_up, w_down, nwf_d, cfw_d, cfb_d, wfin_d, cst, out2):
    sb, ps = A.sb, A.ps
    cf, cb = load_consts(P, A, cst, "b")
    idb = cb[:, C_ID:C_ID + 128]
    epsc = make_epsc(P, A)
    nwf = sb([128, NCH], F32, "nwf")
    P.op("sp", lambda e: e.dma_start(out=nwf[:], in_=nwf_d[:, :]), writes=["nwf0"], chan="nwf")
    P.op("dve", lambda e: e.tensor_scalar(nwf[:], nwf[:], 32.0, None, ALU.mult), reads=["nwf0"], writes=["nwf"])
    cfw = sb([128, 44, 3], F32, "cfw")
    cfb = sb([128, 44], F32, "cfb")
    wfb = sb([128, D], F32, "wfb")
    P.op("sp", lambda e: e.dma_start(out=cfw[:], in_=cfw_d[:, :, :]), writes=["cfw"], chan="cfw")
    P.op("sp", lambda e: e.dma_start(out=cfb[:], in_=cfb_d[:, :]), writes=["cfb"], chan="cfb")
    P.op("sp", lambda e: e.dma_start(out=wfb[:], in_=wfin_d[:, :]), writes=["wfb0"], chan="wfb")
    P.op("pool", lambda e: e.tensor_scalar(wfb[:], wfb[:], 32.0, None, ALU.mult), reads=["wfb0"], writes=["wfb"])
    Wu = sb([128, NCH, 2 * D_FF], BF16, "Wu")
    Wd = sb([128, 22, D], BF16, "Wd")
    es_stg = ExitStack()
    stg = [es_stg.enter_context(nc.sbuf_tensor("bstg%d" % i, [128, 2048], F32)) for i in range(2)]
    cnt = [0]
    prep_weight(P, stg, "bstg", w_up, 8, 2 * D_FF, Wu, 0, lambda c: nwf[:, c:c + 1], "Wu", ["nwf"], cnt)
    prep_weight(P, stg, "bstg", w_down[0:1408, :], 11, D, Wd, 0, None, "Wd", [], cnt, dst_c0=0)
    prep_weight(P, stg, "bstg", w_down[1408:2816, :], 11, D, Wd, 0, None, "Wd", [], cnt, dst_c0=11)
    es_stg.close()
    P.barrier()

    xm = [sb([128, TB, D], F32, "xm") for _ in range(2)]
    junk = sb([128, D], BF16, "junk")
    ss = sb([128, 8], F32, "ss")
    rs = sb([128, 8], F32, "rs")
    xn = [sb([128, D], BF16, "xn") for _ in range(2)]
    h2T = sb([128, NCH, TW], BF16, "h2T")
    ubuf = [sb([128, 2, TW + 2], F32, "ubuf") for _ in range(2)]
    cv = [sb([128, 2, TW], F32, "cv") for _ in range(2)]
    sgt = [sb([128, TW], F32, "sgt") for _ in range(2)]
    uh = sb([128, 22, 2, 2], F32, "uh")
    actT = sb([128, 22, TW], BF16, "actT")
    outt = [sb([128, D], F32, "outt") for _ in range(2)]
    P.op("pool", lambda e: e.memset(uh[:], 0.0), writes=["uh"])

    ps_tr = ps([128, NCH, 128], BF16, "ps_tr")
    pbig = [ps([128, 512], F32, "pbig") for _ in range(6)]
    bi = [0]

    def big():
        i = bi[0] % 6
        bi[0] += 1
        return pbig[i], "bpbig%d" % i

    for u in range(NTM + 1):
        tp = u % 2
        xk = "bxm%d" % tp
        P.op("sp", lambda e: e.dma_start(out=xm[tp][:], in_=xmid[u * TW:(u + 1) * TW, :].rearrange("(j p) d -> p j d", p=128)),
             reads=["xmid_d"], writes=[xk], chan=xk)
        for j in range(TB):
            norm_block(P, epsc, xm[tp][:, j, :], xk, ss[:, j:j + 1], rs[:, j:j + 1], "bss%d" % j, junk[:], "bjunk",
                       xn[j % 2][:], "bxn%d" % (j % 2), ps_tr, "bps_tr", idb, h2T[:, :, j * 128:(j + 1) * 128], "h2T")
        for m in range(22):
            q2 = m % 2
            pg, pgk = big()
            for half in range(2):
                col0 = half * D_FF + m * 128
                for c in range(NCH):
                    P.op("pe", lambda e: e.matmul(pg[:, half * TW:(half + 1) * TW], lhsT=Wu[:, c, col0:col0 + 128], rhs=h2T[:, c, :],
                                                  start=(c == 0), stop=(c == NCH - 1)), reads=["Wu", "h2T"], writes=[pgk])
            uk = "ubuf%d" % q2
            P.op("pool", lambda e: e.tensor_copy(ubuf[q2][:, :, 0:2], uh[:, m, :, :]), reads=["uh"], writes=[uk + "h"])
            P.op("act", lambda e: e.copy(ubuf[q2][:, :, 2:TW + 2], pg[:, :].rearrange("p (s n) -> p s n", s=2)), reads=[pgk], writes=[uk])
            P.op("pool", lambda e: e.tensor_copy(uh[:, m, :, :], ubuf[q2][:, :, TW:TW + 2]), reads=[uk, uk + "h"], writes=["uh"])
            if u == 0:
                continue
            ck = "cv%d" % q2
            for s_ in range(2):
                ch = s_ * 22 + m
                eng = "dve"
                P.op("act", lambda e: e.activation(out=cv[q2][:, s_, :], in_=ubuf[q2][:, s_, 0:TW], func=AF.Identity,
                                                   scale=cfw[:, ch, 0:1], bias=cfb[:, ch:ch + 1]),
                     reads=[uk, uk + "h", "cfw", "cfb"], writes=[ck + str(s_)])
                for k in range(1, 3):
                    P.op(eng, lambda e: e.scalar_tensor_tensor(out=cv[q2][:, s_, :], in0=ubuf[q2][:, s_, k:k + TW], scalar=cfw[:, ch, k:k + 1],
                                                               in1=cv[q2][:, s_, :], op0=ALU.mult, op1=ALU.add),
                         reads=[uk, uk + "h", "cfw", ck + str(s_)], writes=[ck + str(s_)])
            P.op("act", lambda e: e.activation(out=sgt[q2][:], in_=cv[q2][:, 0, :], func=AF.Silu), reads=[ck + "0"], writes=["sgt%d" % q2])
            P.op("dve", lambda e: e.tensor_tensor(out=actT[:, m, :], in0=sgt[q2][:], in1=cv[q2][:, 1, :], op=ALU.mult),
                 reads=["sgt%d" % q2, ck + "1"], writes=["actT"])
        if u == 0:
            continue
        for j in range(TB):
            for half in range(2):
                po, pok = big()
                for m in range(22):
                    P.op("pe", lambda e: e.matmul(po[:, :], lhsT=actT[:, m, j * 128:(j + 1) * 128], rhs=Wd[:, m, half * 512:(half + 1) * 512],
                                                  start=(m == 0), stop=(m == 21)), reads=["actT", "Wd"], writes=[pok])
                P.op("dve", lambda e: e.tensor_tensor(out=xm[tp][:, j, half * 512:(half + 1) * 512], in0=po[:, :],
                                                      in1=xm[tp][:, j, half * 512:(half + 1) * 512], op=ALU.add), reads=[pok, xk], writes=[xk])
            o2 = j % 2
            P.op("act", lambda e: e.activation(out=junk[:], in_=xm[tp][:, j, :], func=AF.Square, accum_out=ss[:, 4 + j:5 + j]),
                 reads=[xk], writes=["bjunk", "fss%d" % j])
            P.op("act", lambda e: e.activation(out=rs[:, 4 + j:5 + j], in_=ss[:, 4 + j:5 + j], func=AF.Ln, bias=epsc[:, 0:1]),
                 reads=["fss%d" % j, "epsc"], writes=["frs%da" % j])
            P.op("act", lambda e: e.activation(out=rs[:, 4 + j:5 + j], in_=rs[:, 4 + j:5 + j], func=AF.Exp, scale=-0.5),
                 reads=["frs%da" % j], writes=["frs%d" % j])
            P.op("dve", lambda e: e.scalar_tensor_tensor(out=outt[o2][:], in0=xm[tp][:, j, :], scalar=rs[:, 4 + j:5 + j], in1=wfb[:],
                                                         op0=ALU.mult, op1=ALU.mult), reads=[xk, "frs%d" % j, "wfb"], writes=["outt%d" % o2])
            P.op("sp", lambda e: e.dma_start(out=out2[(u - 1) * TW + j * 128:(u - 1) * TW + (j + 1) * 128, :], in_=outt[o2][:]),
                 reads=["outt%d" % o2], chan="ost%d" % o2)


def _phase1_inputs(inp, T):
    x = np.asarray(inp["x"], np.float32)
    w_in = np.asarray(inp["w_in"], np.float32)[0]
    conv = np.asarray(inp["conv_qkv_w"], np.float32)[0]
    a_log = np.asarray(inp["a_log"], np.float32)[0]
    dtb = np.asarray(inp["dt_bias"], np.float32)[0]
    nw = np.asarray(inp["norm_mix_w"], np.float32)[0]
    cst = chunk_consts()
    maps = []
    for core in range(8):
        b, h = core // 4, core % 4
        cols = np.concatenate([np.arange(h * 128, (h + 1) * 128), 512 + np.arange(h * 128, (h + 1) * 128),
                               1024 + np.arange(h * 128, (h + 1) * 128), [2048 + h], [2052 + h]])
        w1 = np.ascontiguousarray(w_in[:, cols])
        cw = np.zeros((128, 12), np.float32)
        for g in range(3):
            cw[:, 4 * g:4 * g + 4] = conv[:, g * 512 + h * 128:g * 512 + (h + 1) * 128].T
        sc = np.zeros((128, 2), np.float32)
        sc[:, 0] = a_log[h]
        sc[:, 1] = dtb[h]
        maps.append({"x1": np.ascontiguousarray(x[b, :T]), "w1": w1, "cw1": cw, "sc1": sc,
                     "nw1": np.ascontiguousarray(nw.reshape(8, 128).T), "cst": cst})
    return maps


def build_p1_program(T):
    nc = bass.Bass("TRN2", target_bir_lowering=False)
    x1 = nc.dram_tensor("x1", [T, D], F32, kind="ExternalInput").ap()
    w1 = nc.dram_tensor("w1", [D, 386], F32, kind="ExternalInput").ap()
    cw1 = nc.dram_tensor("cw1", [128, 12], F32, kind="ExternalInput").ap()
    sc1 = nc.dram_tensor("sc1", [128, 2], F32, kind="ExternalInput").ap()
    nw1 = nc.dram_tensor("nw1", [128, 8], F32, kind="ExternalInput").ap()
    cst = nc.dram_tensor("cst", [128, NCONST], F32, kind="ExternalInput").ap()
    o_out = nc.dram_tensor("o1", [T, 128], F32, kind="ExternalOutput").ap()
    es = ExitStack()
    P = Prog(nc)
    A = Ctx(nc, es, P)
    build_phase1(nc, es, P, A, T, x1, w1, cw1, sc1, nw1, cst, o_out)
    P.finish()
    P.emit(es)
    es.close()
    return nc, P


def run_phase1(inp, T):
    nc, P = build_p1_program(T)
    maps = _phase1_inputs(inp, T)
    res = run_bass_kernel_spmd(nc, maps, core_ids=list(range(8)))
    o = np.zeros((2, T, 4, 128), np.float32)
    for core in range(8):
        o[core // 4, :, core % 4, :] = res.results[core]["o1"]
    return o


def _bias_tile(rel):
    ki = np.arange(128)[:, None]
    qi = np.arange(128)[None, :]
    out = np.zeros((128, 8, 640), np.float32)
    for kb in range(5):
        dist = qi - ki + (4 - kb) * 128
        idx = np.clip(dist, -128, 128) + 128
        cdiff = 2 * (4 - kb) + qi // 64 - ki // 64
        ok = (cdiff >= 0) & (cdiff <= 8)
        for h in range(8):
            out[:, h, kb * 128:(kb + 1) * 128] = np.where(ok, rel[h][idx], NEG)
    return out


def _phase2_inputs(inp, o1, T):
    TC = T // 4
    NTM = TC // TW
    x = np.asarray(inp["x"], np.float32)
    w_in = np.ascontiguousarray(np.asarray(inp["w_in"], np.float32)[0])
    cfw_ = np.asarray(inp["conv_ffn_w"], np.float32)[0]
    cfb_ = np.asarray(inp["conv_ffn_b"], np.float32)[0]
    shared = {
        "w_in": w_in,
        "w_ba": np.ascontiguousarray(np.asarray(inp["w_branch_a"], np.float32)[0]),
        "w_bb": np.ascontiguousarray(np.asarray(inp["w_branch_b"], np.float32)[0]),
        "w_out": np.ascontiguousarray(np.asarray(inp["w_out"], np.float32)[0]),
        "w_up": np.ascontiguousarray(np.asarray(inp["w_up"], np.float32)[0]),
        "w_down": np.ascontiguousarray(np.asarray(inp["w_down"], np.float32)[0]),
        "nwm": np.ascontiguousarray(np.asarray(inp["norm_mix_w"], np.float32)[0].reshape(8, 128).T),
        "nwf": np.ascontiguousarray(np.asarray(inp["norm_ffn_w"], np.float32)[0].reshape(8, 128).T),
        "gnw": np.ascontiguousarray(np.asarray(inp["gdn_norm_w"], np.float32)[0].reshape(128, 1)),
        "biasT": _bias_tile(np.asarray(inp["rel_bias"], np.float32)[0]),
        "cfw": np.ascontiguousarray(cfw_.reshape(3, 44, 128).transpose(2, 1, 0)),
        "cfb": np.ascontiguousarray(cfb_.reshape(44, 128).T),
        "wfin": np.ascontiguousarray(np.broadcast_to(np.asarray(inp["norm_final_w"], np.float32)[None, :], (128, D))),
        "cst2": chunk_consts(),
    }
    maps = []
    for core in range(8):
        b, q = core // 4, core % 4
        t0 = q * TC
        lo = t0 - 3 * TW
        x2 = np.zeros(((NTM + 3) * TW, D), np.float32)
        s0 = max(lo, 0)
        x2[s0 - lo:] = x[b, s0:t0 + TC]
        pos = lo + np.arange((NTM + 3) * TW)
        valid = (pos >= 0).astype(np.float32).reshape((NTM + 3) * TB, 128).T
        m = dict(shared)
        m["x2"] = x2
        m["validc"] = np.ascontiguousarray(valid)
        if o1 is not None:
            lo2 = t0 - TW
            oa2 = np.zeros(((NTM + 1) * TW, 512), np.float32)
            s1 = max(lo2, 0)
            oa2[s1 - lo2:] = o1[b, s1:t0 + TC].reshape(-1, 512)
            m["oa2"] = oa2
        maps.append(m)
    return maps


def _declare_p2(nc, NTM, with_oa):
    d = {}
    def inp(name, shape):
        d[name] = nc.dram_tensor(name, list(shape), F32, kind="ExternalInput").ap()
    inp("x2", [(NTM + 3) * TW, D])
    if with_oa:
        inp("oa2", [(NTM + 1) * TW, 512])
    inp("validc", [128, (NTM + 3) * TB])
    inp("w_in", [D, D_IN]); inp("w_ba", [512, D]); inp("w_bb", [512, D]); inp("w_out", [D, D])
    inp("w_up", [D, 2 * D_FF]); inp("w_down", [D_FF, D]); inp("nwm", [128, 8]); inp("nwf", [128, 8]); inp("gnw", [128, 1])
    inp("biasT", [128, 8, 640]); inp("cfw", [128, 44, 3]); inp("cfb", [128, 44]); inp("wfin", [128, D]); inp("cst2", [128, NCONST])
    d["xmid"] = nc.dram_tensor("xmid", [(NTM + 1) * TW, D], F32, kind="Internal").ap()
    d["out2"] = nc.dram_tensor("out2", [NTM * TW, D], F32, kind="ExternalOutput").ap()
    return d


def _emit_phase2(nc, P, d, NTM, oa_ap, o_all=None, qsel_d=None, RT=None):
    es_a = ExitStack()
    build_phase2a(nc, P, Ctx(nc, es_a, P), NTM, d["x2"], oa_ap, d["validc"], d["w_in"], d["w_ba"], d["w_bb"], d["w_out"],
                  d["nwm"], d["gnw"], d["biasT"], d["cst2"], d["xmid"], o_all=o_all, qsel_d=qsel_d, RT=RT)
    es_a.close()
    P.pool_hold = False
    P.barrier()
    es_b = ExitStack()
    build_phase2b(nc, P, Ctx(nc, es_b, P), NTM, d["xmid"], d["w_up"], d["w_down"], d["nwf"], d["cfw"], d["cfb"], d["wfin"],
                  d["cst2"], d["out2"])
    es_b.close()


def build_p2_program(T):
    NTM = (T // 4) // TW
    nc = bass.Bass("TRN2", target_bir_lowering=False)
    d = _declare_p2(nc, NTM, True)
    P = Prog(nc)
    _emit_phase2(nc, P, d, NTM, d["oa2"])
    P.finish()
    es = ExitStack()
    P.emit(es)
    es.close()
    return nc, P


def run_phase2(inp, o1, T):
    nc, P = build_p2_program(T)
    maps = _phase2_inputs(inp, o1, T)
    res = run_bass_kernel_spmd(nc, maps, core_ids=list(range(8)))
    TC = T // 4
    out = np.zeros((2, T, D), np.float32)
    for core in range(8):
        out[core // 4, (core % 4) * TC:(core % 4 + 1) * TC] = res.results[core]["out2"]
    return out


T_FULL = 16384


def build_fused_program(T):
    NTM = (T // 4) // TW
    RT = T + TW
    nc = bass.Bass("TRN2", target_bir_lowering=False)
    x1 = nc.dram_tensor("x1", [T, D], F32, kind="ExternalInput").ap()
    w1 = nc.dram_tensor("w1", [D, 386], F32, kind="ExternalInput").ap()
    cw1 = nc.dram_tensor("cw1", [128, 12], F32, kind="ExternalInput").ap()
    sc1 = nc.dram_tensor("sc1", [128, 2], F32, kind="ExternalInput").ap()
    nw1 = nc.dram_tensor("nw1", [128, 8], F32, kind="ExternalInput").ap()
    cst = nc.dram_tensor("cst", [128, NCONST], F32, kind="ExternalInput").ap()
    qsel_d = nc.dram_tensor("qsel", [128, 8], F32, kind="ExternalInput").ap()
    o_loc = nc.dram_tensor("o_loc", [RT, 128], F32, kind="Internal").ap()
    CR = RT
    chunks = [(r0, min(CR, RT - r0)) for r0 in range(0, RT, CR)]
    o_alls = [nc.dram_tensor("o_all%d" % i, [8 * n, 128], F32, kind="Internal").ap() for i, (r0, n) in enumerate(chunks)]
    d = _declare_p2(nc, NTM, False)
    P = Prog(nc)
    es1 = ExitStack()
    A1 = Ctx(nc, es1, P)
    zt = A1.sb([128, 128], F32, "zt")
    P.op("pool", lambda e: e.memset(zt[:], 0.0), writes=["zt"])
    for i in range(TW // 128):
        P.op("sp", lambda e: e.dma_start(out=o_loc[i * 128:(i + 1) * 128, :], in_=zt[:]), reads=["zt"], chan="zt")
    build_phase1(nc, es1, P, A1, T, x1, w1, cw1, sc1, nw1, cst, o_loc[TW:RT, :])
    es1.close()
    P.barrier()
    for i, (r0, n) in enumerate(chunks):
        P.op("pool", lambda e: e.collective_compute("AllGather", ALU.bypass, replica_groups=[list(range(8))],
                                                    ins=[o_loc[r0:r0 + n, :]], outs=[o_alls[i][:, :]]),
             writes=["oall"], chan="cc", inc_override=1)
    P.pool_hold = True
    _emit_phase2(nc, P, d, NTM, None, o_all=(o_alls, chunks, CR), qsel_d=qsel_d, RT=RT)
    P.finish()
    es = ExitStack()
    P.emit(es)
    es.close()
    return nc, P


def build_fused_nocc(T):
    NTM = (T // 4) // TW
    RT = T + TW
    nc = bass.Bass("TRN2", target_bir_lowering=False)
    x1 = nc.dram_tensor("x1", [T, D], F32, kind="ExternalInput").ap()
    w1a = nc.dram_tensor("w1a", [4, D, 386], F32, kind="ExternalInput").ap()
    cw1a = nc.dram_tensor("cw1a", [4, 128, 12], F32, kind="ExternalInput").ap()
    sc1a = nc.dram_tensor("sc1a", [4, 128, 2], F32, kind="ExternalInput").ap()
    nw1 = nc.dram_tensor("nw1", [128, 8], F32, kind="ExternalInput").ap()
    cst = nc.dram_tensor("cst", [128, NCONST], F32, kind="ExternalInput").ap()
    qsel_d = nc.dram_tensor("qsel", [128, 4], F32, kind="ExternalInput").ap()
    o_loc = nc.dram_tensor("o_loc", [RT, 512], F32, kind="Internal").ap()
    hts = nc.dram_tensor("hts", [T // 512, 128, NCH, 512], BF16, kind="Internal").ap()
    d = _declare_p2(nc, NTM, False)
    P = Prog(nc)
    for h in range(4):
        es1 = ExitStack()
        A1 = Ctx(nc, es1, P)
        if h == 0:
            zt = A1.sb([128, 512], F32, "zt")
            P.op("pool", lambda e: e.memset(zt[:], 0.0), writes=["zt"])
            for i in range(TW // 128):
                P.op("sp", lambda e: e.dma_start(out=o_loc[i * 128:(i + 1) * 128, :], in_=zt[:]), reads=["zt"], chan="zt")
        build_phase1(nc, es1, P, A1, T, x1, w1a[h], cw1a[h], sc1a[h], nw1, cst, o_loc[TW:RT, h * 128:(h + 1) * 128],
                     hts=hts, hts_mode=("save" if h == 0 else "load"))
        es1.close()
        P.barrier()
    _emit_phase2(nc, P, d, NTM, None, o_all=o_loc, qsel_d=qsel_d, RT=None)
    P.finish()
    es = ExitStack()
    P.emit(es)
    es.close()
    return nc, P


def run_fused_nocc(inp, T):
    nc, P = build_fused_nocc(T)
    m1 = _phase1_inputs(inp, T)
    m2 = _phase2_inputs(inp, None, T)
    maps = []
    for core in range(8):
        b, q = core // 4, core % 4
        m = dict(m2[core])
        m["x1"] = m1[core]["x1"]
        m["nw1"] = m1[core]["nw1"]
        m["cst"] = m1[core]["cst"]
        m["w1a"] = np.stack([m1[4 * b + h]["w1"] for h in range(4)])
        m["cw1a"] = np.stack([m1[4 * b + h]["cw1"] for h in range(4)])
        m["sc1a"] = np.stack([m1[4 * b + h]["sc1"] for h in range(4)])
        qs = np.zeros((128, 4), np.float32)
        qs[:, q] = 1.0
        m["qsel"] = qs
        maps.append(m)
    res = run_bass_kernel_spmd(nc, maps, core_ids=list(range(8)))
    TC = T // 4
    out = np.zeros((2, T, D), np.float32)
    for core in range(8):
        out[core // 4, (core % 4) * TC:(core % 4 + 1) * TC] = res.results[core]["out2"]
    return out


def run_fused(inp, T):
    nc, P = build_fused_program(T)
    m1 = _phase1_inputs(inp, T)
    m2 = _phase2_inputs(inp, None, T)
    maps = []
    for core in range(8):
        m = dict(m1[core])
        m.update(m2[core])
        qs = np.zeros((128, 8), np.float32)
        qs[:, core] = 1.0
        m["qsel"] = qs
        maps.append(m)
    res = run_bass_kernel_spmd(nc, maps, core_ids=list(range(8)))
    TC = T // 4
    out = np.zeros((2, T, D), np.float32)
    for core in range(8):
        out[core // 4, (core % 4) * TC:(core % 4 + 1) * TC] = res.results[core]["out2"]
    return out


def kernel(**inputs):
    return run_fused_nocc(inputs, T_FULL)
```

```python
from collections import defaultdict
from contextlib import ExitStack

import numpy as np
import concourse.bass as bass
import concourse.mybir as mybir
from concourse.bass_utils import run_bass_kernel_spmd

F32 = mybir.dt.float32
BF16 = mybir.dt.bfloat16
AF = mybir.ActivationFunctionType
ALU = mybir.AluOpType

D = 1024
NCH = 8
EPS = 1e-6
CH = 64
DK = 128
D_IN = 5640
D_FF = 2816
NEG = -30000.0


PSUM_PREFIXES = ("psl", "ptb", "ppj", "ps_tr", "aps_tr", "bps_tr", "pbig", "bpbig", "ps_o")


class _Rec:
    def __getattr__(self, name):
        def f(*a, **k):
            self.call = (name, a, k)
            return self
        return f


class Prog:
    ENGS = ("pe", "act", "dve", "pool", "sp")

    def __init__(self, nc):
        self.nc = nc
        self.streams = {e: [] for e in self.ENGS}
        self.count = defaultdict(int)
        self.lastw = {}
        self.readers = defaultdict(list)
        self.waited = defaultdict(int)
        self.nops = 0
        self.epoch = 0
        self.pool_hold = False
        import os
        self.cut = int(os.environ["PCUT"]) if "PCUT" in os.environ else None

    def _dep(self, eng, rec):
        semkey, val = rec[0], rec[1]
        if eng == "pool" and (semkey.startswith("dma_cc@") or self.pool_hold):
            return
        if self.waited[(eng, semkey)] < val:
            self.waited[(eng, semkey)] = val
            self.streams[eng].append(("wait", semkey, val))

    def op(self, eng, fn, reads=(), writes=(), chan=None, inc_override=None):
        if self.cut is not None and self.nops >= self.cut:
            return
        isdma = chan is not None
        for k in reads:
            w = self.lastw.get(k)
            if w is not None:
                self._dep(eng, w)
            if k.startswith(PSUM_PREFIXES):
                for r in self.readers[k]:
                    if r[2] != eng:
                        self._dep(eng, r)
        for k in writes:
            w = self.lastw.get(k)
            if w is not None:
                if not (w[2] == eng == "pe" and not w[3] and not isdma):
                    self._dep(eng, w)
            for r in self.readers[k]:
                if r[2] != eng or r[3] or isdma:
                    self._dep(eng, r)
        if isdma:
            semkey, inc = "dma_%s@%d" % (chan, self.epoch), (inc_override or 16)
        else:
            semkey, inc = "%s@%d" % (eng, self.epoch), 1
        self.count[semkey] += inc
        rec = (semkey, self.count[semkey], eng, isdma)
        rec_ = _Rec()
        fn(rec_)
        self.streams[eng].append(("op", rec_.call, semkey, inc))
        for k in writes:
            self.lastw[k] = rec
            self.readers[k] = []
        for k in reads:
            self.readers[k].append(rec)
        self.nops += 1

    def barrier(self):
        for e in self.ENGS:
            for semkey, val in list(self.count.items()):
                if val:
                    self._dep(e, (semkey, val))
        self.lastw.clear()
        self.readers.clear()
        self.epoch += 1

    def finish(self):
        for semkey, val in list(self.count.items()):
            if semkey.startswith("dma_"):
                self._dep("sp", (semkey, val))
        for semkey, val in list(self.count.items()):
            if not semkey.startswith("dma_") and val:
                self._dep("sp", (semkey, val))

    def emit(self, es):
        nc = self.nc
        sems = {}
        for i, k in enumerate(sorted(self.count)):
            sems[k] = es.enter_context(nc.semaphore("s%d" % i))
        block = es.enter_context(nc.Block())
        streams = self.streams

        def run(eng_handle, items):
            for it in items:
                if it[0] == "wait":
                    eng_handle.wait_ge(sems[it[1]], it[2])
                else:
                    name, a, k = it[1]
                    getattr(eng_handle, name)(*a, **k).then_inc(sems[it[2]], it[3])

        @block.tensor
        def _(e):
            run(e, streams["pe"])

        @block.scalar
        def _(e):
            run(e, streams["act"])

        @block.vector
        def _(e):
            run(e, streams["dve"])

        @block.gpsimd
        def _(e):
            run(e, streams["pool"])

        @block.sync
        def _(e):
            run(e, streams["sp"])


class Ctx:
    _uid = [0]

    def __init__(self, nc, es, P):
        self.nc, self.es, self.P = nc, es, P
        Ctx._uid[0] += 1
        self.n = Ctx._uid[0] * 1000

    def sb(self, shape, dt=F32, name=None):
        self.n += 1
        return self.es.enter_context(self.nc.sbuf_tensor("%s_%d" % (name or "t", self.n), list(shape), dt))

    def ps(self, shape, dt=F32, name=None):
        self.n += 1
        return self.es.enter_context(self.nc.psum_tensor("%s_%d" % (name or "p", self.n), list(shape), dt))


def chunk_consts():
    j = np.arange(128)
    same = (j[:, None] // CH) == (j[None, :] // CH)
    m1 = (same & (j[:, None] <= j[None, :])).astype(np.float32)
    m2 = (same & (j[:, None] > j[None, :])).astype(np.float32)
    ident = np.eye(128, dtype=np.float32)
    ones = np.ones((128, 128), np.float32)
    cind = np.zeros((128, 128), np.float32)
    cind[:64, 0] = 1.0
    cind[64:, 1] = 1.0
    return np.concatenate([m1, m2, ident, ones, cind], axis=1)


C_M1, C_M2, C_ID, C_ONES, C_CIND = 0, 128, 256, 384, 512
NCONST = 640


def make_epsc(P, A, eng="pool"):
    epsc = A.sb([128, 2], F32, "epsc")
    P.op(eng, lambda e: e.memset(epsc[:, 0:1], D * EPS), writes=["epsc0"])
    P.op(eng, lambda e: e.memset(epsc[:, 1:2], EPS), reads=["epsc0"], writes=["epsc"])
    return epsc


def norm_block(P, epsc, x_blk, xkey, ss, rs, sskey, junk, junkkey, xn, xnkey, ps_tr, pskey, idb, hT_dst, hTkey,
               wrow=None, wkey=None):
    P.op("act", lambda e: e.activation(out=junk, in_=x_blk, func=AF.Square, accum_out=ss),
         reads=[xkey], writes=[junkkey, sskey])
    P.op("act", lambda e: e.activation(out=rs, in_=ss, func=AF.Ln, bias=epsc[:, 0:1]),
         reads=[sskey, "epsc"], writes=[sskey + "r0"])
    P.op("act", lambda e: e.activation(out=rs, in_=rs, func=AF.Exp, scale=-0.5),
         reads=[sskey + "r0"], writes=[sskey + "r"])
    if wrow is None:
        P.op("dve", lambda e: e.tensor_scalar(xn, x_blk, rs, None, ALU.mult),
             reads=[xkey, sskey + "r"], writes=[xnkey])
    else:
        P.op("dve", lambda e: e.scalar_tensor_tensor(out=xn, in0=x_blk, scalar=rs, in1=wrow,
                                                      op0=ALU.mult, op1=ALU.mult),
             reads=[xkey, sskey + "r", wkey], writes=[xnkey])
    for c in range(NCH):
        P.op("pe", lambda e, c=c: e.transpose(ps_tr[:, c, :], xn[:, c * 128:(c + 1) * 128], idb),
             reads=[xnkey, "consts_b"], writes=[pskey])
    P.op("act", lambda e: e.copy(hT_dst, ps_tr[:, :, :]), reads=[pskey], writes=[hTkey])


def build_phase1(nc, es, P, A, T, x1, w1, cw1, sc1, nw1, cst, o_out, hts=None, hts_mode=None):
    NT = T // 512
    sb, ps = A.sb, A.ps
    cf = sb([128, NCONST], F32, "cf")
    cb = sb([128, NCONST], BF16, "cb")
    P.op("sp", lambda e: e.dma_start(out=cf[:], in_=cst[:, :]), writes=["consts_f"], chan="cf")
    P.op("dve", lambda e: e.tensor_copy(cb[:], cf[:]), reads=["consts_f"], writes=["consts_b"])
    m1f, m2f = cf[:, C_M1:C_M1 + 128], cf[:, C_M2:C_M2 + 128]
    idf, onesf, cindf = cf[:, C_ID:C_ID + 128], cf[:, C_ONES:C_ONES + 128], cf[:, C_CIND:C_CIND + 2]
    idb, onesb = cb[:, C_ID:C_ID + 128], cb[:, C_ONES:C_ONES + 128]

    epsc = make_epsc(P, A)
    wf = sb([128, NCH, 386], F32, "wf")
    wb = sb([128, NCH, 386], BF16, "wb")
    nw = sb([128, NCH], F32, "nw")
    cw = sb([128, 12], F32, "cw")
    sc = sb([128, 2], F32, "sc")
    negA = sb([128, 1], F32, "negA")
    P.op("sp", lambda e: e.dma_start(out=wf[:], in_=w1.rearrange("(c p) n -> p c n", p=128)), writes=["wf"], chan="wf")
    P.op("sp", lambda e: e.dma_start(out=nw[:], in_=nw1[:, :]), writes=["nw"], chan="nw")
    P.op("sp", lambda e: e.dma_start(out=cw[:], in_=cw1[:, :]), writes=["cw"], chan="cw")
    P.op("sp", lambda e: e.dma_start(out=sc[:], in_=sc1[:, :]), writes=["sc"], chan="sc")
    for c in range(NCH):
        P.op("dve", lambda e, c=c: e.tensor_scalar(wb[:, c, :], wf[:, c, :], nw[:, c:c + 1], 32.0, ALU.mult, ALU.mult),
             reads=["wf", "nw"], writes=["wb"])
    P.op("act", lambda e: e.activation(out=negA[:], in_=sc[:, 0:1], func=AF.Exp), reads=["sc"], writes=["negA0"])
    P.op("dve", lambda e: e.tensor_scalar(negA[:], negA[:], -1.0, None, ALU.mult), reads=["negA0"], writes=["negA"])

    xt = [sb([128, 4, D], F32, "xt") for _ in range(2)]
    junk = sb([128, D], BF16, "junk")
    ss = sb([128, 8], F32, "ss")
    rs = sb([128, 8], F32, "rs")
    xn = [sb([128, D], BF16, "xn") for _ in range(2)]
    hT = [sb([128, NCH, 512], BF16, "hT") for _ in range(2)]
    cbuf = [sb([128, 3 + 512], F32, "cbuf") for _ in range(3)]
    acc = [sb([128, 512], F32, "acc") for _ in range(3)]
    sil = [sb([128, 512], F32, "sil") for _ in range(2)]
    sq = [sb([128, 512], BF16, "sq") for _ in range(2)]
    rn = [sb([128, 512], F32, "rn") for _ in range(2)]
    QT = [sb([128, 512], BF16, "QT") for _ in range(3)]
    KT = [sb([128, 512], BF16, "KT") for _ in range(2)]
    VT = [sb([128, 512], BF16, "VT") for _ in range(2)]
    bdt = [sb([128, 4, 2], F32, "bdt") for _ in range(2)]
    gsc = [sb([128, 8, 4], F32, "gsc") for _ in range(2)]
    def four(shape, dt, name):
        return [sb(shape, dt, name) for _ in range(4)]

    def eight(shape, dt, name):
        return [[sb(shape, dt, name) for _ in range(4)] for _ in range(2)]

    gM = four([128, 128], F32, "gM")
    rgc = four([128, 2], F32, "rgc")
    D1 = four([128, 128], F32, "D1")
    D2 = four([128, 128], F32, "D2")
    bg = four([128, 1], F32, "bg")
    bgK = four([128, 128], BF16, "bgK")
    Bm = four([128, 128], F32, "Bm")
    Bq = four([128, 128], F32, "Bq")
    Nq = four([128, 128], F32, "Nq")
    Rq = four([128, 128], F32, "Rq")
    Rt = four([128, 128], F32, "Rt")
    PTm = four([128, 128], F32, "PTm")
    smx = eight([128, 4], F32, "smx")
    KD = eight([128, 128], BF16, "KD")
    bV = eight([128, 128], BF16, "bV")
    TTb = eight([128, 128], BF16, "TTb")
    PT = eight([128, 128], BF16, "PT")
    nWT = eight([128, 128], BF16, "nWT")
    Ub = [sb([128, 128], BF16, "Ub") for _ in range(2)]
    pus = [sb([128, 128], F32, "pus") for _ in range(2)]
    Osb = [sb([128, 128], F32, "Osb") for _ in range(2)]
    Sf = [sb([128, 128], F32, "Sf") for _ in range(2)]
    Sb = [sb([128, 128], BF16, "Sb") for _ in range(2)]

    ps_tr = ps([128, NCH, 128], BF16, "ps_tr")
    ps_tb = ps([128, 8, 128], BF16, "ps_tb")
    ps_pj = [ps([128, 512], F32, "ps_pj") for _ in range(1)]
    ps_ch = ps([128, 4, 128], F32, "ps_ch")
    ps_sl = [ps([128, 4, 128], F32, "ps_sl") for _ in range(4)]
    pj_i = [0]

    def pjslot():
        i = pj_i[0] % len(ps_pj)
        pj_i[0] += 1
        return ps_pj[i], "ppj%d" % i


    P.op("pool", lambda e: e.memset(Sf[0][:], 0.0), writes=["Sf0"])
    P.op("pool", lambda e: e.memset(Sb[0][:], 0.0), writes=["Sb0"])
    for g in range(3):
        P.op("pool", lambda e, g=g: e.memset(cbuf[g][:, 0:3], 0.0), writes=["cbufh%d" % g])
    sidx = [0]
    chain_q = []

    def tile_level(ti):
        tp = ti % 2
        xk = "xt%d" % tp
        hk = "hT%d" % tp
        bk = "bdt%d" % tp
        cq = ti % 3
        qk, kk, vk = "QT%d" % cq, "KT%d" % tp, "VT%d" % tp
        G = gsc[tp]
        gk = "gsc%d" % tp
        xg, ax, ee, ll, sp_, gg, be, nbe = (G[:, i, :] for i in range(8))
        pieces = []

        def p_load():
            P.op("sp", lambda e, ti=ti, tp=tp: e.dma_start(
                out=xt[tp][:], in_=x1[ti * 512:(ti + 1) * 512, :].rearrange("(j p) d -> p j d", p=128)),
                writes=[xk], chan=xk)
        def p_norm(j):
            bp = j % 2
            norm_block(P, epsc, xt[tp][:, j, :], xk, ss[:, j + 4 * tp:j + 4 * tp + 1], rs[:, j + 4 * tp:j + 4 * tp + 1],
                       "ss%d_%d" % (tp, j), junk[:], "junk", xn[bp][:], "xn%d" % bp, ps_tr, "ps_tr", idb,
                       hT[tp][:, :, j * 128:(j + 1) * 128], hk)

        def p_hload():
            P.op("sp", lambda e: e.dma_start(out=hT[tp][:], in_=hts[ti]), writes=[hk], chan=hk)

        def p_hsave():
            P.op("sp", lambda e: e.dma_start(out=hts[ti], in_=hT[tp][:]), reads=[hk], chan="hsv%d" % tp)

        if hts_mode == "load":
            pieces.append(p_hload)
        else:
            pieces.append(p_load)
            for j in range(4):
                pieces.append(lambda j=j: p_norm(j))
            if hts_mode == "save":
                pieces.append(p_hsave)
        def p_proj(g):
            pj, pjk = pjslot()
            for c in range(NCH):
                P.op("pe", lambda e, g=g, c=c, pj=pj: e.matmul(pj[:], lhsT=wb[:, c, g * 128:(g + 1) * 128],
                                                              rhs=hT[tp][:, c, :], start=(c == 0), stop=(c == NCH - 1)),
                     reads=["wb", hk], writes=[pjk])
            P.op("act", lambda e, g=g, pj=pj: e.copy(cbuf[g][:, 3:515], pj[:]), reads=[pjk], writes=["cbufm%d" % g])
        for g in range(3):
            pieces.append(lambda g=g: p_proj(g))
        def p_bd():
            pj, pjk = pjslot()
            for j in range(4):
                for c in range(NCH):
                    P.op("pe", lambda e, j=j, c=c, pj=pj: e.matmul(pj[:, 2 * j:2 * j + 2], lhsT=hT[tp][:, c, j * 128:(j + 1) * 128],
                                                                  rhs=wb[:, c, 384:386], start=(c == 0), stop=(c == NCH - 1)),
                         reads=["wb", hk], writes=[pjk])
            P.op("dve", lambda e, pj=pj: e.tensor_copy(bdt[tp][:].rearrange("p a b -> p (a b)"), pj[:, 0:8]), reads=[pjk], writes=[bk])
        pieces.append(p_bd)
        def p_conv_a(g):
            ck = ["cbufh%d" % g, "cbufm%d" % g]
            ak = "acc%d" % g
            P.op("dve", lambda e: e.tensor_scalar(acc[g][:], cbuf[g][:, 0:512], cw[:, 4 * g:4 * g + 1], None, ALU.mult),
                 reads=ck + ["cw"], writes=[ak])
            P.op("dve", lambda e: e.scalar_tensor_tensor(
                out=acc[g][:], in0=cbuf[g][:, 1:513], scalar=cw[:, 4 * g + 1:4 * g + 2], in1=acc[g][:],
                op0=ALU.mult, op1=ALU.add), reads=ck + ["cw", ak], writes=[ak])

        def p_conv_b(g):
            ck = ["cbufh%d" % g, "cbufm%d" % g]
            ak = "acc%d" % g
            for k in range(2, 4):
                P.op("dve", lambda e: e.scalar_tensor_tensor(
                    out=acc[g][:], in0=cbuf[g][:, k:k + 512], scalar=cw[:, 4 * g + k:4 * g + k + 1], in1=acc[g][:],
                    op0=ALU.mult, op1=ALU.add), reads=ck + ["cw", ak], writes=[ak])
            P.op("pool", lambda e: e.tensor_copy(cbuf[g][:, 0:3], cbuf[g][:, 512:515]),
                 reads=["cbufm%d" % g, ak], writes=["cbufh%d" % g])

        def p_silu_v():
            P.op("act", lambda e: e.activation(out=VT[tp][:], in_=acc[2][:], func=AF.Silu), reads=["acc2"], writes=[vk])

        def p_l2_a(g):
            P.op("act", lambda e: e.activation(out=sil[g][:], in_=acc[g][:], func=AF.Silu), reads=["acc%d" % g], writes=["sil%d" % g])
            P.op("act", lambda e: e.activation(out=sq[g][:], in_=sil[g][:], func=AF.Square), reads=["sil%d" % g], writes=["sq%d" % g])

        def p_l2_b(g):
            pj, pjk = pjslot()
            P.op("pe", lambda e: e.matmul(pj[:], lhsT=onesb, rhs=sq[g][:], start=True, stop=True),
                 reads=["consts_b", "sq%d" % g], writes=[pjk])
            P.op("act", lambda e: e.activation(out=rn[g][:], in_=pj[:], func=AF.Ln, bias=epsc[:, 1:2]),
                 reads=[pjk, "epsc"], writes=["rn%da" % g])
            P.op("act", lambda e: e.activation(out=rn[g][:], in_=rn[g][:], func=AF.Exp, scale=-0.5),
                 reads=["rn%da" % g], writes=["rn%d" % g])

        def p_qk_out():
            P.op("dve", lambda e: e.scalar_tensor_tensor(out=QT[cq][:], in0=sil[0][:], scalar=float(DK) ** -0.5, in1=rn[0][:],
                                                          op0=ALU.mult, op1=ALU.mult), reads=["sil0", "rn0"], writes=[qk])
            P.op("dve", lambda e: e.tensor_tensor(out=KT[tp][:], in0=sil[1][:], in1=rn[1][:], op=ALU.mult), reads=["sil1", "rn1"], writes=[kk])

        def p_gate_a():
            P.op("dve", lambda e: e.tensor_scalar(xg, bdt[tp][:, :, 1], sc[:, 1:2], None, ALU.add), reads=[bk, "sc"], writes=[gk + "a"])
            P.op("dve", lambda e: e.scalar_tensor_tensor(out=ax, in0=xg, scalar=-1.0, in1=xg, op0=ALU.mult, op1=ALU.max), reads=[gk + "a"], writes=[gk + "b"])
            P.op("act", lambda e: e.activation(out=ee, in_=ax, func=AF.Exp, scale=-1.0), reads=[gk + "b"], writes=[gk + "c"])
            P.op("act", lambda e: e.activation(out=ll, in_=ee, func=AF.Ln, bias=1.0), reads=[gk + "c"], writes=[gk + "d"])

        def p_gate_b():
            P.op("dve", lambda e: e.scalar_tensor_tensor(out=sp_, in0=xg, scalar=0.0, in1=ll, op0=ALU.max, op1=ALU.add),
                 reads=[gk + "a", gk + "d"], writes=[gk + "e"])
            P.op("dve", lambda e: e.tensor_scalar(gg, sp_, negA[:, 0:1], None, ALU.mult), reads=[gk + "e", "negA"], writes=[gk + "g"])
            P.op("act", lambda e: e.activation(out=be, in_=bdt[tp][:, :, 0], func=AF.Sigmoid), reads=[bk], writes=[gk + "be"])
            P.op("dve", lambda e: e.tensor_scalar(nbe, be, -1.0, None, ALU.mult), reads=[gk + "be"], writes=[gk + "nb"])

        pieces.append(p_gate_a)
        for g in (2, 0, 1):
            pieces.append(lambda g=g: p_conv_a(g))
            pieces.append(lambda g=g: p_conv_b(g))
            if g == 2:
                pieces.append(p_silu_v)
                pieces.append(p_gate_b)
            else:
                pieces.append(lambda g=g: p_l2_a(g))
                pieces.append(lambda g=g: p_l2_b(g))
        pieces.append(p_qk_out)
        return pieces

    def block_level(ti):
        tp = ti % 2
        cq = ti % 3
        qk, kk, vk = "QT%d" % cq, "KT%d" % tp, "VT%d" % tp
        G = gsc[tp]
        gk = "gsc%d" % tp
        xg, ax, ee, ll, sp_, gg, be, nbe = (G[:, i, :] for i in range(8))
        def bk(j):
            return ps_sl[j], "psl%d" % j

        hopn = [0]

        def hop():
            if chain_q:
                chain_q.pop(0)()
            hopn[0] += 1
            if hopn[0] % 3 == 0 and pre_q:
                pre_q.pop(0)()

        def stage_done():
            hop()
            if pre_q:
                pre_q.pop(0)()

        J = range(4)
        sfx = ["_%d" % j for j in J]
        csl = [slice(j * 128, (j + 1) * 128) for j in J]
        g_ = [gg[:, j:j + 1] for j in J]
        be_ = [be[:, j:j + 1] for j in J]
        nbe_ = [nbe[:, j:j + 1] for j in J]
        ck = ["_%d_%d" % (tp, j) for j in J]
        for j in J:
            P.op("dve", lambda e: e.tensor_scalar(gM[j][:], m1f, g_[j], None, ALU.mult), reads=["consts_f", gk + "g"], writes=["gM" + sfx[j]])
            P.op("dve", lambda e: e.tensor_scalar(rgc[j][:], cindf, g_[j], None, ALU.mult), reads=["consts_f", gk + "g"], writes=["rgc" + sfx[j]])
        hop()
        for j in J:
            b_, bkk = bk(j)
            P.op("pe", lambda e: e.matmul(b_[:, 0, :], lhsT=gM[j][:], rhs=m2f, start=True, stop=True), reads=["gM" + sfx[j], "consts_f"], writes=[bkk])
            P.op("pe", lambda e: e.matmul(b_[:, 1, :], lhsT=m2f, rhs=gM[j][:], start=True, stop=True), reads=["gM" + sfx[j], "consts_f"], writes=[bkk])
            P.op("pe", lambda e: e.matmul(b_[:, 2, 0:1], lhsT=m1f, rhs=g_[j], start=True, stop=True), reads=[gk + "g", "consts_f"], writes=[bkk])
            P.op("pe", lambda e: e.matmul(b_[:, 2, 1:2], lhsT=m2f, rhs=g_[j], start=True, stop=True), reads=[gk + "g", "consts_f"], writes=[bkk])
            P.op("pe", lambda e: e.matmul(b_[:, 2, 2:4], lhsT=onesf, rhs=rgc[j][:], start=True, stop=True), reads=["rgc" + sfx[j], "consts_f"], writes=[bkk])
        hop()
        for j in J:
            b_, bkk = bk(j)
            P.op("act", lambda e: e.activation(out=D1[j][:], in_=b_[:, 0, :], func=AF.Exp), reads=[bkk], writes=["D1" + sfx[j]])
            P.op("act", lambda e: e.activation(out=D2[j][:], in_=b_[:, 1, :], func=AF.Exp), reads=[bkk], writes=["D2" + sfx[j]])
            P.op("act", lambda e: e.activation(out=smx[tp][j][:], in_=b_[:, 2, 0:4], func=AF.Exp), reads=[bkk], writes=["smx" + ck[j]])
        hop()
        for j in J:
            P.op("dve", lambda e: e.tensor_tensor(out=bg[j][:], in0=be_[j], in1=smx[tp][j][:, 0:1], op=ALU.mult),
                 reads=[gk + "be", "smx" + ck[j]], writes=["bg" + sfx[j]])
            P.op("pool", lambda e: e.tensor_tensor(out=PTm[j][:], in0=D2[j][:], in1=m1f, op=ALU.mult), reads=["D2" + sfx[j], "consts_f"], writes=["PTm" + sfx[j]])
        hop()
        stage_done()
        for j in J:
            P.op("pe", lambda e: e.transpose(ps_tb[:, 2 * j, :], KT[tp][:, csl[j]], idb), reads=[kk, "consts_b"], writes=["ptb"])
            P.op("pe", lambda e: e.transpose(ps_tb[:, 2 * j + 1, :], VT[tp][:, csl[j]], idb), reads=[vk, "consts_b"], writes=["ptb"])
        hop()
        for j in J:
            P.op("dve", lambda e: e.tensor_scalar(bgK[j][:], ps_tb[:, 2 * j, :], bg[j][:, 0:1], None, ALU.mult), reads=["ptb", "bg" + sfx[j]], writes=["bgK" + sfx[j]])
        hop()
        for j in J:
            P.op("act", lambda e: e.activation(out=KD[tp][j][:], in_=ps_tb[:, 2 * j, :], func=AF.Copy, scale=smx[tp][j][:, 1:2]),
                 reads=["ptb", "smx" + ck[j]], writes=["KD" + ck[j]])
            P.op("act", lambda e: e.activation(out=bV[tp][j][:], in_=ps_tb[:, 2 * j + 1, :], func=AF.Copy, scale=be_[j]),
                 reads=["ptb", gk + "be"], writes=["bV" + ck[j]])
        hop()
        stage_done()
        for j in J:
            b_, bkk = bk(j)
            P.op("pe", lambda e: e.matmul(b_[:, 0, :], lhsT=KT[tp][:, csl[j]], rhs=KT[tp][:, csl[j]], start=True, stop=True), reads=[kk], writes=[bkk])
            P.op("pe", lambda e: e.matmul(b_[:, 1, :], lhsT=KT[tp][:, csl[j]], rhs=QT[cq][:, csl[j]], start=True, stop=True), reads=[kk, qk], writes=[bkk])
        hop()
        for j in J:
            b_, bkk = bk(j)
            P.op("dve", lambda e: e.tensor_tensor(out=Bm[j][:], in0=b_[:, 0, :], in1=D1[j][:], op=ALU.mult), reads=[bkk, "D1" + sfx[j]], writes=["Bm" + sfx[j]])
            P.op("dve", lambda e: e.tensor_tensor(out=PT[tp][j][:], in0=b_[:, 1, :], in1=PTm[j][:], op=ALU.mult), reads=[bkk, "PTm" + sfx[j]], writes=["PT" + ck[j]])
            P.op("dve", lambda e: e.scalar_tensor_tensor(out=Bq[j][:], in0=Bm[j][:], scalar=nbe_[j], in1=m2f, op0=ALU.mult, op1=ALU.mult),
                 reads=["Bm" + sfx[j], gk + "nb", "consts_f"], writes=["B" + sfx[j]])
        hop()
        stage_done()
        for j in J:
            b_, bkk = bk(j)
            P.op("pe", lambda e: e.transpose(b_[:, 2, :], Bq[j][:], idf), reads=["B" + sfx[j], "consts_f"], writes=[bkk])
        hop()
        for j in J:
            b_, bkk = bk(j)
            P.op("act", lambda e: e.copy(Nq[j][:], b_[:, 2, :]), reads=[bkk], writes=["N" + sfx[j]])
            P.op("pool", lambda e: e.tensor_tensor(out=Rt[j][:], in0=Bq[j][:], in1=idf, op=ALU.add), reads=["B" + sfx[j], "consts_f"], writes=["Rt" + sfx[j]])
        hop()
        for j in J:
            P.op("dve", lambda e: e.tensor_tensor(out=Rq[j][:], in0=Nq[j][:], in1=idf, op=ALU.add), reads=["N" + sfx[j], "consts_f"], writes=["R" + sfx[j]])
        hop()
        stage_done()
        for lvl in range(5):
            last = lvl == 4
            for j in J:
                b_, bkk = bk(j)
                P.op("pe", lambda e: e.matmul(b_[:, 0, :], lhsT=Bq[j][:], rhs=Nq[j][:], start=True, stop=True), reads=["B" + sfx[j], "N" + sfx[j]], writes=[bkk])
                if not last:
                    P.op("pe", lambda e: e.matmul(b_[:, 1, :], lhsT=Nq[j][:], rhs=Bq[j][:], start=True, stop=True), reads=["B" + sfx[j], "N" + sfx[j]], writes=[bkk])
            hop()
            for j in J:
                b_, bkk = bk(j)
                P.op("act", lambda e: e.copy(Nq[j][:], b_[:, 0, :]), reads=[bkk], writes=["N" + sfx[j]])
                if not last:
                    P.op("act", lambda e: e.copy(Bq[j][:], b_[:, 1, :]), reads=[bkk], writes=["B" + sfx[j]])
            hop()
            stage_done()
            for j in J:
                b_, bkk = bk(j)
                P.op("pe", lambda e: e.matmul(b_[:, 2, :], lhsT=Rt[j][:], rhs=Nq[j][:], start=True, stop=True), reads=["Rt" + sfx[j], "N" + sfx[j]], writes=[bkk])
                if not last:
                    P.op("pe", lambda e: e.matmul(b_[:, 3, :], lhsT=Rq[j][:], rhs=Bq[j][:], start=True, stop=True), reads=["R" + sfx[j], "B" + sfx[j]], writes=[bkk])
            hop()
            for j in J:
                b_, bkk = bk(j)
                if not last:
                    P.op("dve", lambda e: e.tensor_tensor(out=Rq[j][:], in0=b_[:, 2, :], in1=Rq[j][:], op=ALU.add), reads=[bkk, "R" + sfx[j]], writes=["R" + sfx[j]])
                    P.op("dve", lambda e: e.tensor_tensor(out=Rt[j][:], in0=b_[:, 3, :], in1=Rt[j][:], op=ALU.add), reads=[bkk, "Rt" + sfx[j]], writes=["Rt" + sfx[j]])
                else:
                    P.op("dve", lambda e: e.tensor_tensor(out=TTb[tp][j][:], in0=b_[:, 2, :], in1=Rq[j][:], op=ALU.add), reads=[bkk, "R" + sfx[j]], writes=["TTb" + ck[j]])
            hop()
            stage_done()
        for j in J:
            b_, bkk = bk(j)
            P.op("pe", lambda e: e.matmul(b_[:, 0, :], lhsT=bgK[j][:], rhs=TTb[tp][j][:], start=True, stop=True), reads=["bgK" + sfx[j], "TTb" + ck[j]], writes=[bkk])
        hop()
        for j in J:
            b_, bkk = bk(j)
            P.op("act", lambda e: e.mul(nWT[tp][j][:], b_[:, 0, :], -1.0), reads=[bkk], writes=["nWT" + ck[j]])
        hop()
        stage_done()
        while chain_q:
            chain_q.pop(0)()
        while pre_q:
            pre_q.pop(0)()

        def chunk_hops(j, c, tp=tp, cq=cq, qk=qk, ck=ck, ti=ti, csl=csl):
            r = slice(64 * c, 64 * c + 64)
            si = sidx[0]
            so, sn_ = si % 2, (si + 1) % 2
            sidx[0] += 1
            o2 = j % 2
            u, qs, sn, pu = ps_ch[:, 0, :], ps_ch[:, 1, :], ps_ch[:, 2, :], ps_ch[:, 3, :]

            def h1():
                P.op("pe", lambda e: e.matmul(u, lhsT=TTb[tp][j][r, :], rhs=bV[tp][j][r, :], start=True, stop=False),
                     reads=["TTb" + ck[j], "bV" + ck[j]], writes=["ps_ch"])
                P.op("pe", lambda e: e.matmul(u, lhsT=nWT[tp][j][:], rhs=Sb[so][:], start=False, stop=True),
                     reads=["nWT" + ck[j], "Sb%d" % so], writes=["ps_ch"])
                P.op("pe", lambda e: e.matmul(qs, lhsT=QT[cq][:, csl[j]], rhs=Sb[so][:], start=True, stop=True),
                     reads=[qk, "Sb%d" % so], writes=["ps_ch"])

            def h2():
                P.op("dve", lambda e: e.tensor_copy(Ub[o2][r, :], u[r, :]), reads=["ps_ch"], writes=["Ub%d_%d" % (o2, c)])

            def h3():
                P.op("pe", lambda e: e.matmul(sn, lhsT=KD[tp][j][r, :], rhs=Ub[o2][r, :], start=True, stop=True),
                     reads=["KD" + ck[j], "Ub%d_%d" % (o2, c)], writes=["ps_ch"])
                P.op("pe", lambda e: e.matmul(pu, lhsT=PT[tp][j][r, :], rhs=Ub[o2][r, :], start=True, stop=True),
                     reads=["PT" + ck[j], "Ub%d_%d" % (o2, c)], writes=["ps_ch"])

            def h4():
                P.op("dve", lambda e: e.scalar_tensor_tensor(out=Sb[sn_][:], in0=Sf[so][:], scalar=smx[tp][j][:, 2 + c:3 + c], in1=sn,
                                                             op0=ALU.mult, op1=ALU.add), reads=["Sf%d" % so, "smx" + ck[j], "ps_ch"], writes=["Sb%d" % sn_])
                P.op("dve", lambda e: e.scalar_tensor_tensor(out=Sf[sn_][:], in0=Sf[so][:], scalar=smx[tp][j][:, 2 + c:3 + c], in1=sn,
                                                             op0=ALU.mult, op1=ALU.add), reads=["Sf%d" % so, "smx" + ck[j], "ps_ch"], writes=["Sf%d" % sn_])

            def h5():
                P.op("dve", lambda e: e.tensor_copy(pus[o2][r, :], pu[r, :]), reads=["ps_ch"], writes=["pus%d_%d" % (o2, c)])
                P.op("dve", lambda e: e.scalar_tensor_tensor(out=Osb[o2][r, :], in0=qs[r, :], scalar=smx[tp][j][r, 0:1], in1=pus[o2][r, :],
                                                             op0=ALU.mult, op1=ALU.add), reads=["ps_ch", "smx" + ck[j], "pus%d_%d" % (o2, c)],
                     writes=["Osb%d_%d" % (o2, c)])
                if c == 1:
                    blk = ti * 4 + j
                    P.op("sp", lambda e: e.dma_start(out=o_out[blk * 128:(blk + 1) * 128, :], in_=Osb[o2][:]),
                         reads=["Osb%d_0" % o2, "Osb%d_1" % o2], chan="ost%d" % o2)
            return [h1, h2, h3, h4, h5]

        for j in J:
            for c in range(2):
                chain_q.extend(chunk_hops(j, c))
    pre_q = []
    for f in tile_level(0):
        f()
    for ti in range(NT):
        if ti + 1 < NT:
            pre_q.extend(tile_level(ti + 1))
        block_level(ti)
    while chain_q:
        chain_q.pop(0)()


def prep_weight(P, stg, stgkey, src2d, n_c, ncols, dst, dst_col0, scale_fn, dkey, skeys, cnt, dst_c0=0):
    pw = min(2048 // n_c, ncols)
    for col in range(0, ncols, pw):
        w = min(pw, ncols - col)
        b = cnt[0] % len(stg)
        cnt[0] += 1
        sv = stg[b][:, 0:n_c * w].rearrange("p (c n) -> p c n", c=n_c)
        k = stgkey + str(b)
        P.op("sp", lambda e: e.dma_start(out=sv, in_=src2d[:, col:col + w].rearrange("(c p) n -> p c n", p=128)),
             writes=[k], chan=k)
        dv = dst[:, dst_c0:dst_c0 + n_c, dst_col0 + col:dst_col0 + col + w]
        if scale_fn is None:
            eng = "act" if (cnt[0] % 2) else "dve"
            if eng == "act":
                P.op("act", lambda e: e.copy(dv, sv), reads=[k], writes=[dkey])
            else:
                P.op("dve", lambda e: e.tensor_copy(dv, sv), reads=[k], writes=[dkey])
        else:
            for c in range(n_c):
                sc_ = scale_fn(c)
                if c % 2:
                    P.op("act", lambda e: e.activation(out=dst[:, c, dst_col0 + col:dst_col0 + col + w], in_=sv[:, c, :],
                                                       func=AF.Copy, scale=sc_), reads=[k] + skeys, writes=[dkey])
                else:
                    P.op("dve", lambda e: e.tensor_scalar(dst[:, c, dst_col0 + col:dst_col0 + col + w], sv[:, c, :], sc_, None, ALU.mult),
                         reads=[k] + skeys, writes=[dkey])


TB = 2
TW = TB * 128


def load_consts(P, A, cst, pre):
    cf = A.sb([128, NCONST], F32, "cf")
    cb = A.sb([128, NCONST], BF16, "cb")
    P.op("sp", lambda e: e.dma_start(out=cf[:], in_=cst[:, :]), writes=[pre + "consts_f"], chan=pre + "cf")
    P.op("dve", lambda e: e.tensor_copy(cb[:], cf[:]), reads=[pre + "consts_f"], writes=["consts_b"])
    return cf, cb


def build_phase2a(nc, P, A, NTM, x2, oa2, validc, w_in, w_ba, w_bb, w_out, nwm_d, gnw_d, biasT_d, cst, xmid,
                  o_all=None, qsel_d=None, RT=None):
    NT2 = NTM + 3
    TC = NTM * TW
    sb, ps = A.sb, A.ps
    cf, cb = load_consts(P, A, cst, "a")
    idb = cb[:, C_ID:C_ID + 128]
    epsc = make_epsc(P, A, "dve")
    nwm = sb([128, NCH], F32, "nwm")
    gnw = sb([128, 1], F32, "gnw")
    P.op("sp", lambda e: e.dma_start(out=nwm[:], in_=nwm_d[:, :]), writes=["nwm0"], chan="nwm")
    P.op("sp", lambda e: e.dma_start(out=gnw[:], in_=gnw_d[:, :]), writes=["gnw"], chan="gnw")
    P.op("dve", lambda e: e.tensor_scalar(nwm[:], nwm[:], 32.0, None, ALU.mult), reads=["nwm0"], writes=["nwm"])
    Wi = sb([128, NCH, 4096], BF16, "Wi")
    WbA = sb([128, 4, 1024], BF16, "WbA")
    WbB = sb([128, 4, 1024], BF16, "WbB")
    Wo = sb([128, NCH, 1024], BF16, "Wo")
    es_stg = ExitStack()
    stg = [es_stg.enter_context(nc.sbuf_tensor("astg%d" % i, [128, 2048], F32)) for i in range(2)]
    cnt = [0]
    prep_weight(P, stg, "astg", w_in[:, 1536:2048], 8, 512, Wi, 0, lambda c: nwm[:, c:c + 1], "Wi", ["nwm"], cnt)
    prep_weight(P, stg, "astg", w_in[:, 2056:5640], 8, 3584, Wi, 512, lambda c: nwm[:, c:c + 1], "Wi", ["nwm"], cnt)
    prep_weight(P, stg, "astg", w_ba, 4, 1024, WbA, 0, lambda c: gnw[:, 0:1], "WbA", ["gnw"], cnt)
    prep_weight(P, stg, "astg", w_bb, 4, 1024, WbB, 0, None, "WbB", [], cnt)
    prep_weight(P, stg, "astg", w_out, 8, 1024, Wo, 0, None, "Wo", [], cnt)
    es_stg.close()
    P.barrier()
    biasT = sb([128, 8, 640], F32, "biasT")
    P.op("sp", lambda e: e.dma_start(out=biasT[:], in_=biasT_d[:, :, :]), writes=["biasT"], chan="biasT")
    valid = sb([128, NT2 * TB], F32, "valid")
    P.op("sp", lambda e: e.dma_start(out=valid[:], in_=validc[:, :]), writes=["valid"], chan="valid")
    ones8 = sb([128, 8, 1], F32, "ones8")
    P.op("dve", lambda e: e.memset(ones8[:], 1.0), writes=["ones8"])

    xt = [sb([128, TB, D], F32, "xt") for _ in range(2)]
    junk = sb([128, D], BF16, "junk")
    ss = sb([128, 8], F32, "ss")
    rs = sb([128, 8], F32, "rs")
    xn = [sb([128, D], BF16, "xn") for _ in range(2)]
    hT = sb([128, NCH, TW], BF16, "hT")
    KTb = sb([128, 4, 8 * 128], BF16, "KTb")
    Vaug = sb([128, 8, 8, 65], BF16, "Vaug")
    QTb = sb([128, 4, TW], BF16, "QTb")
    zs = sb([128, TB, 512], F32, "zs")
    oat = sb([128, TB, 512], F32, "oat")
    cands = None
    if o_all is not None:
        cand = sb([128, 4, 512], F32, "cand")
        if RT is None:
            cands = [(lambda r, q_=q_: o_all[q_ * TC + r:q_ * TC + r + 128, :].rearrange("p (h d) -> p h d", h=4)) for q_ in range(4)]
        else:
            o_alls, chunks, CR = o_all
            views = [a.rearrange("(r t) d -> t r d", r=8) for a in o_alls]

            def cand_ap(row, b_):
                i, off = row // CR, row % CR
                return views[i][off:off + 128, 4 * b_:4 * b_ + 4, :]
            cands = [(lambda r, b_=c_ // 4, q_=c_ % 4: cand_ap(q_ * TC + r, b_)) for c_ in range(8)]
        qsel = sb([128, len(cands)], F32, "qsel")
        P.op("sp", lambda e: e.dma_start(out=qsel[:], in_=qsel_d[:, :]), writes=["qsel"], chan="qsel")
    ssa = sb([128, 4], F32, "ssa")
    ra = sb([128, 4], F32, "ra")
    oan = sb([128, 512], BF16, "oan")
    oaT = sb([128, 4, TW], BF16, "oaT")
    ob = sb([128, 512], BF16, "ob")
    obT = sb([128, 4, TW], BF16, "obT")
    scs = [sb([128, 640], F32, "scs")] * 2
    PTb = [sb([128, 640], BF16, "PTb") for _ in range(2)]
    rden = sb([128, 8], F32, "rden")
    sg = [sb([128, 2 * TW], F32, "sg") for _ in range(2)]
    tt_ = [sb([128, 2 * TW], F32, "tt") for _ in range(2)]
    mixT = sb([128, NCH, TW], BF16, "mixT")

    ps_tr = ps([128, NCH, 128], BF16, "ps_tr")
    pbig = [ps([128, 512], F32, "pbig") for _ in range(5)]
    ps_o = [ps([128, 4, 65], F32, "ps_o") for _ in range(2)]
    bi = [0]

    def big():
        i = bi[0] % 5
        bi[0] += 1
        return pbig[i], "pbig%d" % i

    scale_q = 64.0 ** -0.5
    def load_x(t_):
        p_ = t_ % 2
        P.op("sp", lambda e: e.dma_start(out=xt[p_][:], in_=x2[t_ * TW:(t_ + 1) * TW, :].rearrange("(j p) d -> p j d", p=128)),
             writes=["axt%d" % p_], chan="axt%d" % p_)

    load_x(0)
    for tt in range(NT2):
        tp = tt % 2
        xk = "axt%d" % tp
        if tt + 1 < NT2:
            load_x(tt + 1)
        for j in range(TB):
            norm_block(P, epsc, xt[tp][:, j, :], xk, ss[:, j:j + 1], rs[:, j:j + 1], "ass%d" % j, junk[:], "ajunk",
                       xn[j % 2][:], "axn%d" % (j % 2), ps_tr, "aps_tr", idb, hT[:, :, j * 128:(j + 1) * 128], "ahT")
        ring0 = (tt * TB) % 8
        for m in range(4):
            pb_, pk = big()
            for c in range(NCH):
                P.op("pe", lambda e: e.matmul(pb_[:, 0:TW], lhsT=Wi[:, c, 1024 + m * 128:1024 + (m + 1) * 128], rhs=hT[:, c, :],
                                              start=(c == 0), stop=(c == NCH - 1)), reads=["Wi", "ahT"], writes=[pk])
            P.op("act", lambda e: e.copy(KTb[:, m, ring0 * 128:ring0 * 128 + TW], pb_[:, 0:TW]), reads=[pk], writes=["KTb"])
        for j in range(TB):
            slot = ring0 + j
            pb_, pk = big()
            for c in range(NCH):
                P.op("pe", lambda e: e.matmul(pb_[:, :], lhsT=hT[:, c, j * 128:(j + 1) * 128], rhs=Wi[:, c, 1536:2048],
                                              start=(c == 0), stop=(c == NCH - 1)), reads=["Wi", "ahT"], writes=[pk])
            P.op("dve", lambda e: e.tensor_copy(Vaug[:, slot, :, 0:64], pb_[:, :].rearrange("p (h d) -> p h d", h=8)),
                 reads=[pk], writes=["Vaug"])
            P.op("act", lambda e: e.activation(out=Vaug[:, slot, :, 64:65], in_=ones8[:], func=AF.Copy,
                                               scale=valid[:, tt * TB + j:tt * TB + j + 1]),
                 reads=["ones8", "valid"], writes=["Vaug"])
        if tt < 2:
            continue
        for m in range(4):
            pb_, pk = big()
            for c in range(NCH):
                P.op("pe", lambda e: e.matmul(pb_[:, 0:TW], lhsT=Wi[:, c, 512 + m * 128:512 + (m + 1) * 128], rhs=hT[:, c, :],
                                              start=(c == 0), stop=(c == NCH - 1)), reads=["Wi", "ahT"], writes=[pk])
            P.op("act", lambda e: e.mul(QTb[:, m, :], pb_[:, 0:TW], scale_q), reads=[pk], writes=["QTb"])
        for j in range(TB):
            pb_, pk = big()
            for c in range(NCH):
                P.op("pe", lambda e: e.matmul(pb_[:, :], lhsT=hT[:, c, j * 128:(j + 1) * 128], rhs=Wi[:, c, 0:512],
                                              start=(c == 0), stop=(c == NCH - 1)), reads=["Wi", "ahT"], writes=[pk])
            P.op("act", lambda e: e.activation(out=zs[:, j, :], in_=pb_[:, :], func=AF.Silu), reads=[pk], writes=["zs%d" % j])
        if o_all is None:
            P.op("sp", lambda e: e.dma_start(out=oat[:], in_=oa2[(tt - 2) * TW:(tt - 1) * TW, :].rearrange("(j p) d -> p j d", p=128)),
                 writes=["oat"], chan="oat")
        else:
            for j in range(TB):
                for cc_, cf_ in enumerate(cands):
                    k = cc_ % 4
                    ck_ = "cand_%d" % k
                    P.op("sp", lambda e: e.dma_start(out=cand[:, k, :].rearrange("p (h d) -> p h d", h=4),
                                                     in_=cf_((tt - 2) * TW + j * 128)),
                         reads=["oall"], writes=[ck_], chan=ck_)
                    if cc_ == 0:
                        P.op("dve", lambda e: e.tensor_scalar(oat[:, j, :], cand[:, k, :], qsel[:, cc_:cc_ + 1], None, ALU.mult),
                             reads=[ck_, "qsel"], writes=["oat"])
                    else:
                        P.op("dve", lambda e: e.scalar_tensor_tensor(out=oat[:, j, :], in0=cand[:, k, :], scalar=qsel[:, cc_:cc_ + 1],
                                                                     in1=oat[:, j, :], op0=ALU.mult, op1=ALU.add),
                             reads=[ck_, "qsel", "oat"], writes=["oat"])
        for j in range(TB):
            g = tt * TB + j
            for h in range(8):
                m, r = h // 2, slice(64 * (h % 2), 64 * (h % 2) + 64)
                p1, p1k = big()
                p2, p2k = big()
                for kb in range(5):
                    slot = (g - 4 + kb) % 8
                    dst = p1[:, kb * 128:(kb + 1) * 128] if kb < 4 else p2[:, 0:128]
                    P.op("pe", lambda e: e.matmul(dst, lhsT=KTb[r, m, slot * 128:(slot + 1) * 128], rhs=QTb[r, m, j * 128:(j + 1) * 128],
                                                  start=True, stop=True), reads=["KTb", "QTb"], writes=[p1k if kb < 4 else p2k])
                sp_ = h % 2
                P.op("dve", lambda e: e.tensor_tensor(out=scs[sp_][:, 0:512], in0=p1[:, :], in1=biasT[:, h, 0:512], op=ALU.add),
                     reads=[p1k, "biasT"], writes=["scsa"])
                P.op("dve", lambda e: e.tensor_tensor(out=scs[sp_][:, 512:640], in0=p2[:, 0:128], in1=biasT[:, h, 512:640], op=ALU.add),
                     reads=[p2k, "biasT"], writes=["scsb"])
                P.op("act", lambda e: e.activation(out=PTb[sp_][:], in_=scs[sp_][:], func=AF.Exp),
                     reads=["scsa", "scsb"], writes=["PTb%d" % sp_])
                for kb in range(5):
                    slot = (g - 4 + kb) % 8
                    P.op("pe", lambda e: e.matmul(ps_o[h // 4][:, h % 4, :], lhsT=PTb[sp_][:, kb * 128:(kb + 1) * 128],
                                                  rhs=Vaug[:, slot, h, :], start=(kb == 0), stop=(kb == 4)),
                         reads=["PTb%d" % sp_, "Vaug"], writes=["ps_o%d" % (h // 4)])
            for hg in range(2):
                P.op("dve", lambda e: e.tensor_scalar(rden[:, hg * 4:hg * 4 + 4], ps_o[hg][:, :, 64], 1e-30, None, ALU.add),
                     reads=["ps_o%d" % hg], writes=["rden%da" % hg])
                P.op("dve", lambda e: e.reciprocal(rden[:, hg * 4:hg * 4 + 4], rden[:, hg * 4:hg * 4 + 4]),
                     reads=["rden%da" % hg], writes=["rden%d" % hg])
            for h in range(8):
                P.op("act", lambda e: e.activation(out=ob[:, h * 64:(h + 1) * 64], in_=ps_o[h // 4][:, h % 4, 0:64], func=AF.Copy,
                                                   scale=rden[:, h:h + 1]), reads=["ps_o%d" % (h // 4), "rden%d" % (h // 4)], writes=["ob"])
            for c in range(4):
                P.op("pe", lambda e: e.transpose(ps_tr[:, c, :], ob[:, c * 128:(c + 1) * 128], idb), reads=["ob", "consts_b"], writes=["aps_tr"])
            P.op("act", lambda e: e.copy(obT[:, :, j * 128:(j + 1) * 128], ps_tr[:, 0:4, :]), reads=["aps_tr"], writes=["obT"])
            for hh in range(4):
                P.op("act", lambda e: e.activation(out=junk[:, 0:128], in_=oat[:, j, hh * 128:(hh + 1) * 128], func=AF.Square,
                                                   accum_out=ssa[:, hh:hh + 1]), reads=["oat"], writes=["ajunk", "ssa"])
            P.op("act", lambda e: e.activation(out=ra[:], in_=ssa[:], func=AF.Ln, scale=1.0 / 128.0, bias=epsc[:, 1:2]),
                 reads=["ssa", "epsc"], writes=["ra0"])
            P.op("act", lambda e: e.activation(out=ra[:], in_=ra[:], func=AF.Exp, scale=-0.5), reads=["ra0"], writes=["ra"])
            for hh in range(4):
                P.op("dve", lambda e: e.scalar_tensor_tensor(out=oan[:, hh * 128:(hh + 1) * 128], in0=oat[:, j, hh * 128:(hh + 1) * 128],
                                                             scalar=ra[:, hh:hh + 1], in1=zs[:, j, hh * 128:(hh + 1) * 128],
                                                             op0=ALU.mult, op1=ALU.mult), reads=["oat", "ra", "zs%d" % j], writes=["oan"])
            for c in range(4):
                P.op("pe", lambda e: e.transpose(ps_tr[:, 4 + c, :], oan[:, c * 128:(c + 1) * 128], idb), reads=["oan", "consts_b"], writes=["aps_tr"])
            P.op("act", lambda e: e.copy(oaT[:, :, j * 128:(j + 1) * 128], ps_tr[:, 4:8, :]), reads=["aps_tr"], writes=["oaT"])
        for mo in range(8):
            py, pyk = big()
            pg, pgk = big()
            for half, (Wb, src, skey) in enumerate(((WbA, oaT, "oaT"), (WbB, obT, "obT"))):
                for c in range(4):
                    P.op("pe", lambda e: e.matmul(py[:, half * TW:(half + 1) * TW], lhsT=Wb[:, c, mo * 128:(mo + 1) * 128], rhs=src[:, c, :],
                                                  start=(c == 0), stop=(c == 3)), reads=["WbA", "WbB", skey], writes=[pyk])
            for half in range(2):
                col0 = 2048 + half * 1024 + mo * 128
                for c in range(NCH):
                    P.op("pe", lambda e: e.matmul(pg[:, half * TW:(half + 1) * TW], lhsT=Wi[:, c, col0:col0 + 128], rhs=hT[:, c, :],
                                                  start=(c == 0), stop=(c == NCH - 1)), reads=["Wi", "ahT"], writes=[pgk])
            q2 = mo % 2
            P.op("act", lambda e: e.activation(out=sg[q2][:], in_=pg[:, :], func=AF.Sigmoid), reads=[pgk], writes=["sg%d" % q2])
            P.op("dve", lambda e: e.tensor_tensor(out=tt_[q2][:], in0=py[:, :], in1=sg[q2][:], op=ALU.mult),
                 reads=[pyk, "sg%d" % q2], writes=["tt%d" % q2])
            P.op("dve", lambda e: e.tensor_tensor(out=mixT[:, mo, :], in0=tt_[q2][:, 0:TW], in1=tt_[q2][:, TW:2 * TW], op=ALU.add),
                 reads=["tt%d" % q2], writes=["mixT"])
        for j in range(TB):
            for half in range(2):
                po, pok = big()
                for c in range(NCH):
                    P.op("pe", lambda e: e.matmul(po[:, :], lhsT=mixT[:, c, j * 128:(j + 1) * 128], rhs=Wo[:, c, half * 512:(half + 1) * 512],
                                                  start=(c == 0), stop=(c == NCH - 1)), reads=["mixT", "Wo"], writes=[pok])
                P.op("dve", lambda e: e.tensor_tensor(out=xt[tp][:, j, half * 512:(half + 1) * 512], in0=po[:, :],
                                                      in1=xt[tp][:, j, half * 512:(half + 1) * 512], op=ALU.add), reads=[pok, xk], writes=[xk])
        P.op("sp", lambda e: e.dma_start(out=xmid[(tt - 2) * TW:(tt - 1) * TW, :].rearrange("(j p) d -> p j d", p=128), in_=xt[tp][:]),
             reads=[xk], writes=["xmid_d"], chan="xmst%d" % tp)


def build_phase2b(nc, P, A, NTM, xmid, w_up, w_down, nwf_d, cfw_d, cfb_d, wfin_d, cst, out2):
    sb, ps = A.sb, A.ps
    cf, cb = load_consts(P, A, cst, "b")
    idb = cb[:, C_ID:C_ID + 128]
    epsc = make_epsc(P, A)
    nwf = sb([128, NCH], F32, "nwf")
    P.op("sp", lambda e: e.dma_start(out=nwf[:], in_=nwf_d[:, :]), writes=["nwf0"], chan="nwf")
    P.op("dve", lambda e: e.tensor_scalar(nwf[:], nwf[:], 32.0, None, ALU.mult), reads=["nwf0"], writes=["nwf"])
    cfw = sb([128, 44, 3], F32, "cfw")
    cfb = sb([128, 44], F32, "cfb")
    wfb = sb([128, D], F32, "wfb")
    P.op("sp", lambda e: e.dma_start(out=cfw[:], in_=cfw_d[:, :, :]), writes=["cfw"], chan="cfw")
    P.op("sp", lambda e: e.dma_start(out=cfb[:], in_=cfb_d[:, :]), writes=["cfb"], chan="cfb")
    P.op("sp", lambda e: e.dma_start(out=wfb[:], in_=wfin_d[:, :]), writes=["wfb0"], chan="wfb")
    P.op("pool", lambda e: e.tensor_scalar(wfb[:], wfb[:], 32.0, None, ALU.mult), reads=["wfb0"], writes=["wfb"])
    Wu = sb([128, NCH, 2 * D_FF], BF16, "Wu")
    Wd = sb([128, 22, D], BF16, "Wd")
    es_stg = ExitStack()
    stg = [es_stg.enter_context(nc.sbuf_tensor("bstg%d" % i, [128, 2048], F32)) for i in range(2)]
    cnt = [0]
    prep_weight(P, stg, "bstg", w_up, 8, 2 * D_FF, Wu, 0, lambda c: nwf[:, c:c + 1], "Wu", ["nwf"], cnt)
    prep_weight(P, stg, "bstg", w_down[0:1408, :], 11, D, Wd, 0, None, "Wd", [], cnt, dst_c0=0)
    prep_weight(P, stg, "bstg", w_down[1408:2816, :], 11, D, Wd, 0, None, "Wd", [], cnt, dst_c0=11)
    es_stg.close()
    P.barrier()

    xm = [sb([128, TB, D], F32, "xm") for _ in range(2)]
    junk = sb([128, D], BF16, "junk")
    ss = sb([128, 8], F32, "ss")
    rs = sb([128, 8], F32, "rs")
    xn = [sb([128, D], BF16, "xn") for _ in range(2)]
    h2T = sb([128, NCH, TW], BF16, "h2T")
    ubuf = [sb([128, 2, TW + 2], F32, "ubuf") for _ in range(2)]
    cv = [sb([128, 2, TW], F32, "cv") for _ in range(2)]
    sgt = [sb([128, TW], F32, "sgt") for _ in range(2)]
    uh = sb([128, 22, 2, 2], F32, "uh")
    actT = sb([128, 22, TW], BF16, "actT")
    outt = [sb([128, D], F32, "outt") for _ in range(2)]
    P.op("pool", lambda e: e.memset(uh[:], 0.0), writes=["uh"])

    ps_tr = ps([128, NCH, 128], BF16, "ps_tr")
    pbig = [ps([128, 512], F32, "pbig") for _ in range(6)]
    bi = [0]

    def big():
        i = bi[0] % 6
        bi[0] += 1
        return pbig[i], "bpbig%d" % i

    for u in range(NTM + 1):
        tp = u % 2
        xk = "bxm%d" % tp
        P.op("sp", lambda e: e.dma_start(out=xm[tp][:], in_=xmid[u * TW:(u + 1) * TW, :].rearrange("(j p) d -> p j d", p=128)),
             reads=["xmid_d"], writes=[xk], chan=xk)
        for j in range(TB):
            norm_block(P, epsc, xm[tp][:, j, :], xk, ss[:, j:j + 1], rs[:, j:j + 1], "bss%d" % j, junk[:], "bjunk",
                       xn[j % 2][:], "bxn%d" % (j % 2), ps_tr, "bps_tr", idb, h2T[:, :, j * 128:(j + 1) * 128], "h2T")
        for m in range(22):
            q2 = m % 2
            pg, pgk = big()
            for half in range(2):
                col0 = half * D_FF + m * 128
                for c in range(NCH):
                    P.op("pe", lambda e: e.matmul(pg[:, half * TW:(half + 1) * TW], lhsT=Wu[:, c, col0:col0 + 128], rhs=h2T[:, c, :],
                                                  start=(c == 0), stop=(c == NCH - 1)), reads=["Wu", "h2T"], writes=[pgk])
            uk = "ubuf%d" % q2
            P.op("pool", lambda e: e.tensor_copy(ubuf[q2][:, :, 0:2], uh[:, m, :, :]), reads=["uh"], writes=[uk + "h"])
            P.op("act", lambda e: e.copy(ubuf[q2][:, :, 2:TW + 2], pg[:, :].rearrange("p (s n) -> p s n", s=2)), reads=[pgk], writes=[uk])
            P.op("pool", lambda e: e.tensor_copy(uh[:, m, :, :], ubuf[q2][:, :, TW:TW + 2]), reads=[uk, uk + "h"], writes=["uh"])
            if u == 0:
                continue
            ck = "cv%d" % q2
            for s_ in range(2):
                ch = s_ * 22 + m
                eng = "dve"
                P.op("act", lambda e: e.activation(out=cv[q2][:, s_, :], in_=ubuf[q2][:, s_, 0:TW], func=AF.Identity,
                                                   scale=cfw[:, ch, 0:1], bias=cfb[:, ch:ch + 1]),
                     reads=[uk, uk + "h", "cfw", "cfb"], writes=[ck + str(s_)])
                for k in range(1, 3):
                    P.op(eng, lambda e: e.scalar_tensor_tensor(out=cv[q2][:, s_, :], in0=ubuf[q2][:, s_, k:k + TW], scalar=cfw[:, ch, k:k + 1],
                                                               in1=cv[q2][:, s_, :], op0=ALU.mult, op1=ALU.add),
                         reads=[uk, uk + "h", "cfw", ck + str(s_)], writes=[ck + str(s_)])
            P.op("act", lambda e: e.activation(out=sgt[q2][:], in_=cv[q2][:, 0, :], func=AF.Silu), reads=[ck + "0"], writes=["sgt%d" % q2])
            P.op("dve", lambda e: e.tensor_tensor(out=actT[:, m, :], in0=sgt[q2][:], in1=cv[q2][:, 1, :], op=ALU.mult),
                 reads=["sgt%d" % q2, ck + "1"], writes=["actT"])
        if u == 0:
            continue
        for j in range(TB):
            for half in range(2):
                po, pok = big()
                for m in range(22):
                    P.op("pe", lambda e: e.matmul(po[:, :], lhsT=actT[:, m, j * 128:(j + 1) * 128], rhs=Wd[:, m, half * 512:(half + 1) * 512],
                                                  start=(m == 0), stop=(m == 21)), reads=["actT", "Wd"], writes=[pok])
                P.op("dve", lambda e: e.tensor_tensor(out=xm[tp][:, j, half * 512:(half + 1) * 512], in0=po[:, :],
                                                      in1=xm[tp][:, j, half * 512:(half + 1) * 512], op=ALU.add), reads=[pok, xk], writes=[xk])
            o2 = j % 2
            P.op("act", lambda e: e.activation(out=junk[:], in_=xm[tp][:, j, :], func=AF.Square, accum_out=ss[:, 4 + j:5 + j]),
                 reads=[xk], writes=["bjunk", "fss%d" % j])
            P.op("act", lambda e: e.activation(out=rs[:, 4 + j:5 + j], in_=ss[:, 4 + j:5 + j], func=AF.Ln, bias=epsc[:, 0:1]),
                 reads=["fss%d" % j, "epsc"], writes=["frs%da" % j])
            P.op("act", lambda e: e.activation(out=rs[:, 4 + j:5 + j], in_=rs[:, 4 + j:5 + j], func=AF.Exp, scale=-0.5),
                 reads=["frs%da" % j], writes=["frs%d" % j])
            P.op("dve", lambda e: e.scalar_tensor_tensor(out=outt[o2][:], in0=xm[tp][:, j, :], scalar=rs[:, 4 + j:5 + j], in1=wfb[:],
                                                         op0=ALU.mult, op1=ALU.mult), reads=[xk, "frs%d" % j, "wfb"], writes=["outt%d" % o2])
            P.op("sp", lambda e: e.dma_start(out=out2[(u - 1) * TW + j * 128:(u - 1) * TW + (j + 1) * 128, :], in_=outt[o2][:]),
                 reads=["outt%d" % o2], chan="ost%d" % o2)


def _phase1_inputs(inp, T):
    x = np.asarray(inp["x"], np.float32)
    w_in = np.asarray(inp["w_in"], np.float32)[0]
    conv = np.asarray(inp["conv_qkv_w"], np.float32)[0]
    a_log = np.asarray(inp["a_log"], np.float32)[0]
    dtb = np.asarray(inp["dt_bias"], np.float32)[0]
    nw = np.asarray(inp["norm_mix_w"], np.float32)[0]
    cst = chunk_consts()
    maps = []
    for core in range(8):
        b, h = core // 4, core % 4
        cols = np.concatenate([np.arange(h * 128, (h + 1) * 128), 512 + np.arange(h * 128, (h + 1) * 128),
                               1024 + np.arange(h * 128, (h + 1) * 128), [2048 + h], [2052 + h]])
        w1 = np.ascontiguousarray(w_in[:, cols])
        cw = np.zeros((128, 12), np.float32)
        for g in range(3):
            cw[:, 4 * g:4 * g + 4] = conv[:, g * 512 + h * 128:g * 512 + (h + 1) * 128].T
        sc = np.zeros((128, 2), np.float32)
        sc[:, 0] = a_log[h]
        sc[:, 1] = dtb[h]
        maps.append({"x1": np.ascontiguousarray(x[b, :T]), "w1": w1, "cw1": cw, "sc1": sc,
                     "nw1": np.ascontiguousarray(nw.reshape(8, 128).T), "cst": cst})
    return maps


def build_p1_program(T):
    nc = bass.Bass("TRN2", target_bir_lowering=False)
    x1 = nc.dram_tensor("x1", [T, D], F32, kind="ExternalInput").ap()
    w1 = nc.dram_tensor("w1", [D, 386], F32, kind="ExternalInput").ap()
    cw1 = nc.dram_tensor("cw1", [128, 12], F32, kind="ExternalInput").ap()
    sc1 = nc.dram_tensor("sc1", [128, 2], F32, kind="ExternalInput").ap()
    nw1 = nc.dram_tensor("nw1", [128, 8], F32, kind="ExternalInput").ap()
    cst = nc.dram_tensor("cst", [128, NCONST], F32, kind="ExternalInput").ap()
    o_out = nc.dram_tensor("o1", [T, 128], F32, kind="ExternalOutput").ap()
    es = ExitStack()
    P = Prog(nc)
    A = Ctx(nc, es, P)
    build_phase1(nc, es, P, A, T, x1, w1, cw1, sc1, nw1, cst, o_out)
    P.finish()
    P.emit(es)
    es.close()
    return nc, P


def run_phase1(inp, T):
    nc, P = build_p1_program(T)
    maps = _phase1_inputs(inp, T)
    res = run_bass_kernel_spmd(nc, maps, core_ids=list(range(8)))
    o = np.zeros((2, T, 4, 128), np.float32)
    for core in range(8):
        o[core // 4, :, core % 4, :] = res.results[core]["o1"]
    return o


def _bias_tile(rel):
    ki = np.arange(128)[:, None]
    qi = np.arange(128)[None, :]
    out = np.zeros((128, 8, 640), np.float32)
    for kb in range(5):
        dist = qi - ki + (4 - kb) * 128
        idx = np.clip(dist, -128, 128) + 128
        cdiff = 2 * (4 - kb) + qi // 64 - ki // 64
        ok = (cdiff >= 0) & (cdiff <= 8)
        for h in range(8):
            out[:, h, kb * 128:(kb + 1) * 128] = np.where(ok, rel[h][idx], NEG)
    return out


def _phase2_inputs(inp, o1, T):
    TC = T // 4
    NTM = TC // TW
    x = np.asarray(inp["x"], np.float32)
    w_in = np.ascontiguousarray(np.asarray(inp["w_in"], np.float32)[0])
    cfw_ = np.asarray(inp["conv_ffn_w"], np.float32)[0]
    cfb_ = np.asarray(inp["conv_ffn_b"], np.float32)[0]
    shared = {
        "w_in": w_in,
        "w_ba": np.ascontiguousarray(np.asarray(inp["w_branch_a"], np.float32)[0]),
        "w_bb": np.ascontiguousarray(np.asarray(inp["w_branch_b"], np.float32)[0]),
        "w_out": np.ascontiguousarray(np.asarray(inp["w_out"], np.float32)[0]),
        "w_up": np.ascontiguousarray(np.asarray(inp["w_up"], np.float32)[0]),
        "w_down": np.ascontiguousarray(np.asarray(inp["w_down"], np.float32)[0]),
        "nwm": np.ascontiguousarray(np.asarray(inp["norm_mix_w"], np.float32)[0].reshape(8, 128).T),
        "nwf": np.ascontiguousarray(np.asarray(inp["norm_ffn_w"], np.float32)[0].reshape(8, 128).T),
        "gnw": np.ascontiguousarray(np.asarray(inp["gdn_norm_w"], np.float32)[0].reshape(128, 1)),
        "biasT": _bias_tile(np.asarray(inp["rel_bias"], np.float32)[0]),
        "cfw": np.ascontiguousarray(cfw_.reshape(3, 44, 128).transpose(2, 1, 0)),
        "cfb": np.ascontiguousarray(cfb_.reshape(44, 128).T),
        "wfin": np.ascontiguousarray(np.broadcast_to(np.asarray(inp["norm_final_w"], np.float32)[None, :], (128, D))),
        "cst2": chunk_consts(),
    }
    maps = []
    for core in range(8):
        b, q = core // 4, core % 4
        t0 = q * TC
        lo = t0 - 3 * TW
        x2 = np.zeros(((NTM + 3) * TW, D), np.float32)
        s0 = max(lo, 0)
        x2[s0 - lo:] = x[b, s0:t0 + TC]
        pos = lo + np.arange((NTM + 3) * TW)
        valid = (pos >= 0).astype(np.float32).reshape((NTM + 3) * TB, 128).T
        m = dict(shared)
        m["x2"] = x2
        m["validc"] = np.ascontiguousarray(valid)
        if o1 is not None:
            lo2 = t0 - TW
            oa2 = np.zeros(((NTM + 1) * TW, 512), np.float32)
            s1 = max(lo2, 0)
            oa2[s1 - lo2:] = o1[b, s1:t0 + TC].reshape(-1, 512)
            m["oa2"] = oa2
        maps.append(m)
    return maps


def _declare_p2(nc, NTM, with_oa):
    d = {}
    def inp(name, shape):
        d[name] = nc.dram_tensor(name, list(shape), F32, kind="ExternalInput").ap()
    inp("x2", [(NTM + 3) * TW, D])
    if with_oa:
        inp("oa2", [(NTM + 1) * TW, 512])
    inp("validc", [128, (NTM + 3) * TB])
    inp("w_in", [D, D_IN]); inp("w_ba", [512, D]); inp("w_bb", [512, D]); inp("w_out", [D, D])
    inp("w_up", [D, 2 * D_FF]); inp("w_down", [D_FF, D]); inp("nwm", [128, 8]); inp("nwf", [128, 8]); inp("gnw", [128, 1])
    inp("biasT", [128, 8, 640]); inp("cfw", [128, 44, 3]); inp("cfb", [128, 44]); inp("wfin", [128, D]); inp("cst2", [128, NCONST])
    d["xmid"] = nc.dram_tensor("xmid", [(NTM + 1) * TW, D], F32, kind="Internal").ap()
    d["out2"] = nc.dram_tensor("out2", [NTM * TW, D], F32, kind="ExternalOutput").ap()
    return d


def _emit_phase2(nc, P, d, NTM, oa_ap, o_all=None, qsel_d=None, RT=None):
    es_a = ExitStack()
    build_phase2a(nc, P, Ctx(nc, es_a, P), NTM, d["x2"], oa_ap, d["validc"], d["w_in"], d["w_ba"], d["w_bb"], d["w_out"],
                  d["nwm"], d["gnw"], d["biasT"], d["cst2"], d["xmid"], o_all=o_all, qsel_d=qsel_d, RT=RT)
    es_a.close()
    P.pool_hold = False
    P.barrier()
    es_b = ExitStack()
    build_phase2b(nc, P, Ctx(nc, es_b, P), NTM, d["xmid"], d["w_up"], d["w_down"], d["nwf"], d["cfw"], d["cfb"], d["wfin"],
                  d["cst2"], d["out2"])
    es_b.close()


def build_p2_program(T):
    NTM = (T // 4) // TW
    nc = bass.Bass("TRN2", target_bir_lowering=False)
    d = _declare_p2(nc, NTM, True)
    P = Prog(nc)
    _emit_phase2(nc, P, d, NTM, d["oa2"])
    P.finish()
    es = ExitStack()
    P.emit(es)
    es.close()
    return nc, P


def run_phase2(inp, o1, T):
    nc, P = build_p2_program(T)
    maps = _phase2_inputs(inp, o1, T)
    res = run_bass_kernel_spmd(nc, maps, core_ids=list(range(8)))
    TC = T // 4
    out = np.zeros((2, T, D), np.float32)
    for core in range(8):
        out[core // 4, (core % 4) * TC:(core % 4 + 1) * TC] = res.results[core]["out2"]
    return out


T_FULL = 16384


def build_fused_program(T):
    NTM = (T // 4) // TW
    RT = T + TW
    nc = bass.Bass("TRN2", target_bir_lowering=False)
    x1 = nc.dram_tensor("x1", [T, D], F32, kind="ExternalInput").ap()
    w1 = nc.dram_tensor("w1", [D, 386], F32, kind="ExternalInput").ap()
    cw1 = nc.dram_tensor("cw1", [128, 12], F32, kind="ExternalInput").ap()
    sc1 = nc.dram_tensor("sc1", [128, 2], F32, kind="ExternalInput").ap()
    nw1 = nc.dram_tensor("nw1", [128, 8], F32, kind="ExternalInput").ap()
    cst = nc.dram_tensor("cst", [128, NCONST], F32, kind="ExternalInput").ap()
    qsel_d = nc.dram_tensor("qsel", [128, 8], F32, kind="ExternalInput").ap()
    o_loc = nc.dram_tensor("o_loc", [RT, 128], F32, kind="Internal").ap()
    CR = RT
    chunks = [(r0, min(CR, RT - r0)) for r0 in range(0, RT, CR)]
    o_alls = [nc.dram_tensor("o_all%d" % i, [8 * n, 128], F32, kind="Internal").ap() for i, (r0, n) in enumerate(chunks)]
    d = _declare_p2(nc, NTM, False)
    P = Prog(nc)
    es1 = ExitStack()
    A1 = Ctx(nc, es1, P)
    zt = A1.sb([128, 128], F32, "zt")
    P.op("pool", lambda e: e.memset(zt[:], 0.0), writes=["zt"])
    for i in range(TW // 128):
        P.op("sp", lambda e: e.dma_start(out=o_loc[i * 128:(i + 1) * 128, :], in_=zt[:]), reads=["zt"], chan="zt")
    build_phase1(nc, es1, P, A1, T, x1, w1, cw1, sc1, nw1, cst, o_loc[TW:RT, :])
    es1.close()
    P.barrier()
    for i, (r0, n) in enumerate(chunks):
        P.op("pool", lambda e: e.collective_compute("AllGather", ALU.bypass, replica_groups=[list(range(8))],
                                                    ins=[o_loc[r0:r0 + n, :]], outs=[o_alls[i][:, :]]),
             writes=["oall"], chan="cc", inc_override=1)
    P.pool_hold = True
    _emit_phase2(nc, P, d, NTM, None, o_all=(o_alls, chunks, CR), qsel_d=qsel_d, RT=RT)
    P.finish()
    es = ExitStack()
    P.emit(es)
    es.close()
    return nc, P


def build_fused_nocc(T):
    NTM = (T // 4) // TW
    RT = T + TW
    nc = bass.Bass("TRN2", target_bir_lowering=False)
    x1 = nc.dram_tensor("x1", [T, D], F32, kind="ExternalInput").ap()
    w1a = nc.dram_tensor("w1a", [4, D, 386], F32, kind="ExternalInput").ap()
    cw1a = nc.dram_tensor("cw1a", [4, 128, 12], F32, kind="ExternalInput").ap()
    sc1a = nc.dram_tensor("sc1a", [4, 128, 2], F32, kind="ExternalInput").ap()
    nw1 = nc.dram_tensor("nw1", [128, 8], F32, kind="ExternalInput").ap()
    cst = nc.dram_tensor("cst", [128, NCONST], F32, kind="ExternalInput").ap()
    qsel_d = nc.dram_tensor("qsel", [128, 4], F32, kind="ExternalInput").ap()
    o_loc = nc.dram_tensor("o_loc", [RT, 512], F32, kind="Internal").ap()
    hts = nc.dram_tensor("hts", [T // 512, 128, NCH, 512], BF16, kind="Internal").ap()
    d = _declare_p2(nc, NTM, False)
    P = Prog(nc)
    for h in range(4):
        es1 = ExitStack()
        A1 = Ctx(nc, es1, P)
        if h == 0:
            zt = A1.sb([128, 512], F32, "zt")
            P.op("pool", lambda e: e.memset(zt[:], 0.0), writes=["zt"])
            for i in range(TW // 128):
                P.op("sp", lambda e: e.dma_start(out=o_loc[i * 128:(i + 1) * 128, :], in_=zt[:]), reads=["zt"], chan="zt")
        build_phase1(nc, es1, P, A1, T, x1, w1a[h], cw1a[h], sc1a[h], nw1, cst, o_loc[TW:RT, h * 128:(h + 1) * 128],
                     hts=hts, hts_mode=("save" if h == 0 else "load"))
        es1.close()
        P.barrier()
    _emit_phase2(nc, P, d, NTM, None, o_all=o_loc, qsel_d=qsel_d, RT=None)
    P.finish()
    es = ExitStack()
    P.emit(es)
    es.close()
    return nc, P


def run_fused_nocc(inp, T):
    nc, P = build_fused_nocc(T)
    m1 = _phase1_inputs(inp, T)
    m2 = _phase2_inputs(inp, None, T)
    maps = []
    for core in range(8):
        b, q = core // 4, core % 4
        m = dict(m2[core])
        m["x1"] = m1[core]["x1"]
        m["nw1"] = m1[core]["nw1"]
        m["cst"] = m1[core]["cst"]
        m["w1a"] = np.stack([m1[4 * b + h]["w1"] for h in range(4)])
        m["cw1a"] = np.stack([m1[4 * b + h]["cw1"] for h in range(4)])
        m["sc1a"] = np.stack([m1[4 * b + h]["sc1"] for h in range(4)])
        qs = np.zeros((128, 4), np.float32)
        qs[:, q] = 1.0
        m["qsel"] = qs
        maps.append(m)
    res = run_bass_kernel_spmd(nc, maps, core_ids=list(range(8)))
    TC = T // 4
    out = np.zeros((2, T, D), np.float32)
    for core in range(8):
        out[core // 4, (core % 4) * TC:(core % 4 + 1) * TC] = res.results[core]["out2"]
    return out


def run_fused(inp, T):
    nc, P = build_fused_program(T)
    m1 = _phase1_inputs(inp, T)
    m2 = _phase2_inputs(inp, None, T)
    maps = []
    for core in range(8):
        m = dict(m1[core])
        m.update(m2[core])
        qs = np.zeros((128, 8), np.float32)
        qs[:, core] = 1.0
        m["qsel"] = qs
        maps.append(m)
    res = run_bass_kernel_spmd(nc, maps, core_ids=list(range(8)))
    TC = T // 4
    out = np.zeros((2, T, D), np.float32)
    for core in range(8):
        out[core // 4, (core % 4) * TC:(core % 4 + 1) * TC] = res.results[core]["out2"]
    return out


def kernel(**inputs):
    return run_fused_nocc(inputs, T_FULL)
```
